# Optimizing a Trainium2 kernel written in Bass

```python
import math, functools
import jax, jax.numpy as jnp
from jax import lax
import numpy as np

D_MODEL = 2048
BATCH = 1
SEQ = 8192
DEPTH = 4

GRID_W = 64
CTX_LEN = 256
F32 = jnp.float32

D_MIX = D_MODEL
H_ATT = 6
DH_ATT = 64
DV_ATT = 2 * DH_ATT
W_ATT = H_ATT * DV_ATT
Q_BLOCK = 128
ROPE_THETA = 10000.0
H_RWKV = 10
N_RWKV = 64
W_RWKV = H_RWKV * N_RWKV
R_DECAY = 64
R_ICL = 64
R_GATE = 128
LN_X_EPS = 64e-5
H_SSM = 10
P_SSM = 64
W_SSM = H_SSM * P_SSM
G_SSM = 2
N_SSM = 128
CONV_W = 3
CHUNK = 128
CONV_CH = W_SSM + 2 * G_SSM * N_SSM
D_FF = -(-8 * D_MODEL // (3 * 256)) * 256
EPS = 1e-6

ATT_COLS = 3 * W_ATT
RWKV_COLS = 3 * W_RWKV + 2 * R_DECAY + 2 * R_ICL + R_GATE
SSM_COLS = W_SSM + CONV_CH + 2 * H_SSM
IN_COLS = ATT_COLS + RWKV_COLS + SSM_COLS

kernel_name = 'hybrid_diffattn_rwkv7_ssd_flow_block'


def split_cols(u, sizes):
    return jnp.split(u, np.cumsum(sizes)[:-1].tolist(), axis=-1)


def flip_t(a):
    return jnp.flip(a, axis=1)


def rms_norm(x, g, eps=EPS):
    xf = x.astype(F32)
    y = xf * lax.rsqrt(jnp.mean(xf * xf, axis=-1, keepdims=True) + eps)
    return (y * g.astype(F32)).astype(x.dtype)


def head_layer_norm(y, g, b, eps):
    yf = y.astype(F32)
    yc = yf - jnp.mean(yf, axis=-1, keepdims=True)
    yn = yc * lax.rsqrt(jnp.mean(yc * yc, axis=-1, keepdims=True) + eps)
    yn = yn.reshape(*y.shape[:-2], -1)
    return (yn * g.astype(F32) + b.astype(F32)).astype(y.dtype)


def l2_normalize(a, eps=1e-12):
    af = a.astype(F32)
    return (af * lax.rsqrt(jnp.sum(af * af, axis=-1, keepdims=True) + eps)).astype(a.dtype)


def modulate(h, shift, scale):
    return h * (1.0 + scale) + shift


def token_shift3(u, mu):
    prev = jnp.pad(u, ((0, 0), (1, 0), (0, 0)))[:, :-1]
    nxt = jnp.pad(u, ((0, 0), (0, 1), (0, 0)))[:, 1:]
    return u + mu[0] * (prev - u) + mu[1] * (nxt - u)


def centred_dwconv(u, w, b):
    y = lax.conv_general_dilated(u, w[:, None, :], window_strides=(1,),
                                 padding=[(CONV_W // 2, CONV_W // 2)],
                                 dimension_numbers=('NWC', 'WIO', 'NWC'),
                                 feature_group_count=u.shape[-1])
    return y + b


def axial_rope_tables(n_tokens):
    rows = n_tokens // GRID_W
    row = jnp.broadcast_to(jnp.arange(rows)[:, None], (rows, GRID_W)).reshape(-1).astype(F32)
    col = jnp.broadcast_to(jnp.arange(GRID_W)[None, :], (rows, GRID_W)).reshape(-1).astype(F32)
    n_freq = DH_ATT // 4
    inv = ROPE_THETA ** (-jnp.arange(n_freq, dtype=F32) / n_freq)
    ang = jnp.concatenate([row[:, None] * inv, col[:, None] * inv], axis=-1)
    return jnp.cos(ang), jnp.sin(ang)


def apply_rope(x, cos, sin):
    half = x.shape[-1] // 2
    x1, x2 = x[..., :half], x[..., half:]
    cs = cos[None, :, None, None, :].astype(x.dtype)
    sn = sin[None, :, None, None, :].astype(x.dtype)
    return jnp.concatenate([x1 * cs - x2 * sn, x2 * cs + x1 * sn], axis=-1)


def diff_attention(q, k, v, lam):
    b, tq = q.shape[:2]
    nb = tq // Q_BLOCK
    qb = jnp.moveaxis(q.reshape(b, nb, Q_BLOCK, *q.shape[2:]), 1, 0)
    scale = DH_ATT ** -0.5

    def one_block(qblk):
        s = jnp.einsum('bqhcd,bkhcd->bhcqk', qblk, k).astype(F32) * scale
        p = jax.nn.softmax(s, axis=-1)
        a = p[:, :, 0] - lam.astype(F32) * p[:, :, 1]
        return jnp.einsum('bhqk,bkhd->bqhd', a.astype(v.dtype), v)

    o = lax.map(one_block, qb)
    return jnp.moveaxis(o, 0, 1).reshape(b, tq, *v.shape[2:])


def wkv7_scan(r, decay, k, v, kk, bvec, s0):
    def step(S, inp):
        r_t, w_t, k_t, v_t, kk_t, b_t = inp
        sa = jnp.einsum('bhvk,bhk->bhv', S, kk_t)
        S = S * w_t[:, :, None, :] - sa[..., None] * b_t[:, :, None, :] + v_t[..., None] * k_t[:, :, None, :]
        return S, jnp.einsum('bhvk,bhk->bhv', S, r_t)

    xs = tuple(jnp.moveaxis(a.astype(F32), 1, 0) for a in (r, decay, k, v, kk, bvec))
    sT, ys = lax.scan(step, s0.astype(F32), xs)
    return jnp.moveaxis(ys, 0, 1).astype(r.dtype), sT


def ssd_chunked(x, dt, Bm, Cm, h0, A):
    b, T, H, P = x.shape
    nc = T // CHUNK
    rep = H // Bm.shape[2]
    chunked = lambda a: a.astype(F32).reshape(b, nc, CHUNK, *a.shape[2:])
    xc, dtc = chunked(x), chunked(dt)
    Bc = chunked(jnp.repeat(Bm, rep, axis=2))
    Cc = chunked(jnp.repeat(Cm, rep, axis=2))
    acum = jnp.cumsum(dtc * A.astype(F32), axis=2)
    lower = jnp.tril(jnp.ones((CHUNK, CHUNK), bool))[None, None, :, :, None]
    seg = acum[:, :, :, None, :] - acum[:, :, None, :, :]
    decay_ij = jnp.exp(jnp.where(lower, seg, -jnp.inf))
    cb = jnp.einsum('bcihn,bcjhn->bcijh', Cc, Bc)
    y_diag = jnp.einsum('bcijh,bcjhp->bcihp', cb * decay_ij * dtc[:, :, None, :, :], xc)
    to_end = jnp.exp(acum[:, :, -1:, :] - acum) * dtc
    states = jnp.einsum('bcjhn,bcjh,bcjhp->bchpn', Bc, to_end, xc)
    chunk_decay = jnp.exp(acum[:, :, -1, :])

    def step(h, inp):
        s, d = inp
        return h * d[:, :, None, None] + s, h

    hT, h_in = lax.scan(step, h0.astype(F32), (jnp.moveaxis(states, 1, 0), jnp.moveaxis(chunk_decay, 1, 0)))
    h_in = jnp.moveaxis(h_in, 0, 1)
    y_off = jnp.einsum('bcihn,bchpn->bcihp', Cc * jnp.exp(acum)[..., None], h_in)
    return (y_diag + y_off).reshape(b, T, H, P).astype(x.dtype), hT


def bidirectional_scan(scan_fwd, scan_bwd, ctx_fwd, lat_fwd, ctx_bwd, lat_bwd, state0):
    yc_f, sc_f = scan_fwd(*ctx_fwd, state0)
    yl_f, _ = scan_fwd(*lat_fwd, sc_f)
    yc_b, sc_b = scan_bwd(*[flip_t(a) for a in ctx_bwd], state0)
    yl_b, _ = scan_bwd(*[flip_t(a) for a in lat_bwd], sc_b)
    return yc_f + flip_t(yc_b), yl_f + flip_t(yl_b)


def swiglu(h, w_in, w_out):
    gate, up = jnp.split(h @ w_in, 2, axis=-1)
    return (jax.nn.silu(gate) * up) @ w_out


def hybrid_mixer(hl, hc, cos, sin, layer, need_ctx_out,
                 w_in, w_out, qk_gain, lam_q, lam_k, subln_g,
                 shift_mu, w0, w_up, a0, a_up, g_up, k_k, k_a, r_k, lnx_g, lnx_b,
                 conv_w, conv_b, dt_bias, a_log, d_skip, ssm_norm_g):
    bsz = hl.shape[0]
    ul = hl @ w_in
    uc = hc @ w_in
    att_l, rw_l, ssm_l = split_cols(ul, (ATT_COLS, RWKV_COLS, SSM_COLS))
    att_c, rw_c, ssm_c = split_cols(uc, (ATT_COLS, RWKV_COLS, SSM_COLS))

    def att_proj(u):
        q, k, v = split_cols(u, (W_ATT, W_ATT, W_ATT))
        k = rms_norm(k.reshape(*k.shape[:-1], H_ATT, 2, DH_ATT), qk_gain[1])
        return q, k, v.reshape(*v.shape[:-1], H_ATT, DV_ATT)

    q_norm = lambda q: rms_norm(q.reshape(*q.shape[:-1], H_ATT, 2, DH_ATT), qk_gain[0])
    ql_raw, kl, vl = att_proj(att_l)
    qc_raw, kc, vc = att_proj(att_c)
    ql = apply_rope(q_norm(ql_raw), cos, sin)
    kl = apply_rope(kl, cos, sin)
    lam_init = 0.8 - 0.6 * math.exp(-0.3 * layer)
    lam = jnp.exp(jnp.sum(lam_q[0] * lam_k[0])) - jnp.exp(jnp.sum(lam_q[1] * lam_k[1])) + lam_init

    def att_out(o):
        return (rms_norm(o, subln_g) * (1.0 - lam_init)).reshape(*o.shape[:2], W_ATT)

    a_lat = att_out(diff_attention(ql, jnp.concatenate([kc, kl], axis=1),
                                   jnp.concatenate([vc, vl], axis=1), lam))

    heads_r = lambda a: a.reshape(*a.shape[:-1], H_RWKV, N_RWKV)

    def rwkv_prep(u):
        u = token_shift3(u, shift_mu)
        r, k, v, wdf, wdb, adf, adb, gd = split_cols(
            u, (W_RWKV, W_RWKV, W_RWKV, R_DECAY, R_DECAY, R_ICL, R_ICL, R_GATE))
        kk = l2_normalize(heads_r(k * k_k))
        dirs = []
        for d, (wd, ad) in enumerate(((wdf, adf), (wdb, adb))):
            w_log = -jax.nn.softplus(-(w0[d] + jnp.tanh(wd) @ w_up[d])) - 0.5
            icl = jax.nn.sigmoid(a0[d] + ad @ a_up[d])
            k_d = k * (1.0 + (icl - 1.0) * k_a)
            dirs.append((heads_r(jnp.exp(-jnp.exp(w_log))), heads_r(k_d), kk * heads_r(icl)))
        return heads_r(r), heads_r(v), kk, gd, dirs

    rc, rvc, kkc, gdc, dirs_c = rwkv_prep(rw_c)
    rl, rvl, kkl, gdl, dirs_l = rwkv_prep(rw_l)
    s0 = jnp.zeros((bsz, H_RWKV, N_RWKV, N_RWKV), F32)
    ry_c, ry_l = bidirectional_scan(
        wkv7_scan, wkv7_scan,
        (rc, dirs_c[0][0], dirs_c[0][1], rvc, kkc, dirs_c[0][2]),
        (rl, dirs_l[0][0], dirs_l[0][1], rvl, kkl, dirs_l[0][2]),
        (rc, dirs_c[1][0], dirs_c[1][1], rvc, kkc, dirs_c[1][2]),
        (rl, dirs_l[1][0], dirs_l[1][1], rvl, kkl, dirs_l[1][2]), s0)

    def rwkv_out(r, v, gd, y, dirs):
        bonus = jnp.sum(r * (dirs[0][1] + dirs[1][1]) * r_k, axis=-1, keepdims=True) * v
        o = head_layer_norm(y, lnx_g, lnx_b, LN_X_EPS) + bonus.reshape(*r.shape[:2], W_RWKV)
        return o * (jax.nn.sigmoid(gd) @ g_up)

    r_lat = rwkv_out(rl, rvl, gdl, ry_l, dirs_l)

    def ssm_prep(u):
        z, xbc, dtf, dtb = split_cols(u, (W_SSM, CONV_CH, H_SSM, H_SSM))
        xbc = jax.nn.silu(centred_dwconv(xbc, conv_w, conv_b))
        xs, bm, cm = split_cols(xbc, (W_SSM, G_SSM * N_SSM, G_SSM * N_SSM))
        b, t = u.shape[:2]
        return (z, xs.reshape(b, t, H_SSM, P_SSM), bm.reshape(b, t, G_SSM, N_SSM),
                cm.reshape(b, t, G_SSM, N_SSM),
                jax.nn.softplus(dtf + dt_bias[0]), jax.nn.softplus(dtb + dt_bias[1]))

    zc, xsc, bmc, cmc, dtc_f, dtc_b = ssm_prep(ssm_c)
    zl, xsl, bml, cml, dtl_f, dtl_b = ssm_prep(ssm_l)
    h0 = jnp.zeros((bsz, H_SSM, P_SSM, N_SSM), F32)
    sy_c, sy_l = bidirectional_scan(
        functools.partial(ssd_chunked, A=-jnp.exp(a_log[0])),
        functools.partial(ssd_chunked, A=-jnp.exp(a_log[1])),
        (xsc, dtc_f, bmc, cmc), (xsl, dtl_f, bml, cml),
        (xsc, dtc_b, bmc, cmc), (xsl, dtl_b, bml, cml), h0)

    def ssm_out(z, xs, y):
        b, t = z.shape[:2]
        y = y + (d_skip[0] + d_skip[1])[:, None] * xs
        yg = (y.reshape(b, t, W_SSM) * jax.nn.silu(z)).reshape(b, t, G_SSM, W_SSM // G_SSM)
        return rms_norm(yg, ssm_norm_g.reshape(G_SSM, W_SSM // G_SSM)).reshape(b, t, W_SSM)

    s_lat = ssm_out(zl, xsl, sy_l)

    mix_l = jnp.concatenate([a_lat, r_lat, s_lat], axis=-1) @ w_out
    if not need_ctx_out:
        return mix_l, None
    a_ctx = att_out(diff_attention(q_norm(qc_raw), kc, vc, lam))
    r_ctx = rwkv_out(rc, rvc, gdc, ry_c, dirs_c)
    s_ctx = ssm_out(zc, xsc, sy_c)
    mix_c = jnp.concatenate([a_ctx, r_ctx, s_ctx], axis=-1) @ w_out
    return mix_l, mix_c


def setup_inputs(seed: int = 0) -> dict:
    key = jax.random.key(seed)
    ks = iter(jax.random.split(key, 40))
    nrm = lambda shape, s: s * jax.random.normal(next(ks), shape, F32)
    L = DEPTH
    dt0 = jnp.exp(jax.random.uniform(next(ks), (L, 2, H_SSM), F32, math.log(1e-3), math.log(1e-1)))
    return {
        'x': nrm((BATCH, SEQ, D_MODEL), 1.0),
        'c': nrm((BATCH, D_MODEL), 1.0),
        'ctx': nrm((BATCH, CTX_LEN, D_MODEL), 1.0),
        'c_ctx': nrm((D_MODEL,), 1.0),
        'ada_w': nrm((L, D_MODEL, 6 * D_MODEL), 0.5 * D_MODEL ** -0.5),
        'ada_b': nrm((L, 6 * D_MODEL), 0.02),
        'norm1_g': 1.0 + nrm((L, D_MODEL), 0.02),
        'norm2_g': 1.0 + nrm((L, D_MODEL), 0.02),
        'w_in': nrm((L, D_MODEL, IN_COLS), D_MODEL ** -0.5),
        'w_out': nrm((L, D_MIX, D_MODEL), D_MIX ** -0.5),
        'qk_gain': 1.0 + nrm((L, 2, DH_ATT), 0.02),
        'lam_q': nrm((L, 2, DH_ATT), 0.1),
        'lam_k': nrm((L, 2, DH_ATT), 0.1),
        'subln_g': 1.0 + nrm((L, DV_ATT), 0.02),
        'shift_mu': jax.random.uniform(next(ks), (L, 2, RWKV_COLS), F32, 0.0, 0.5),
        'w0': jax.random.uniform(next(ks), (L, 2, W_RWKV), F32, -6.0, -1.0),
        'w_up': nrm((L, 2, R_DECAY, W_RWKV), 0.5 * R_DECAY ** -0.5),
        'a0': nrm((L, 2, W_RWKV), 0.1),
        'a_up': nrm((L, 2, R_ICL, W_RWKV), 0.5 * R_ICL ** -0.5),
        'g_up': nrm((L, R_GATE, W_RWKV), R_GATE ** -0.5),
        'k_k': 0.85 + nrm((L, W_RWKV), 0.02),
        'k_a': 1.0 + nrm((L, W_RWKV), 0.02),
        'r_k': nrm((L, H_RWKV, N_RWKV), 0.1),
        'lnx_g': 1.0 + nrm((L, W_RWKV), 0.02),
        'lnx_b': nrm((L, W_RWKV), 0.02),
        'conv_w': nrm((L, CONV_W, CONV_CH), CONV_W ** -0.5),
        'conv_b': nrm((L, CONV_CH), 0.02),
        'dt_bias': dt0 + jnp.log(-jnp.expm1(-dt0)),
        'a_log': jnp.log(jax.random.uniform(next(ks), (L, 2, H_SSM), F32, 1.0, 16.0)),
        'd_skip': 1.0 + nrm((L, 2, H_SSM), 0.02),
        'ssm_norm_g': 1.0 + nrm((L, W_SSM), 0.02),
        'w_ffn_in': nrm((L, D_MODEL, 2 * D_FF), D_MODEL ** -0.5),
        'w_ffn_out': nrm((L, D_FF, D_MODEL), D_FF ** -0.5),
    }


def reference(x, c, ctx, c_ctx, ada_w, ada_b, norm1_g, norm2_g, w_in, w_out,
              qk_gain, lam_q, lam_k, subln_g,
              shift_mu, w0, w_up, a0, a_up, g_up, k_k, k_a, r_k, lnx_g, lnx_b,
              conv_w, conv_b, dt_bias, a_log, d_skip, ssm_norm_g,
              w_ffn_in, w_ffn_out):
    cos, sin = axial_rope_tables(x.shape[1])
    xl, xc = x, ctx
    for l in range(DEPTH):
        last = l == DEPTH - 1
        mod_l = jnp.split((jax.nn.silu(c) @ ada_w[l] + ada_b[l])[:, None, :], 6, axis=-1)
        mod_c = jnp.split(jax.nn.silu(c_ctx) @ ada_w[l] + ada_b[l], 6, axis=-1)
        hl = modulate(rms_norm(xl, norm1_g[l]), mod_l[0], mod_l[1])
        hc = modulate(rms_norm(xc, norm1_g[l]), mod_c[0], mod_c[1])
        mix_l, mix_c = hybrid_mixer(
            hl, hc, cos, sin, l, not last,
            w_in[l], w_out[l], qk_gain[l], lam_q[l], lam_k[l], subln_g[l],
            shift_mu[l], w0[l], w_up[l], a0[l], a_up[l], g_up[l], k_k[l], k_a[l], r_k[l],
            lnx_g[l], lnx_b[l], conv_w[l], conv_b[l], dt_bias[l], a_log[l], d_skip[l], ssm_norm_g[l])
        xl = xl + mod_l[2] * mix_l
        xl = xl + mod_l[5] * swiglu(modulate(rms_norm(xl, norm2_g[l]), mod_l[3], mod_l[4]),
                                    w_ffn_in[l], w_ffn_out[l])
        if not last:
            xc = xc + mod_c[2] * mix_c
            xc = xc + mod_c[5] * swiglu(modulate(rms_norm(xc, norm2_g[l]), mod_c[3], mod_c[4]),
                                        w_ffn_in[l], w_ffn_out[l])
    return xl
```

```python
import math
from contextlib import ExitStack
import numpy as np
import ml_dtypes
import concourse.bass as bass
import concourse.mybir as mybir
from concourse.bass_utils import run_bass_kernel_spmd

F32 = mybir.dt.float32
BF16 = mybir.dt.bfloat16
AF = mybir.ActivationFunctionType
ALU = mybir.AluOpType
AX = mybir.AxisListType

COMPUTE = ("pe", "act", "dve", "pool")


class _Ins:
    __slots__ = ("eng", "fn", "waits", "is_dma", "tok", "idx", "need_inc")

    def __init__(self, eng, fn, is_dma):
        self.eng, self.fn, self.is_dma = eng, fn, is_dma
        self.waits = []
        self.tok = None
        self.need_inc = False


class _Trk:
    __slots__ = ("writers", "readers", "const")

    def __init__(self):
        self.writers = []
        self.readers = {}
        self.const = False


class Prog:
    def __init__(self, n_dma_sems=24):
        self.nc = bass.Bass("TRN2", target_bir_lowering=False)
        self.es = ExitStack()
        self.ins = []
        self.trk = {}
        self.n_dma_sems = n_dma_sems
        self.dma_rr = {"sp": 0, "pool": 0, "act": 0}
        self.dma_prev = {}
        self.out_dmas = []
        self._uid = 0
        self.rings = {}
        self.debug = False
        self.dbg_out = {}

    def sb(self, name, shape, dtype=F32):
        t = self.es.enter_context(self.nc.sbuf_tensor(name, list(shape), dtype))
        return t

    def ps(self, name, shape, dtype=F32):
        t = self.es.enter_context(self.nc.psum_tensor(name, list(shape), dtype))
        return t

    def dram(self, name, shape, dtype=F32, kind="Internal"):
        t = self.nc.dram_tensor(name, list(shape), dtype, kind=kind)
        if kind == "ExternalInput":
            self._t(name).const = True
        return t

    def ring(self, name, n, shape, dtype=F32, psum=False):
        tiles = [(self.ps if psum else self.sb)(f"{name}{i}", shape, dtype) for i in range(n)]
        self.rings[name] = [tiles, 0]
        return name

    def nxt(self, name):
        r = self.rings[name]
        t = r[0][r[1] % len(r[0])]
        r[1] += 1
        return t

    def _t(self, name):
        t = self.trk.get(name)
        if t is None:
            t = self.trk[name] = _Trk()
        return t

    def _rec(self, eng, fn, reads, writes, is_dma=False):
        x = _Ins(eng, fn, is_dma)
        x.idx = len(self.ins)
        deps = []
        rn = []
        for a in reads:
            if a is None or isinstance(a, (int, float)):
                continue
            rn.append(a.tensor.name)
        wn = [a.tensor.name for a in writes]
        raw = set()
        for n in rn:
            deps.extend(self._t(n).writers)
            for w_ in self._t(n).writers:
                raw.add(w_.idx)
        for n in wn:
            t = self._t(n)
            deps.extend(t.writers)
            deps.extend(t.readers.values())
        seen = set()
        for d in deps:
            if d.idx in seen:
                continue
            seen.add(d.idx)
            if (not d.is_dma) and (not is_dma) and d.eng == eng:
                if eng == "pe" or d.idx not in raw:
                    continue
            x.waits.append(d)
            d.need_inc = True
        for n in rn:
            t = self._t(n)
            if t.const:
                continue
            key = ("dma", x.idx) if is_dma else eng
            t.readers[key] = x
            if len(t.readers) > 48:
                ks = [k for k in t.readers if isinstance(k, tuple)]
                pass
        for n in wn:
            t = self._t(n)
            t.writers = [x]
            t.readers = {}
        self.ins.append(x)
        return x

    def mm(self, out, lhsT, rhs, start=True, stop=True):
        rd = [lhsT, rhs] + ([] if start else [out])
        return self._rec("pe", lambda e: e.matmul(out, lhsT, rhs, start=start, stop=stop), rd, [out])

    def transpose(self, out, in_, ident):
        return self._rec("pe", lambda e: e.transpose(out, in_, ident), [in_, ident], [out])

    def act(self, out, in_, func, bias=None, scale=None, accum=None, eng="act"):
        kw = {}
        if bias is not None:
            kw["bias"] = bias
        if scale is not None:
            kw["scale"] = scale
        if accum is not None:
            kw["accum_out"] = accum
        rd = [in_, bias if hasattr(bias, "tensor") else None, scale if hasattr(scale, "tensor") else None]
        wr = [out] + ([accum] if accum is not None else [])
        return self._rec("act", lambda e: e.activation(out, in_, func, **kw), rd, wr)

    def tt(self, out, a, b, op, eng="dve"):
        return self._rec(eng, lambda e: e.tensor_tensor(out, a, b, op), [a, b], [out])

    def ts(self, out, a, s1, op0, s2=None, op1=None, eng="dve", accum=None):
        rd = [a, s1 if hasattr(s1, "tensor") else None, s2 if hasattr(s2, "tensor") else None]
        if op1 is None:
            return self._rec(eng, lambda e: e.tensor_scalar(out, a, s1, None, op0), rd, [out])
        kw = {}
        wr = [out]
        if accum is not None:
            kw["accum_out"] = accum
            wr.append(accum)
        return self._rec(eng, lambda e: e.tensor_scalar(out, a, s1, s2, op0, op1, **kw), rd, wr)

    def stt(self, out, in0, scalar, in1, op0, op1, eng="dve"):
        rd = [in0, in1, scalar if hasattr(scalar, "tensor") else None]
        return self._rec(eng, lambda e: e.scalar_tensor_tensor(out, in0, scalar, in1, op0, op1), rd, [out])

    def copy(self, out, in_, eng="dve"):
        if eng == "act":
            return self._rec("act", lambda e: e.copy(out, in_), [in_], [out])
        return self._rec(eng, lambda e: e.tensor_copy(out, in_), [in_], [out])

    def memset(self, out, val, eng="dve"):
        return self._rec(eng, lambda e: e.memset(out, val), [], [out])

    def reduce(self, out, in_, op, axis=AX.X, eng="dve"):
        return self._rec(eng, lambda e: e.tensor_reduce(out, in_, axis, op), [in_], [out])

    def scan(self, out, d0, d1, init, op0, op1):
        return self._rec("dve", lambda e: e.tensor_tensor_scan(out, d0, d1, init, op0, op1), [d0, d1], [out])

    def recip(self, out, in_):
        return self._rec("dve", lambda e: e.reciprocal(out, in_), [in_], [out])

    def dma(self, out, in_, q="sp", is_out=False):
        x = self._rec(q, lambda e: e.dma_start(out=out, in_=in_), [in_], [out], is_dma=True)
        k = self.dma_rr[q] % self.n_dma_sems
        self.dma_rr[q] += 1
        key = (q, k)
        prev = self.dma_prev.get(key)
        val = (prev.tok[1] if prev is not None else 0) + 16
        x.tok = (key, val)
        if prev is not None and prev not in x.waits:
            x.waits.append(prev)
        self.dma_prev[key] = x
        if is_out:
            self.out_dmas.append(x)
        return x

    def dbg(self, name, ap):
        if not getattr(self, "debug", False):
            return
        if name in self.dbg_out:
            return
        t = self.dram("dbg_" + name, list(ap.shape), F32, "ExternalOutput")
        self.dbg_out[name] = t
        self.dma(t.ap(), ap, is_out=True)

    def finish(self):
        nc = self.nc
        engs = {"pe": [], "act": [], "dve": [], "pool": [], "sp": []}
        for x in self.ins:
            engs[x.eng].append(x)
        sems = {}
        for e in COMPUTE:
            sems[e] = self.es.enter_context(nc.semaphore(f"s_{e}"))
            c = 0
            for x in engs[e]:
                if x.is_dma:
                    continue
                if x.need_inc:
                    c += 1
                    x.tok = (e, c)
        for q in ("sp", "pool", "act"):
            for k in range(self.n_dma_sems):
                if (q, k) in self.dma_prev:
                    sems[(q, k)] = self.es.enter_context(nc.semaphore(f"d_{q}{k}"))
        block = self.es.enter_context(nc.Block())
        out_dmas = self.out_dmas

        def emit(engname):
            def body(e):
                waited = {}
                for x in engs[engname]:
                    for d in x.waits:
                        s, v = d.tok
                        if waited.get(s, 0) >= v:
                            continue
                        waited[s] = v
                        e.wait_ge(sems[s], v)
                    r = x.fn(e)
                    if x.is_dma:
                        r.then_inc(sems[x.tok[0]], 16)
                    elif x.need_inc:
                        r.then_inc(sems[x.tok[0]], 1)
                if engname == "sp":
                    for d in out_dmas:
                        s, v = d.tok
                        if waited.get(s, 0) >= v:
                            continue
                        waited[s] = v
                        e.wait_ge(sems[s], v)
            return body

        block.sync(emit("sp"))
        block.tensor(emit("pe"))
        block.scalar(emit("act"))
        block.vector(emit("dve"))
        block.gpsimd(emit("pool"))
        self.es.close()
        return nc


RW_C = 64
E05 = math.exp(-0.5)
LN_X_EPS = 64e-5
PR_MU0, PR_MU1, PR_M2, PR_W0, PR_A0, PR_KK, PR_KA, PR_1MKA, PR_RK, PR_LG, PR_LB = 0, 6, 12, 18, 20, 22, 23, 24, 25, 26, 27
PR_N = 28


def rwkv_consts():
    c = {}
    blk = np.zeros((128, 128), np.float32); blk[:64, :64] = 1; blk[64:, 64:] = 1
    c["blk64"] = blk
    c["ident"] = np.eye(128, dtype=np.float32)
    i2 = np.zeros((128, 64), np.float32); i2[:64] = np.eye(64); i2[64:] = np.eye(64)
    c["i2"] = i2
    p = np.arange(64)[:, None]; f = np.arange(64)[None, :]
    m = np.zeros((2, 64, 5, 64), np.float32)
    m[0, :, 0] = f < p; m[0, :, 1] = f > p; m[0, :, 2] = f > p; m[0, :, 3] = f >= p; m[0, :, 4] = f >= p
    m[1, :, 0] = f > p; m[1, :, 1] = f < p; m[1, :, 2] = f < p; m[1, :, 3] = f <= p; m[1, :, 4] = f <= p
    c["mask5"] = np.concatenate([m, m], axis=1).reshape(2, 128, 320)
    return c


def rwkv_build(P, PS, Tc, Tl, TB=512):
    C = RW_C
    io = {}
    io["uc"] = P.dram("rw_uc", [128, 6, Tc + 2], F32, "ExternalInput")
    io["ul"] = P.dram("rw_ul", [128, 6, Tl + 2], F32, "ExternalInput")
    io["pr"] = P.dram("rw_pr", [128, PR_N], F32, "ExternalInput")
    io["wup"] = P.dram("rw_wup", [128, 128], F32, "ExternalInput")
    io["aup"] = P.dram("rw_aup", [128, 128], F32, "ExternalInput")
    io["gup"] = P.dram("rw_gup", [128, 128], F32, "ExternalInput")
    io["blk64"] = P.dram("c_blk64", [128, 128], F32, "ExternalInput")
    io["ident"] = P.dram("c_ident", [128, 128], F32, "ExternalInput")
    io["i2"] = P.dram("c_i2", [128, 64], F32, "ExternalInput")
    io["mask5"] = P.dram("c_mask5", [2, 128, 320], F32, "ExternalInput")
    io["oc"] = P.dram("rw_oc", [128, Tc], F32, "ExternalOutput")
    io["ol"] = P.dram("rw_ol", [128, Tl], F32, "ExternalOutput")
    yf = {"c": P.dram("rw_yfc", [128, Tc], F32), "l": P.dram("rw_yfl", [128, Tl], F32)}
    usrc = {"c": io["uc"], "l": io["ul"]}
    odst = {"c": io["oc"], "l": io["ol"]}

    def cst(name, shape):
        t = P.sb("k_" + name, shape)
        P.dma(t[:], io[name].ap())
        return t
    pr = cst("pr", [128, PR_N]); wup = cst("wup", [128, 128]); aup = cst("aup", [128, 128]); gup = cst("gup", [128, 128])
    blk = cst("blk64", [128, 128]); ident = cst("ident", [128, 128]); i2 = cst("i2", [128, 64])
    mask5 = []
    for d in range(2):
        t = P.sb(f"k_mask5_{d}", [128, 320])
        P.dma(t[:], io["mask5"].ap()[d])
        mask5.append(t)
    rst = P.sb("k_rst", [128, TB])
    P.memset(rst[:], 1.0)
    P.memset(rst[:].rearrange("p (c t) -> p c t", t=C)[:, :, 0:1], 0.0)
    pc = lambda i: pr[:, i:i + 1]
    P.ts(pr[:, PR_M2:PR_M2 + 6], pr[:, PR_MU0:PR_MU0 + 6], -1.0, ALU.mult, 1.0, ALU.add)
    P.tt(pr[:, PR_M2:PR_M2 + 6], pr[:, PR_M2:PR_M2 + 6], pr[:, PR_MU1:PR_MU1 + 6], ALU.subtract)
    P.ts(pr[:, PR_1MKA:PR_1MKA + 1], pr[:, PR_KA:PR_KA + 1], -1.0, ALU.mult, 1.0, ALU.add)

    for nm in ("u6",):
        P.ring("rw_" + nm, 1, [128, 6, TB + 2])
    for nm in ("xs", ):
        P.ring("rw_" + nm, 1, [128, 6, TB])
    for nm in ("tw", "sg", "kx", "rn", "kk", "g", "lw0", "lw1", "ic0", "ic1", "k0", "k1", "b0", "b1"):
        P.ring("rw_" + nm, 1, [128, TB])
    for nm in ("sq", "f1", "t1"):
        P.ring("rw_" + nm, 2, [128, TB])
    for nm in ("cw", "cwb", "ep", "en", "ea", "At", "Bt", "Kt", "Rt", "yT", "t2", "t3", "t4", "yfl"):
        P.ring("rw_" + nm, 1, [128, TB])
    P.ring("rw_tok", 10, [128, 3, 64])
    P.ring("rw_M5", 10, [128, 5, 64])
    P.ring("rw_TT", 10, [128, 64])
    P.ring("rw_PP", 4, [128, 128])
    P.ring("rw_Xs", 3, [128, 64])
    P.ring("rw_Us", 3, [128, 64])
    S = [P.sb("rw_S0", [128, 64]), P.sb("rw_S1", [128, 64])]
    Stmp = P.sb("rw_Stmp", [128, 64])
    ps_tok, ps5, psL, psT, psX, psU, psY, psS = PS

    def prep(seq, t0, n):
        u6 = P.nxt("rw_u6")
        P.dma(u6[:, :, 0:n + 2], usrc[seq].ap()[:, :, t0:t0 + n + 2])
        xs = P.nxt("rw_xs")
        P.dbg("u6", u6[:, 0, 0:n + 2]); P.dbg("pr", pr[:])
        for j in range(6):
            P.ts(xs[:, j, 0:n], u6[:, j, 1:n + 1], pc(PR_M2 + j), ALU.mult)
            P.stt(xs[:, j, 0:n], u6[:, j, 0:n], pc(PR_MU0 + j), xs[:, j, 0:n], ALU.mult, ALU.add)
            P.stt(xs[:, j, 0:n], u6[:, j, 2:n + 2], pc(PR_MU1 + j), xs[:, j, 0:n], ALU.mult, ALU.add)
        r, k, v, wd, ad, gd = [xs[:, j, 0:n] for j in range(6)]
        o = {"r": r, "v": v}
        tw = P.nxt("rw_tw")[:, 0:n]
        P.act(tw, wd, AF.Tanh)
        for d in range(2):
            hs = slice(64 * d, 64 * d + 64)
            P.mm(psL[:, 0:n], wup[hs, :], tw[hs, :])
            lw = P.nxt(f"rw_lw{d}")[:, 0:n]
            P.act(lw, psL[:, 0:n], AF.Sigmoid, bias=pc(PR_W0 + d))
            P.ts(lw, lw, -E05, ALU.mult)
            o[f"lw{d}"] = lw
            P.mm(psT[:, 0:n], aup[hs, :], ad[hs, :])
            ic = P.nxt(f"rw_ic{d}")[:, 0:n]
            P.act(ic, psT[:, 0:n], AF.Sigmoid, bias=pc(PR_A0 + d))
            o[f"ic{d}"] = ic
        kx = P.nxt("rw_kx")[:, 0:n]
        P.ts(kx, k, pc(PR_KK), ALU.mult)
        sq = P.nxt("rw_sq")[:, 0:n]
        P.tt(sq, kx, kx, ALU.mult)
        P.mm(psX[:, 0:n], blk[:], sq)
        rn = P.nxt("rw_rn")[:, 0:n]
        P.act(rn, psX[:, 0:n], AF.Sqrt, bias=1e-12)
        P.recip(rn, rn)
        kk = P.nxt("rw_kk")[:, 0:n]
        P.tt(kk, kx, rn, ALU.mult)
        o["kk"] = kk
        for d in range(2):
            f1 = P.nxt("rw_f1")[:, 0:n]
            P.ts(f1, o[f"ic{d}"], pc(PR_KA), ALU.mult, pc(PR_1MKA), ALU.add)
            kd = P.nxt(f"rw_k{d}")[:, 0:n]
            P.tt(kd, k, f1, ALU.mult)
            bd = P.nxt(f"rw_b{d}")[:, 0:n]
            P.tt(bd, kk, o[f"ic{d}"], ALU.mult)
            o[f"k{d}"] = kd
            o[f"b{d}"] = bd
        sg = P.nxt("rw_sg")[:, 0:n]
        P.act(sg, gd, AF.Sigmoid)
        P.mm(psU[:, 0:n], gup[:], sg)
        g = P.nxt("rw_g")[:, 0:n]
        P.copy(g, psU[:, 0:n], eng="act")
        o["g"] = g
        for kname in ("r", "v", "kk", "lw0", "lw1", "ic0", "k0", "b0", "g"):
            P.dbg("p_" + kname, o[kname])
        return o

    def scan_block(d, o, n):
        nch = n // C
        lw = o[f"lw{d}"]
        cw = P.nxt("rw_cw")[:, 0:n]
        P.scan(cw, rst[:, 0:n], lw, 0.0, ALU.mult, ALU.add)
        v3 = lambda a: a.rearrange("p (c t) -> p c t", t=C)
        if d == 1:
            cwb = P.nxt("rw_cwb")[:, 0:n]
            P.tt(cwb, lw, cw, ALU.subtract)
            for c in range(nch):
                P.ts(cwb[:, c * C:(c + 1) * C], cwb[:, c * C:(c + 1) * C], cw[:, c * C + C - 1:c * C + C], ALU.add)
            cw = cwb
        ep = P.nxt("rw_ep")[:, 0:n]; en = P.nxt("rw_en")[:, 0:n]; ea = P.nxt("rw_ea")[:, 0:n]
        P.act(ep, cw, AF.Exp)
        P.act(en, cw, AF.Exp, scale=-1.0)
        t1 = P.nxt("rw_t1")[:, 0:n]
        P.tt(t1, cw, lw, ALU.subtract)
        P.act(ea, t1, AF.Exp)
        At = P.nxt("rw_At")[:, 0:n]; Bt = P.nxt("rw_Bt")[:, 0:n]; Kt = P.nxt("rw_Kt")[:, 0:n]; Rt = P.nxt("rw_Rt")[:, 0:n]
        P.stt(At, o["kk"], -1.0, ea, ALU.mult, ALU.mult)
        P.tt(Bt, o[f"b{d}"], en, ALU.mult)
        P.tt(Kt, o[f"k{d}"], en, ALU.mult)
        P.tt(Rt, o["r"], ep, ALU.mult)
        wcol = (C - 1) if d == 0 else 0
        v = o["v"]
        P.dbg(f"cw{d}", cw); P.dbg(f"At{d}", At); P.dbg(f"Bt{d}", Bt); P.dbg(f"Rt{d}", Rt)
        pre = []
        for c in range(nch):
            cs = slice(c * C, (c + 1) * C)
            tok = P.nxt("rw_tok")
            for j, X in enumerate((Bt, Kt, v)):
                for s in range(2):
                    hs = slice(64 * s, 64 * s + 64)
                    P.mm(ps_tok[hs, j * 64:(j + 1) * 64], X[hs, cs], ident[hs, hs])
            P.copy(tok[:].rearrange("p a b -> p (a b)"), ps_tok[:, 0:192], eng="act")
            M5 = P.nxt("rw_M5")
            for s in range(2):
                hs = slice(64 * s, 64 * s + 64)
                for j, (l, rr) in enumerate(((At, Bt), (Bt, At), (Kt, At), (Bt, Rt), (Kt, Rt))):
                    P.mm(ps5[hs, j * 64:(j + 1) * 64], l[hs, cs], rr[hs, cs])
            P.tt(M5[:].rearrange("p a b -> p (a b)"), ps5[:, 0:320], mask5[d][:], ALU.mult)
            TT = P.nxt("rw_TT")
            P.tt(TT[:], M5[:, 1, :], i2[:], ALU.add)
            Pm, PTm = M5[:, 0, :], M5[:, 1, :]
            for lvl in range(1, 6):
                for s in range(2):
                    hs = slice(64 * s, 64 * s + 64)
                    P.mm(psL[hs, 0:64], PTm[hs, :], Pm[hs, :])
                    if lvl < 5:
                        P.mm(psL[hs, 64:128], Pm[hs, :], PTm[hs, :])
                PP = P.nxt("rw_PP")
                w = 128 if lvl < 5 else 64
                P.copy(PP[:, 0:w], psL[:, 0:w], eng="act")
                Pm, PTm = PP[:, 0:64], PP[:, 64:128]
                for s in range(2):
                    hs = slice(64 * s, 64 * s + 64)
                    P.mm(psT[hs, 0:64], Pm[hs, :], TT[hs, :])
                P.tt(TT[:], TT[:], psT[:, 0:64], ALU.add)
            pre.append((tok, M5, TT))
            if c == 0:
                P.dbg(f"tok{d}", tok[:].rearrange("p a b -> p (a b)")); P.dbg(f"M5{d}", M5[:].rearrange("p a b -> p (a b)")); P.dbg(f"TT{d}", TT[:])
        Sd = S[d]
        order = range(nch) if d == 0 else range(nch - 1, -1, -1)
        for c in order:
            cs = slice(c * C, (c + 1) * C)
            tok, M5, TT = pre[c]
            Btok, Ktok, Vtok = tok[:, 0, :], tok[:, 1, :], tok[:, 2, :]
            AkT, LrbT, LrkT = M5[:, 2, :], M5[:, 3, :], M5[:, 4, :]
            for s in range(2):
                hs = slice(64 * s, 64 * s + 64)
                P.mm(psX[hs, 0:64], At[hs, cs], Sd[hs, :], start=True, stop=False)
                P.mm(psX[hs, 0:64], AkT[hs, :], Vtok[hs, :], start=False, stop=True)
            Xs = P.nxt("rw_Xs")
            P.copy(Xs[:], psX[:, 0:64], eng="act")
            for s in range(2):
                hs = slice(64 * s, 64 * s + 64)
                P.mm(psU[hs, 0:64], TT[hs, :], Xs[hs, :])
            Us = P.nxt("rw_Us")
            P.copy(Us[:], psU[:, 0:64])
            for s in range(2):
                hs = slice(64 * s, 64 * s + 64)
                P.mm(psY[hs, cs], Sd[hs, :], Rt[hs, cs], start=True, stop=False)
                P.mm(psY[hs, cs], Us[hs, :], LrbT[hs, :], start=False, stop=False)
                P.mm(psY[hs, cs], Vtok[hs, :], LrkT[hs, :], start=False, stop=True)
                P.mm(psS[hs, 0:64], Btok[hs, :], Us[hs, :], start=True, stop=False)
                P.mm(psS[hs, 0:64], Ktok[hs, :], Vtok[hs, :], start=False, stop=True)
            P.tt(Stmp[:], Sd[:], psS[:, 0:64], ALU.add)
            P.ts(Sd[:], Stmp[:], ep[:, c * C + wcol:c * C + wcol + 1], ALU.mult)
        yT = P.nxt("rw_yT")[:, 0:n]
        P.copy(yT, psY[:, 0:n], eng="act")
        P.dbg(f"yT{d}", yT)
        return yT

    def blocks(seq, T):
        return [(seq, t0, min(TB, T - t0)) for t0 in range(0, T, TB)]
    fwd = blocks("c", Tc) + blocks("l", Tl)
    bwd = blocks("c", Tc)[::-1] + blocks("l", Tl)[::-1]
    P.memset(S[0][:], 0.0); P.memset(S[1][:], 0.0)
    for (seq, t0, n) in fwd:
        o = prep(seq, t0, n)
        yT = scan_block(0, o, n)
        P.dma(yf[seq].ap()[:, t0:t0 + n], yT)
    for (seq, t0, n) in bwd:
        o = prep(seq, t0, n)
        yb = scan_block(1, o, n)
        yfl = P.nxt("rw_yfl")[:, 0:n]
        P.dma(yfl, yf[seq].ap()[:, t0:t0 + n])
        y = P.nxt("rw_t1")[:, 0:n]
        P.tt(y, yb, yfl, ALU.add)
        P.mm(psL[:, 0:n], blk[:], y)
        yc = P.nxt("rw_t2")[:, 0:n]
        P.stt(yc, psL[:, 0:n], -1.0 / 64, y, ALU.mult, ALU.add)
        sq = P.nxt("rw_sq")[:, 0:n]
        P.tt(sq, yc, yc, ALU.mult)
        P.mm(psT[:, 0:n], blk[:], sq)
        sd = P.nxt("rw_rn")[:, 0:n]
        P.act(sd, psT[:, 0:n], AF.Sqrt, bias=LN_X_EPS, scale=1.0 / 64)
        P.recip(sd, sd)
        P.tt(yc, yc, sd, ALU.mult)
        ov = P.nxt("rw_t3")[:, 0:n]
        P.ts(ov, yc, pc(PR_LG), ALU.mult, pc(PR_LB), ALU.add)
        ks = P.nxt("rw_t4")[:, 0:n]
        P.tt(ks, o["k0"], o["k1"], ALU.add)
        P.stt(ks, ks, pc(PR_RK), o["r"], ALU.mult, ALU.mult)
        P.mm(psX[:, 0:n], blk[:], ks)
        P.tt(ks, psX[:, 0:n], o["v"], ALU.mult)
        P.tt(ov, ov, ks, ALU.add)
        P.tt(ov, ov, o["g"], ALU.mult)
        P.dma(odst[seq].ap()[:, t0:t0 + n], ov, is_out=True)
    return io


def rwkv_host_inputs(rw_c, rw_l, p, heads):
    W = 640
    cols = []
    for base in (0, W, 2 * W):
        cols.append(np.concatenate([np.arange(base + h * 64, base + h * 64 + 64) for h in heads]))
    cols.append(np.arange(3 * W, 3 * W + 128)); cols.append(np.arange(3 * W + 128, 3 * W + 256)); cols.append(np.arange(3 * W + 256, 3 * W + 384))
    cols = np.stack(cols)

    def lay(u):
        T = u.shape[0]
        o = np.zeros((128, 6, T + 2), np.float32)
        o[:, :, 1:T + 1] = u[:, cols].transpose(2, 1, 0)
        return o
    hc = np.concatenate([np.arange(h * 64, h * 64 + 64) for h in heads])
    pr = np.zeros((128, PR_N), np.float32)
    mu = p["shift_mu"]
    pr[:, PR_MU0:PR_MU0 + 6] = mu[0][cols].T
    pr[:, PR_MU1:PR_MU1 + 6] = mu[1][cols].T
    pr[:, PR_W0:PR_W0 + 2] = p["w0"][:, hc].T
    pr[:, PR_A0:PR_A0 + 2] = p["a0"][:, hc].T
    pr[:, PR_KK] = p["k_k"][hc]; pr[:, PR_KA] = p["k_a"][hc]
    pr[:, PR_RK] = p["r_k"].reshape(-1)[hc]; pr[:, PR_LG] = p["lnx_g"][hc]; pr[:, PR_LB] = p["lnx_b"][hc]
    d = {"rw_uc": lay(rw_c), "rw_ul": lay(rw_l), "rw_pr": pr,
         "rw_wup": np.ascontiguousarray(p["w_up"][:, :, hc].reshape(128, 128)),
         "rw_aup": np.ascontiguousarray(p["a_up"][:, :, hc].reshape(128, 128)),
         "rw_gup": np.ascontiguousarray(p["g_up"][:, hc])}
    for k, v in rwkv_consts().items():
        d["c_" + k] = v
    return d


SQ = 128


def ssd_consts():
    p = np.arange(128)[:, None]; f = np.arange(128)[None, :]
    tri = np.stack([(f >= p), (f <= p)]).astype(np.float32)
    return {"tri": tri, "ident": np.eye(128, dtype=np.float32), "ones": np.ones((128, 128), np.float32)}


def ssd_build(P, PS, Tc, Tl, TB=512):
    Q = SQ
    io = {}
    T_ = {"c": Tc, "l": Tl}
    for sq in ("c", "l"):
        T = T_[sq]
        io["x" + sq] = P.dram("sd_x" + sq, [2, 64, T + 2], F32, "ExternalInput")
        io["b" + sq] = P.dram("sd_b" + sq, [2, 128, T + 2], F32, "ExternalInput")
        io["c" + sq] = P.dram("sd_c" + sq, [2, 128, T + 2], F32, "ExternalInput")
        io["dt" + sq] = P.dram("sd_dt" + sq, [2, 128, T // Q, 2], F32, "ExternalInput")
        io["y" + sq] = P.dram("sd_y" + sq, [2, 128, T // Q, 64], F32, "ExternalOutput")
    io["px"] = P.dram("sd_px", [2, 64, 4], F32, "ExternalInput")
    io["pb"] = P.dram("sd_pb", [2, 128, 4], F32, "ExternalInput")
    io["pc"] = P.dram("sd_pc", [2, 128, 4], F32, "ExternalInput")
    io["pd"] = P.dram("sd_pd", [2, 128, 6], F32, "ExternalInput")
    io["tri"] = P.dram("c_tri", [2, 128, 128], F32, "ExternalInput")
    io["ident"] = P.dram("c_identb", [128, 128], F32, "ExternalInput")
    io["ones"] = P.dram("c_ones", [128, 128], F32, "ExternalInput")
    yfd = {sq: P.dram("sd_yf" + sq, [2, 128, T_[sq] // Q, 64], F32) for sq in ("c", "l")}

    def cst(nm, src, shape):
        t = P.sb("sk_" + nm, shape)
        P.dma(t[:], src)
        return t
    tri = [cst(f"tri{d}", io["tri"].ap()[d], [128, 128]) for d in range(2)]
    ident = cst("ident", io["ident"].ap(), [128, 128])
    ones = cst("ones", io["ones"].ap(), [128, 128])
    px = [cst(f"px{s}", io["px"].ap()[s], [64, 4]) for s in range(2)]
    pb = [cst(f"pb{s}", io["pb"].ap()[s], [128, 4]) for s in range(2)]
    pcc = [cst(f"pc{s}", io["pc"].ap()[s], [128, 4]) for s in range(2)]
    pd = [cst(f"pd{s}", io["pd"].ap()[s], [128, 6]) for s in range(2)]
    Aneg = [P.sb(f"sk_A{s}", [128, 2]) for s in range(2)]
    dsum = [P.sb(f"sk_ds{s}", [128, 1]) for s in range(2)]
    for s in range(2):
        P.act(Aneg[s][:], pd[s][:, 2:4], AF.Exp)
        P.ts(Aneg[s][:], Aneg[s][:], -1.0, ALU.mult)
        P.tt(dsum[s][:], pd[s][:, 4:5], pd[s][:, 5:6], ALU.add)

    P.ring("sd_xin", 2, [64, TB + 2]); P.ring("sd_bin", 2, [128, TB + 2]); P.ring("sd_cin", 2, [128, TB + 2])
    P.ring("sd_xf", 4, [64, TB]); P.ring("sd_bf", 4, [128, TB]); P.ring("sd_cf", 4, [128, TB])
    P.ring("sd_dt", 4, [128, TB // Q, 2]); P.ring("sd_dta", 4, [128, TB // Q, 2]); P.ring("sd_tmp8", 6, [128, TB // Q, 2])
    P.ring("sd_xtok", 12, [128, 64]); P.ring("sd_btok", 12, [128, 128]); P.ring("sd_cbm", 12, [128, 128])
    P.ring("sd_bc", 3, [128, 128]); P.ring("sd_E", 3, [128, 128]); P.ring("sd_G", 3, [128, 128]); P.ring("sd_Cs", 3, [128, 128])
    P.ring("sd_col", 12, [128, 4]); P.ring("sd_xdt", 3, [128, 64]); P.ring("sd_xw", 3, [128, 64])
    P.ring("sd_yblk", 3, [128, TB // Q, 64]); P.ring("sd_yfl", 3, [128, TB // Q, 64])
    H = [[P.sb(f"sd_h{s}{d}", [128, 64]) for d in range(2)] for s in range(2)]
    ps_a, ps_b, ps_c, ps_d, ps_e, ps_f, ps_g, ps_h = PS

    def prep(s, seq, t0, n):
        nq = n // Q
        o = {}
        for nm, ring_in, ring_f, src, par, rows in (("x", "sd_xin", "sd_xf", io["x" + seq], px[s], 64),
                                                    ("b", "sd_bin", "sd_bf", io["b" + seq], pb[s], 128),
                                                    ("c", "sd_cin", "sd_cf", io["c" + seq], pcc[s], 128)):
            tin = P.nxt(ring_in)
            P.dma(tin[:, 0:n + 2], src.ap()[s][:, t0:t0 + n + 2])
            tf = P.nxt(ring_f)[:, 0:n]
            P.ts(tf, tin[:, 1:n + 1], par[:, 1:2], ALU.mult, par[:, 3:4], ALU.add)
            P.stt(tf, tin[:, 0:n], par[:, 0:1], tf, ALU.mult, ALU.add)
            P.stt(tf, tin[:, 2:n + 2], par[:, 2:3], tf, ALU.mult, ALU.add)
            P.act(tf, tf, AF.Silu)
            o[nm] = tf
        dtr = P.nxt("sd_dt")[:, 0:nq, :]
        P.dma(dtr, io["dt" + seq].ap()[s][:, t0 // Q:t0 // Q + nq, :])
        dt = P.nxt("sd_dt")[:, 0:nq, :]
        dta = P.nxt("sd_dta")[:, 0:nq, :]
        for d in range(2):
            xb = P.nxt("sd_tmp8")[:, 0:nq, d:d + 1]
            P.ts(xb, dtr[:, :, d:d + 1], pd[s][:, d:d + 1], ALU.add)
            ab = P.nxt("sd_tmp8")[:, 0:nq, d:d + 1]
            P.act(ab, xb, AF.Abs)
            P.act(ab, ab, AF.Exp, scale=-1.0)
            P.act(ab, ab, AF.Ln, bias=1.0)
            P.ts(xb, xb, 0.0, ALU.max)
            P.tt(dt[:, :, d:d + 1], xb, ab, ALU.add)
            P.ts(dta[:, :, d:d + 1], dt[:, :, d:d + 1], Aneg[s][:, d:d + 1], ALU.mult)
        o["dt"] = dt; o["dta"] = dta
        o["xtok"] = []; o["btok"] = []; o["cb"] = []
        for c in range(nq):
            cs = slice(c * Q, (c + 1) * Q)
            P.mm(ps_a[:, 0:64], o["x"][:, cs], ident[0:64, 0:64])
            xt = P.nxt("sd_xtok")
            P.copy(xt[:], ps_a[:, 0:64], eng="act")
            P.mm(ps_b[:, 0:128], o["b"][:, cs], ident[:])
            bt = P.nxt("sd_btok")
            P.copy(bt[:], ps_b[:, 0:128], eng="act")
            P.mm(ps_c[:, 0:128], o["b"][:, cs], o["c"][:, cs])
            o["xtok"].append(xt); o["btok"].append(bt); o["cb"].append(ps_c)
            cbm = []
            for d in range(2):
                m = P.nxt("sd_cbm")
                P.tt(m[:], ps_c[:, 0:128], tri[d][:], ALU.mult)
                cbm.append(m)
            o["cb"][-1] = cbm
        return o

    def scan(s, d, o, n):
        nq = n // Q
        yb = P.nxt("sd_yblk")
        h = H[s][d]
        order = range(nq) if d == 0 else range(nq - 1, -1, -1)
        last = Q - 1 if d == 0 else 0
        for c in order:
            cs = slice(c * Q, (c + 1) * Q)
            dta = o["dta"][:, c, d:d + 1]
            dt = o["dt"][:, c, d:d + 1]
            bc = P.nxt("sd_bc")
            P.ts(bc[:], ones[:], dta, ALU.mult)
            P.mm(ps_d[:, 0:128], bc[:], tri[d][:])
            P.mm(ps_e[:, 0:1], tri[d][:], dta)
            col = P.nxt("sd_col")
            P.copy(col[:, 0:1], ps_e[:, 0:1])
            E = P.nxt("sd_E")
            P.ts(E[:], ps_d[:, 0:128], col[:, 0:1], ALU.subtract, 0.0, ALU.min)
            P.act(E[:], E[:], AF.Exp)
            G = P.nxt("sd_G")
            P.tt(G[:], E[:], o["cb"][c][d][:], ALU.mult)
            Cs = P.nxt("sd_Cs")
            P.act(Cs[:], ps_d[:, 0:128], AF.Exp)
            P.tt(Cs[:], Cs[:], o["c"][:, cs], ALU.mult)
            xdt = P.nxt("sd_xdt")
            P.ts(xdt[:], o["xtok"][c][:], dt, ALU.mult)
            P.mm(ps_f[:, 0:64], G[:], xdt[:], start=True, stop=False)
            P.mm(ps_f[:, 0:64], Cs[:], h[:], start=False, stop=True)
            P.copy(yb[:, c, :], ps_f[:, 0:64], eng="act")
            P.ts(col[:, 1:2], col[:, 0:1], -1.0, ALU.mult, ps_d[:, last:last + 1], ALU.add)
            P.act(col[:, 1:2], col[:, 1:2], AF.Exp)
            P.tt(col[:, 1:2], col[:, 1:2], dt, ALU.mult)
            P.act(col[:, 2:3], ps_d[:, last:last + 1], AF.Exp)
            xw = P.nxt("sd_xw")
            P.ts(xw[:], o["xtok"][c][:], col[:, 1:2], ALU.mult)
            P.mm(ps_g[:, 0:64], o["btok"][c][:], xw[:])
            P.stt(h[:], h[:], col[:, 2:3], ps_g[:, 0:64], ALU.mult, ALU.add)
        return yb

    def blocks(seq, T):
        return [(seq, t0, min(TB, T - t0)) for t0 in range(0, T, TB)]
    fwd = blocks("c", Tc) + blocks("l", Tl)
    bwd = blocks("c", Tc)[::-1] + blocks("l", Tl)[::-1]
    for s in range(2):
        for d in range(2):
            P.memset(H[s][d][:], 0.0)
    for (seq, t0, n) in fwd:
        for s in range(2):
            o = prep(s, seq, t0, n)
            yb = scan(s, 0, o, n)
            nq = n // Q
            P.dma(yfd[seq].ap()[s][:, t0 // Q:t0 // Q + nq, :], yb[:, 0:nq, :])
    for (seq, t0, n) in bwd:
        for s in range(2):
            o = prep(s, seq, t0, n)
            yb = scan(s, 1, o, n)
            nq = n // Q
            yfl = P.nxt("sd_yfl")
            P.dma(yfl[:, 0:nq, :], yfd[seq].ap()[s][:, t0 // Q:t0 // Q + nq, :])
            P.tt(yb[:, 0:nq, :], yb[:, 0:nq, :], yfl[:, 0:nq, :], ALU.add)
            for c in range(nq):
                P.stt(yb[:, c, :], o["xtok"][c][:], dsum[s][:, 0:1], yb[:, c, :], ALU.mult, ALU.add)
            P.dma(io["y" + seq].ap()[s][:, t0 // Q:t0 // Q + nq, :], yb[:, 0:nq, :], is_out=True)
    return io


def ssd_host_inputs(ssm_c, ssm_l, p, heads):
    d = {}
    for sq, u in (("c", ssm_c), ("l", ssm_l)):
        T = u.shape[0]
        xbc = u[:, 640:640 + 1152]
        X = np.zeros((2, 64, T + 2), np.float32); B = np.zeros((2, 128, T + 2), np.float32); Cc = np.zeros((2, 128, T + 2), np.float32)
        DT = np.zeros((2, 128, T // SQ, 2), np.float32)
        for s, h in enumerate(heads):
            g = h // 5
            X[s, :, 1:T + 1] = xbc[:, h * 64:h * 64 + 64].T
            B[s, :, 1:T + 1] = xbc[:, 640 + g * 128:640 + g * 128 + 128].T
            Cc[s, :, 1:T + 1] = xbc[:, 896 + g * 128:896 + g * 128 + 128].T
            DT[s, :, :, 0] = u[:, 1792 + h].reshape(T // SQ, SQ).T
            DT[s, :, :, 1] = u[:, 1802 + h].reshape(T // SQ, SQ).T
        d["sd_x" + sq] = X; d["sd_b" + sq] = B; d["sd_c" + sq] = Cc; d["sd_dt" + sq] = DT
    cw = np.concatenate([p["conv_w"], p["conv_b"][None]], 0).T
    px = np.zeros((2, 64, 4), np.float32); pb = np.zeros((2, 128, 4), np.float32); pc = np.zeros((2, 128, 4), np.float32)
    pd = np.zeros((2, 128, 6), np.float32)
    for s, h in enumerate(heads):
        g = h // 5
        px[s] = cw[h * 64:h * 64 + 64]; pb[s] = cw[640 + g * 128:640 + g * 128 + 128]; pc[s] = cw[896 + g * 128:896 + g * 128 + 128]
        pd[s, :, 0] = p["dt_bias"][0, h]; pd[s, :, 1] = p["dt_bias"][1, h]
        pd[s, :, 2] = p["a_log"][0, h]; pd[s, :, 3] = p["a_log"][1, h]
        pd[s, :, 4] = p["d_skip"][0, h]; pd[s, :, 5] = p["d_skip"][1, h]
    d.update({"sd_px": px, "sd_pb": pb, "sd_pc": pc, "sd_pd": pd})
    c = ssd_consts()
    d["c_tri"] = c["tri"]; d["c_identb"] = c["ident"]; d["c_ones"] = c["ones"]
    return d


def modvec_build(P, PS, L, KD, NCOL):
    D = KD * 128
    io = {}
    io["cT"] = P.dram("m_cT", [128, KD, 2], F32, "ExternalInput")
    io["w"] = P.dram("m_w", [L, KD, 128, NCOL], F32, "ExternalInput")
    io["b"] = P.dram("m_b", [L, 2, NCOL], F32, "ExternalInput")
    io["o"] = P.dram("m_o", [L, 2, NCOL], F32, "ExternalOutput")
    cT = P.sb("m_cTs", [128, KD, 2])
    P.dma(cT[:], io["cT"].ap())
    P.act(cT[:], cT[:], AF.Silu)
    P.ring("m_wt", 4, [128, NCOL]); P.ring("m_bt", 2, [2, NCOL]); P.ring("m_ot", 2, [2, NCOL])
    nt = (NCOL + 511) // 512
    for l in range(L):
        bt = P.nxt("m_bt")
        P.dma(bt[:], io["b"].ap()[l])
        for k in range(KD):
            wt = P.nxt("m_wt")
            P.dma(wt[:], io["w"].ap()[l, k])
            for j in range(nt):
                cs = slice(j * 512, min(NCOL, (j + 1) * 512))
                P.mm(PS[j][0:2, 0:cs.stop - cs.start], cT[:, k, :], wt[:, cs], start=(k == 0), stop=(k == KD - 1))
        ot = P.nxt("m_ot")
        for j in range(nt):
            cs = slice(j * 512, min(NCOL, (j + 1) * 512))
            P.tt(ot[:, cs], PS[j][0:2, 0:cs.stop - cs.start], bt[:, cs], ALU.add)
        P.dma(io["o"].ap()[l], ot[:], is_out=True)
    return io


EPS = 1e-6


def norm_mod(P, ps, ones, xT, hT, KD, segs, ab, tmpring):
    D = KD * 128
    P.ring(tmpring + "_r", 2, [128, 512])
    for (t0, n, si) in segs:
        ts_ = slice(t0, t0 + n)
        for k in range(KD):
            sq = P.nxt(tmpring)[:, 0:n]
            P.act(sq, xT[:, k, ts_], AF.Square)
            P.mm(ps[:, 0:n], ones[:], sq, start=(k == 0), stop=(k == KD - 1))
        rstd = P.nxt(tmpring + "_r")[:, 0:n]
        P.act(rstd, ps[:, 0:n], AF.Sqrt, bias=EPS, scale=1.0 / D)
        P.recip(rstd, rstd)
        a, b = ab[si]
        for k in range(KD):
            t = P.nxt(tmpring)[:, 0:n]
            P.tt(t, xT[:, k, ts_], rstd, ALU.mult)
            P.ts(hT[:, k, ts_], t, a[:, k:k + 1], ALU.mult, b[:, k:k + 1], ALU.add)


def load_mods(P, io_mod, io_g, KD, which):
    mod = P.sb("mod_t", [128, 2, 6, KD])
    P.dma(mod[:], io_mod.ap())
    g = P.sb("mod_g", [128, KD])
    P.dma(g[:], io_g.ap())
    out = []
    for si in range(2):
        a = P.sb(f"mod_a{si}", [128, KD])
        P.ts(a[:], mod[:, si, 3 * which + 1, :], 1.0, ALU.add)
        P.tt(a[:], a[:], g[:], ALU.mult)
        out.append((a, mod[:, si, 3 * which + 0, :], mod[:, si, 3 * which + 2, :]))
    return out


IN_COLS = 6420
QK_EPS = 1e-6


def rope_consts():
    blk = np.zeros((128, 128), np.float32); blk[:64, :64] = 1; blk[64:, 64:] = 1
    rot = np.zeros((128, 128), np.float32)
    for base in (0, 64):
        for m in range(32):
            rot[base + m + 32, base + m] = -1.0
            rot[base + m, base + m + 32] = 1.0
    return blk, rot


def rope_tables(positions, is_ctx):
    n_freq = 16
    inv = (10000.0 ** (-np.arange(n_freq, dtype=np.float32) / n_freq)).astype(np.float32)
    row = (positions // 64).astype(np.float32); col = (positions % 64).astype(np.float32)
    ang = np.concatenate([row[:, None] * inv, col[:, None] * inv], axis=-1).astype(np.float32)
    cos = np.cos(ang).astype(np.float32); sin = np.sin(ang).astype(np.float32)
    cos = np.where(is_ctx[:, None], 1.0, cos).astype(np.float32); sin = np.where(is_ctx[:, None], 0.0, sin).astype(np.float32)
    cosT = np.tile(cos.T, (4, 1)); sinT = np.tile(sin.T, (4, 1))
    return np.ascontiguousarray(cosT), np.ascontiguousarray(sinT)


def projA_build(P, PS, KD, TCc, TLc):
    T = TCc + TLc
    io = {}
    io["xT"] = P.dram("a_xT", [128, KD, T], F32, "ExternalInput")
    io["mod"] = P.dram("a_mod", [128, 2, 6, KD], F32, "ExternalInput")
    io["g"] = P.dram("a_g", [128, KD], F32, "ExternalInput")
    io["w"] = P.dram("a_w", [KD, 128, IN_COLS], F32, "ExternalInput")
    io["cos"] = P.dram("a_cos", [128, T], F32, "ExternalInput")
    io["sin"] = P.dram("a_sin", [128, T], F32, "ExternalInput")
    io["gain"] = P.dram("a_gain", [128, 2], F32, "ExternalInput")
    io["blk"] = P.dram("a_blk", [128, 128], F32, "ExternalInput")
    io["rot"] = P.dram("a_rot", [128, 128], F32, "ExternalInput")
    io["ones"] = P.dram("a_ones", [128, 128], F32, "ExternalInput")
    io["qk"] = P.dram("a_qk", [12, 128, T], BF16, "ExternalOutput")
    io["v"] = P.dram("a_v", [6, 128, T], BF16, "ExternalOutput")
    io["u"] = P.dram("a_u", [IN_COLS - 2304, T], F32, "ExternalOutput")
    xT = P.sb("a_xTs", [128, KD, T]); hT = P.sb("a_hT", [128, KD, T], BF16)
    P.dma(xT[:], io["xT"].ap())
    cos = P.sb("a_cos_s", [128, T]); sin = P.sb("a_sin_s", [128, T])
    P.dma(cos[:], io["cos"].ap()); P.dma(sin[:], io["sin"].ap())
    gain = P.sb("a_gain_s", [128, 2]); P.dma(gain[:], io["gain"].ap())
    blk = P.sb("a_blk_s", [128, 128]); P.dma(blk[:], io["blk"].ap())
    rot = P.sb("a_rot_s", [128, 128]); P.dma(rot[:], io["rot"].ap())
    ones = P.sb("a_ones_s", [128, 128]); P.dma(ones[:], io["ones"].ap())
    mods = load_mods(P, io["mod"], io["g"], KD, 0)
    P.ring("a_tmp", 6, [128, 512])
    segs = [(0, TCc, 1)] + [(TCc + t0, min(512, TLc - t0), 0) for t0 in range(0, TLc, 512)]
    norm_mod(P, PS[7], ones, xT, hT, KD, segs, [(m[0], m[1]) for m in mods], "a_tmp")
    P.ring("a_w", 3, [128, KD, 128], BF16)
    P.ring("a_o", 4, [128, 512]); P.ring("a_ob", 4, [128, 512], BF16)
    nchunk = (IN_COLS + 127) // 128
    pi = 0
    for m in range(nchunk):
        c0 = m * 128; mc = min(128, IN_COLS - c0)
        wt = P.nxt("a_w")
        for k in range(KD):
            P.dma(wt[:, k, 0:mc], io["w"].ap()[k, :, c0:c0 + mc], q="pool")
        for (t0, n, si) in segs:
            ts_ = slice(t0, t0 + n)
            ps = PS[pi % 4]; pi += 1
            for k in range(KD):
                P.mm(ps[0:mc, 0:n], wt[:, k, 0:mc], hT[:, k, ts_], start=(k == 0), stop=(k == KD - 1))
            if m < 12:
                x = P.nxt("a_o")[:, 0:n]
                P.copy(x, ps[:, 0:n], eng="act")
                sq = P.nxt("a_o")[:, 0:n]
                P.tt(sq, x, x, ALU.mult)
                P.mm(PS[4][:, 0:n], blk[:], sq)
                rs = P.nxt("a_o")[:, 0:n]
                P.act(rs, PS[4][:, 0:n], AF.Sqrt, bias=QK_EPS, scale=1.0 / 64)
                P.recip(rs, rs)
                P.stt(x, x, gain[:, (0 if m < 6 else 1):(1 if m < 6 else 2)], rs, ALU.mult, ALU.mult)
                P.mm(PS[5][:, 0:n], rot[:], x)
                r2 = P.nxt("a_o")[:, 0:n]
                P.tt(r2, PS[5][:, 0:n], sin[:, ts_], ALU.mult)
                P.tt(x, x, cos[:, ts_], ALU.mult)
                ob = P.nxt("a_ob")[:, 0:n]
                P.tt(ob, x, r2, ALU.add)
                P.dma(io["qk"].ap()[m][:, ts_], ob, is_out=True)
            elif m < 18:
                ob = P.nxt("a_ob")[:, 0:n]
                P.copy(ob, ps[:, 0:n], eng="act")
                P.dma(io["v"].ap()[m - 12][:, ts_], ob, is_out=True)
            else:
                o = P.nxt("a_o")
                P.copy(o[0:mc, 0:n], ps[0:mc, 0:n], eng="act")
                P.dma(io["u"].ap()[c0 - 2304:c0 - 2304 + mc, ts_], o[0:mc, 0:n], is_out=True)
    return io


def attn_consts():
    blk = np.zeros((128, 128), np.float32); blk[:64, :64] = 1; blk[64:, 64:] = 1
    sgn = np.zeros((128, 128), np.float32); sgn[:64, :] = 1.0 / 64; sgn[64:, :] = -1.0 / 64
    E = np.zeros((640, 640), np.float32); E[:320, :320] = 1; E[320:, 320:] = 1
    return blk, sgn, np.ascontiguousarray(E.reshape(5, 128, 640))


def mixB1_build(P, PS, KD, TCc, TLc, TCX, TLAT):
    T = TCc + TLc; TK = TCX + TLAT; KT = TK // 128; D = KD * 128
    io = {}
    dr = lambda nm, shp, dt=F32, kind="ExternalInput": P.dram("b_" + nm, shp, dt, kind)
    io["xT"] = dr("xT", [128, KD, T]); io["q"] = dr("q", [6, 128, T], BF16); io["kT"] = dr("kT", [6, 128, TK], BF16)
    io["v"] = dr("v", [128, KT, 768], BF16)
    io["rT"] = dr("rT", [128, 5, T]); io["yT"] = dr("yT", [128, 5, T]); io["zT"] = dr("zT", [128, 5, T])
    io["mod"] = dr("mod", [128, 2, 6, KD]); io["g"] = dr("g", [128, KD])
    io["w"] = dr("w", [16, 128, D])
    io["subln"] = dr("subln", [128, 1]); io["sng"] = dr("sng", [128, 5]); io["lamq"] = dr("lamq", [128, 1]); io["lamk"] = dr("lamk", [128, 1])
    io["lami"] = dr("lami", [128, 1]); io["gains"] = dr("gains", [128, 2, 64])
    io["blk"] = dr("blk", [128, 128]); io["sgn"] = dr("sgn", [128, 128]); io["E"] = dr("E", [5, 128, 640]); io["ones"] = dr("ones", [128, 128])
    io["o"] = dr("o", [128, KD, T], F32, "ExternalOutput")

    def cst(nm, shape, src=None, dt=F32, q="sp"):
        t = P.sb("bk_" + nm, shape, dt)
        P.dma(t[:], io[nm].ap() if src is None else src, q=q)
        return t
    subln = cst("subln", [128, 1]); sng = cst("sng", [128, 5]); lamq = cst("lamq", [128, 1]); lamk = cst("lamk", [128, 1])
    lami = cst("lami", [128, 1]); gains = cst("gains", [128, 2, 64]); blk = cst("blk", [128, 128]); sgn = cst("sgn", [128, 128])
    ones = cst("ones", [128, 128])
    onesb = P.sb("bk_onesb", [128, 128], BF16)
    P.copy(onesb[:], ones[:])
    Eg = [cst(f"E{k}", [128, 640], io["E"].ap()[k]) for k in range(5)]
    mods = load_mods(P, io["mod"], io["g"], KD, 0)
    qs = P.sb("bk_q", [128, 6, T], BF16)
    P.dma(qs[:], io["q"].ap().rearrange("h p t -> p h t"))
    sm = P.sb("bk_sm", [128, 8])
    P.tt(sm[:, 0:1], lamq[:], lamk[:], ALU.mult)
    P.mm(PS[0][:, 0:1], blk[:], sm[:, 0:1])
    P.act(sm[:, 1:2], PS[0][:, 0:1], AF.Exp)
    P.mm(PS[1][:, 0:1], sgn[:], sm[:, 1:2])
    P.tt(sm[:, 2:3], PS[1][:, 0:1], lami[:], ALU.add)
    P.ts(sm[:, 3:4], sm[:, 2:3], -1.0, ALU.mult)
    P.ts(sm[:, 4:5], lami[:], -1.0, ALU.mult, 1.0, ALU.add)
    P.tt(sm[:, 4:5], sm[:, 4:5], subln[:], ALU.mult)
    ga = P.sb("bk_ga", [128, 2, 64])
    P.act(ga[:], gains[:], AF.Abs)
    P.reduce(sm[:, 5:7], ga[:], ALU.max)
    P.tt(sm[:, 7:8], sm[:, 5:6], sm[:, 6:7], ALU.mult)
    P.ts(sm[:, 7:8], sm[:, 7:8], -8.0, ALU.mult)
    neg_lam, sgl, ebias = sm[:, 3:4], sm[:, 4:5], sm[:, 7:8]

    mixT = P.sb("bk_mixT", [128, 16, T], BF16)
    P.ring("b_kT", 2, [128, TK], BF16); P.ring("b_vh", 2, [128, KT, 128], BF16)
    P.ring("b_E", 6, [128, 512], BF16); P.ring("b_t", 8, [128, 512])
    qtiles = [(0, TCc, TCX)] + [(TCc + t0, min(512, TLc - t0), TK) for t0 in range(0, TLc, 512)]
    si_ = 0
    for h in range(6):
        kh = P.nxt("b_kT"); vh = P.nxt("b_vh")
        P.dma(kh[:], io["kT"].ap()[h])
        P.dma(vh[:], io["v"].ap()[:, :, h * 128:(h + 1) * 128])
        for (t0, n, nkeys) in qtiles:
            ts_ = slice(t0, t0 + n)
            nkt = nkeys // 128
            for kt in range(nkt):
                ks = slice(kt * 128, (kt + 1) * 128)
                for c in range(2):
                    hs = slice(64 * c, 64 * c + 64)
                    Sp = PS[si_ % 4]; si_ += 1
                    P.mm(Sp[:, 0:n], kh[hs, ks], qs[hs, h, ts_])
                    E = P.nxt("b_E")[:, 0:n]
                    P.act(E, Sp[:, 0:n], AF.Exp, bias=ebias, scale=0.125)
                    P.mm(PS[4 + c][:, 0:n], vh[:, kt, :], E, start=(kt == 0), stop=(kt == nkt - 1))
                    P.mm(PS[6 + c][:, 0:n], onesb[:], E, start=(kt == 0), stop=(kt == nkt - 1))
            rl0 = P.nxt("b_t")[:, 0:n]; rl1 = P.nxt("b_t")[:, 0:n]
            P.recip(rl0, PS[6][:, 0:n]); P.recip(rl1, PS[7][:, 0:n])
            o0 = P.nxt("b_t")[:, 0:n]; o1 = P.nxt("b_t")[:, 0:n]
            P.tt(o0, PS[4][:, 0:n], rl0, ALU.mult)
            P.tt(o1, PS[5][:, 0:n], rl1, ALU.mult)
            P.stt(o0, o1, neg_lam, o0, ALU.mult, ALU.add)
            sq = P.nxt("b_t")[:, 0:n]
            P.tt(sq, o0, o0, ALU.mult)
            P.mm(PS[0][:, 0:n], ones[:], sq)
            rs = P.nxt("b_t")[:, 0:n]
            P.act(rs, PS[0][:, 0:n], AF.Sqrt, bias=EPS, scale=1.0 / 128)
            P.recip(rs, rs)
            P.stt(mixT[:, h, ts_], o0, sgl, rs, ALU.mult, ALU.mult)
    P.ring("b_in5", 3, [128, 5, 512])
    toks = [(0, TCc, 1)] + [(TCc + t0, min(512, TLc - t0), 0) for t0 in range(0, TLc, 512)]
    for (t0, n, si) in toks:
        ts_ = slice(t0, t0 + n)
        rt = P.nxt("b_in5"); yt = P.nxt("b_in5"); zt = P.nxt("b_in5")
        P.dma(rt[:, :, 0:n], io["rT"].ap()[:, :, ts_]); P.dma(yt[:, :, 0:n], io["yT"].ap()[:, :, ts_]); P.dma(zt[:, :, 0:n], io["zT"].ap()[:, :, ts_])
        P.copy(mixT[:, 6:11, ts_], rt[:, :, 0:n])
        P.act(zt[:, :, 0:n], zt[:, :, 0:n], AF.Silu)
        P.tt(yt[:, :, 0:n], yt[:, :, 0:n], zt[:, :, 0:n], ALU.mult)
        P.tt(zt[:, :, 0:n], yt[:, :, 0:n], yt[:, :, 0:n], ALU.mult)
        for m in range(5):
            for k in range(5):
                P.mm(PS[m % 4][:, 0:n], Eg[k][:, m * 128:(m + 1) * 128], zt[:, k, 0:n], start=(k == 0), stop=(k == 4))
            rs = P.nxt("b_t")[:, 0:n]
            P.act(rs, PS[m % 4][:, 0:n], AF.Sqrt, bias=EPS, scale=1.0 / 320)
            P.recip(rs, rs)
            P.stt(mixT[:, 11 + m, ts_], yt[:, m, 0:n], sng[:, m:m + 1], rs, ALU.mult, ALU.mult)
    P.ring("b_w", 3, [128, 16, 128], BF16); P.ring("b_x", 4, [128, 512])
    pi = 0
    for m in range(KD):
        wt = P.nxt("b_w")
        P.dma(wt[:], io["w"].ap()[:, :, m * 128:(m + 1) * 128].rearrange("k p c -> p k c"), q="pool")
        for (t0, n, si) in toks:
            ts_ = slice(t0, t0 + n)
            ps = PS[pi % 4]; pi += 1
            for k in range(16):
                P.mm(ps[:, 0:n], wt[:, k, :], mixT[:, k, ts_], start=(k == 0), stop=(k == 15))
            xt = P.nxt("b_x")[:, 0:n]
            P.dma(xt, io["xT"].ap()[:, m, ts_])
            P.stt(xt, ps[:, 0:n], mods[si][2][:, m:m + 1], xt, ALU.mult, ALU.add)
            P.dma(io["o"].ap()[:, m, ts_], xt, is_out=True)
    return io


def ffnB2_build(P, PS, KD, FC, TCc, TLc):
    T = TCc + TLc; D = KD * 128
    io = {}
    dr = lambda nm, shp, dt=F32, kind="ExternalInput": P.dram("f_" + nm, shp, dt, kind)
    io["xT"] = dr("xT", [128, KD, T]); io["mod"] = dr("mod", [128, 2, 6, KD]); io["g"] = dr("g", [128, KD])
    io["wi"] = dr("wi", [KD, 128, 2 * FC * 128]); io["wo"] = dr("wo", [FC, 128, D]); io["ones"] = dr("ones", [128, 128])
    io["o"] = dr("o", [128, KD, T], F32, "ExternalOutput")
    xT = P.sb("f_xTs", [128, KD, T]); hT = P.sb("f_hT", [128, KD, T], BF16)
    P.dma(xT[:], io["xT"].ap())
    ones = P.sb("f_ones_s", [128, 128]); P.dma(ones[:], io["ones"].ap())
    mods = load_mods(P, io["mod"], io["g"], KD, 1)
    P.ring("f_tmp", 6, [128, 512])
    toks = [(0, TCc, 1)] + [(TCc + t0, min(512, TLc - t0), 0) for t0 in range(0, TLc, 512)]
    norm_mod(P, PS[7], ones, xT, hT, KD, toks, [(m[0], m[1]) for m in mods], "f_tmp")
    actT = P.sb("f_actT", [128, FC, 512], BF16)
    P.ring("f_wi", 2, [128, KD, 256], BF16); P.ring("f_wo", 2, [128, FC, 128], BF16); P.ring("f_o", 3, [128, 512])
    pi = 0
    for (t0, n, si) in toks:
        ts_ = slice(t0, t0 + n)
        for m in range(FC):
            wt = P.nxt("f_wi")
            P.dma(wt[:, :, 0:128], io["wi"].ap()[:, :, m * 128:(m + 1) * 128].rearrange("k p c -> p k c"), q="pool")
            P.dma(wt[:, :, 128:256], io["wi"].ap()[:, :, (FC + m) * 128:(FC + m + 1) * 128].rearrange("k p c -> p k c"), q="pool")
            pg = PS[pi % 6]; pu = PS[(pi + 1) % 6]; pi += 2
            for k in range(KD):
                P.mm(pg[:, 0:n], wt[:, k, 0:128], hT[:, k, ts_], start=(k == 0), stop=(k == KD - 1))
            for k in range(KD):
                P.mm(pu[:, 0:n], wt[:, k, 128:256], hT[:, k, ts_], start=(k == 0), stop=(k == KD - 1))
            gs = P.nxt("f_tmp")[:, 0:n]
            P.act(gs, pg[:, 0:n], AF.Silu)
            P.tt(actT[:, m, 0:n], gs, pu[:, 0:n], ALU.mult)
        for mo in range(KD):
            wt = P.nxt("f_wo")
            P.dma(wt[:], io["wo"].ap()[:, :, mo * 128:(mo + 1) * 128].rearrange("m p c -> p m c"), q="pool")
            ps = PS[pi % 6]; pi += 1
            for m in range(FC):
                P.mm(ps[:, 0:n], wt[:, m, :], actT[:, m, 0:n], start=(m == 0), stop=(m == FC - 1))
            ot = P.nxt("f_o")[:, 0:n]
            P.stt(ot, ps[:, 0:n], mods[si][2][:, mo:mo + 1], xT[:, mo, ts_], ALU.mult, ALU.add)
            P.dma(io["o"].ap()[:, mo, ts_], ot, is_out=True)
    return io


NCORES = 8
_PROGS = {}
_DBG = None


def _fm(x2d, KD):
    T = x2d.shape[0]
    return np.ascontiguousarray(x2d.T.reshape(KD, 128, T).transpose(1, 0, 2))


def _unfm(a):
    p, KD, T = a.shape
    return np.ascontiguousarray(a.transpose(1, 0, 2).reshape(KD * 128, T).T)


def _run(nc, in_maps):
    res = run_bass_kernel_spmd(nc, in_maps, core_ids=list(range(NCORES)))
    return res.results


def _prog(key, builder):
    if key not in _PROGS:
        P = Prog()
        PS = [P.ps(f"ps{i}", [128, 512]) for i in range(8)]
        builder(P, PS)
        _PROGS[key] = P.finish()
    return _PROGS[key]


def _slots(j):
    rh = (2 * j, 2 * j + 1) if j < 5 else (0, 1)
    sh = (2 * (j - 3), 2 * (j - 3) + 1) if j >= 3 else (0, 1)
    return rh, sh


def kernel(x, c, ctx, c_ctx, ada_w, ada_b, norm1_g, norm2_g, w_in, w_out, qk_gain, lam_q, lam_k, subln_g,
           shift_mu, w0, w_up, a0, a_up, g_up, k_k, k_a, r_k, lnx_g, lnx_b, conv_w, conv_b, dt_bias, a_log,
           d_skip, ssm_norm_g, w_ffn_in, w_ffn_out):
    f = lambda a: np.ascontiguousarray(np.asarray(a, dtype=np.float32))
    x, c, ctx, c_ctx = f(x), f(c), f(ctx), f(c_ctx)
    L = ada_w.shape[0]; SEQ = x.shape[1]; CTX = ctx.shape[1]; D = x.shape[2]; KD = D // 128
    DFF = w_ffn_out.shape[1]; FC = DFF // 128
    NC = NCORES; TCc = CTX // NC; TLc = SEQ // NC; T = TCc + TLc
    ones = np.ones((128, 128), np.float32)
    NCOL = 6 * D // NC
    ncM = _prog(("M", L, KD, NCOL), lambda P, PS: modvec_build(P, PS, L, KD, NCOL))
    cT = _fm(np.stack([c[0], c_ctx]), KD)
    ada_w = np.asarray(ada_w, np.float32); ada_b = np.asarray(ada_b, np.float32)
    ims = []
    for j in range(NC):
        cs = slice(j * NCOL, (j + 1) * NCOL)
        ims.append({"m_cT": cT, "m_w": np.ascontiguousarray(ada_w[:, :, cs].reshape(L, KD, 128, NCOL)),
                    "m_b": np.ascontiguousarray(np.repeat(ada_b[:, None, cs], 2, axis=1))})
    rs = _run(ncM, ims)
    mod = np.concatenate([r["m_o"] for r in rs], axis=-1)
    modT = [np.ascontiguousarray(mod[l].reshape(2, 6, KD, 128).transpose(3, 0, 1, 2)) for l in range(L)]

    xl = x[0].copy(); xc = ctx[0].copy()
    blk, rot = rope_consts()
    ablk, sgn, Eg = attn_consts()
    pos = np.arange(SEQ)
    ncA = _prog(("A", KD, TCc, TLc), lambda P, PS: projA_build(P, PS, KD, TCc, TLc))
    ncS = _prog(("S", CTX, SEQ), lambda P, PS: (rwkv_build(P, PS, CTX, SEQ), ssd_build(P, PS, CTX, SEQ)))
    ncB1 = _prog(("B1", KD, TCc, TLc, CTX, SEQ), lambda P, PS: mixB1_build(P, PS, KD, TCc, TLc, CTX, SEQ))
    ncB2 = _prog(("B2", KD, FC, TCc, TLc), lambda P, PS: ffnB2_build(P, PS, KD, FC, TCc, TLc))
    tabs = []
    for j in range(NC):
        pj = np.concatenate([np.zeros(TCc, np.int64), pos[j * TLc:(j + 1) * TLc]])
        isc = np.concatenate([np.ones(TCc, bool), np.zeros(TLc, bool)])
        tabs.append(rope_tables(pj, isc))
    for l in range(L):
        g = lambda a: np.asarray(a[l], np.float32)
        xTs = [_fm(np.concatenate([xc[j * TCc:(j + 1) * TCc], xl[j * TLc:(j + 1) * TLc]]), KD) for j in range(NC)]
        gain = np.ascontiguousarray(np.tile(g(qk_gain), (1, 2)).T)
        wA = np.ascontiguousarray(g(w_in).reshape(KD, 128, IN_COLS))
        n1 = _fm(g(norm1_g)[None], KD)[:, :, 0]
        ims = [{"a_xT": xTs[j], "a_mod": modT[l], "a_g": n1, "a_w": wA, "a_cos": tabs[j][0], "a_sin": tabs[j][1],
                "a_gain": gain, "a_blk": blk, "a_rot": rot, "a_ones": ones} for j in range(NC)]
        ra = _run(ncA, ims)
        kT_all = np.ascontiguousarray(np.concatenate([r["a_qk"][6:12, :, :TCc] for r in ra] + [r["a_qk"][6:12, :, TCc:] for r in ra], axis=2))
        vT_all = np.concatenate([r["a_v"][:, :, :TCc] for r in ra] + [r["a_v"][:, :, TCc:] for r in ra], axis=2)
        TK = CTX + SEQ
        v_all = np.ascontiguousarray(vT_all.reshape(768, TK).T.reshape(TK // 128, 128, 768).transpose(1, 0, 2))
        u_c = np.concatenate([r["a_u"][:, :TCc] for r in ra], axis=1).T
        u_l = np.concatenate([r["a_u"][:, TCc:] for r in ra], axis=1).T
        p = {"shift_mu": g(shift_mu), "w0": g(w0), "w_up": g(w_up), "a0": g(a0), "a_up": g(a_up), "g_up": g(g_up),
             "k_k": g(k_k), "k_a": g(k_a), "r_k": g(r_k), "lnx_g": g(lnx_g), "lnx_b": g(lnx_b), "conv_w": g(conv_w),
             "conv_b": g(conv_b), "dt_bias": g(dt_bias), "a_log": g(a_log), "d_skip": g(d_skip)}
        ims = []
        for j in range(NC):
            rh, sh = _slots(j)
            d = rwkv_host_inputs(u_c[:, :2304], u_l[:, :2304], p, rh)
            d.update(ssd_host_inputs(u_c[:, 2304:], u_l[:, 2304:], p, sh))
            ims.append(d)
        rsS = _run(ncS, ims)
        if _DBG is not None:
            _DBG[f'mod{l}'] = mod[l]; _DBG[f'u_c{l}'] = u_c; _DBG[f'u_l{l}'] = u_l; _DBG[f'kT{l}'] = kT_all; _DBG[f'v{l}'] = v_all; _DBG[f'q{l}'] = [r['a_qk'][0:6] for r in ra]
        r_c = np.zeros((CTX, 640), np.float32); r_l = np.zeros((SEQ, 640), np.float32)
        y_c = np.zeros((CTX, 640), np.float32); y_l = np.zeros((SEQ, 640), np.float32)
        for h in range(10):
            j, s = h // 2, h % 2
            r_c[:, h * 64:(h + 1) * 64] = rsS[j]["rw_oc"][64 * s:64 * s + 64].T
            r_l[:, h * 64:(h + 1) * 64] = rsS[j]["rw_ol"][64 * s:64 * s + 64].T
            j = 3 + h // 2
            y_c[:, h * 64:(h + 1) * 64] = rsS[j]["sd_yc"][s].transpose(1, 0, 2).reshape(CTX, 64)
            y_l[:, h * 64:(h + 1) * 64] = rsS[j]["sd_yl"][s].transpose(1, 0, 2).reshape(SEQ, 64)
        lam_init = 0.8 - 0.6 * math.exp(-0.3 * l)
        own = lambda ac, al, j: np.concatenate([ac[j * TCc:(j + 1) * TCc], al[j * TLc:(j + 1) * TLc]])
        wO = np.ascontiguousarray(g(w_out).reshape(16, 128, D))
        common = {"b_kT": kT_all, "b_v": v_all, "b_mod": modT[l], "b_g": n1, "b_w": wO,
                  "b_subln": np.ascontiguousarray(g(subln_g)[:, None]), "b_sng": _fm(g(ssm_norm_g)[None], 5)[:, :, 0],
                  "b_lamq": np.ascontiguousarray(g(lam_q).reshape(128, 1)), "b_lamk": np.ascontiguousarray(g(lam_k).reshape(128, 1)),
                  "b_lami": np.full((128, 1), lam_init, np.float32),
                  "b_gains": np.ascontiguousarray(np.broadcast_to(g(qk_gain)[None], (128, 2, 64))),
                  "b_blk": ablk, "b_sgn": sgn, "b_E": Eg, "b_ones": ones}
        ims = []
        for j in range(NC):
            d = dict(common)
            d["b_xT"] = xTs[j]; d["b_q"] = np.ascontiguousarray(ra[j]["a_qk"][0:6])
            d["b_rT"] = _fm(own(r_c, r_l, j), 5); d["b_yT"] = _fm(own(y_c, y_l, j), 5)
            d["b_zT"] = _fm(own(u_c[:, 2304:2304 + 640], u_l[:, 2304:2304 + 640], j), 5)
            ims.append(d)
        rb1 = _run(ncB1, ims)
        if _DBG is not None:
            _DBG[f'r_l{l}'] = r_l; _DBG[f'y_l{l}'] = y_l; _DBG[f'r_c{l}'] = r_c; _DBG[f'y_c{l}'] = y_c; _DBG[f'x1_{l}'] = [_unfm(r['b_o']) for r in rb1]
        n2 = _fm(g(norm2_g)[None], KD)[:, :, 0]
        wi = np.ascontiguousarray(g(w_ffn_in).reshape(KD, 128, 2 * DFF)); wo = np.ascontiguousarray(g(w_ffn_out).reshape(FC, 128, D))
        ims = [{"f_xT": rb1[j]["b_o"], "f_mod": modT[l], "f_g": n2, "f_wi": wi, "f_wo": wo, "f_ones": ones} for j in range(NC)]
        rb2 = _run(ncB2, ims)
        for j in range(NC):
            xo = _unfm(rb2[j]["f_o"])
            xc[j * TCc:(j + 1) * TCc] = xo[:TCc]
            xl[j * TLc:(j + 1) * TLc] = xo[TCc:]
        if _DBG is not None:
            _DBG[f'xl{l}'] = xl.copy(); _DBG[f'xc{l}'] = xc.copy()
            if _DBG.get('stop_after') == l:
                return xl[None].astype(np.float32)
    return xl[None].astype(np.float32)
```

```python
import math
from contextlib import ExitStack
import numpy as np
import ml_dtypes
import concourse.bass as bass
import concourse.mybir as mybir
from concourse.bass_utils import run_bass_kernel_spmd

F32 = mybir.dt.float32
BF16 = mybir.dt.bfloat16
AF = mybir.ActivationFunctionType
ALU = mybir.AluOpType
AX = mybir.AxisListType

COMPUTE = ("pe", "act", "dve", "pool")
STRICT_SYNC = True


class _Ins:
    __slots__ = ("eng", "fn", "waits", "is_dma", "tok", "idx", "need_inc")

    def __init__(self, eng, fn, is_dma):
        self.eng, self.fn, self.is_dma = eng, fn, is_dma
        self.waits = []
        self.tok = None
        self.need_inc = False


class _Trk:
    __slots__ = ("writers", "readers", "const")

    def __init__(self):
        self.writers = []
        self.readers = {}
        self.const = False


class Prog:
    def __init__(self, n_dma_sems=24):
        self.nc = bass.Bass("TRN2", target_bir_lowering=False)
        self.es = ExitStack()
        self.ins = []
        self.trk = {}
        self.n_dma_sems = n_dma_sems
        self.dma_rr = {"sp": 0, "pool": 0, "act": 0}
        self.dma_prev = {}
        self.out_dmas = []
        self._uid = 0
        self.rings = {}
        self.psum_names = set()
        self.debug = False
        self.dbg_out = {}

    def sb(self, name, shape, dtype=F32):
        t = self.es.enter_context(self.nc.sbuf_tensor(name, list(shape), dtype))
        return t

    def ps(self, name, shape, dtype=F32):
        t = self.es.enter_context(self.nc.psum_tensor(name, list(shape), dtype))
        self.psum_names.add(name)
        return t

    def dram(self, name, shape, dtype=F32, kind="Internal"):
        t = self.nc.dram_tensor(name, list(shape), dtype, kind=kind)
        if kind == "ExternalInput":
            self._t(name).const = True
        return t

    def ring(self, name, n, shape, dtype=F32, psum=False):
        tiles = [(self.ps if psum else self.sb)(f"{name}{i}", shape, dtype) for i in range(n)]
        self.rings[name] = [tiles, 0]
        return name

    def nxt(self, name):
        r = self.rings[name]
        t = r[0][r[1] % len(r[0])]
        r[1] += 1
        return t

    def _t(self, name):
        t = self.trk.get(name)
        if t is None:
            t = self.trk[name] = _Trk()
        return t

    def _rec(self, eng, fn, reads, writes, is_dma=False):
        x = _Ins(eng, fn, is_dma)
        x.idx = len(self.ins)
        deps = []
        rn = []
        for a in reads:
            if a is None or isinstance(a, (int, float)):
                continue
            rn.append(a.tensor.name)
        wn = [a.tensor.name for a in writes]
        raw = set()
        for n in rn:
            deps.extend(self._t(n).writers)
            for w_ in self._t(n).writers:
                raw.add(w_.idx)
            if n in self.psum_names:
                deps.extend(r_ for k_, r_ in self._t(n).readers.items() if k_ != eng)
        for n in wn:
            t = self._t(n)
            deps.extend(t.writers)
            deps.extend(t.readers.values())
        seen = set()
        for d in deps:
            if d.idx in seen:
                continue
            seen.add(d.idx)
            if (not d.is_dma) and (not is_dma) and d.eng == eng:
                if eng == "pe" or (d.idx not in raw and not STRICT_SYNC):
                    continue
            x.waits.append(d)
            d.need_inc = True
        for n in rn:
            t = self._t(n)
            if t.const:
                continue
            key = ("dma", x.idx) if is_dma else eng
            t.readers[key] = x
            if len(t.readers) > 48:
                ks = [k for k in t.readers if isinstance(k, tuple)]
                pass
        for n in wn:
            t = self._t(n)
            t.writers = [x]
            t.readers = {}
        self.ins.append(x)
        return x

    def mm(self, out, lhsT, rhs, start=True, stop=True):
        rd = [lhsT, rhs] + ([] if start else [out])
        return self._rec("pe", lambda e: e.matmul(out, lhsT, rhs, start=start, stop=stop), rd, [out])

    def transpose(self, out, in_, ident):
        return self._rec("pe", lambda e: e.transpose(out, in_, ident), [in_, ident], [out])

    def act(self, out, in_, func, bias=None, scale=None, accum=None, eng="act"):
        kw = {}
        if bias is not None:
            kw["bias"] = bias
        if scale is not None:
            kw["scale"] = scale
        if accum is not None:
            kw["accum_out"] = accum
        rd = [in_, bias if hasattr(bias, "tensor") else None, scale if hasattr(scale, "tensor") else None]
        wr = [out] + ([accum] if accum is not None else [])
        return self._rec("act", lambda e: e.activation(out, in_, func, **kw), rd, wr)

    def tt(self, out, a, b, op, eng="dve"):
        return self._rec(eng, lambda e: e.tensor_tensor(out, a, b, op), [a, b], [out])

    def ts(self, out, a, s1, op0, s2=None, op1=None, eng="dve", accum=None):
        rd = [a, s1 if hasattr(s1, "tensor") else None, s2 if hasattr(s2, "tensor") else None]
        if op1 is None:
            return self._rec(eng, lambda e: e.tensor_scalar(out, a, s1, None, op0), rd, [out])
        kw = {}
        wr = [out]
        if accum is not None:
            kw["accum_out"] = accum
            wr.append(accum)
        return self._rec(eng, lambda e: e.tensor_scalar(out, a, s1, s2, op0, op1, **kw), rd, wr)

    def stt(self, out, in0, scalar, in1, op0, op1, eng="dve"):
        rd = [in0, in1, scalar if hasattr(scalar, "tensor") else None]
        return self._rec(eng, lambda e: e.scalar_tensor_tensor(out, in0, scalar, in1, op0, op1), rd, [out])

    def copy(self, out, in_, eng="dve"):
        if eng == "act":
            return self._rec("act", lambda e: e.copy(out, in_), [in_], [out])
        return self._rec(eng, lambda e: e.tensor_copy(out, in_), [in_], [out])

    def memset(self, out, val, eng="dve"):
        return self._rec(eng, lambda e: e.memset(out, val), [], [out])

    def reduce(self, out, in_, op, axis=AX.X, eng="dve"):
        return self._rec(eng, lambda e: e.tensor_reduce(out, in_, axis, op), [in_], [out])

    def scan(self, out, d0, d1, init, op0, op1):
        return self._rec("dve", lambda e: e.tensor_tensor_scan(out, d0, d1, init, op0, op1), [d0, d1], [out])

    def recip(self, out, in_):
        return self._rec("dve", lambda e: e.reciprocal(out, in_), [in_], [out])

    def dma(self, out, in_, q="sp", is_out=False):
        x = self._rec(q, lambda e: e.dma_start(out=out, in_=in_), [in_], [out], is_dma=True)
        k = self.dma_rr[q] % self.n_dma_sems
        self.dma_rr[q] += 1
        key = (q, k)
        prev = self.dma_prev.get(key)
        val = (prev.tok[1] if prev is not None else 0) + 16
        x.tok = (key, val)
        if prev is not None and prev not in x.waits:
            x.waits.append(prev)
        self.dma_prev[key] = x
        if is_out:
            self.out_dmas.append(x)
        return x

    def dbg(self, name, ap):
        if not getattr(self, "debug", False):
            return
        if name in self.dbg_out:
            return
        t = self.dram("dbg_" + name, list(ap.shape), F32, "ExternalOutput")
        self.dbg_out[name] = t
        self.dma(t.ap(), ap, is_out=True)

    def finish(self):
        nc = self.nc
        engs = {"pe": [], "act": [], "dve": [], "pool": [], "sp": []}
        for x in self.ins:
            engs[x.eng].append(x)
        sems = {}
        for e in COMPUTE:
            sems[e] = self.es.enter_context(nc.semaphore(f"s_{e}"))
            c = 0
            for x in engs[e]:
                if x.is_dma:
                    continue
                if x.need_inc:
                    c += 1
                    x.tok = (e, c)
        for q in ("sp", "pool", "act"):
            for k in range(self.n_dma_sems):
                if (q, k) in self.dma_prev:
                    sems[(q, k)] = self.es.enter_context(nc.semaphore(f"d_{q}{k}"))
        block = self.es.enter_context(nc.Block())
        out_dmas = self.out_dmas

        def emit(engname):
            def body(e):
                waited = {}
                for x in engs[engname]:
                    for d in x.waits:
                        s, v = d.tok
                        if waited.get(s, 0) >= v:
                            continue
                        waited[s] = v
                        e.wait_ge(sems[s], v)
                    r = x.fn(e)
                    if x.is_dma:
                        r.then_inc(sems[x.tok[0]], 16)
                    elif x.need_inc:
                        r.then_inc(sems[x.tok[0]], 1)
                if engname == "sp":
                    for d in out_dmas:
                        s, v = d.tok
                        if waited.get(s, 0) >= v:
                            continue
                        waited[s] = v
                        e.wait_ge(sems[s], v)
            return body

        block.sync(emit("sp"))
        block.tensor(emit("pe"))
        block.scalar(emit("act"))
        block.vector(emit("dve"))
        block.gpsimd(emit("pool"))
        self.es.close()
        return nc


RW_C = 64
E05 = math.exp(-0.5)
LN_X_EPS = 64e-5
PR_MU0, PR_MU1, PR_M2, PR_W0, PR_A0, PR_KK, PR_KA, PR_1MKA, PR_RK, PR_LG, PR_LB = 0, 6, 12, 18, 20, 22, 23, 24, 25, 26, 27
PR_N = 28


def rwkv_consts():
    c = {}
    blk = np.zeros((128, 128), np.float32); blk[:64, :64] = 1; blk[64:, 64:] = 1
    c["blk64"] = blk
    c["ident"] = np.eye(128, dtype=np.float32)
    i2 = np.zeros((128, 64), np.float32); i2[:64] = np.eye(64); i2[64:] = np.eye(64)
    c["i2"] = i2
    p = np.arange(64)[:, None]; f = np.arange(64)[None, :]
    m = np.zeros((2, 64, 5, 64), np.float32)
    m[0, :, 0] = f < p; m[0, :, 1] = f > p; m[0, :, 2] = f > p; m[0, :, 3] = f >= p; m[0, :, 4] = f >= p
    m[1, :, 0] = f > p; m[1, :, 1] = f < p; m[1, :, 2] = f < p; m[1, :, 3] = f <= p; m[1, :, 4] = f <= p
    c["mask5"] = np.concatenate([m, m], axis=1).reshape(2, 128, 320)
    return c


def rwkv_build(P, PS, Tc, Tl, TB=512):
    C = RW_C
    io = {}
    io["uc"] = P.dram("rw_uc", [128, 6, Tc + 2], F32, "ExternalInput")
    io["ul"] = P.dram("rw_ul", [128, 6, Tl + 2], F32, "ExternalInput")
    io["pr"] = P.dram("rw_pr", [128, PR_N], F32, "ExternalInput")
    io["wup"] = P.dram("rw_wup", [128, 128], F32, "ExternalInput")
    io["aup"] = P.dram("rw_aup", [128, 128], F32, "ExternalInput")
    io["gup"] = P.dram("rw_gup", [128, 128], F32, "ExternalInput")
    io["blk64"] = P.dram("c_blk64", [128, 128], F32, "ExternalInput")
    io["ident"] = P.dram("c_ident", [128, 128], F32, "ExternalInput")
    io["i2"] = P.dram("c_i2", [128, 64], F32, "ExternalInput")
    io["mask5"] = P.dram("c_mask5", [2, 128, 320], F32, "ExternalInput")
    io["oc"] = P.dram("rw_oc", [128, Tc], F32, "ExternalOutput")
    io["ol"] = P.dram("rw_ol", [128, Tl], F32, "ExternalOutput")
    yf = {"c": P.dram("rw_yfc", [128, Tc], F32), "l": P.dram("rw_yfl", [128, Tl], F32)}
    usrc = {"c": io["uc"], "l": io["ul"]}
    odst = {"c": io["oc"], "l": io["ol"]}

    def cst(name, shape):
        t = P.sb("k_" + name, shape)
        P.dma(t[:], io[name].ap())
        return t
    pr = cst("pr", [128, PR_N]); wup = cst("wup", [128, 128]); aup = cst("aup", [128, 128]); gup = cst("gup", [128, 128])
    blk = cst("blk64", [128, 128]); ident = cst("ident", [128, 128]); i2 = cst("i2", [128, 64])
    mask5 = []
    for d in range(2):
        t = P.sb(f"k_mask5_{d}", [128, 320])
        P.dma(t[:], io["mask5"].ap()[d])
        mask5.append(t)
    rst = P.sb("k_rst", [128, TB])
    P.memset(rst[:], 1.0)
    P.memset(rst[:].rearrange("p (c t) -> p c t", t=C)[:, :, 0:1], 0.0)
    pc = lambda i: pr[:, i:i + 1]
    P.ts(pr[:, PR_M2:PR_M2 + 6], pr[:, PR_MU0:PR_MU0 + 6], -1.0, ALU.mult, 1.0, ALU.add)
    P.tt(pr[:, PR_M2:PR_M2 + 6], pr[:, PR_M2:PR_M2 + 6], pr[:, PR_MU1:PR_MU1 + 6], ALU.subtract)
    P.ts(pr[:, PR_1MKA:PR_1MKA + 1], pr[:, PR_KA:PR_KA + 1], -1.0, ALU.mult, 1.0, ALU.add)

    for nm in ("u6",):
        P.ring("rw_" + nm, 1, [128, 6, TB + 2])
    for nm in ("xs", ):
        P.ring("rw_" + nm, 1, [128, 6, TB])
    for nm in ("tw", "sg", "kx", "rn", "kk", "g", "lw0", "lw1", "ic0", "ic1", "k0", "k1", "b0", "b1"):
        P.ring("rw_" + nm, 1, [128, TB])
    for nm in ("sq", "f1", "t1"):
        P.ring("rw_" + nm, 2, [128, TB])
    for nm in ("cw", "cwb", "ep", "en", "ea", "At", "Bt", "Kt", "Rt", "yT", "t2", "t3", "t4", "yfl"):
        P.ring("rw_" + nm, 1, [128, TB])
    P.ring("rw_tok", 10, [128, 3, 64])
    P.ring("rw_M5", 10, [128, 5, 64])
    P.ring("rw_TT", 10, [128, 64])
    P.ring("rw_PP", 4, [128, 128])
    P.ring("rw_Xs", 3, [128, 64])
    P.ring("rw_Us", 3, [128, 64])
    S = [P.sb("rw_S0", [128, 64]), P.sb("rw_S1", [128, 64])]
    Stmp = P.sb("rw_Stmp", [128, 64])
    ps_tok, ps5, psL, psT, psX, psU, psY, psS = PS

    def prep(seq, t0, n):
        u6 = P.nxt("rw_u6")
        P.dma(u6[:, :, 0:n + 2], usrc[seq].ap()[:, :, t0:t0 + n + 2])
        xs = P.nxt("rw_xs")
        P.dbg("u6", u6[:, 0, 0:n + 2]); P.dbg("pr", pr[:])
        for j in range(6):
            P.ts(xs[:, j, 0:n], u6[:, j, 1:n + 1], pc(PR_M2 + j), ALU.mult)
            P.stt(xs[:, j, 0:n], u6[:, j, 0:n], pc(PR_MU0 + j), xs[:, j, 0:n], ALU.mult, ALU.add)
            P.stt(xs[:, j, 0:n], u6[:, j, 2:n + 2], pc(PR_MU1 + j), xs[:, j, 0:n], ALU.mult, ALU.add)
        r, k, v, wd, ad, gd = [xs[:, j, 0:n] for j in range(6)]
        o = {"r": r, "v": v}
        tw = P.nxt("rw_tw")[:, 0:n]
        P.act(tw, wd, AF.Tanh)
        for d in range(2):
            hs = slice(64 * d, 64 * d + 64)
            P.mm(psL[:, 0:n], wup[hs, :], tw[hs, :])
            lw = P.nxt(f"rw_lw{d}")[:, 0:n]
            P.act(lw, psL[:, 0:n], AF.Sigmoid, bias=pc(PR_W0 + d))
            P.ts(lw, lw, -E05, ALU.mult)
            o[f"lw{d}"] = lw
            P.mm(psT[:, 0:n], aup[hs, :], ad[hs, :])
            ic = P.nxt(f"rw_ic{d}")[:, 0:n]
            P.act(ic, psT[:, 0:n], AF.Sigmoid, bias=pc(PR_A0 + d))
            o[f"ic{d}"] = ic
        kx = P.nxt("rw_kx")[:, 0:n]
        P.ts(kx, k, pc(PR_KK), ALU.mult)
        sq = P.nxt("rw_sq")[:, 0:n]
        P.tt(sq, kx, kx, ALU.mult)
        P.mm(psX[:, 0:n], blk[:], sq)
        rn = P.nxt("rw_rn")[:, 0:n]
        P.act(rn, psX[:, 0:n], AF.Sqrt, bias=1e-12)
        P.recip(rn, rn)
        kk = P.nxt("rw_kk")[:, 0:n]
        P.tt(kk, kx, rn, ALU.mult)
        o["kk"] = kk
        for d in range(2):
            f1 = P.nxt("rw_f1")[:, 0:n]
            P.ts(f1, o[f"ic{d}"], pc(PR_KA), ALU.mult, pc(PR_1MKA), ALU.add)
            kd = P.nxt(f"rw_k{d}")[:, 0:n]
            P.tt(kd, k, f1, ALU.mult)
            bd = P.nxt(f"rw_b{d}")[:, 0:n]
            P.tt(bd, kk, o[f"ic{d}"], ALU.mult)
            o[f"k{d}"] = kd
            o[f"b{d}"] = bd
        sg = P.nxt("rw_sg")[:, 0:n]
        P.act(sg, gd, AF.Sigmoid)
        P.mm(psU[:, 0:n], gup[:], sg)
        g = P.nxt("rw_g")[:, 0:n]
        P.copy(g, psU[:, 0:n], eng="act")
        o["g"] = g
        for kname in ("r", "v", "kk", "lw0", "lw1", "ic0", "k0", "b0", "g"):
            P.dbg("p_" + kname, o[kname])
        return o

    def scan_block(d, o, n):
        nch = n // C
        lw = o[f"lw{d}"]
        cw = P.nxt("rw_cw")[:, 0:n]
        P.scan(cw, rst[:, 0:n], lw, 0.0, ALU.mult, ALU.add)
        v3 = lambda a: a.rearrange("p (c t) -> p c t", t=C)
        if d == 1:
            cwb = P.nxt("rw_cwb")[:, 0:n]
            P.tt(cwb, lw, cw, ALU.subtract)
            for c in range(nch):
                P.ts(cwb[:, c * C:(c + 1) * C], cwb[:, c * C:(c + 1) * C], cw[:, c * C + C - 1:c * C + C], ALU.add)
            cw = cwb
        ep = P.nxt("rw_ep")[:, 0:n]; en = P.nxt("rw_en")[:, 0:n]; ea = P.nxt("rw_ea")[:, 0:n]
        P.act(ep, cw, AF.Exp)
        P.act(en, cw, AF.Exp, scale=-1.0)
        t1 = P.nxt("rw_t1")[:, 0:n]
        P.tt(t1, cw, lw, ALU.subtract)
        P.act(ea, t1, AF.Exp)
        At = P.nxt("rw_At")[:, 0:n]; Bt = P.nxt("rw_Bt")[:, 0:n]; Kt = P.nxt("rw_Kt")[:, 0:n]; Rt = P.nxt("rw_Rt")[:, 0:n]
        P.stt(At, o["kk"], -1.0, ea, ALU.mult, ALU.mult)
        P.tt(Bt, o[f"b{d}"], en, ALU.mult)
        P.tt(Kt, o[f"k{d}"], en, ALU.mult)
        P.tt(Rt, o["r"], ep, ALU.mult)
        wcol = (C - 1) if d == 0 else 0
        v = o["v"]
        P.dbg(f"cw{d}", cw); P.dbg(f"At{d}", At); P.dbg(f"Bt{d}", Bt); P.dbg(f"Rt{d}", Rt)
        pre = []
        for c in range(nch):
            cs = slice(c * C, (c + 1) * C)
            tok = P.nxt("rw_tok")
            for j, X in enumerate((Bt, Kt, v)):
                for s in range(2):
                    hs = slice(64 * s, 64 * s + 64)
                    P.mm(ps_tok[hs, j * 64:(j + 1) * 64], X[hs, cs], ident[hs, hs])
            P.copy(tok[:].rearrange("p a b -> p (a b)"), ps_tok[:, 0:192], eng="act")
            M5 = P.nxt("rw_M5")
            for s in range(2):
                hs = slice(64 * s, 64 * s + 64)
                for j, (l, rr) in enumerate(((At, Bt), (Bt, At), (Kt, At), (Bt, Rt), (Kt, Rt))):
                    P.mm(ps5[hs, j * 64:(j + 1) * 64], l[hs, cs], rr[hs, cs])
            P.tt(M5[:].rearrange("p a b -> p (a b)"), ps5[:, 0:320], mask5[d][:], ALU.mult)
            TT = P.nxt("rw_TT")
            P.tt(TT[:], M5[:, 1, :], i2[:], ALU.add)
            Pm, PTm = M5[:, 0, :], M5[:, 1, :]
            for lvl in range(1, 6):
                for s in range(2):
                    hs = slice(64 * s, 64 * s + 64)
                    P.mm(psL[hs, 0:64], PTm[hs, :], Pm[hs, :])
                    if lvl < 5:
                        P.mm(psL[hs, 64:128], Pm[hs, :], PTm[hs, :])
                PP = P.nxt("rw_PP")
                w = 128 if lvl < 5 else 64
                P.copy(PP[:, 0:w], psL[:, 0:w], eng="act")
                Pm, PTm = PP[:, 0:64], PP[:, 64:128]
                for s in range(2):
                    hs = slice(64 * s, 64 * s + 64)
                    P.mm(psT[hs, 0:64], Pm[hs, :], TT[hs, :])
                P.tt(TT[:], TT[:], psT[:, 0:64], ALU.add)
            pre.append((tok, M5, TT))
            if c == 0:
                P.dbg(f"tok{d}", tok[:].rearrange("p a b -> p (a b)")); P.dbg(f"M5{d}", M5[:].rearrange("p a b -> p (a b)")); P.dbg(f"TT{d}", TT[:])
        Sd = S[d]
        order = range(nch) if d == 0 else range(nch - 1, -1, -1)
        for c in order:
            cs = slice(c * C, (c + 1) * C)
            tok, M5, TT = pre[c]
            Btok, Ktok, Vtok = tok[:, 0, :], tok[:, 1, :], tok[:, 2, :]
            AkT, LrbT, LrkT = M5[:, 2, :], M5[:, 3, :], M5[:, 4, :]
            for s in range(2):
                hs = slice(64 * s, 64 * s + 64)
                P.mm(psX[hs, 0:64], At[hs, cs], Sd[hs, :], start=True, stop=False)
                P.mm(psX[hs, 0:64], AkT[hs, :], Vtok[hs, :], start=False, stop=True)
            Xs = P.nxt("rw_Xs")
            P.copy(Xs[:], psX[:, 0:64], eng="act")
            for s in range(2):
                hs = slice(64 * s, 64 * s + 64)
                P.mm(psU[hs, 0:64], TT[hs, :], Xs[hs, :])
            Us = P.nxt("rw_Us")
            P.copy(Us[:], psU[:, 0:64])
            for s in range(2):
                hs = slice(64 * s, 64 * s + 64)
                P.mm(psY[hs, cs], Sd[hs, :], Rt[hs, cs], start=True, stop=False)
                P.mm(psY[hs, cs], Us[hs, :], LrbT[hs, :], start=False, stop=False)
                P.mm(psY[hs, cs], Vtok[hs, :], LrkT[hs, :], start=False, stop=True)
                P.mm(psS[hs, 0:64], Btok[hs, :], Us[hs, :], start=True, stop=False)
                P.mm(psS[hs, 0:64], Ktok[hs, :], Vtok[hs, :], start=False, stop=True)
            P.tt(Stmp[:], Sd[:], psS[:, 0:64], ALU.add)
            P.ts(Sd[:], Stmp[:], ep[:, c * C + wcol:c * C + wcol + 1], ALU.mult)
        yT = P.nxt("rw_yT")[:, 0:n]
        P.copy(yT, psY[:, 0:n], eng="act")
        P.dbg(f"yT{d}", yT)
        return yT

    def blocks(seq, T):
        return [(seq, t0, min(TB, T - t0)) for t0 in range(0, T, TB)]
    fwd = blocks("c", Tc) + blocks("l", Tl)
    bwd = blocks("c", Tc)[::-1] + blocks("l", Tl)[::-1]
    P.memset(S[0][:], 0.0); P.memset(S[1][:], 0.0)
    for (seq, t0, n) in fwd:
        o = prep(seq, t0, n)
        yT = scan_block(0, o, n)
        P.dma(yf[seq].ap()[:, t0:t0 + n], yT)
    for (seq, t0, n) in bwd:
        o = prep(seq, t0, n)
        yb = scan_block(1, o, n)
        yfl = P.nxt("rw_yfl")[:, 0:n]
        P.dma(yfl, yf[seq].ap()[:, t0:t0 + n])
        y = P.nxt("rw_t1")[:, 0:n]
        P.tt(y, yb, yfl, ALU.add)
        P.mm(psL[:, 0:n], blk[:], y)
        yc = P.nxt("rw_t2")[:, 0:n]
        P.stt(yc, psL[:, 0:n], -1.0 / 64, y, ALU.mult, ALU.add)
        sq = P.nxt("rw_sq")[:, 0:n]
        P.tt(sq, yc, yc, ALU.mult)
        P.mm(psT[:, 0:n], blk[:], sq)
        sd = P.nxt("rw_rn")[:, 0:n]
        P.act(sd, psT[:, 0:n], AF.Sqrt, bias=LN_X_EPS, scale=1.0 / 64)
        P.recip(sd, sd)
        P.tt(yc, yc, sd, ALU.mult)
        ov = P.nxt("rw_t3")[:, 0:n]
        P.ts(ov, yc, pc(PR_LG), ALU.mult, pc(PR_LB), ALU.add)
        ks = P.nxt("rw_t4")[:, 0:n]
        P.tt(ks, o["k0"], o["k1"], ALU.add)
        P.stt(ks, ks, pc(PR_RK), o["r"], ALU.mult, ALU.mult)
        P.mm(psX[:, 0:n], blk[:], ks)
        P.tt(ks, psX[:, 0:n], o["v"], ALU.mult)
        P.tt(ov, ov, ks, ALU.add)
        P.tt(ov, ov, o["g"], ALU.mult)
        P.dma(odst[seq].ap()[:, t0:t0 + n], ov, is_out=True)
    return io


def rwkv_host_inputs(rw_c, rw_l, p, heads):
    W = 640
    cols = []
    for base in (0, W, 2 * W):
        cols.append(np.concatenate([np.arange(base + h * 64, base + h * 64 + 64) for h in heads]))
    cols.append(np.arange(3 * W, 3 * W + 128)); cols.append(np.arange(3 * W + 128, 3 * W + 256)); cols.append(np.arange(3 * W + 256, 3 * W + 384))
    cols = np.stack(cols)

    def lay(u):
        T = u.shape[0]
        o = np.zeros((128, 6, T + 2), np.float32)
        o[:, :, 1:T + 1] = u[:, cols].transpose(2, 1, 0)
        return o
    hc = np.concatenate([np.arange(h * 64, h * 64 + 64) for h in heads])
    pr = np.zeros((128, PR_N), np.float32)
    mu = p["shift_mu"]
    pr[:, PR_MU0:PR_MU0 + 6] = mu[0][cols].T
    pr[:, PR_MU1:PR_MU1 + 6] = mu[1][cols].T
    pr[:, PR_W0:PR_W0 + 2] = p["w0"][:, hc].T
    pr[:, PR_A0:PR_A0 + 2] = p["a0"][:, hc].T
    pr[:, PR_KK] = p["k_k"][hc]; pr[:, PR_KA] = p["k_a"][hc]
    pr[:, PR_RK] = p["r_k"].reshape(-1)[hc]; pr[:, PR_LG] = p["lnx_g"][hc]; pr[:, PR_LB] = p["lnx_b"][hc]
    d = {"rw_uc": lay(rw_c), "rw_ul": lay(rw_l), "rw_pr": pr,
         "rw_wup": np.ascontiguousarray(p["w_up"][:, :, hc].reshape(128, 128)),
         "rw_aup": np.ascontiguousarray(p["a_up"][:, :, hc].reshape(128, 128)),
         "rw_gup": np.ascontiguousarray(p["g_up"][:, hc])}
    for k, v in rwkv_consts().items():
        d["c_" + k] = v
    return d


SQ = 128


def ssd_consts():
    p = np.arange(128)[:, None]; f = np.arange(128)[None, :]
    tri = np.stack([(f >= p), (f <= p)]).astype(np.float32)
    return {"tri": tri, "ident": np.eye(128, dtype=np.float32), "ones": np.ones((128, 128), np.float32)}


def ssd_build(P, PS, Tc, Tl, TB=512):
    Q = SQ
    io = {}
    T_ = {"c": Tc, "l": Tl}
    for sq in ("c", "l"):
        T = T_[sq]
        io["x" + sq] = P.dram("sd_x" + sq, [2, 64, T + 2], F32, "ExternalInput")
        io["b" + sq] = P.dram("sd_b" + sq, [2, 128, T + 2], F32, "ExternalInput")
        io["c" + sq] = P.dram("sd_c" + sq, [2, 128, T + 2], F32, "ExternalInput")
        io["dt" + sq] = P.dram("sd_dt" + sq, [2, 128, T // Q, 2], F32, "ExternalInput")
        io["y" + sq] = P.dram("sd_y" + sq, [2, 128, T // Q, 64], F32, "ExternalOutput")
    io["px"] = P.dram("sd_px", [2, 64, 4], F32, "ExternalInput")
    io["pb"] = P.dram("sd_pb", [2, 128, 4], F32, "ExternalInput")
    io["pc"] = P.dram("sd_pc", [2, 128, 4], F32, "ExternalInput")
    io["pd"] = P.dram("sd_pd", [2, 128, 6], F32, "ExternalInput")
    io["tri"] = P.dram("c_tri", [2, 128, 128], F32, "ExternalInput")
    io["ident"] = P.dram("c_identb", [128, 128], F32, "ExternalInput")
    io["ones"] = P.dram("c_ones", [128, 128], F32, "ExternalInput")
    yfd = {sq: P.dram("sd_yf" + sq, [2, 128, T_[sq] // Q, 64], F32) for sq in ("c", "l")}

    def cst(nm, src, shape):
        t = P.sb("sk_" + nm, shape)
        P.dma(t[:], src)
        return t
    tri = [cst(f"tri{d}", io["tri"].ap()[d], [128, 128]) for d in range(2)]
    ident = cst("ident", io["ident"].ap(), [128, 128])
    ones = cst("ones", io["ones"].ap(), [128, 128])
    px = [cst(f"px{s}", io["px"].ap()[s], [64, 4]) for s in range(2)]
    pb = [cst(f"pb{s}", io["pb"].ap()[s], [128, 4]) for s in range(2)]
    pcc = [cst(f"pc{s}", io["pc"].ap()[s], [128, 4]) for s in range(2)]
    pd = [cst(f"pd{s}", io["pd"].ap()[s], [128, 6]) for s in range(2)]
    Aneg = [P.sb(f"sk_A{s}", [128, 2]) for s in range(2)]
    dsum = [P.sb(f"sk_ds{s}", [128, 1]) for s in range(2)]
    for s in range(2):
        P.act(Aneg[s][:], pd[s][:, 2:4], AF.Exp)
        P.ts(Aneg[s][:], Aneg[s][:], -1.0, ALU.mult)
        P.tt(dsum[s][:], pd[s][:, 4:5], pd[s][:, 5:6], ALU.add)

    P.ring("sd_xin", 2, [64, TB + 2]); P.ring("sd_bin", 2, [128, TB + 2]); P.ring("sd_cin", 2, [128, TB + 2])
    P.ring("sd_xf", 4, [64, TB]); P.ring("sd_bf", 4, [128, TB]); P.ring("sd_cf", 4, [128, TB])
    P.ring("sd_dt", 4, [128, TB // Q, 2]); P.ring("sd_dta", 4, [128, TB // Q, 2]); P.ring("sd_tmp8", 6, [128, TB // Q, 2])
    P.ring("sd_xtok", 12, [128, 64]); P.ring("sd_btok", 12, [128, 128]); P.ring("sd_cbm", 12, [128, 128])
    P.ring("sd_bc", 3, [128, 128]); P.ring("sd_E", 3, [128, 128]); P.ring("sd_G", 3, [128, 128]); P.ring("sd_Cs", 3, [128, 128])
    P.ring("sd_col", 12, [128, 4]); P.ring("sd_xdt", 3, [128, 64]); P.ring("sd_xw", 3, [128, 64])
    P.ring("sd_yblk", 3, [128, TB // Q, 64]); P.ring("sd_yfl", 3, [128, TB // Q, 64])
    H = [[P.sb(f"sd_h{s}{d}", [128, 64]) for d in range(2)] for s in range(2)]
    ps_a, ps_b, ps_c, ps_d, ps_e, ps_f, ps_g, ps_h = PS

    def prep(s, seq, t0, n):
        nq = n // Q
        o = {}
        for nm, ring_in, ring_f, src, par, rows in (("x", "sd_xin", "sd_xf", io["x" + seq], px[s], 64),
                                                    ("b", "sd_bin", "sd_bf", io["b" + seq], pb[s], 128),
                                                    ("c", "sd_cin", "sd_cf", io["c" + seq], pcc[s], 128)):
            tin = P.nxt(ring_in)
            P.dma(tin[:, 0:n + 2], src.ap()[s][:, t0:t0 + n + 2])
            tf = P.nxt(ring_f)[:, 0:n]
            P.ts(tf, tin[:, 1:n + 1], par[:, 1:2], ALU.mult, par[:, 3:4], ALU.add)
            P.stt(tf, tin[:, 0:n], par[:, 0:1], tf, ALU.mult, ALU.add)
            P.stt(tf, tin[:, 2:n + 2], par[:, 2:3], tf, ALU.mult, ALU.add)
            P.act(tf, tf, AF.Silu)
            o[nm] = tf
        dtr = P.nxt("sd_dt")[:, 0:nq, :]
        P.dma(dtr, io["dt" + seq].ap()[s][:, t0 // Q:t0 // Q + nq, :])
        dt = P.nxt("sd_dt")[:, 0:nq, :]
        dta = P.nxt("sd_dta")[:, 0:nq, :]
        for d in range(2):
            xb = P.nxt("sd_tmp8")[:, 0:nq, d:d + 1]
            P.ts(xb, dtr[:, :, d:d + 1], pd[s][:, d:d + 1], ALU.add)
            ab = P.nxt("sd_tmp8")[:, 0:nq, d:d + 1]
            P.act(ab, xb, AF.Abs)
            P.act(ab, ab, AF.Exp, scale=-1.0)
            P.act(ab, ab, AF.Ln, bias=1.0)
            P.ts(xb, xb, 0.0, ALU.max)
            P.tt(dt[:, :, d:d + 1], xb, ab, ALU.add)
            P.ts(dta[:, :, d:d + 1], dt[:, :, d:d + 1], Aneg[s][:, d:d + 1], ALU.mult)
        o["dt"] = dt; o["dta"] = dta
        o["xtok"] = []; o["btok"] = []; o["cb"] = []
        for c in range(nq):
            cs = slice(c * Q, (c + 1) * Q)
            P.mm(ps_a[:, 0:64], o["x"][:, cs], ident[0:64, 0:64])
            xt = P.nxt("sd_xtok")
            P.copy(xt[:], ps_a[:, 0:64], eng="act")
            P.mm(ps_b[:, 0:128], o["b"][:, cs], ident[:])
            bt = P.nxt("sd_btok")
            P.copy(bt[:], ps_b[:, 0:128], eng="act")
            P.mm(ps_c[:, 0:128], o["b"][:, cs], o["c"][:, cs])
            o["xtok"].append(xt); o["btok"].append(bt); o["cb"].append(ps_c)
            cbm = []
            for d in range(2):
                m = P.nxt("sd_cbm")
                P.tt(m[:], ps_c[:, 0:128], tri[d][:], ALU.mult)
                cbm.append(m)
            o["cb"][-1] = cbm
        return o

    def scan(s, d, o, n):
        nq = n // Q
        yb = P.nxt("sd_yblk")
        h = H[s][d]
        order = range(nq) if d == 0 else range(nq - 1, -1, -1)
        last = Q - 1 if d == 0 else 0
        for c in order:
            cs = slice(c * Q, (c + 1) * Q)
            dta = o["dta"][:, c, d:d + 1]
            dt = o["dt"][:, c, d:d + 1]
            bc = P.nxt("sd_bc")
            P.ts(bc[:], ones[:], dta, ALU.mult)
            P.mm(ps_d[:, 0:128], bc[:], tri[d][:])
            P.mm(ps_e[:, 0:1], tri[d][:], dta)
            col = P.nxt("sd_col")
            P.copy(col[:, 0:1], ps_e[:, 0:1])
            E = P.nxt("sd_E")
            P.ts(E[:], ps_d[:, 0:128], col[:, 0:1], ALU.subtract, 0.0, ALU.min)
            P.act(E[:], E[:], AF.Exp)
            G = P.nxt("sd_G")
            P.tt(G[:], E[:], o["cb"][c][d][:], ALU.mult)
            Cs = P.nxt("sd_Cs")
            P.act(Cs[:], ps_d[:, 0:128], AF.Exp)
            P.tt(Cs[:], Cs[:], o["c"][:, cs], ALU.mult)
            xdt = P.nxt("sd_xdt")
            P.ts(xdt[:], o["xtok"][c][:], dt, ALU.mult)
            P.mm(ps_f[:, 0:64], G[:], xdt[:], start=True, stop=False)
            P.mm(ps_f[:, 0:64], Cs[:], h[:], start=False, stop=True)
            P.copy(yb[:, c, :], ps_f[:, 0:64], eng="act")
            P.ts(col[:, 1:2], col[:, 0:1], -1.0, ALU.mult, ps_d[:, last:last + 1], ALU.add)
            P.act(col[:, 1:2], col[:, 1:2], AF.Exp)
            P.tt(col[:, 1:2], col[:, 1:2], dt, ALU.mult)
            P.act(col[:, 2:3], ps_d[:, last:last + 1], AF.Exp)
            xw = P.nxt("sd_xw")
            P.ts(xw[:], o["xtok"][c][:], col[:, 1:2], ALU.mult)
            P.mm(ps_g[:, 0:64], o["btok"][c][:], xw[:])
            P.stt(h[:], h[:], col[:, 2:3], ps_g[:, 0:64], ALU.mult, ALU.add)
        return yb

    def blocks(seq, T):
        return [(seq, t0, min(TB, T - t0)) for t0 in range(0, T, TB)]
    fwd = blocks("c", Tc) + blocks("l", Tl)
    bwd = blocks("c", Tc)[::-1] + blocks("l", Tl)[::-1]
    for s in range(2):
        for d in range(2):
            P.memset(H[s][d][:], 0.0)
    for (seq, t0, n) in fwd:
        for s in range(2):
            o = prep(s, seq, t0, n)
            yb = scan(s, 0, o, n)
            nq = n // Q
            P.dma(yfd[seq].ap()[s][:, t0 // Q:t0 // Q + nq, :], yb[:, 0:nq, :])
    for (seq, t0, n) in bwd:
        for s in range(2):
            o = prep(s, seq, t0, n)
            yb = scan(s, 1, o, n)
            nq = n // Q
            yfl = P.nxt("sd_yfl")
            P.dma(yfl[:, 0:nq, :], yfd[seq].ap()[s][:, t0 // Q:t0 // Q + nq, :])
            P.tt(yb[:, 0:nq, :], yb[:, 0:nq, :], yfl[:, 0:nq, :], ALU.add)
            for c in range(nq):
                P.stt(yb[:, c, :], o["xtok"][c][:], dsum[s][:, 0:1], yb[:, c, :], ALU.mult, ALU.add)
            P.dma(io["y" + seq].ap()[s][:, t0 // Q:t0 // Q + nq, :], yb[:, 0:nq, :], is_out=True)
    return io


def ssd_host_inputs(ssm_c, ssm_l, p, heads):
    d = {}
    for sq, u in (("c", ssm_c), ("l", ssm_l)):
        T = u.shape[0]
        xbc = u[:, 640:640 + 1152]
        X = np.zeros((2, 64, T + 2), np.float32); B = np.zeros((2, 128, T + 2), np.float32); Cc = np.zeros((2, 128, T + 2), np.float32)
        DT = np.zeros((2, 128, T // SQ, 2), np.float32)
        for s, h in enumerate(heads):
            g = h // 5
            X[s, :, 1:T + 1] = xbc[:, h * 64:h * 64 + 64].T
            B[s, :, 1:T + 1] = xbc[:, 640 + g * 128:640 + g * 128 + 128].T
            Cc[s, :, 1:T + 1] = xbc[:, 896 + g * 128:896 + g * 128 + 128].T
            DT[s, :, :, 0] = u[:, 1792 + h].reshape(T // SQ, SQ).T
            DT[s, :, :, 1] = u[:, 1802 + h].reshape(T // SQ, SQ).T
        d["sd_x" + sq] = X; d["sd_b" + sq] = B; d["sd_c" + sq] = Cc; d["sd_dt" + sq] = DT
    cw = np.concatenate([p["conv_w"], p["conv_b"][None]], 0).T
    px = np.zeros((2, 64, 4), np.float32); pb = np.zeros((2, 128, 4), np.float32); pc = np.zeros((2, 128, 4), np.float32)
    pd = np.zeros((2, 128, 6), np.float32)
    for s, h in enumerate(heads):
        g = h // 5
        px[s] = cw[h * 64:h * 64 + 64]; pb[s] = cw[640 + g * 128:640 + g * 128 + 128]; pc[s] = cw[896 + g * 128:896 + g * 128 + 128]
        pd[s, :, 0] = p["dt_bias"][0, h]; pd[s, :, 1] = p["dt_bias"][1, h]
        pd[s, :, 2] = p["a_log"][0, h]; pd[s, :, 3] = p["a_log"][1, h]
        pd[s, :, 4] = p["d_skip"][0, h]; pd[s, :, 5] = p["d_skip"][1, h]
    d.update({"sd_px": px, "sd_pb": pb, "sd_pc": pc, "sd_pd": pd})
    c = ssd_consts()
    d["c_tri"] = c["tri"]; d["c_identb"] = c["ident"]; d["c_ones"] = c["ones"]
    return d


def modvec_build(P, PS, L, KD, NCOL):
    D = KD * 128
    io = {}
    io["cT"] = P.dram("m_cT", [128, KD, 2], F32, "ExternalInput")
    io["w"] = P.dram("m_w", [L, KD, 128, NCOL], F32, "ExternalInput")
    io["b"] = P.dram("m_b", [L, 2, NCOL], F32, "ExternalInput")
    io["o"] = P.dram("m_o", [L, 2, NCOL], F32, "ExternalOutput")
    cT = P.sb("m_cTs", [128, KD, 2])
    P.dma(cT[:], io["cT"].ap())
    P.act(cT[:], cT[:], AF.Silu)
    P.ring("m_wt", 4, [128, NCOL]); P.ring("m_bt", 2, [2, NCOL]); P.ring("m_ot", 2, [2, NCOL])
    nt = (NCOL + 511) // 512
    for l in range(L):
        bt = P.nxt("m_bt")
        P.dma(bt[:], io["b"].ap()[l])
        for k in range(KD):
            wt = P.nxt("m_wt")
            P.dma(wt[:], io["w"].ap()[l, k])
            for j in range(nt):
                cs = slice(j * 512, min(NCOL, (j + 1) * 512))
                P.mm(PS[j][0:2, 0:cs.stop - cs.start], cT[:, k, :], wt[:, cs], start=(k == 0), stop=(k == KD - 1))
        ot = P.nxt("m_ot")
        for j in range(nt):
            cs = slice(j * 512, min(NCOL, (j + 1) * 512))
            P.tt(ot[:, cs], PS[j][0:2, 0:cs.stop - cs.start], bt[:, cs], ALU.add)
        P.dma(io["o"].ap()[l], ot[:], is_out=True)
    return io


EPS = 1e-6


def norm_mod(P, ps, ones, xT, hT, KD, segs, ab, tmpring, x_local=False, mkring=True):
    D = KD * 128
    if mkring:
        P.ring(tmpring + "_r", 2, [128, 512])
    for (t0, n, si) in segs:
        ts_ = slice(t0, t0 + n)
        xs_ = slice(0, n) if x_local else ts_
        for k in range(KD):
            sq = P.nxt(tmpring)[:, 0:n]
            P.act(sq, xT[:, k, xs_], AF.Square)
            P.mm(ps[:, 0:n], ones[:], sq, start=(k == 0), stop=(k == KD - 1))
        rstd = P.nxt(tmpring + "_r")[:, 0:n]
        P.act(rstd, ps[:, 0:n], AF.Sqrt, bias=EPS, scale=1.0 / D)
        P.recip(rstd, rstd)
        a, b = ab[si]
        for k in range(KD):
            t = P.nxt(tmpring)[:, 0:n]
            P.tt(t, xT[:, k, xs_], rstd, ALU.mult)
            P.ts(hT[:, k, ts_], t, a[:, k:k + 1], ALU.mult, b[:, k:k + 1], ALU.add)


def load_mods(P, io_mod, io_g, KD, which):
    mod = P.sb("mod_t", [128, 2, 6, KD])
    P.dma(mod[:], io_mod.ap())
    g = P.sb("mod_g", [128, KD])
    P.dma(g[:], io_g.ap())
    out = []
    for si in range(2):
        a = P.sb(f"mod_a{si}", [128, KD])
        P.ts(a[:], mod[:, si, 3 * which + 1, :], 1.0, ALU.add)
        P.tt(a[:], a[:], g[:], ALU.mult)
        out.append((a, mod[:, si, 3 * which + 0, :], mod[:, si, 3 * which + 2, :]))
    return out


IN_COLS = 6420
QK_EPS = 1e-6


def rope_consts():
    blk = np.zeros((128, 128), np.float32); blk[:64, :64] = 1; blk[64:, 64:] = 1
    rot = np.zeros((128, 128), np.float32)
    for base in (0, 64):
        for m in range(32):
            rot[base + m + 32, base + m] = -1.0
            rot[base + m, base + m + 32] = 1.0
    return blk, rot


def rope_tables(positions, is_ctx):
    n_freq = 16
    inv = (10000.0 ** (-np.arange(n_freq, dtype=np.float32) / n_freq)).astype(np.float32)
    row = (positions // 64).astype(np.float32); col = (positions % 64).astype(np.float32)
    ang = np.concatenate([row[:, None] * inv, col[:, None] * inv], axis=-1).astype(np.float32)
    cos = np.cos(ang).astype(np.float32); sin = np.sin(ang).astype(np.float32)
    cos = np.where(is_ctx[:, None], 1.0, cos).astype(np.float32); sin = np.where(is_ctx[:, None], 0.0, sin).astype(np.float32)
    cosT = np.tile(cos.T, (4, 1)); sinT = np.tile(sin.T, (4, 1))
    return np.ascontiguousarray(cosT), np.ascontiguousarray(sinT)


def projA_build(P, PS, KD, TCc, TLc):
    T = TCc + TLc
    io = {}
    io["xT"] = P.dram("a_xT", [128, KD, T], F32, "ExternalInput")
    io["mod"] = P.dram("a_mod", [128, 2, 6, KD], F32, "ExternalInput")
    io["g"] = P.dram("a_g", [128, KD], F32, "ExternalInput")
    io["w"] = P.dram("a_w", [KD, 128, IN_COLS], F32, "ExternalInput")
    io["cos"] = P.dram("a_cos", [128, T], F32, "ExternalInput")
    io["sin"] = P.dram("a_sin", [128, T], F32, "ExternalInput")
    io["gain"] = P.dram("a_gain", [128, 2], F32, "ExternalInput")
    io["blk"] = P.dram("a_blk", [128, 128], F32, "ExternalInput")
    io["rot"] = P.dram("a_rot", [128, 128], F32, "ExternalInput")
    io["ones"] = P.dram("a_ones", [128, 128], F32, "ExternalInput")
    io["qk"] = P.dram("a_qk", [12, 128, T], BF16, "ExternalOutput")
    io["v"] = P.dram("a_v", [6, 128, T], BF16, "ExternalOutput")
    io["u"] = P.dram("a_u", [IN_COLS - 2304, T], F32, "ExternalOutput")
    xT = P.sb("a_xTs", [128, KD, T]); hT = P.sb("a_hT", [128, KD, T], BF16)
    P.dma(xT[:], io["xT"].ap())
    cos = P.sb("a_cos_s", [128, T]); sin = P.sb("a_sin_s", [128, T])
    P.dma(cos[:], io["cos"].ap()); P.dma(sin[:], io["sin"].ap())
    gain = P.sb("a_gain_s", [128, 2]); P.dma(gain[:], io["gain"].ap())
    blk = P.sb("a_blk_s", [128, 128]); P.dma(blk[:], io["blk"].ap())
    rot = P.sb("a_rot_s", [128, 128]); P.dma(rot[:], io["rot"].ap())
    ones = P.sb("a_ones_s", [128, 128]); P.dma(ones[:], io["ones"].ap())
    mods = load_mods(P, io["mod"], io["g"], KD, 0)
    P.ring("a_tmp", 6, [128, 512])
    segs = [(0, TCc, 1)] + [(TCc + t0, min(512, TLc - t0), 0) for t0 in range(0, TLc, 512)]
    norm_mod(P, PS[7], ones, xT, hT, KD, segs, [(m[0], m[1]) for m in mods], "a_tmp")
    P.ring("a_w", 3, [128, KD, 128], BF16)
    P.ring("a_o", 4, [128, 512]); P.ring("a_ob", 4, [128, 512], BF16)
    nchunk = (IN_COLS + 127) // 128
    pi = 0
    for m in range(nchunk):
        c0 = m * 128; mc = min(128, IN_COLS - c0)
        wt = P.nxt("a_w")
        P.dma(wt[:, :, 0:mc], io["w"].ap()[:, :, c0:c0 + mc].rearrange("k p c -> p k c"), q="pool")
        for (t0, n, si) in segs:
            ts_ = slice(t0, t0 + n)
            ps = PS[pi % 4]; pi += 1
            for k in range(KD):
                P.mm(ps[0:mc, 0:n], wt[:, k, 0:mc], hT[:, k, ts_], start=(k == 0), stop=(k == KD - 1))
            if m < 12:
                x = P.nxt("a_o")[:, 0:n]
                P.copy(x, ps[:, 0:n], eng="act")
                sq = P.nxt("a_o")[:, 0:n]
                P.tt(sq, x, x, ALU.mult)
                P.mm(PS[4][:, 0:n], blk[:], sq)
                rs = P.nxt("a_o")[:, 0:n]
                P.act(rs, PS[4][:, 0:n], AF.Sqrt, bias=QK_EPS, scale=1.0 / 64)
                P.recip(rs, rs)
                P.stt(x, x, gain[:, (0 if m < 6 else 1):(1 if m < 6 else 2)], rs, ALU.mult, ALU.mult)
                P.mm(PS[5][:, 0:n], rot[:], x)
                r2 = P.nxt("a_o")[:, 0:n]
                P.tt(r2, PS[5][:, 0:n], sin[:, ts_], ALU.mult)
                P.tt(x, x, cos[:, ts_], ALU.mult)
                ob = P.nxt("a_ob")[:, 0:n]
                P.tt(ob, x, r2, ALU.add)
                P.dma(io["qk"].ap()[m][:, ts_], ob, is_out=True)
            elif m < 18:
                ob = P.nxt("a_ob")[:, 0:n]
                P.copy(ob, ps[:, 0:n], eng="act")
                P.dma(io["v"].ap()[m - 12][:, ts_], ob, is_out=True)
            else:
                o = P.nxt("a_o")
                P.copy(o[0:mc, 0:n], ps[0:mc, 0:n], eng="act")
                P.dma(io["u"].ap()[c0 - 2304:c0 - 2304 + mc, ts_], o[0:mc, 0:n], is_out=True)
    return io


def attn_consts():
    blk = np.zeros((128, 128), np.float32); blk[:64, :64] = 1; blk[64:, 64:] = 1
    sgn = np.zeros((128, 128), np.float32); sgn[:64, :] = 1.0 / 64; sgn[64:, :] = -1.0 / 64
    E = np.zeros((640, 640), np.float32); E[:320, :320] = 1; E[320:, 320:] = 1
    return blk, sgn, np.ascontiguousarray(E.reshape(5, 128, 640))


def mixB1_build(P, PS, KD, TCc, TLc, TCX, TLAT):
    T = TCc + TLc; TK = TCX + TLAT; KT = TK // 128; D = KD * 128
    io = {}
    dr = lambda nm, shp, dt=F32, kind="ExternalInput": P.dram("b_" + nm, shp, dt, kind)
    io["xT"] = dr("xT", [128, KD, T]); io["q"] = dr("q", [6, 128, T], BF16); io["kT"] = dr("kT", [6, 128, TK], BF16)
    io["v"] = dr("v", [128, KT, 768], BF16)
    io["rT"] = dr("rT", [128, 5, T]); io["yT"] = dr("yT", [128, 5, T]); io["zT"] = dr("zT", [128, 5, T])
    io["mod"] = dr("mod", [128, 2, 6, KD]); io["g"] = dr("g", [128, KD])
    io["w"] = dr("w", [16, 128, D])
    io["subln"] = dr("subln", [128, 1]); io["sng"] = dr("sng", [128, 5]); io["lamq"] = dr("lamq", [128, 1]); io["lamk"] = dr("lamk", [128, 1])
    io["lami"] = dr("lami", [128, 1]); io["gains"] = dr("gains", [128, 2, 64])
    io["blk"] = dr("blk", [128, 128]); io["sgn"] = dr("sgn", [128, 128]); io["E"] = dr("E", [5, 128, 640]); io["ones"] = dr("ones", [128, 128])
    io["o"] = dr("o", [128, KD, T], F32, "ExternalOutput")

    def cst(nm, shape, src=None, dt=F32, q="sp"):
        t = P.sb("bk_" + nm, shape, dt)
        P.dma(t[:], io[nm].ap() if src is None else src, q=q)
        return t
    subln = cst("subln", [128, 1]); sng = cst("sng", [128, 5]); lamq = cst("lamq", [128, 1]); lamk = cst("lamk", [128, 1])
    lami = cst("lami", [128, 1]); gains = cst("gains", [128, 2, 64]); blk = cst("blk", [128, 128]); sgn = cst("sgn", [128, 128])
    ones = cst("ones", [128, 128])
    onesb = P.sb("bk_onesb", [128, 128], BF16)
    P.copy(onesb[:], ones[:])
    Eg = [cst(f"E{k}", [128, 640], io["E"].ap()[k]) for k in range(5)]
    mods = load_mods(P, io["mod"], io["g"], KD, 0)
    qs = P.sb("bk_q", [128, 6, T], BF16)
    P.dma(qs[:], io["q"].ap().rearrange("h p t -> p h t"))
    sm = P.sb("bk_sm", [128, 8])
    P.tt(sm[:, 0:1], lamq[:], lamk[:], ALU.mult)
    P.mm(PS[0][:, 0:1], blk[:], sm[:, 0:1])
    P.act(sm[:, 1:2], PS[0][:, 0:1], AF.Exp)
    P.mm(PS[1][:, 0:1], sgn[:], sm[:, 1:2])
    P.tt(sm[:, 2:3], PS[1][:, 0:1], lami[:], ALU.add)
    P.ts(sm[:, 3:4], sm[:, 2:3], -1.0, ALU.mult)
    P.ts(sm[:, 4:5], lami[:], -1.0, ALU.mult, 1.0, ALU.add)
    P.tt(sm[:, 4:5], sm[:, 4:5], subln[:], ALU.mult)
    ga = P.sb("bk_ga", [128, 2, 64])
    P.act(ga[:], gains[:], AF.Abs)
    P.reduce(sm[:, 5:7], ga[:], ALU.max)
    P.tt(sm[:, 7:8], sm[:, 5:6], sm[:, 6:7], ALU.mult)
    P.ts(sm[:, 7:8], sm[:, 7:8], -8.0, ALU.mult)
    neg_lam, sgl, ebias = sm[:, 3:4], sm[:, 4:5], sm[:, 7:8]

    mixT = P.sb("bk_mixT", [128, 16, T], BF16)
    P.ring("b_kT", 2, [128, TK], BF16); P.ring("b_vh", 2, [128, KT, 128], BF16)
    P.ring("b_E", 6, [128, 512], BF16); P.ring("b_t", 8, [128, 512])
    qtiles = [(0, TCc, TCX)] + [(TCc + t0, min(512, TLc - t0), TK) for t0 in range(0, TLc, 512)]
    si_ = 0
    for h in range(6):
        kh = P.nxt("b_kT"); vh = P.nxt("b_vh")
        P.dma(kh[:], io["kT"].ap()[h])
        P.dma(vh[:], io["v"].ap()[:, :, h * 128:(h + 1) * 128])
        for (t0, n, nkeys) in qtiles:
            ts_ = slice(t0, t0 + n)
            nkt = nkeys // 128
            Eq = []
            for it in range(nkt + 1):
                if it < nkt:
                    ks = slice(it * 128, (it + 1) * 128)
                    Es = []
                    for c in range(2):
                        hs = slice(64 * c, 64 * c + 64)
                        Sp = PS[si_ % 4]; si_ += 1
                        P.mm(Sp[:, 0:n], kh[hs, ks], qs[hs, h, ts_])
                        E = P.nxt("b_E")[:, 0:n]
                        P.act(E, Sp[:, 0:n], AF.Exp, bias=ebias, scale=0.125)
                        Es.append(E)
                    Eq.append(Es)
                if it >= 1:
                    kt = it - 1
                    for c in range(2):
                        E = Eq[kt][c]
                        P.mm(PS[4 + c][:, 0:n], vh[:, kt, :], E, start=(kt == 0), stop=(kt == nkt - 1))
                        P.mm(PS[6 + c][:, 0:n], onesb[:], E, start=(kt == 0), stop=(kt == nkt - 1))
            rl0 = P.nxt("b_t")[:, 0:n]; rl1 = P.nxt("b_t")[:, 0:n]
            P.recip(rl0, PS[6][:, 0:n]); P.recip(rl1, PS[7][:, 0:n])
            o0 = P.nxt("b_t")[:, 0:n]; o1 = P.nxt("b_t")[:, 0:n]
            P.tt(o0, PS[4][:, 0:n], rl0, ALU.mult)
            P.tt(o1, PS[5][:, 0:n], rl1, ALU.mult)
            P.stt(o0, o1, neg_lam, o0, ALU.mult, ALU.add)
            sq = P.nxt("b_t")[:, 0:n]
            P.tt(sq, o0, o0, ALU.mult)
            P.mm(PS[0][:, 0:n], ones[:], sq)
            rs = P.nxt("b_t")[:, 0:n]
            P.act(rs, PS[0][:, 0:n], AF.Sqrt, bias=EPS, scale=1.0 / 128)
            P.recip(rs, rs)
            P.stt(mixT[:, h, ts_], o0, sgl, rs, ALU.mult, ALU.mult)
    P.ring("b_in5", 3, [128, 5, 512])
    toks = [(0, TCc, 1)] + [(TCc + t0, min(512, TLc - t0), 0) for t0 in range(0, TLc, 512)]
    for (t0, n, si) in toks:
        ts_ = slice(t0, t0 + n)
        rt = P.nxt("b_in5"); yt = P.nxt("b_in5"); zt = P.nxt("b_in5")
        P.dma(rt[:, :, 0:n], io["rT"].ap()[:, :, ts_]); P.dma(yt[:, :, 0:n], io["yT"].ap()[:, :, ts_]); P.dma(zt[:, :, 0:n], io["zT"].ap()[:, :, ts_])
        P.copy(mixT[:, 6:11, ts_], rt[:, :, 0:n])
        P.act(zt[:, :, 0:n], zt[:, :, 0:n], AF.Silu)
        P.tt(yt[:, :, 0:n], yt[:, :, 0:n], zt[:, :, 0:n], ALU.mult)
        P.tt(zt[:, :, 0:n], yt[:, :, 0:n], yt[:, :, 0:n], ALU.mult)
        for m in range(5):
            for k in range(5):
                P.mm(PS[m % 4][:, 0:n], Eg[k][:, m * 128:(m + 1) * 128], zt[:, k, 0:n], start=(k == 0), stop=(k == 4))
            rs = P.nxt("b_t")[:, 0:n]
            P.act(rs, PS[m % 4][:, 0:n], AF.Sqrt, bias=EPS, scale=1.0 / 320)
            P.recip(rs, rs)
            P.stt(mixT[:, 11 + m, ts_], yt[:, m, 0:n], sng[:, m:m + 1], rs, ALU.mult, ALU.mult)
    P.ring("b_w", 3, [128, 16, 128], BF16); P.ring("b_x", 4, [128, 512])
    pi = 0
    for m in range(KD):
        wt = P.nxt("b_w")
        P.dma(wt[:], io["w"].ap()[:, :, m * 128:(m + 1) * 128].rearrange("k p c -> p k c"), q="pool")
        for (t0, n, si) in toks:
            ts_ = slice(t0, t0 + n)
            ps = PS[pi % 4]; pi += 1
            for k in range(16):
                P.mm(ps[:, 0:n], wt[:, k, :], mixT[:, k, ts_], start=(k == 0), stop=(k == 15))
            xt = P.nxt("b_x")[:, 0:n]
            P.dma(xt, io["xT"].ap()[:, m, ts_])
            P.stt(xt, ps[:, 0:n], mods[si][2][:, m:m + 1], xt, ALU.mult, ALU.add)
            P.dma(io["o"].ap()[:, m, ts_], xt, is_out=True)
    return io


def ffnB2_build(P, PS, KD, FC, TCc, TLc):
    T = TCc + TLc; D = KD * 128
    io = {}
    dr = lambda nm, shp, dt=F32, kind="ExternalInput": P.dram("f_" + nm, shp, dt, kind)
    io["xT"] = dr("xT", [128, KD, T]); io["mod"] = dr("mod", [128, 2, 6, KD]); io["g"] = dr("g", [128, KD])
    io["wi"] = dr("wi", [KD, 128, 2 * FC * 128]); io["wo"] = dr("wo", [FC, 128, D]); io["ones"] = dr("ones", [128, 128])
    io["o"] = dr("o", [128, KD, T], F32, "ExternalOutput")
    hT = P.sb("f_hT", [128, KD, T], BF16)
    ones = P.sb("f_ones_s", [128, 128]); P.dma(ones[:], io["ones"].ap())
    mods = load_mods(P, io["mod"], io["g"], KD, 1)
    P.ring("f_tmp", 6, [128, 512]); P.ring("f_xt", 1, [128, KD, 256])
    toks = [(0, TCc, 1)] + [(TCc + t0, min(512, TLc - t0), 0) for t0 in range(0, TLc, 512)]
    nsegs = [(0, TCc, 1)] + [(TCc + t0, min(256, TLc - t0), 0) for t0 in range(0, TLc, 256)]
    ab = [(m[0], m[1]) for m in mods]
    for i, (t0, n, si) in enumerate(nsegs):
        xt = P.nxt("f_xt")
        P.dma(xt[:, :, 0:n], io["xT"].ap()[:, :, t0:t0 + n])
        norm_mod(P, PS[7], ones, xt, hT, KD, [(t0, n, si)], ab, "f_tmp", x_local=True, mkring=(i == 0))
    actT = P.sb("f_actT", [128, FC, T], BF16)
    P.ring("f_wi", 2, [128, KD, 256], BF16); P.ring("f_wo", 2, [128, FC, 128], BF16); P.ring("f_o", 3, [128, 512])
    pi = 0
    for m in range(FC):
        wt = P.nxt("f_wi")
        P.dma(wt[:, :, 0:128], io["wi"].ap()[:, :, m * 128:(m + 1) * 128].rearrange("k p c -> p k c"), q="pool")
        P.dma(wt[:, :, 128:256], io["wi"].ap()[:, :, (FC + m) * 128:(FC + m + 1) * 128].rearrange("k p c -> p k c"), q="pool")
        for (t0, n, si) in toks:
            ts_ = slice(t0, t0 + n)
            pg = PS[pi % 6]; pu = PS[(pi + 1) % 6]; pi += 2
            for k in range(KD):
                P.mm(pg[:, 0:n], wt[:, k, 0:128], hT[:, k, ts_], start=(k == 0), stop=(k == KD - 1))
            for k in range(KD):
                P.mm(pu[:, 0:n], wt[:, k, 128:256], hT[:, k, ts_], start=(k == 0), stop=(k == KD - 1))
            gs = P.nxt("f_tmp")[:, 0:n]
            P.act(gs, pg[:, 0:n], AF.Silu)
            P.tt(actT[:, m, ts_], gs, pu[:, 0:n], ALU.mult)
    for mo in range(KD):
        wt = P.nxt("f_wo")
        P.dma(wt[:], io["wo"].ap()[:, :, mo * 128:(mo + 1) * 128].rearrange("m p c -> p m c"), q="pool")
        for (t0, n, si) in toks:
            ts_ = slice(t0, t0 + n)
            ps = PS[pi % 6]; pi += 1
            for m in range(FC):
                P.mm(ps[:, 0:n], wt[:, m, :], actT[:, m, ts_], start=(m == 0), stop=(m == FC - 1))
            ot = P.nxt("f_o")[:, 0:n]
            P.dma(ot, io["xT"].ap()[:, mo, ts_])
            P.stt(ot, ps[:, 0:n], mods[si][2][:, mo:mo + 1], ot, ALU.mult, ALU.add)
            P.dma(io["o"].ap()[:, mo, ts_], ot, is_out=True)
    return io


NCORES = 8
_PROGS = {}
_DBG = None


def _fm(x2d, KD):
    T = x2d.shape[0]
    return np.ascontiguousarray(x2d.T.reshape(KD, 128, T).transpose(1, 0, 2))


def _unfm(a):
    p, KD, T = a.shape
    return np.ascontiguousarray(a.transpose(1, 0, 2).reshape(KD * 128, T).T)


def _run(nc, in_maps):
    res = run_bass_kernel_spmd(nc, in_maps, core_ids=list(range(NCORES)))
    return res.results


def _prog(key, builder):
    if key not in _PROGS:
        P = Prog()
        PS = [P.ps(f"ps{i}", [128, 512]) for i in range(8)]
        builder(P, PS)
        _PROGS[key] = P.finish()
    return _PROGS[key]


def _slots(j):
    rh = (2 * j, 2 * j + 1) if j < 5 else (0, 1)
    sh = (2 * (j - 3), 2 * (j - 3) + 1) if j >= 3 else (0, 1)
    return rh, sh


def kernel(x, c, ctx, c_ctx, ada_w, ada_b, norm1_g, norm2_g, w_in, w_out, qk_gain, lam_q, lam_k, subln_g,
           shift_mu, w0, w_up, a0, a_up, g_up, k_k, k_a, r_k, lnx_g, lnx_b, conv_w, conv_b, dt_bias, a_log,
           d_skip, ssm_norm_g, w_ffn_in, w_ffn_out):
    f = lambda a: np.ascontiguousarray(np.asarray(a, dtype=np.float32))
    x, c, ctx, c_ctx = f(x), f(c), f(ctx), f(c_ctx)
    L = ada_w.shape[0]; SEQ = x.shape[1]; CTX = ctx.shape[1]; D = x.shape[2]; KD = D // 128
    DFF = w_ffn_out.shape[1]; FC = DFF // 128
    NC = NCORES; TCc = CTX // NC; TLc = SEQ // NC; T = TCc + TLc
    ones = np.ones((128, 128), np.float32)
    NCOL = 6 * D // NC
    ncM = _prog(("M", L, KD, NCOL), lambda P, PS: modvec_build(P, PS, L, KD, NCOL))
    cT = _fm(np.stack([c[0], c_ctx]), KD)
    ada_w = np.asarray(ada_w, np.float32); ada_b = np.asarray(ada_b, np.float32)
    ims = []
    for j in range(NC):
        cs = slice(j * NCOL, (j + 1) * NCOL)
        ims.append({"m_cT": cT, "m_w": np.ascontiguousarray(ada_w[:, :, cs].reshape(L, KD, 128, NCOL)),
                    "m_b": np.ascontiguousarray(np.repeat(ada_b[:, None, cs], 2, axis=1))})
    rs = _run(ncM, ims)
    mod = np.concatenate([r["m_o"] for r in rs], axis=-1)
    modT = [np.ascontiguousarray(mod[l].reshape(2, 6, KD, 128).transpose(3, 0, 1, 2)) for l in range(L)]

    xl = x[0].copy(); xc = ctx[0].copy()
    blk, rot = rope_consts()
    ablk, sgn, Eg = attn_consts()
    pos = np.arange(SEQ)
    ncA = _prog(("A", KD, TCc, TLc), lambda P, PS: projA_build(P, PS, KD, TCc, TLc))
    ncS = _prog(("S", CTX, SEQ), lambda P, PS: (rwkv_build2(P, PS, CTX, SEQ), ssd_build(P, PS, CTX, SEQ)))
    ncB1 = _prog(("B1", KD, TCc, TLc, CTX, SEQ), lambda P, PS: mixB1_build(P, PS, KD, TCc, TLc, CTX, SEQ))
    ncB2 = _prog(("B2", KD, FC, TCc, TLc), lambda P, PS: ffnB2_build(P, PS, KD, FC, TCc, TLc))
    tabs = []
    for j in range(NC):
        pj = np.concatenate([np.zeros(TCc, np.int64), pos[j * TLc:(j + 1) * TLc]])
        isc = np.concatenate([np.ones(TCc, bool), np.zeros(TLc, bool)])
        tabs.append(rope_tables(pj, isc))
    for l in range(L):
        g = lambda a: np.asarray(a[l], np.float32)
        xTs = [_fm(np.concatenate([xc[j * TCc:(j + 1) * TCc], xl[j * TLc:(j + 1) * TLc]]), KD) for j in range(NC)]
        gain = np.ascontiguousarray(np.tile(g(qk_gain), (1, 2)).T)
        wA = np.ascontiguousarray(g(w_in).reshape(KD, 128, IN_COLS))
        n1 = _fm(g(norm1_g)[None], KD)[:, :, 0]
        ims = [{"a_xT": xTs[j], "a_mod": modT[l], "a_g": n1, "a_w": wA, "a_cos": tabs[j][0], "a_sin": tabs[j][1],
                "a_gain": gain, "a_blk": blk, "a_rot": rot, "a_ones": ones} for j in range(NC)]
        ra = _run(ncA, ims)
        kT_all = np.ascontiguousarray(np.concatenate([r["a_qk"][6:12, :, :TCc] for r in ra] + [r["a_qk"][6:12, :, TCc:] for r in ra], axis=2))
        vT_all = np.concatenate([r["a_v"][:, :, :TCc] for r in ra] + [r["a_v"][:, :, TCc:] for r in ra], axis=2)
        TK = CTX + SEQ
        v_all = np.ascontiguousarray(vT_all.reshape(768, TK).T.reshape(TK // 128, 128, 768).transpose(1, 0, 2))
        u_c = np.concatenate([r["a_u"][:, :TCc] for r in ra], axis=1).T
        u_l = np.concatenate([r["a_u"][:, TCc:] for r in ra], axis=1).T
        p = {"shift_mu": g(shift_mu), "w0": g(w0), "w_up": g(w_up), "a0": g(a0), "a_up": g(a_up), "g_up": g(g_up),
             "k_k": g(k_k), "k_a": g(k_a), "r_k": g(r_k), "lnx_g": g(lnx_g), "lnx_b": g(lnx_b), "conv_w": g(conv_w),
             "conv_b": g(conv_b), "dt_bias": g(dt_bias), "a_log": g(a_log), "d_skip": g(d_skip)}
        ims = []
        for j in range(NC):
            rh, sh = _slots(j)
            d = rwkv_host_inputs(u_c[:, :2304], u_l[:, :2304], p, rh)
            d.update(ssd_host_inputs(u_c[:, 2304:], u_l[:, 2304:], p, sh))
            ims.append(d)
        rsS = _run(ncS, ims)
        if _DBG is not None:
            _DBG[f'mod{l}'] = mod[l]; _DBG[f'u_c{l}'] = u_c; _DBG[f'u_l{l}'] = u_l; _DBG[f'kT{l}'] = kT_all; _DBG[f'v{l}'] = v_all; _DBG[f'q{l}'] = [r['a_qk'][0:6] for r in ra]
        r_c = np.zeros((CTX, 640), np.float32); r_l = np.zeros((SEQ, 640), np.float32)
        y_c = np.zeros((CTX, 640), np.float32); y_l = np.zeros((SEQ, 640), np.float32)
        for h in range(10):
            j, s = h // 2, h % 2
            r_c[:, h * 64:(h + 1) * 64] = rsS[j]["rw_oc"][64 * s:64 * s + 64].T
            r_l[:, h * 64:(h + 1) * 64] = rsS[j]["rw_ol"][64 * s:64 * s + 64].T
            j = 3 + h // 2
            y_c[:, h * 64:(h + 1) * 64] = rsS[j]["sd_yc"][s].transpose(1, 0, 2).reshape(CTX, 64)
            y_l[:, h * 64:(h + 1) * 64] = rsS[j]["sd_yl"][s].transpose(1, 0, 2).reshape(SEQ, 64)
        lam_init = 0.8 - 0.6 * math.exp(-0.3 * l)
        own = lambda ac, al, j: np.concatenate([ac[j * TCc:(j + 1) * TCc], al[j * TLc:(j + 1) * TLc]])
        wO = np.ascontiguousarray(g(w_out).reshape(16, 128, D))
        common = {"b_kT": kT_all, "b_v": v_all, "b_mod": modT[l], "b_g": n1, "b_w": wO,
                  "b_subln": np.ascontiguousarray(g(subln_g)[:, None]), "b_sng": _fm(g(ssm_norm_g)[None], 5)[:, :, 0],
                  "b_lamq": np.ascontiguousarray(g(lam_q).reshape(128, 1)), "b_lamk": np.ascontiguousarray(g(lam_k).reshape(128, 1)),
                  "b_lami": np.full((128, 1), lam_init, np.float32),
                  "b_gains": np.ascontiguousarray(np.broadcast_to(g(qk_gain)[None], (128, 2, 64))),
                  "b_blk": ablk, "b_sgn": sgn, "b_E": Eg, "b_ones": ones}
        ims = []
        for j in range(NC):
            d = dict(common)
            d["b_xT"] = xTs[j]; d["b_q"] = np.ascontiguousarray(ra[j]["a_qk"][0:6])
            d["b_rT"] = _fm(own(r_c, r_l, j), 5); d["b_yT"] = _fm(own(y_c, y_l, j), 5)
            d["b_zT"] = _fm(own(u_c[:, 2304:2304 + 640], u_l[:, 2304:2304 + 640], j), 5)
            ims.append(d)
        rb1 = _run(ncB1, ims)
        if _DBG is not None:
            _DBG[f'r_l{l}'] = r_l; _DBG[f'y_l{l}'] = y_l; _DBG[f'r_c{l}'] = r_c; _DBG[f'y_c{l}'] = y_c; _DBG[f'x1_{l}'] = [_unfm(r['b_o']) for r in rb1]
        n2 = _fm(g(norm2_g)[None], KD)[:, :, 0]
        wi = np.ascontiguousarray(g(w_ffn_in).reshape(KD, 128, 2 * DFF)); wo = np.ascontiguousarray(g(w_ffn_out).reshape(FC, 128, D))
        ims = [{"f_xT": rb1[j]["b_o"], "f_mod": modT[l], "f_g": n2, "f_wi": wi, "f_wo": wo, "f_ones": ones} for j in range(NC)]
        rb2 = _run(ncB2, ims)
        for j in range(NC):
            xo = _unfm(rb2[j]["f_o"])
            xc[j * TCc:(j + 1) * TCc] = xo[:TCc]
            xl[j * TLc:(j + 1) * TLc] = xo[TCc:]
        if _DBG is not None:
            _DBG[f'xl{l}'] = xl.copy(); _DBG[f'xc{l}'] = xc.copy()
            if _DBG.get('stop_after') == l:
                return xl[None].astype(np.float32)
    return xl[None].astype(np.float32)


_NO_ILV = False


def _interleave(gens):
    gens = [g for g in gens if g is not None]
    while gens:
        for g in list(gens):
            try:
                next(g)
            except StopIteration:
                gens.remove(g)


def rwkv_build2(P, PS, Tc, Tl):
    C = RW_C; TB = 256; NCB = TB // C
    io = {}
    io["uc"] = P.dram("rw_uc", [128, 6, Tc + 2], F32, "ExternalInput")
    io["ul"] = P.dram("rw_ul", [128, 6, Tl + 2], F32, "ExternalInput")
    io["pr"] = P.dram("rw_pr", [128, PR_N], F32, "ExternalInput")
    io["wup"] = P.dram("rw_wup", [128, 128], F32, "ExternalInput")
    io["aup"] = P.dram("rw_aup", [128, 128], F32, "ExternalInput")
    io["gup"] = P.dram("rw_gup", [128, 128], F32, "ExternalInput")
    io["blk64"] = P.dram("c_blk64", [128, 128], F32, "ExternalInput")
    io["ident"] = P.dram("c_ident", [128, 128], F32, "ExternalInput")
    io["i2"] = P.dram("c_i2", [128, 64], F32, "ExternalInput")
    io["mask5"] = P.dram("c_mask5", [2, 128, 320], F32, "ExternalInput")
    io["oc"] = P.dram("rw_oc", [128, Tc], F32, "ExternalOutput")
    io["ol"] = P.dram("rw_ol", [128, Tl], F32, "ExternalOutput")
    yf = {"c": P.dram("rw_yfc", [128, Tc], F32), "l": P.dram("rw_yfl", [128, Tl], F32)}
    usrc = {"c": io["uc"], "l": io["ul"]}
    odst = {"c": io["oc"], "l": io["ol"]}

    def cst(name, shape):
        t = P.sb("k_" + name, shape)
        P.dma(t[:], io[name].ap())
        return t
    pr = cst("pr", [128, PR_N]); wup = cst("wup", [128, 128]); aup = cst("aup", [128, 128]); gup = cst("gup", [128, 128])
    blk = cst("blk64", [128, 128]); ident = cst("ident", [128, 128]); i2 = cst("i2", [128, 64])
    i24 = P.sb("k_i24", [128, NCB, 64])
    for ci in range(NCB):
        P.copy(i24[:, ci, :], i2[:])
    mask5 = []
    for d in range(2):
        t = P.sb(f"k_mask5_{d}", [128, 320])
        P.dma(t[:], io["mask5"].ap()[d])
        mask5.append(t)
    rst = P.sb("k_rst", [128, TB])
    P.memset(rst[:], 1.0)
    P.memset(rst[:].rearrange("p (c t) -> p c t", t=C)[:, :, 0:1], 0.0)
    pc = lambda i: pr[:, i:i + 1]
    P.ts(pr[:, PR_M2:PR_M2 + 6], pr[:, PR_MU0:PR_MU0 + 6], -1.0, ALU.mult, 1.0, ALU.add)
    P.tt(pr[:, PR_M2:PR_M2 + 6], pr[:, PR_M2:PR_M2 + 6], pr[:, PR_MU1:PR_MU1 + 6], ALU.subtract)
    P.ts(pr[:, PR_1MKA:PR_1MKA + 1], pr[:, PR_KA:PR_KA + 1], -1.0, ALU.mult, 1.0, ALU.add)

    P.ring("rw_u6", 2, [128, 6, TB + 2]); P.ring("rw_xs", 2, [128, 6, TB])
    for nm in ("tw", "sg", "kx", "rn", "kk", "g", "lw0", "lw1", "ic0", "ic1", "k0", "k1", "b0", "b1",
               "cw", "cwb", "ep", "en", "ea", "At", "Bt", "Kt", "Rt", "yT", "t2", "t3", "t4", "yfl"):
        P.ring("rw_" + nm, 2, [128, TB])
    for nm in ("sq", "f1", "t1"):
        P.ring("rw_" + nm, 3, [128, TB])
    P.ring("rw_tok", 2, [128, NCB, 3, 64]); P.ring("rw_M5", 2, [128, NCB, 5, 64]); P.ring("rw_TT", 2, [128, NCB, 64])
    P.ring("rw_PP", 3, [128, NCB, 128]); P.ring("rw_Xs", 3, [128, 64]); P.ring("rw_Us", 3, [128, 64])
    S = [P.sb("rw_S0", [128, 64]), P.sb("rw_S1", [128, 64])]
    Stmp = P.sb("rw_Stmp", [128, 64])
    HS = [slice(0, 64), slice(64, 128)]

    def prep_gen(seq, t0, n, o):
        u6 = P.nxt("rw_u6")
        P.dma(u6[:, :, 0:n + 2], usrc[seq].ap()[:, :, t0:t0 + n + 2])
        xs = P.nxt("rw_xs")
        for j in range(6):
            P.ts(xs[:, j, 0:n], u6[:, j, 1:n + 1], pc(PR_M2 + j), ALU.mult)
            P.stt(xs[:, j, 0:n], u6[:, j, 0:n], pc(PR_MU0 + j), xs[:, j, 0:n], ALU.mult, ALU.add)
            P.stt(xs[:, j, 0:n], u6[:, j, 2:n + 2], pc(PR_MU1 + j), xs[:, j, 0:n], ALU.mult, ALU.add)
        yield
        r, k, v, wd, ad, gd = [xs[:, j, 0:n] for j in range(6)]
        o["r"] = r; o["v"] = v
        tw = P.nxt("rw_tw")[:, 0:n]
        P.act(tw, wd, AF.Tanh)
        sg = P.nxt("rw_sg")[:, 0:n]
        P.act(sg, gd, AF.Sigmoid)
        for d in range(2):
            hs = HS[d]
            P.mm(PS[2 + d][:, 0:n], wup[hs, :], tw[hs, :])
            P.mm(PS[4 + d][:, 0:n], aup[hs, :], ad[hs, :])
        P.mm(PS[2][:, 256:256 + n], gup[:], sg)
        for d in range(2):
            lw = P.nxt(f"rw_lw{d}")[:, 0:n]
            P.act(lw, PS[2 + d][:, 0:n], AF.Sigmoid, bias=pc(PR_W0 + d))
            ic = P.nxt(f"rw_ic{d}")[:, 0:n]
            P.act(ic, PS[4 + d][:, 0:n], AF.Sigmoid, bias=pc(PR_A0 + d))
            P.ts(lw, lw, -E05, ALU.mult)
            o[f"lw{d}"] = lw; o[f"ic{d}"] = ic
        g = P.nxt("rw_g")[:, 0:n]
        P.copy(g, PS[2][:, 256:256 + n], eng="act")
        o["g"] = g
        yield
        kx = P.nxt("rw_kx")[:, 0:n]
        P.ts(kx, k, pc(PR_KK), ALU.mult)
        sq = P.nxt("rw_sq")[:, 0:n]
        P.tt(sq, kx, kx, ALU.mult)
        P.mm(PS[3][:, 256:256 + n], blk[:], sq)
        rn = P.nxt("rw_rn")[:, 0:n]
        P.act(rn, PS[3][:, 256:256 + n], AF.Sqrt, bias=1e-12)
        P.recip(rn, rn)
        kk = P.nxt("rw_kk")[:, 0:n]
        P.tt(kk, kx, rn, ALU.mult)
        o["kk"] = kk
        for d in range(2):
            f1 = P.nxt("rw_f1")[:, 0:n]
            P.ts(f1, o[f"ic{d}"], pc(PR_KA), ALU.mult, pc(PR_1MKA), ALU.add)
            kd = P.nxt(f"rw_k{d}")[:, 0:n]
            P.tt(kd, k, f1, ALU.mult)
            bd = P.nxt(f"rw_b{d}")[:, 0:n]
            P.tt(bd, kk, o[f"ic{d}"], ALU.mult)
            o[f"k{d}"] = kd; o[f"b{d}"] = bd
        yield

    def pre_gen(d, o, n, X):
        nch = n // C
        lw = o[f"lw{d}"]
        cw = P.nxt("rw_cw")[:, 0:n]
        P.scan(cw, rst[:, 0:n], lw, 0.0, ALU.mult, ALU.add)
        if d == 1:
            cwb = P.nxt("rw_cwb")[:, 0:n]
            P.tt(cwb, lw, cw, ALU.subtract)
            for c in range(nch):
                P.ts(cwb[:, c * C:(c + 1) * C], cwb[:, c * C:(c + 1) * C], cw[:, c * C + C - 1:c * C + C], ALU.add)
            cw = cwb
        ep = P.nxt("rw_ep")[:, 0:n]; en = P.nxt("rw_en")[:, 0:n]; ea = P.nxt("rw_ea")[:, 0:n]
        P.act(ep, cw, AF.Exp)
        P.act(en, cw, AF.Exp, scale=-1.0)
        t1 = P.nxt("rw_t1")[:, 0:n]
        P.tt(t1, cw, lw, ALU.subtract)
        P.act(ea, t1, AF.Exp)
        At = P.nxt("rw_At")[:, 0:n]; Bt = P.nxt("rw_Bt")[:, 0:n]; Kt = P.nxt("rw_Kt")[:, 0:n]; Rt = P.nxt("rw_Rt")[:, 0:n]
        P.stt(At, o["kk"], -1.0, ea, ALU.mult, ALU.mult)
        P.tt(Bt, o[f"b{d}"], en, ALU.mult)
        P.tt(Kt, o[f"k{d}"], en, ALU.mult)
        P.tt(Rt, o["r"], ep, ALU.mult)
        v = o["v"]
        X.update(At=At, Rt=Rt, ep=ep)
        yield
        tok = P.nxt("rw_tok")
        for ci in range(nch):
            cs = slice(ci * C, (ci + 1) * C)
            bank = PS[ci // 2]; off = (ci % 2) * 192
            for j, Xm in enumerate((Bt, Kt, v)):
                for s in range(2):
                    P.mm(bank[HS[s], off + j * 64:off + (j + 1) * 64], Xm[HS[s], cs], ident[HS[s], HS[s]])
        for b in range((nch + 1) // 2):
            k2 = min(2, nch - 2 * b)
            P.copy(tok[:, 2 * b:2 * b + k2].rearrange("p a b c -> p (a b c)"), PS[b][:, 0:192 * k2], eng="act")
        yield
        M5 = P.nxt("rw_M5")
        for ci in range(nch):
            cs = slice(ci * C, (ci + 1) * C)
            for s in range(2):
                for j, (l, rr) in enumerate(((At, Bt), (Bt, At), (Kt, At), (Bt, Rt), (Kt, Rt))):
                    P.mm(PS[2 + ci][HS[s], j * 64:(j + 1) * 64], l[HS[s], cs], rr[HS[s], cs])
            P.tt(M5[:, ci].rearrange("p a b -> p (a b)"), PS[2 + ci][:, 0:320], mask5[d][:], ALU.mult)
        yield
        TT = P.nxt("rw_TT")
        for ci in range(nch):
            P.tt(TT[:, ci, :], M5[:, ci, 1, :], i2[:], ALU.add)
        Pm = [M5[:, ci, 0, :] for ci in range(nch)]; PTm = [M5[:, ci, 1, :] for ci in range(nch)]
        for lvl in range(1, 6):
            for ci in range(nch):
                for s in range(2):
                    P.mm(PS[6][HS[s], ci * 128:ci * 128 + 64], PTm[ci][HS[s], :], Pm[ci][HS[s], :])
                    if lvl < 5:
                        P.mm(PS[6][HS[s], ci * 128 + 64:ci * 128 + 128], Pm[ci][HS[s], :], PTm[ci][HS[s], :])
            PP = P.nxt("rw_PP")
            P.copy(PP[:, 0:nch].rearrange("p a b -> p (a b)"), PS[6][:, 0:128 * nch], eng="act")
            yield
            Pm = [PP[:, ci, 0:64] for ci in range(nch)]; PTm = [PP[:, ci, 64:128] for ci in range(nch)]
            for ci in range(nch):
                for s in range(2):
                    P.mm(PS[ci // 2][HS[s], 384 + (ci % 2) * 64:384 + (ci % 2) * 64 + 64], Pm[ci][HS[s], :], TT[HS[s], ci, :])
            for b in range((nch + 1) // 2):
                k2 = min(2, nch - 2 * b)
                P.tt(TT[:, 2 * b:2 * b + k2].rearrange("p a b -> p (a b)"), TT[:, 2 * b:2 * b + k2].rearrange("p a b -> p (a b)"),
                     PS[b][:, 384:384 + 64 * k2], ALU.add)
            yield
        X.update(tok=tok, M5=M5, TT=TT)

    def chain_gen(d, n, X, fin):
        nch = n // C
        Sd = S[d]
        At, Rt, ep, tok, M5, TT = X["At"], X["Rt"], X["ep"], X["tok"], X["M5"], X["TT"]
        yT = P.nxt("rw_yT")[:, 0:n]
        wcol = (C - 1) if d == 0 else 0
        order = range(nch) if d == 0 else range(nch - 1, -1, -1)
        bank = PS[7]
        for c in order:
            cs = slice(c * C, (c + 1) * C)
            Btok, Ktok, Vtok = tok[:, c, 0, :], tok[:, c, 1, :], tok[:, c, 2, :]
            AkT, LrbT, LrkT = M5[:, c, 2, :], M5[:, c, 3, :], M5[:, c, 4, :]
            for s in range(2):
                hs = HS[s]
                P.mm(bank[hs, 0:64], At[hs, cs], Sd[hs, :], start=True, stop=False)
                P.mm(bank[hs, 0:64], AkT[hs, :], Vtok[hs, :], start=False, stop=True)
            Xs = P.nxt("rw_Xs")
            P.copy(Xs[:], bank[:, 0:64], eng="act")
            yield
            for s in range(2):
                hs = HS[s]
                P.mm(bank[hs, 64:128], TT[hs, c, :], Xs[hs, :])
            Us = P.nxt("rw_Us")
            P.copy(Us[:], bank[:, 64:128])
            yield
            for s in range(2):
                hs = HS[s]
                P.mm(bank[hs, 192:256], Sd[hs, :], Rt[hs, cs], start=True, stop=False)
                P.mm(bank[hs, 192:256], Us[hs, :], LrbT[hs, :], start=False, stop=False)
                P.mm(bank[hs, 192:256], Vtok[hs, :], LrkT[hs, :], start=False, stop=True)
                P.mm(bank[hs, 128:192], Btok[hs, :], Us[hs, :], start=True, stop=False)
                P.mm(bank[hs, 128:192], Ktok[hs, :], Vtok[hs, :], start=False, stop=True)
            P.copy(yT[:, cs], bank[:, 192:256], eng="act")
            P.tt(Stmp[:], Sd[:], bank[:, 128:192], ALU.add)
            P.ts(Sd[:], Stmp[:], ep[:, c * C + wcol:c * C + wcol + 1], ALU.mult)
            yield
        fin(yT)

    def blocks(seq, T):
        return [(seq, t0, min(TB, T - t0)) for t0 in range(0, T, TB)]
    fwd = blocks("c", Tc) + blocks("l", Tl)
    bwd = blocks("c", Tc)[::-1] + blocks("l", Tl)[::-1]
    P.memset(S[0][:], 0.0); P.memset(S[1][:], 0.0)

    def fin_fwd(seq, t0, n, o):
        def f(yT):
            P.dma(yf[seq].ap()[:, t0:t0 + n], yT)
        return f

    def fin_bwd(seq, t0, n, o):
        def f(yb):
            yfl = P.nxt("rw_yfl")[:, 0:n]
            P.dma(yfl, yf[seq].ap()[:, t0:t0 + n])
            y = P.nxt("rw_t1")[:, 0:n]
            P.tt(y, yb, yfl, ALU.add)
            P.mm(PS[2][:, 0:n], blk[:], y)
            yc = P.nxt("rw_t2")[:, 0:n]
            P.stt(yc, PS[2][:, 0:n], -1.0 / 64, y, ALU.mult, ALU.add)
            sq = P.nxt("rw_sq")[:, 0:n]
            P.tt(sq, yc, yc, ALU.mult)
            P.mm(PS[3][:, 0:n], blk[:], sq)
            sd = P.nxt("rw_rn")[:, 0:n]
            P.act(sd, PS[3][:, 0:n], AF.Sqrt, bias=LN_X_EPS, scale=1.0 / 64)
            P.recip(sd, sd)
            P.tt(yc, yc, sd, ALU.mult)
            ov = P.nxt("rw_t3")[:, 0:n]
            P.ts(ov, yc, pc(PR_LG), ALU.mult, pc(PR_LB), ALU.add)
            ks = P.nxt("rw_t4")[:, 0:n]
            P.tt(ks, o["k0"], o["k1"], ALU.add)
            P.stt(ks, ks, pc(PR_RK), o["r"], ALU.mult, ALU.mult)
            P.mm(PS[4][:, 0:n], blk[:], ks)
            P.tt(ks, PS[4][:, 0:n], o["v"], ALU.mult)
            P.tt(ov, ov, ks, ALU.add)
            P.tt(ov, ov, o["g"], ALU.mult)
            P.dma(odst[seq].ap()[:, t0:t0 + n], ov, is_out=True)
        return f

    for d, blist, finf in ((0, fwd, fin_fwd), (1, bwd, fin_bwd)):
        prev = None
        for (seq, t0, n) in blist:
            o = {}; X = {}

            def both(seq=seq, t0=t0, n=n, o=o, X=X, d=d):
                yield from prep_gen(seq, t0, n, o)
                yield from pre_gen(d, o, n, X)
            if _NO_ILV:
                _interleave([prev]); _interleave([both()])
            else:
                _interleave([prev, both()])
            prev = chain_gen(d, n, X, finf(seq, t0, n, o))
        _interleave([prev])
    return io
```

```python
import math
from contextlib import ExitStack
import numpy as np
import ml_dtypes
import concourse.bass as bass
import concourse.mybir as mybir
from concourse.bass_utils import run_bass_kernel_spmd

F32 = mybir.dt.float32
BF16 = mybir.dt.bfloat16
AF = mybir.ActivationFunctionType
ALU = mybir.AluOpType
AX = mybir.AxisListType

COMPUTE = ("pe", "act", "dve", "pool")
STRICT_SYNC = True


class _Ins:
    __slots__ = ("eng", "fn", "waits", "is_dma", "tok", "idx", "need_inc")

    def __init__(self, eng, fn, is_dma):
        self.eng, self.fn, self.is_dma = eng, fn, is_dma
        self.waits = []
        self.tok = None
        self.need_inc = False


class _Trk:
    __slots__ = ("writers", "readers", "const")

    def __init__(self):
        self.writers = []
        self.readers = {}
        self.const = False


class Prog:
    def __init__(self, n_dma_sems=24):
        self.nc = bass.Bass("TRN2", target_bir_lowering=False)
        self.es = ExitStack()
        self.ins = []
        self.trk = {}
        self.n_dma_sems = n_dma_sems
        self.dma_rr = {"sp": 0, "pool": 0, "act": 0}
        self.dma_prev = {}
        self.out_dmas = []
        self._uid = 0
        self.rings = {}
        self.psum_names = set()
        self.debug = False
        self.dbg_out = {}

    def sb(self, name, shape, dtype=F32):
        t = self.es.enter_context(self.nc.sbuf_tensor(name, list(shape), dtype))
        return t

    def ps(self, name, shape, dtype=F32):
        t = self.es.enter_context(self.nc.psum_tensor(name, list(shape), dtype))
        self.psum_names.add(name)
        return t

    def dram(self, name, shape, dtype=F32, kind="Internal"):
        t = self.nc.dram_tensor(name, list(shape), dtype, kind=kind)
        if kind == "ExternalInput":
            self._t(name).const = True
        return t

    def ring(self, name, n, shape, dtype=F32, psum=False):
        tiles = [(self.ps if psum else self.sb)(f"{name}{i}", shape, dtype) for i in range(n)]
        self.rings[name] = [tiles, 0]
        return name

    def nxt(self, name):
        r = self.rings[name]
        t = r[0][r[1] % len(r[0])]
        r[1] += 1
        return t

    def _t(self, name):
        t = self.trk.get(name)
        if t is None:
            t = self.trk[name] = _Trk()
        return t

    def _rec(self, eng, fn, reads, writes, is_dma=False):
        x = _Ins(eng, fn, is_dma)
        x.idx = len(self.ins)
        deps = []
        rn = []
        for a in reads:
            if a is None or isinstance(a, (int, float)):
                continue
            rn.append(a.tensor.name)
        wn = [a.tensor.name for a in writes]
        raw = set()
        for n in rn:
            deps.extend(self._t(n).writers)
            for w_ in self._t(n).writers:
                raw.add(w_.idx)
            if n in self.psum_names:
                deps.extend(r_ for k_, r_ in self._t(n).readers.items() if k_ != eng)
        for n in wn:
            t = self._t(n)
            deps.extend(t.writers)
            deps.extend(t.readers.values())
        seen = set()
        for d in deps:
            if d.idx in seen:
                continue
            seen.add(d.idx)
            if (not d.is_dma) and (not is_dma) and d.eng == eng:
                if eng == "pe" or (d.idx not in raw and not STRICT_SYNC):
                    continue
            x.waits.append(d)
            d.need_inc = True
        for n in rn:
            t = self._t(n)
            if t.const:
                continue
            key = ("dma", x.idx) if is_dma else eng
            t.readers[key] = x
            if len(t.readers) > 48:
                ks = [k for k in t.readers if isinstance(k, tuple)]
                pass
        for n in wn:
            t = self._t(n)
            t.writers = [x]
            t.readers = {}
        self.ins.append(x)
        return x

    def mm(self, out, lhsT, rhs, start=True, stop=True):
        rd = [lhsT, rhs] + ([] if start else [out])
        return self._rec("pe", lambda e: e.matmul(out, lhsT, rhs, start=start, stop=stop), rd, [out])

    def transpose(self, out, in_, ident):
        return self._rec("pe", lambda e: e.transpose(out, in_, ident), [in_, ident], [out])

    def act(self, out, in_, func, bias=None, scale=None, accum=None, eng="act"):
        kw = {}
        if bias is not None:
            kw["bias"] = bias
        if scale is not None:
            kw["scale"] = scale
        if accum is not None:
            kw["accum_out"] = accum
        rd = [in_, bias if hasattr(bias, "tensor") else None, scale if hasattr(scale, "tensor") else None]
        wr = [out] + ([accum] if accum is not None else [])
        return self._rec("act", lambda e: e.activation(out, in_, func, **kw), rd, wr)

    def tt(self, out, a, b, op, eng="dve"):
        return self._rec(eng, lambda e: e.tensor_tensor(out, a, b, op), [a, b], [out])

    def ts(self, out, a, s1, op0, s2=None, op1=None, eng="dve", accum=None):
        rd = [a, s1 if hasattr(s1, "tensor") else None, s2 if hasattr(s2, "tensor") else None]
        if op1 is None:
            return self._rec(eng, lambda e: e.tensor_scalar(out, a, s1, None, op0), rd, [out])
        kw = {}
        wr = [out]
        if accum is not None:
            kw["accum_out"] = accum
            wr.append(accum)
        return self._rec(eng, lambda e: e.tensor_scalar(out, a, s1, s2, op0, op1, **kw), rd, wr)

    def stt(self, out, in0, scalar, in1, op0, op1, eng="dve"):
        rd = [in0, in1, scalar if hasattr(scalar, "tensor") else None]
        return self._rec(eng, lambda e: e.scalar_tensor_tensor(out, in0, scalar, in1, op0, op1), rd, [out])

    def copy(self, out, in_, eng="dve"):
        if eng == "act":
            return self._rec("act", lambda e: e.copy(out, in_), [in_], [out])
        return self._rec(eng, lambda e: e.tensor_copy(out, in_), [in_], [out])

    def memset(self, out, val, eng="dve"):
        return self._rec(eng, lambda e: e.memset(out, val), [], [out])

    def reduce(self, out, in_, op, axis=AX.X, eng="dve"):
        return self._rec(eng, lambda e: e.tensor_reduce(out, in_, axis, op), [in_], [out])

    def scan(self, out, d0, d1, init, op0, op1):
        return self._rec("dve", lambda e: e.tensor_tensor_scan(out, d0, d1, init, op0, op1), [d0, d1], [out])

    def recip(self, out, in_):
        return self._rec("dve", lambda e: e.reciprocal(out, in_), [in_], [out])

    def dma(self, out, in_, q="sp", is_out=False):
        x = self._rec(q, lambda e: e.dma_start(out=out, in_=in_), [in_], [out], is_dma=True)
        k = self.dma_rr[q] % self.n_dma_sems
        self.dma_rr[q] += 1
        key = (q, k)
        prev = self.dma_prev.get(key)
        val = (prev.tok[1] if prev is not None else 0) + 16
        x.tok = (key, val)
        if prev is not None and prev not in x.waits:
            x.waits.append(prev)
        self.dma_prev[key] = x
        if is_out:
            self.out_dmas.append(x)
        return x

    def dbg(self, name, ap):
        if not getattr(self, "debug", False):
            return
        if name in self.dbg_out:
            return
        t = self.dram("dbg_" + name, list(ap.shape), F32, "ExternalOutput")
        self.dbg_out[name] = t
        self.dma(t.ap(), ap, is_out=True)

    def finish(self):
        nc = self.nc
        engs = {"pe": [], "act": [], "dve": [], "pool": [], "sp": []}
        for x in self.ins:
            engs[x.eng].append(x)
        sems = {}
        for e in COMPUTE:
            sems[e] = self.es.enter_context(nc.semaphore(f"s_{e}"))
            c = 0
            for x in engs[e]:
                if x.is_dma:
                    continue
                if x.need_inc:
                    c += 1
                    x.tok = (e, c)
        for q in ("sp", "pool", "act"):
            for k in range(self.n_dma_sems):
                if (q, k) in self.dma_prev:
                    sems[(q, k)] = self.es.enter_context(nc.semaphore(f"d_{q}{k}"))
        block = self.es.enter_context(nc.Block())
        out_dmas = self.out_dmas

        def emit(engname):
            def body(e):
                waited = {}
                for x in engs[engname]:
                    for d in x.waits:
                        s, v = d.tok
                        if waited.get(s, 0) >= v:
                            continue
                        waited[s] = v
                        e.wait_ge(sems[s], v)
                    r = x.fn(e)
                    if x.is_dma:
                        r.then_inc(sems[x.tok[0]], 16)
                    elif x.need_inc:
                        r.then_inc(sems[x.tok[0]], 1)
                if engname == "sp":
                    for d in out_dmas:
                        s, v = d.tok
                        if waited.get(s, 0) >= v:
                            continue
                        waited[s] = v
                        e.wait_ge(sems[s], v)
            return body

        block.sync(emit("sp"))
        block.tensor(emit("pe"))
        block.scalar(emit("act"))
        block.vector(emit("dve"))
        block.gpsimd(emit("pool"))
        self.es.close()
        return nc


RW_C = 64
E05 = math.exp(-0.5)
LN_X_EPS = 64e-5
PR_MU0, PR_MU1, PR_M2, PR_W0, PR_A0, PR_KK, PR_KA, PR_1MKA, PR_RK, PR_LG, PR_LB = 0, 6, 12, 18, 20, 22, 23, 24, 25, 26, 27
PR_N = 28


def rwkv_consts():
    c = {}
    blk = np.zeros((128, 128), np.float32); blk[:64, :64] = 1; blk[64:, 64:] = 1
    c["blk64"] = blk
    c["ident"] = np.eye(128, dtype=np.float32)
    i2 = np.zeros((128, 64), np.float32); i2[:64] = np.eye(64); i2[64:] = np.eye(64)
    c["i2"] = i2
    p = np.arange(64)[:, None]; f = np.arange(64)[None, :]
    m = np.zeros((2, 64, 5, 64), np.float32)
    m[0, :, 0] = f < p; m[0, :, 1] = f > p; m[0, :, 2] = f > p; m[0, :, 3] = f >= p; m[0, :, 4] = f >= p
    m[1, :, 0] = f > p; m[1, :, 1] = f < p; m[1, :, 2] = f < p; m[1, :, 3] = f <= p; m[1, :, 4] = f <= p
    c["mask5"] = np.concatenate([m, m], axis=1).reshape(2, 128, 320)
    return c


def rwkv_build(P, PS, Tc, Tl, TB=512):
    C = RW_C
    io = {}
    io["uc"] = P.dram("rw_uc", [128, 6, Tc + 2], F32, "ExternalInput")
    io["ul"] = P.dram("rw_ul", [128, 6, Tl + 2], F32, "ExternalInput")
    io["pr"] = P.dram("rw_pr", [128, PR_N], F32, "ExternalInput")
    io["wup"] = P.dram("rw_wup", [128, 128], F32, "ExternalInput")
    io["aup"] = P.dram("rw_aup", [128, 128], F32, "ExternalInput")
    io["gup"] = P.dram("rw_gup", [128, 128], F32, "ExternalInput")
    io["blk64"] = P.dram("c_blk64", [128, 128], F32, "ExternalInput")
    io["ident"] = P.dram("c_ident", [128, 128], F32, "ExternalInput")
    io["i2"] = P.dram("c_i2", [128, 64], F32, "ExternalInput")
    io["mask5"] = P.dram("c_mask5", [2, 128, 320], F32, "ExternalInput")
    io["oc"] = P.dram("rw_oc", [128, Tc], F32, "ExternalOutput")
    io["ol"] = P.dram("rw_ol", [128, Tl], F32, "ExternalOutput")
    yf = {"c": P.dram("rw_yfc", [128, Tc], F32), "l": P.dram("rw_yfl", [128, Tl], F32)}
    usrc = {"c": io["uc"], "l": io["ul"]}
    odst = {"c": io["oc"], "l": io["ol"]}

    def cst(name, shape):
        t = P.sb("k_" + name, shape)
        P.dma(t[:], io[name].ap())
        return t
    pr = cst("pr", [128, PR_N]); wup = cst("wup", [128, 128]); aup = cst("aup", [128, 128]); gup = cst("gup", [128, 128])
    blk = cst("blk64", [128, 128]); ident = cst("ident", [128, 128]); i2 = cst("i2", [128, 64])
    mask5 = []
    for d in range(2):
        t = P.sb(f"k_mask5_{d}", [128, 320])
        P.dma(t[:], io["mask5"].ap()[d])
        mask5.append(t)
    rst = P.sb("k_rst", [128, TB])
    P.memset(rst[:], 1.0)
    P.memset(rst[:].rearrange("p (c t) -> p c t", t=C)[:, :, 0:1], 0.0)
    pc = lambda i: pr[:, i:i + 1]
    P.ts(pr[:, PR_M2:PR_M2 + 6], pr[:, PR_MU0:PR_MU0 + 6], -1.0, ALU.mult, 1.0, ALU.add)
    P.tt(pr[:, PR_M2:PR_M2 + 6], pr[:, PR_M2:PR_M2 + 6], pr[:, PR_MU1:PR_MU1 + 6], ALU.subtract)
    P.ts(pr[:, PR_1MKA:PR_1MKA + 1], pr[:, PR_KA:PR_KA + 1], -1.0, ALU.mult, 1.0, ALU.add)

    for nm in ("u6",):
        P.ring("rw_" + nm, 1, [128, 6, TB + 2])
    for nm in ("xs", ):
        P.ring("rw_" + nm, 1, [128, 6, TB])
    for nm in ("tw", "sg", "kx", "rn", "kk", "g", "lw0", "lw1", "ic0", "ic1", "k0", "k1", "b0", "b1"):
        P.ring("rw_" + nm, 1, [128, TB])
    for nm in ("sq", "f1", "t1"):
        P.ring("rw_" + nm, 2, [128, TB])
    for nm in ("cw", "cwb", "ep", "en", "ea", "At", "Bt", "Kt", "Rt", "yT", "t2", "t3", "t4", "yfl"):
        P.ring("rw_" + nm, 1, [128, TB])
    P.ring("rw_tok", 10, [128, 3, 64])
    P.ring("rw_M5", 10, [128, 5, 64])
    P.ring("rw_TT", 10, [128, 64])
    P.ring("rw_PP", 4, [128, 128])
    P.ring("rw_Xs", 3, [128, 64])
    P.ring("rw_Us", 3, [128, 64])
    S = [P.sb("rw_S0", [128, 64]), P.sb("rw_S1", [128, 64])]
    Stmp = P.sb("rw_Stmp", [128, 64])
    ps_tok, ps5, psL, psT, psX, psU, psY, psS = PS

    def prep(seq, t0, n):
        u6 = P.nxt("rw_u6")
        P.dma(u6[:, :, 0:n + 2], usrc[seq].ap()[:, :, t0:t0 + n + 2])
        xs = P.nxt("rw_xs")
        P.dbg("u6", u6[:, 0, 0:n + 2]); P.dbg("pr", pr[:])
        for j in range(6):
            P.ts(xs[:, j, 0:n], u6[:, j, 1:n + 1], pc(PR_M2 + j), ALU.mult)
            P.stt(xs[:, j, 0:n], u6[:, j, 0:n], pc(PR_MU0 + j), xs[:, j, 0:n], ALU.mult, ALU.add)
            P.stt(xs[:, j, 0:n], u6[:, j, 2:n + 2], pc(PR_MU1 + j), xs[:, j, 0:n], ALU.mult, ALU.add)
        r, k, v, wd, ad, gd = [xs[:, j, 0:n] for j in range(6)]
        o = {"r": r, "v": v}
        tw = P.nxt("rw_tw")[:, 0:n]
        P.act(tw, wd, AF.Tanh)
        for d in range(2):
            hs = slice(64 * d, 64 * d + 64)
            P.mm(psL[:, 0:n], wup[hs, :], tw[hs, :])
            lw = P.nxt(f"rw_lw{d}")[:, 0:n]
            P.act(lw, psL[:, 0:n], AF.Sigmoid, bias=pc(PR_W0 + d))
            P.ts(lw, lw, -E05, ALU.mult)
            o[f"lw{d}"] = lw
            P.mm(psT[:, 0:n], aup[hs, :], ad[hs, :])
            ic = P.nxt(f"rw_ic{d}")[:, 0:n]
            P.act(ic, psT[:, 0:n], AF.Sigmoid, bias=pc(PR_A0 + d))
            o[f"ic{d}"] = ic
        kx = P.nxt("rw_kx")[:, 0:n]
        P.ts(kx, k, pc(PR_KK), ALU.mult)
        sq = P.nxt("rw_sq")[:, 0:n]
        P.tt(sq, kx, kx, ALU.mult)
        P.mm(psX[:, 0:n], blk[:], sq)
        rn = P.nxt("rw_rn")[:, 0:n]
        P.act(rn, psX[:, 0:n], AF.Sqrt, bias=1e-12)
        P.recip(rn, rn)
        kk = P.nxt("rw_kk")[:, 0:n]
        P.tt(kk, kx, rn, ALU.mult)
        o["kk"] = kk
        for d in range(2):
            f1 = P.nxt("rw_f1")[:, 0:n]
            P.ts(f1, o[f"ic{d}"], pc(PR_KA), ALU.mult, pc(PR_1MKA), ALU.add)
            kd = P.nxt(f"rw_k{d}")[:, 0:n]
            P.tt(kd, k, f1, ALU.mult)
            bd = P.nxt(f"rw_b{d}")[:, 0:n]
            P.tt(bd, kk, o[f"ic{d}"], ALU.mult)
            o[f"k{d}"] = kd
            o[f"b{d}"] = bd
        sg = P.nxt("rw_sg")[:, 0:n]
        P.act(sg, gd, AF.Sigmoid)
        P.mm(psU[:, 0:n], gup[:], sg)
        g = P.nxt("rw_g")[:, 0:n]
        P.copy(g, psU[:, 0:n], eng="act")
        o["g"] = g
        for kname in ("r", "v", "kk", "lw0", "lw1", "ic0", "k0", "b0", "g"):
            P.dbg("p_" + kname, o[kname])
        return o

    def scan_block(d, o, n):
        nch = n // C
        lw = o[f"lw{d}"]
        cw = P.nxt("rw_cw")[:, 0:n]
        P.scan(cw, rst[:, 0:n], lw, 0.0, ALU.mult, ALU.add)
        v3 = lambda a: a.rearrange("p (c t) -> p c t", t=C)
        if d == 1:
            cwb = P.nxt("rw_cwb")[:, 0:n]
            P.tt(cwb, lw, cw, ALU.subtract)
            for c in range(nch):
                P.ts(cwb[:, c * C:(c + 1) * C], cwb[:, c * C:(c + 1) * C], cw[:, c * C + C - 1:c * C + C], ALU.add)
            cw = cwb
        ep = P.nxt("rw_ep")[:, 0:n]; en = P.nxt("rw_en")[:, 0:n]; ea = P.nxt("rw_ea")[:, 0:n]
        P.act(ep, cw, AF.Exp)
        P.act(en, cw, AF.Exp, scale=-1.0)
        t1 = P.nxt("rw_t1")[:, 0:n]
        P.tt(t1, cw, lw, ALU.subtract)
        P.act(ea, t1, AF.Exp)
        At = P.nxt("rw_At")[:, 0:n]; Bt = P.nxt("rw_Bt")[:, 0:n]; Kt = P.nxt("rw_Kt")[:, 0:n]; Rt = P.nxt("rw_Rt")[:, 0:n]
        P.stt(At, o["kk"], -1.0, ea, ALU.mult, ALU.mult)
        P.tt(Bt, o[f"b{d}"], en, ALU.mult)
        P.tt(Kt, o[f"k{d}"], en, ALU.mult)
        P.tt(Rt, o["r"], ep, ALU.mult)
        wcol = (C - 1) if d == 0 else 0
        v = o["v"]
        P.dbg(f"cw{d}", cw); P.dbg(f"At{d}", At); P.dbg(f"Bt{d}", Bt); P.dbg(f"Rt{d}", Rt)
        pre = []
        for c in range(nch):
            cs = slice(c * C, (c + 1) * C)
            tok = P.nxt("rw_tok")
            for j, X in enumerate((Bt, Kt, v)):
                for s in range(2):
                    hs = slice(64 * s, 64 * s + 64)
                    P.mm(ps_tok[hs, j * 64:(j + 1) * 64], X[hs, cs], ident[hs, hs])
            P.copy(tok[:].rearrange("p a b -> p (a b)"), ps_tok[:, 0:192], eng="act")
            M5 = P.nxt("rw_M5")
            for s in range(2):
                hs = slice(64 * s, 64 * s + 64)
                for j, (l, rr) in enumerate(((At, Bt), (Bt, At), (Kt, At), (Bt, Rt), (Kt, Rt))):
                    P.mm(ps5[hs, j * 64:(j + 1) * 64], l[hs, cs], rr[hs, cs])
            P.tt(M5[:].rearrange("p a b -> p (a b)"), ps5[:, 0:320], mask5[d][:], ALU.mult)
            TT = P.nxt("rw_TT")
            P.tt(TT[:], M5[:, 1, :], i2[:], ALU.add)
            Pm, PTm = M5[:, 0, :], M5[:, 1, :]
            for lvl in range(1, 6):
                for s in range(2):
                    hs = slice(64 * s, 64 * s + 64)
                    P.mm(psL[hs, 0:64], PTm[hs, :], Pm[hs, :])
                    if lvl < 5:
                        P.mm(psL[hs, 64:128], Pm[hs, :], PTm[hs, :])
                PP = P.nxt("rw_PP")
                w = 128 if lvl < 5 else 64
                P.copy(PP[:, 0:w], psL[:, 0:w], eng="act")
                Pm, PTm = PP[:, 0:64], PP[:, 64:128]
                for s in range(2):
                    hs = slice(64 * s, 64 * s + 64)
                    P.mm(psT[hs, 0:64], Pm[hs, :], TT[hs, :])
                P.tt(TT[:], TT[:], psT[:, 0:64], ALU.add)
            pre.append((tok, M5, TT))
            if c == 0:
                P.dbg(f"tok{d}", tok[:].rearrange("p a b -> p (a b)")); P.dbg(f"M5{d}", M5[:].rearrange("p a b -> p (a b)")); P.dbg(f"TT{d}", TT[:])
        Sd = S[d]
        order = range(nch) if d == 0 else range(nch - 1, -1, -1)
        for c in order:
            cs = slice(c * C, (c + 1) * C)
            tok, M5, TT = pre[c]
            Btok, Ktok, Vtok = tok[:, 0, :], tok[:, 1, :], tok[:, 2, :]
            AkT, LrbT, LrkT = M5[:, 2, :], M5[:, 3, :], M5[:, 4, :]
            for s in range(2):
                hs = slice(64 * s, 64 * s + 64)
                P.mm(psX[hs, 0:64], At[hs, cs], Sd[hs, :], start=True, stop=False)
                P.mm(psX[hs, 0:64], AkT[hs, :], Vtok[hs, :], start=False, stop=True)
            Xs = P.nxt("rw_Xs")
            P.copy(Xs[:], psX[:, 0:64], eng="act")
            for s in range(2):
                hs = slice(64 * s, 64 * s + 64)
                P.mm(psU[hs, 0:64], TT[hs, :], Xs[hs, :])
            Us = P.nxt("rw_Us")
            P.copy(Us[:], psU[:, 0:64])
            for s in range(2):
                hs = slice(64 * s, 64 * s + 64)
                P.mm(psY[hs, cs], Sd[hs, :], Rt[hs, cs], start=True, stop=False)
                P.mm(psY[hs, cs], Us[hs, :], LrbT[hs, :], start=False, stop=False)
                P.mm(psY[hs, cs], Vtok[hs, :], LrkT[hs, :], start=False, stop=True)
                P.mm(psS[hs, 0:64], Btok[hs, :], Us[hs, :], start=True, stop=False)
                P.mm(psS[hs, 0:64], Ktok[hs, :], Vtok[hs, :], start=False, stop=True)
            P.tt(Stmp[:], Sd[:], psS[:, 0:64], ALU.add)
            P.ts(Sd[:], Stmp[:], ep[:, c * C + wcol:c * C + wcol + 1], ALU.mult)
        yT = P.nxt("rw_yT")[:, 0:n]
        P.copy(yT, psY[:, 0:n], eng="act")
        P.dbg(f"yT{d}", yT)
        return yT

    def blocks(seq, T):
        return [(seq, t0, min(TB, T - t0)) for t0 in range(0, T, TB)]
    fwd = blocks("c", Tc) + blocks("l", Tl)
    bwd = blocks("c", Tc)[::-1] + blocks("l", Tl)[::-1]
    P.memset(S[0][:], 0.0); P.memset(S[1][:], 0.0)
    for (seq, t0, n) in fwd:
        o = prep(seq, t0, n)
        yT = scan_block(0, o, n)
        P.dma(yf[seq].ap()[:, t0:t0 + n], yT)
    for (seq, t0, n) in bwd:
        o = prep(seq, t0, n)
        yb = scan_block(1, o, n)
        yfl = P.nxt("rw_yfl")[:, 0:n]
        P.dma(yfl, yf[seq].ap()[:, t0:t0 + n])
        y = P.nxt("rw_t1")[:, 0:n]
        P.tt(y, yb, yfl, ALU.add)
        P.mm(psL[:, 0:n], blk[:], y)
        yc = P.nxt("rw_t2")[:, 0:n]
        P.stt(yc, psL[:, 0:n], -1.0 / 64, y, ALU.mult, ALU.add)
        sq = P.nxt("rw_sq")[:, 0:n]
        P.tt(sq, yc, yc, ALU.mult)
        P.mm(psT[:, 0:n], blk[:], sq)
        sd = P.nxt("rw_rn")[:, 0:n]
        P.act(sd, psT[:, 0:n], AF.Sqrt, bias=LN_X_EPS, scale=1.0 / 64)
        P.recip(sd, sd)
        P.tt(yc, yc, sd, ALU.mult)
        ov = P.nxt("rw_t3")[:, 0:n]
        P.ts(ov, yc, pc(PR_LG), ALU.mult, pc(PR_LB), ALU.add)
        ks = P.nxt("rw_t4")[:, 0:n]
        P.tt(ks, o["k0"], o["k1"], ALU.add)
        P.stt(ks, ks, pc(PR_RK), o["r"], ALU.mult, ALU.mult)
        P.mm(psX[:, 0:n], blk[:], ks)
        P.tt(ks, psX[:, 0:n], o["v"], ALU.mult)
        P.tt(ov, ov, ks, ALU.add)
        P.tt(ov, ov, o["g"], ALU.mult)
        P.dma(odst[seq].ap()[:, t0:t0 + n], ov, is_out=True)
    return io


def rwkv_host_inputs(rw_c, rw_l, p, heads):
    W = 640
    cols = []
    for base in (0, W, 2 * W):
        cols.append(np.concatenate([np.arange(base + h * 64, base + h * 64 + 64) for h in heads]))
    cols.append(np.arange(3 * W, 3 * W + 128)); cols.append(np.arange(3 * W + 128, 3 * W + 256)); cols.append(np.arange(3 * W + 256, 3 * W + 384))
    cols = np.stack(cols)

    def lay(u):
        T = u.shape[0]
        o = np.zeros((128, 6, T + 2), np.float32)
        o[:, :, 1:T + 1] = u[:, cols].transpose(2, 1, 0)
        return o
    hc = np.concatenate([np.arange(h * 64, h * 64 + 64) for h in heads])
    pr = np.zeros((128, PR_N), np.float32)
    mu = p["shift_mu"]
    pr[:, PR_MU0:PR_MU0 + 6] = mu[0][cols].T
    pr[:, PR_MU1:PR_MU1 + 6] = mu[1][cols].T
    pr[:, PR_W0:PR_W0 + 2] = p["w0"][:, hc].T
    pr[:, PR_A0:PR_A0 + 2] = p["a0"][:, hc].T
    pr[:, PR_KK] = p["k_k"][hc]; pr[:, PR_KA] = p["k_a"][hc]
    pr[:, PR_RK] = p["r_k"].reshape(-1)[hc]; pr[:, PR_LG] = p["lnx_g"][hc]; pr[:, PR_LB] = p["lnx_b"][hc]
    d = {"rw_uc": lay(rw_c), "rw_ul": lay(rw_l), "rw_pr": pr,
         "rw_wup": np.ascontiguousarray(p["w_up"][:, :, hc].reshape(128, 128)),
         "rw_aup": np.ascontiguousarray(p["a_up"][:, :, hc].reshape(128, 128)),
         "rw_gup": np.ascontiguousarray(p["g_up"][:, hc])}
    for k, v in rwkv_consts().items():
        d["c_" + k] = v
    return d


SQ = 128


def ssd_consts():
    p = np.arange(128)[:, None]; f = np.arange(128)[None, :]
    tri = np.stack([(f >= p), (f <= p)]).astype(np.float32)
    return {"tri": tri, "ident": np.eye(128, dtype=np.float32), "ones": np.ones((128, 128), np.float32)}


def ssd_build(P, PS, Tc, Tl, TB=512):
    Q = SQ
    io = {}
    T_ = {"c": Tc, "l": Tl}
    for sq in ("c", "l"):
        T = T_[sq]
        io["x" + sq] = P.dram("sd_x" + sq, [2, 64, T + 2], F32, "ExternalInput")
        io["b" + sq] = P.dram("sd_b" + sq, [2, 128, T + 2], F32, "ExternalInput")
        io["c" + sq] = P.dram("sd_c" + sq, [2, 128, T + 2], F32, "ExternalInput")
        io["dt" + sq] = P.dram("sd_dt" + sq, [2, 128, T // Q, 2], F32, "ExternalInput")
        io["y" + sq] = P.dram("sd_y" + sq, [2, 128, T // Q, 64], F32, "ExternalOutput")
    io["px"] = P.dram("sd_px", [2, 64, 4], F32, "ExternalInput")
    io["pb"] = P.dram("sd_pb", [2, 128, 4], F32, "ExternalInput")
    io["pc"] = P.dram("sd_pc", [2, 128, 4], F32, "ExternalInput")
    io["pd"] = P.dram("sd_pd", [2, 128, 6], F32, "ExternalInput")
    io["tri"] = P.dram("c_tri", [2, 128, 128], F32, "ExternalInput")
    io["ident"] = P.dram("c_identb", [128, 128], F32, "ExternalInput")
    io["ones"] = P.dram("c_ones", [128, 128], F32, "ExternalInput")
    yfd = {sq: P.dram("sd_yf" + sq, [2, 128, T_[sq] // Q, 64], F32) for sq in ("c", "l")}

    def cst(nm, src, shape):
        t = P.sb("sk_" + nm, shape)
        P.dma(t[:], src)
        return t
    tri = [cst(f"tri{d}", io["tri"].ap()[d], [128, 128]) for d in range(2)]
    ident = cst("ident", io["ident"].ap(), [128, 128])
    ones = cst("ones", io["ones"].ap(), [128, 128])
    px = [cst(f"px{s}", io["px"].ap()[s], [64, 4]) for s in range(2)]
    pb = [cst(f"pb{s}", io["pb"].ap()[s], [128, 4]) for s in range(2)]
    pcc = [cst(f"pc{s}", io["pc"].ap()[s], [128, 4]) for s in range(2)]
    pd = [cst(f"pd{s}", io["pd"].ap()[s], [128, 6]) for s in range(2)]
    Aneg = [P.sb(f"sk_A{s}", [128, 2]) for s in range(2)]
    dsum = [P.sb(f"sk_ds{s}", [128, 1]) for s in range(2)]
    for s in range(2):
        P.act(Aneg[s][:], pd[s][:, 2:4], AF.Exp)
        P.ts(Aneg[s][:], Aneg[s][:], -1.0, ALU.mult)
        P.tt(dsum[s][:], pd[s][:, 4:5], pd[s][:, 5:6], ALU.add)

    P.ring("sd_xin", 2, [64, TB + 2]); P.ring("sd_bin", 2, [128, TB + 2]); P.ring("sd_cin", 2, [128, TB + 2])
    P.ring("sd_xf", 4, [64, TB]); P.ring("sd_bf", 4, [128, TB]); P.ring("sd_cf", 4, [128, TB])
    P.ring("sd_dt", 4, [128, TB // Q, 2]); P.ring("sd_dta", 4, [128, TB // Q, 2]); P.ring("sd_tmp8", 6, [128, TB // Q, 2])
    P.ring("sd_xtok", 12, [128, 64]); P.ring("sd_btok", 12, [128, 128]); P.ring("sd_cbm", 12, [128, 128])
    P.ring("sd_bc", 3, [128, 128]); P.ring("sd_E", 3, [128, 128]); P.ring("sd_G", 3, [128, 128]); P.ring("sd_Cs", 3, [128, 128])
    P.ring("sd_col", 12, [128, 4]); P.ring("sd_xdt", 3, [128, 64]); P.ring("sd_xw", 3, [128, 64])
    P.ring("sd_yblk", 3, [128, TB // Q, 64]); P.ring("sd_yfl", 3, [128, TB // Q, 64])
    H = [[P.sb(f"sd_h{s}{d}", [128, 64]) for d in range(2)] for s in range(2)]
    ps_a, ps_b, ps_c, ps_d, ps_e, ps_f, ps_g, ps_h = PS

    def prep(s, seq, t0, n):
        nq = n // Q
        o = {}
        for nm, ring_in, ring_f, src, par, rows in (("x", "sd_xin", "sd_xf", io["x" + seq], px[s], 64),
                                                    ("b", "sd_bin", "sd_bf", io["b" + seq], pb[s], 128),
                                                    ("c", "sd_cin", "sd_cf", io["c" + seq], pcc[s], 128)):
            tin = P.nxt(ring_in)
            P.dma(tin[:, 0:n + 2], src.ap()[s][:, t0:t0 + n + 2])
            tf = P.nxt(ring_f)[:, 0:n]
            P.ts(tf, tin[:, 1:n + 1], par[:, 1:2], ALU.mult, par[:, 3:4], ALU.add)
            P.stt(tf, tin[:, 0:n], par[:, 0:1], tf, ALU.mult, ALU.add)
            P.stt(tf, tin[:, 2:n + 2], par[:, 2:3], tf, ALU.mult, ALU.add)
            P.act(tf, tf, AF.Silu)
            o[nm] = tf
        dtr = P.nxt("sd_dt")[:, 0:nq, :]
        P.dma(dtr, io["dt" + seq].ap()[s][:, t0 // Q:t0 // Q + nq, :])
        dt = P.nxt("sd_dt")[:, 0:nq, :]
        dta = P.nxt("sd_dta")[:, 0:nq, :]
        for d in range(2):
            xb = P.nxt("sd_tmp8")[:, 0:nq, d:d + 1]
            P.ts(xb, dtr[:, :, d:d + 1], pd[s][:, d:d + 1], ALU.add)
            ab = P.nxt("sd_tmp8")[:, 0:nq, d:d + 1]
            P.act(ab, xb, AF.Abs)
            P.act(ab, ab, AF.Exp, scale=-1.0)
            P.act(ab, ab, AF.Ln, bias=1.0)
            P.ts(xb, xb, 0.0, ALU.max)
            P.tt(dt[:, :, d:d + 1], xb, ab, ALU.add)
            P.ts(dta[:, :, d:d + 1], dt[:, :, d:d + 1], Aneg[s][:, d:d + 1], ALU.mult)
        o["dt"] = dt; o["dta"] = dta
        o["xtok"] = []; o["btok"] = []; o["cb"] = []
        for c in range(nq):
            cs = slice(c * Q, (c + 1) * Q)
            P.mm(ps_a[:, 0:64], o["x"][:, cs], ident[0:64, 0:64])
            xt = P.nxt("sd_xtok")
            P.copy(xt[:], ps_a[:, 0:64], eng="act")
            P.mm(ps_b[:, 0:128], o["b"][:, cs], ident[:])
            bt = P.nxt("sd_btok")
            P.copy(bt[:], ps_b[:, 0:128], eng="act")
            P.mm(ps_c[:, 0:128], o["b"][:, cs], o["c"][:, cs])
            o["xtok"].append(xt); o["btok"].append(bt); o["cb"].append(ps_c)
            cbm = []
            for d in range(2):
                m = P.nxt("sd_cbm")
                P.tt(m[:], ps_c[:, 0:128], tri[d][:], ALU.mult)
                cbm.append(m)
            o["cb"][-1] = cbm
        return o

    def scan(s, d, o, n):
        nq = n // Q
        yb = P.nxt("sd_yblk")
        h = H[s][d]
        order = range(nq) if d == 0 else range(nq - 1, -1, -1)
        last = Q - 1 if d == 0 else 0
        for c in order:
            cs = slice(c * Q, (c + 1) * Q)
            dta = o["dta"][:, c, d:d + 1]
            dt = o["dt"][:, c, d:d + 1]
            bc = P.nxt("sd_bc")
            P.ts(bc[:], ones[:], dta, ALU.mult)
            P.mm(ps_d[:, 0:128], bc[:], tri[d][:])
            P.mm(ps_e[:, 0:1], tri[d][:], dta)
            col = P.nxt("sd_col")
            P.copy(col[:, 0:1], ps_e[:, 0:1])
            E = P.nxt("sd_E")
            P.ts(E[:], ps_d[:, 0:128], col[:, 0:1], ALU.subtract, 0.0, ALU.min)
            P.act(E[:], E[:], AF.Exp)
            G = P.nxt("sd_G")
            P.tt(G[:], E[:], o["cb"][c][d][:], ALU.mult)
            Cs = P.nxt("sd_Cs")
            P.act(Cs[:], ps_d[:, 0:128], AF.Exp)
            P.tt(Cs[:], Cs[:], o["c"][:, cs], ALU.mult)
            xdt = P.nxt("sd_xdt")
            P.ts(xdt[:], o["xtok"][c][:], dt, ALU.mult)
            P.mm(ps_f[:, 0:64], G[:], xdt[:], start=True, stop=False)
            P.mm(ps_f[:, 0:64], Cs[:], h[:], start=False, stop=True)
            P.copy(yb[:, c, :], ps_f[:, 0:64], eng="act")
            P.ts(col[:, 1:2], col[:, 0:1], -1.0, ALU.mult, ps_d[:, last:last + 1], ALU.add)
            P.act(col[:, 1:2], col[:, 1:2], AF.Exp)
            P.tt(col[:, 1:2], col[:, 1:2], dt, ALU.mult)
            P.act(col[:, 2:3], ps_d[:, last:last + 1], AF.Exp)
            xw = P.nxt("sd_xw")
            P.ts(xw[:], o["xtok"][c][:], col[:, 1:2], ALU.mult)
            P.mm(ps_g[:, 0:64], o["btok"][c][:], xw[:])
            P.stt(h[:], h[:], col[:, 2:3], ps_g[:, 0:64], ALU.mult, ALU.add)
        return yb

    def blocks(seq, T):
        return [(seq, t0, min(TB, T - t0)) for t0 in range(0, T, TB)]
    fwd = blocks("c", Tc) + blocks("l", Tl)
    bwd = blocks("c", Tc)[::-1] + blocks("l", Tl)[::-1]
    for s in range(2):
        for d in range(2):
            P.memset(H[s][d][:], 0.0)
    for (seq, t0, n) in fwd:
        for s in range(2):
            o = prep(s, seq, t0, n)
            yb = scan(s, 0, o, n)
            nq = n // Q
            P.dma(yfd[seq].ap()[s][:, t0 // Q:t0 // Q + nq, :], yb[:, 0:nq, :])
    for (seq, t0, n) in bwd:
        for s in range(2):
            o = prep(s, seq, t0, n)
            yb = scan(s, 1, o, n)
            nq = n // Q
            yfl = P.nxt("sd_yfl")
            P.dma(yfl[:, 0:nq, :], yfd[seq].ap()[s][:, t0 // Q:t0 // Q + nq, :])
            P.tt(yb[:, 0:nq, :], yb[:, 0:nq, :], yfl[:, 0:nq, :], ALU.add)
            for c in range(nq):
                P.stt(yb[:, c, :], o["xtok"][c][:], dsum[s][:, 0:1], yb[:, c, :], ALU.mult, ALU.add)
            P.dma(io["y" + seq].ap()[s][:, t0 // Q:t0 // Q + nq, :], yb[:, 0:nq, :], is_out=True)
    return io


def ssd_host_inputs(ssm_c, ssm_l, p, heads):
    d = {}
    for sq, u in (("c", ssm_c), ("l", ssm_l)):
        T = u.shape[0]
        xbc = u[:, 640:640 + 1152]
        X = np.zeros((2, 64, T + 2), np.float32); B = np.zeros((2, 128, T + 2), np.float32); Cc = np.zeros((2, 128, T + 2), np.float32)
        DT = np.zeros((2, 128, T // SQ, 2), np.float32)
        for s, h in enumerate(heads):
            g = h // 5
            X[s, :, 1:T + 1] = xbc[:, h * 64:h * 64 + 64].T
            B[s, :, 1:T + 1] = xbc[:, 640 + g * 128:640 + g * 128 + 128].T
            Cc[s, :, 1:T + 1] = xbc[:, 896 + g * 128:896 + g * 128 + 128].T
            DT[s, :, :, 0] = u[:, 1792 + h].reshape(T // SQ, SQ).T
            DT[s, :, :, 1] = u[:, 1802 + h].reshape(T // SQ, SQ).T
        d["sd_x" + sq] = X; d["sd_b" + sq] = B; d["sd_c" + sq] = Cc; d["sd_dt" + sq] = DT
    cw = np.concatenate([p["conv_w"], p["conv_b"][None]], 0).T
    px = np.zeros((2, 64, 4), np.float32); pb = np.zeros((2, 128, 4), np.float32); pc = np.zeros((2, 128, 4), np.float32)
    pd = np.zeros((2, 128, 6), np.float32)
    for s, h in enumerate(heads):
        g = h // 5
        px[s] = cw[h * 64:h * 64 + 64]; pb[s] = cw[640 + g * 128:640 + g * 128 + 128]; pc[s] = cw[896 + g * 128:896 + g * 128 + 128]
        pd[s, :, 0] = p["dt_bias"][0, h]; pd[s, :, 1] = p["dt_bias"][1, h]
        pd[s, :, 2] = p["a_log"][0, h]; pd[s, :, 3] = p["a_log"][1, h]
        pd[s, :, 4] = p["d_skip"][0, h]; pd[s, :, 5] = p["d_skip"][1, h]
    d.update({"sd_px": px, "sd_pb": pb, "sd_pc": pc, "sd_pd": pd})
    c = ssd_consts()
    d["c_tri"] = c["tri"]; d["c_identb"] = c["ident"]; d["c_ones"] = c["ones"]
    return d


def modvec_build(P, PS, L, KD, NCOL):
    D = KD * 128
    io = {}
    io["cT"] = P.dram("m_cT", [128, KD, 2], F32, "ExternalInput")
    io["w"] = P.dram("m_w", [L, KD, 128, NCOL], F32, "ExternalInput")
    io["b"] = P.dram("m_b", [L, 2, NCOL], F32, "ExternalInput")
    io["o"] = P.dram("m_o", [L, 2, NCOL], F32, "ExternalOutput")
    cT = P.sb("m_cTs", [128, KD, 2])
    P.dma(cT[:], io["cT"].ap())
    P.act(cT[:], cT[:], AF.Silu)
    P.ring("m_wt", 4, [128, NCOL]); P.ring("m_bt", 2, [2, NCOL]); P.ring("m_ot", 2, [2, NCOL])
    nt = (NCOL + 511) // 512
    for l in range(L):
        bt = P.nxt("m_bt")
        P.dma(bt[:], io["b"].ap()[l])
        for k in range(KD):
            wt = P.nxt("m_wt")
            P.dma(wt[:], io["w"].ap()[l, k])
            for j in range(nt):
                cs = slice(j * 512, min(NCOL, (j + 1) * 512))
                P.mm(PS[j][0:2, 0:cs.stop - cs.start], cT[:, k, :], wt[:, cs], start=(k == 0), stop=(k == KD - 1))
        ot = P.nxt("m_ot")
        for j in range(nt):
            cs = slice(j * 512, min(NCOL, (j + 1) * 512))
            P.tt(ot[:, cs], PS[j][0:2, 0:cs.stop - cs.start], bt[:, cs], ALU.add)
        P.dma(io["o"].ap()[l], ot[:], is_out=True)
    return io


EPS = 1e-6


def norm_mod(P, ps, ones, xT, hT, KD, segs, ab, tmpring, x_local=False, mkring=True):
    D = KD * 128
    if mkring:
        P.ring(tmpring + "_r", 2, [128, 512])
    for (t0, n, si) in segs:
        ts_ = slice(t0, t0 + n)
        xs_ = slice(0, n) if x_local else ts_
        for k in range(KD):
            sq = P.nxt(tmpring)[:, 0:n]
            P.act(sq, xT[:, k, xs_], AF.Square)
            P.mm(ps[:, 0:n], ones[:], sq, start=(k == 0), stop=(k == KD - 1))
        rstd = P.nxt(tmpring + "_r")[:, 0:n]
        P.act(rstd, ps[:, 0:n], AF.Sqrt, bias=EPS, scale=1.0 / D)
        P.recip(rstd, rstd)
        a, b = ab[si]
        for k in range(KD):
            t = P.nxt(tmpring)[:, 0:n]
            P.tt(t, xT[:, k, xs_], rstd, ALU.mult)
            P.ts(hT[:, k, ts_], t, a[:, k:k + 1], ALU.mult, b[:, k:k + 1], ALU.add)


def load_mods(P, io_mod, io_g, KD, which):
    mod = P.sb("mod_t", [128, 2, 6, KD])
    P.dma(mod[:], io_mod.ap())
    g = P.sb("mod_g", [128, KD])
    P.dma(g[:], io_g.ap())
    out = []
    for si in range(2):
        a = P.sb(f"mod_a{si}", [128, KD])
        P.ts(a[:], mod[:, si, 3 * which + 1, :], 1.0, ALU.add)
        P.tt(a[:], a[:], g[:], ALU.mult)
        out.append((a, mod[:, si, 3 * which + 0, :], mod[:, si, 3 * which + 2, :]))
    return out


IN_COLS = 6420
QK_EPS = 1e-6


def rope_consts():
    blk = np.zeros((128, 128), np.float32); blk[:64, :64] = 1; blk[64:, 64:] = 1
    rot = np.zeros((128, 128), np.float32)
    for base in (0, 64):
        for m in range(32):
            rot[base + m + 32, base + m] = -1.0
            rot[base + m, base + m + 32] = 1.0
    return blk, rot


def rope_tables(positions, is_ctx):
    n_freq = 16
    inv = (10000.0 ** (-np.arange(n_freq, dtype=np.float32) / n_freq)).astype(np.float32)
    row = (positions // 64).astype(np.float32); col = (positions % 64).astype(np.float32)
    ang = np.concatenate([row[:, None] * inv, col[:, None] * inv], axis=-1).astype(np.float32)
    cos = np.cos(ang).astype(np.float32); sin = np.sin(ang).astype(np.float32)
    cos = np.where(is_ctx[:, None], 1.0, cos).astype(np.float32); sin = np.where(is_ctx[:, None], 0.0, sin).astype(np.float32)
    cosT = np.tile(cos.T, (4, 1)); sinT = np.tile(sin.T, (4, 1))
    return np.ascontiguousarray(cosT), np.ascontiguousarray(sinT)


def projA_build(P, PS, KD, TCc, TLc):
    T = TCc + TLc
    io = {}
    io["xT"] = P.dram("a_xT", [128, KD, T], F32, "ExternalInput")
    io["mod"] = P.dram("a_mod", [128, 2, 6, KD], F32, "ExternalInput")
    io["g"] = P.dram("a_g", [128, KD], F32, "ExternalInput")
    io["w"] = P.dram("a_w", [KD, 128, IN_COLS], F32, "ExternalInput")
    io["cos"] = P.dram("a_cos", [128, T], F32, "ExternalInput")
    io["sin"] = P.dram("a_sin", [128, T], F32, "ExternalInput")
    io["gain"] = P.dram("a_gain", [128, 2], F32, "ExternalInput")
    io["blk"] = P.dram("a_blk", [128, 128], F32, "ExternalInput")
    io["rot"] = P.dram("a_rot", [128, 128], F32, "ExternalInput")
    io["ones"] = P.dram("a_ones", [128, 128], F32, "ExternalInput")
    io["qk"] = P.dram("a_qk", [12, 128, T], BF16, "ExternalOutput")
    io["v"] = P.dram("a_v", [6, 128, T], BF16, "ExternalOutput")
    io["u"] = P.dram("a_u", [IN_COLS - 2304, T], F32, "ExternalOutput")
    xT = P.sb("a_xTs", [128, KD, T]); hT = P.sb("a_hT", [128, KD, T], BF16)
    P.dma(xT[:], io["xT"].ap())
    cos = P.sb("a_cos_s", [128, T]); sin = P.sb("a_sin_s", [128, T])
    P.dma(cos[:], io["cos"].ap()); P.dma(sin[:], io["sin"].ap())
    gain = P.sb("a_gain_s", [128, 2]); P.dma(gain[:], io["gain"].ap())
    blk = P.sb("a_blk_s", [128, 128]); P.dma(blk[:], io["blk"].ap())
    rot = P.sb("a_rot_s", [128, 128]); P.dma(rot[:], io["rot"].ap())
    ones = P.sb("a_ones_s", [128, 128]); P.dma(ones[:], io["ones"].ap())
    mods = load_mods(P, io["mod"], io["g"], KD, 0)
    P.ring("a_tmp", 6, [128, 512])
    segs = [(0, TCc, 1)] + [(TCc + t0, min(512, TLc - t0), 0) for t0 in range(0, TLc, 512)]
    norm_mod(P, PS[7], ones, xT, hT, KD, segs, [(m[0], m[1]) for m in mods], "a_tmp")
    P.ring("a_w", 3, [128, KD, 128], BF16)
    P.ring("a_o", 4, [128, 512]); P.ring("a_ob", 4, [128, 512], BF16)
    nchunk = (IN_COLS + 127) // 128
    pi = 0
    for m in range(nchunk):
        c0 = m * 128; mc = min(128, IN_COLS - c0)
        wt = P.nxt("a_w")
        P.dma(wt[:, :, 0:mc], io["w"].ap()[:, :, c0:c0 + mc].rearrange("k p c -> p k c"), q="pool")
        for (t0, n, si) in segs:
            ts_ = slice(t0, t0 + n)
            ps = PS[pi % 4]; pi += 1
            for k in range(KD):
                P.mm(ps[0:mc, 0:n], wt[:, k, 0:mc], hT[:, k, ts_], start=(k == 0), stop=(k == KD - 1))
            if m < 12:
                x = P.nxt("a_o")[:, 0:n]
                P.copy(x, ps[:, 0:n], eng="act")
                sq = P.nxt("a_o")[:, 0:n]
                P.tt(sq, x, x, ALU.mult)
                P.mm(PS[4][:, 0:n], blk[:], sq)
                rs = P.nxt("a_o")[:, 0:n]
                P.act(rs, PS[4][:, 0:n], AF.Sqrt, bias=QK_EPS, scale=1.0 / 64)
                P.recip(rs, rs)
                P.stt(x, x, gain[:, (0 if m < 6 else 1):(1 if m < 6 else 2)], rs, ALU.mult, ALU.mult)
                P.mm(PS[5][:, 0:n], rot[:], x)
                r2 = P.nxt("a_o")[:, 0:n]
                P.tt(r2, PS[5][:, 0:n], sin[:, ts_], ALU.mult)
                P.tt(x, x, cos[:, ts_], ALU.mult)
                ob = P.nxt("a_ob")[:, 0:n]
                P.tt(ob, x, r2, ALU.add)
                P.dma(io["qk"].ap()[m][:, ts_], ob, is_out=True)
            elif m < 18:
                ob = P.nxt("a_ob")[:, 0:n]
                P.copy(ob, ps[:, 0:n], eng="act")
                P.dma(io["v"].ap()[m - 12][:, ts_], ob, is_out=True)
            else:
                o = P.nxt("a_o")
                P.copy(o[0:mc, 0:n], ps[0:mc, 0:n], eng="act")
                P.dma(io["u"].ap()[c0 - 2304:c0 - 2304 + mc, ts_], o[0:mc, 0:n], is_out=True)
    return io


def attn_consts():
    blk = np.zeros((128, 128), np.float32); blk[:64, :64] = 1; blk[64:, 64:] = 1
    sgn = np.zeros((128, 128), np.float32); sgn[:64, :] = 1.0 / 64; sgn[64:, :] = -1.0 / 64
    E = np.zeros((640, 640), np.float32); E[:320, :320] = 1; E[320:, 320:] = 1
    return blk, sgn, np.ascontiguousarray(E.reshape(5, 128, 640))


def mixB1_build(P, PS, KD, TCc, TLc, TCX, TLAT):
    T = TCc + TLc; TK = TCX + TLAT; KT = TK // 128; D = KD * 128
    io = {}
    dr = lambda nm, shp, dt=F32, kind="ExternalInput": P.dram("b_" + nm, shp, dt, kind)
    io["xT"] = dr("xT", [128, KD, T]); io["q"] = dr("q", [6, 128, T], BF16); io["kT"] = dr("kT", [6, 128, TK], BF16)
    io["v"] = dr("v", [128, KT, 768], BF16)
    io["rT"] = dr("rT", [128, 5, T]); io["yT"] = dr("yT", [128, 5, T]); io["zT"] = dr("zT", [128, 5, T])
    io["mod"] = dr("mod", [128, 2, 6, KD]); io["g"] = dr("g", [128, KD])
    io["w"] = dr("w", [16, 128, D])
    io["subln"] = dr("subln", [128, 1]); io["sng"] = dr("sng", [128, 5]); io["lamq"] = dr("lamq", [128, 1]); io["lamk"] = dr("lamk", [128, 1])
    io["lami"] = dr("lami", [128, 1]); io["gains"] = dr("gains", [128, 2, 64])
    io["blk"] = dr("blk", [128, 128]); io["sgn"] = dr("sgn", [128, 128]); io["E"] = dr("E", [5, 128, 640]); io["ones"] = dr("ones", [128, 128])
    io["o"] = dr("o", [128, KD, T], F32, "ExternalOutput")

    def cst(nm, shape, src=None, dt=F32, q="sp"):
        t = P.sb("bk_" + nm, shape, dt)
        P.dma(t[:], io[nm].ap() if src is None else src, q=q)
        return t
    subln = cst("subln", [128, 1]); sng = cst("sng", [128, 5]); lamq = cst("lamq", [128, 1]); lamk = cst("lamk", [128, 1])
    lami = cst("lami", [128, 1]); gains = cst("gains", [128, 2, 64]); blk = cst("blk", [128, 128]); sgn = cst("sgn", [128, 128])
    ones = cst("ones", [128, 128])
    onesb = P.sb("bk_onesb", [128, 128], BF16)
    P.copy(onesb[:], ones[:])
    Eg = [cst(f"E{k}", [128, 640], io["E"].ap()[k]) for k in range(5)]
    mods = load_mods(P, io["mod"], io["g"], KD, 0)
    qs = P.sb("bk_q", [128, 6, T], BF16)
    P.dma(qs[:], io["q"].ap().rearrange("h p t -> p h t"))
    sm = P.sb("bk_sm", [128, 8])
    P.tt(sm[:, 0:1], lamq[:], lamk[:], ALU.mult)
    P.mm(PS[0][:, 0:1], blk[:], sm[:, 0:1])
    P.act(sm[:, 1:2], PS[0][:, 0:1], AF.Exp)
    P.mm(PS[1][:, 0:1], sgn[:], sm[:, 1:2])
    P.tt(sm[:, 2:3], PS[1][:, 0:1], lami[:], ALU.add)
    P.ts(sm[:, 3:4], sm[:, 2:3], -1.0, ALU.mult)
    P.ts(sm[:, 4:5], lami[:], -1.0, ALU.mult, 1.0, ALU.add)
    P.tt(sm[:, 4:5], sm[:, 4:5], subln[:], ALU.mult)
    ga = P.sb("bk_ga", [128, 2, 64])
    P.act(ga[:], gains[:], AF.Abs)
    P.reduce(sm[:, 5:7], ga[:], ALU.max)
    P.tt(sm[:, 7:8], sm[:, 5:6], sm[:, 6:7], ALU.mult)
    P.ts(sm[:, 7:8], sm[:, 7:8], -8.0, ALU.mult)
    neg_lam, sgl, ebias = sm[:, 3:4], sm[:, 4:5], sm[:, 7:8]

    mixT = P.sb("bk_mixT", [128, 16, T], BF16)
    P.ring("b_kT", 2, [128, TK], BF16); P.ring("b_vh", 2, [128, KT, 128], BF16)
    P.ring("b_E", 6, [128, 512], BF16); P.ring("b_t", 8, [128, 512])
    qtiles = [(0, TCc, TCX)] + [(TCc + t0, min(512, TLc - t0), TK) for t0 in range(0, TLc, 512)]
    si_ = 0
    for h in range(6):
        kh = P.nxt("b_kT"); vh = P.nxt("b_vh")
        P.dma(kh[:], io["kT"].ap()[h])
        P.dma(vh[:], io["v"].ap()[:, :, h * 128:(h + 1) * 128])
        for (t0, n, nkeys) in qtiles:
            ts_ = slice(t0, t0 + n)
            nkt = nkeys // 128
            Eq = []
            for it in range(nkt + 1):
                if it < nkt:
                    ks = slice(it * 128, (it + 1) * 128)
                    Es = []
                    for c in range(2):
                        hs = slice(64 * c, 64 * c + 64)
                        Sp = PS[si_ % 4]; si_ += 1
                        P.mm(Sp[:, 0:n], kh[hs, ks], qs[hs, h, ts_])
                        E = P.nxt("b_E")[:, 0:n]
                        P.act(E, Sp[:, 0:n], AF.Exp, bias=ebias, scale=0.125)
                        Es.append(E)
                    Eq.append(Es)
                if it >= 1:
                    kt = it - 1
                    for c in range(2):
                        E = Eq[kt][c]
                        P.mm(PS[4 + c][:, 0:n], vh[:, kt, :], E, start=(kt == 0), stop=(kt == nkt - 1))
                        P.mm(PS[6 + c][:, 0:n], onesb[:], E, start=(kt == 0), stop=(kt == nkt - 1))
            rl0 = P.nxt("b_t")[:, 0:n]; rl1 = P.nxt("b_t")[:, 0:n]
            P.recip(rl0, PS[6][:, 0:n]); P.recip(rl1, PS[7][:, 0:n])
            o0 = P.nxt("b_t")[:, 0:n]; o1 = P.nxt("b_t")[:, 0:n]
            P.tt(o0, PS[4][:, 0:n], rl0, ALU.mult)
            P.tt(o1, PS[5][:, 0:n], rl1, ALU.mult)
            P.stt(o0, o1, neg_lam, o0, ALU.mult, ALU.add)
            sq = P.nxt("b_t")[:, 0:n]
            P.tt(sq, o0, o0, ALU.mult)
            P.mm(PS[0][:, 0:n], ones[:], sq)
            rs = P.nxt("b_t")[:, 0:n]
            P.act(rs, PS[0][:, 0:n], AF.Sqrt, bias=EPS, scale=1.0 / 128)
            P.recip(rs, rs)
            P.stt(mixT[:, h, ts_], o0, sgl, rs, ALU.mult, ALU.mult)
    P.ring("b_in5", 3, [128, 5, 512])
    toks = [(0, TCc, 1)] + [(TCc + t0, min(512, TLc - t0), 0) for t0 in range(0, TLc, 512)]
    for (t0, n, si) in toks:
        ts_ = slice(t0, t0 + n)
        rt = P.nxt("b_in5"); yt = P.nxt("b_in5"); zt = P.nxt("b_in5")
        P.dma(rt[:, :, 0:n], io["rT"].ap()[:, :, ts_]); P.dma(yt[:, :, 0:n], io["yT"].ap()[:, :, ts_]); P.dma(zt[:, :, 0:n], io["zT"].ap()[:, :, ts_])
        P.copy(mixT[:, 6:11, ts_], rt[:, :, 0:n])
        P.act(zt[:, :, 0:n], zt[:, :, 0:n], AF.Silu)
        P.tt(yt[:, :, 0:n], yt[:, :, 0:n], zt[:, :, 0:n], ALU.mult)
        P.tt(zt[:, :, 0:n], yt[:, :, 0:n], yt[:, :, 0:n], ALU.mult)
        for m in range(5):
            for k in range(5):
                P.mm(PS[m % 4][:, 0:n], Eg[k][:, m * 128:(m + 1) * 128], zt[:, k, 0:n], start=(k == 0), stop=(k == 4))
            rs = P.nxt("b_t")[:, 0:n]
            P.act(rs, PS[m % 4][:, 0:n], AF.Sqrt, bias=EPS, scale=1.0 / 320)
            P.recip(rs, rs)
            P.stt(mixT[:, 11 + m, ts_], yt[:, m, 0:n], sng[:, m:m + 1], rs, ALU.mult, ALU.mult)
    P.ring("b_w", 3, [128, 16, 128], BF16); P.ring("b_x", 4, [128, 512])
    pi = 0
    for m in range(KD):
        wt = P.nxt("b_w")
        P.dma(wt[:], io["w"].ap()[:, :, m * 128:(m + 1) * 128].rearrange("k p c -> p k c"), q="pool")
        for (t0, n, si) in toks:
            ts_ = slice(t0, t0 + n)
            ps = PS[pi % 4]; pi += 1
            for k in range(16):
                P.mm(ps[:, 0:n], wt[:, k, :], mixT[:, k, ts_], start=(k == 0), stop=(k == 15))
            xt = P.nxt("b_x")[:, 0:n]
            P.dma(xt, io["xT"].ap()[:, m, ts_])
            P.stt(xt, ps[:, 0:n], mods[si][2][:, m:m + 1], xt, ALU.mult, ALU.add)
            P.dma(io["o"].ap()[:, m, ts_], xt, is_out=True)
    return io


def ffnB2_build(P, PS, KD, FC, TCc, TLc):
    T = TCc + TLc; D = KD * 128
    io = {}
    dr = lambda nm, shp, dt=F32, kind="ExternalInput": P.dram("f_" + nm, shp, dt, kind)
    io["xT"] = dr("xT", [128, KD, T]); io["mod"] = dr("mod", [128, 2, 6, KD]); io["g"] = dr("g", [128, KD])
    io["wi"] = dr("wi", [KD, 128, 2 * FC * 128]); io["wo"] = dr("wo", [FC, 128, D]); io["ones"] = dr("ones", [128, 128])
    io["o"] = dr("o", [128, KD, T], F32, "ExternalOutput")
    hT = P.sb("f_hT", [128, KD, T], BF16)
    ones = P.sb("f_ones_s", [128, 128]); P.dma(ones[:], io["ones"].ap())
    mods = load_mods(P, io["mod"], io["g"], KD, 1)
    P.ring("f_tmp", 6, [128, 512]); P.ring("f_xt", 1, [128, KD, 256])
    toks = [(0, TCc, 1)] + [(TCc + t0, min(512, TLc - t0), 0) for t0 in range(0, TLc, 512)]
    nsegs = [(0, TCc, 1)] + [(TCc + t0, min(256, TLc - t0), 0) for t0 in range(0, TLc, 256)]
    ab = [(m[0], m[1]) for m in mods]
    for i, (t0, n, si) in enumerate(nsegs):
        xt = P.nxt("f_xt")
        P.dma(xt[:, :, 0:n], io["xT"].ap()[:, :, t0:t0 + n])
        norm_mod(P, PS[7], ones, xt, hT, KD, [(t0, n, si)], ab, "f_tmp", x_local=True, mkring=(i == 0))
    actT = P.sb("f_actT", [128, FC, T], BF16)
    P.ring("f_wi", 2, [128, KD, 256], BF16); P.ring("f_wo", 2, [128, FC, 128], BF16); P.ring("f_o", 3, [128, 512])
    pi = 0
    for m in range(FC):
        wt = P.nxt("f_wi")
        P.dma(wt[:, :, 0:128], io["wi"].ap()[:, :, m * 128:(m + 1) * 128].rearrange("k p c -> p k c"), q="pool")
        P.dma(wt[:, :, 128:256], io["wi"].ap()[:, :, (FC + m) * 128:(FC + m + 1) * 128].rearrange("k p c -> p k c"), q="pool")
        for (t0, n, si) in toks:
            ts_ = slice(t0, t0 + n)
            pg = PS[pi % 6]; pu = PS[(pi + 1) % 6]; pi += 2
            for k in range(KD):
                P.mm(pg[:, 0:n], wt[:, k, 0:128], hT[:, k, ts_], start=(k == 0), stop=(k == KD - 1))
            for k in range(KD):
                P.mm(pu[:, 0:n], wt[:, k, 128:256], hT[:, k, ts_], start=(k == 0), stop=(k == KD - 1))
            gs = P.nxt("f_tmp")[:, 0:n]
            P.act(gs, pg[:, 0:n], AF.Silu)
            P.tt(actT[:, m, ts_], gs, pu[:, 0:n], ALU.mult)
    for mo in range(KD):
        wt = P.nxt("f_wo")
        P.dma(wt[:], io["wo"].ap()[:, :, mo * 128:(mo + 1) * 128].rearrange("m p c -> p m c"), q="pool")
        for (t0, n, si) in toks:
            ts_ = slice(t0, t0 + n)
            ps = PS[pi % 6]; pi += 1
            for m in range(FC):
                P.mm(ps[:, 0:n], wt[:, m, :], actT[:, m, ts_], start=(m == 0), stop=(m == FC - 1))
            ot = P.nxt("f_o")[:, 0:n]
            P.dma(ot, io["xT"].ap()[:, mo, ts_])
            P.stt(ot, ps[:, 0:n], mods[si][2][:, mo:mo + 1], ot, ALU.mult, ALU.add)
            P.dma(io["o"].ap()[:, mo, ts_], ot, is_out=True)
    return io


NCORES = 8
_PROGS = {}
_DBG = None


def _fm(x2d, KD):
    T = x2d.shape[0]
    return np.ascontiguousarray(x2d.T.reshape(KD, 128, T).transpose(1, 0, 2))


def _unfm(a):
    p, KD, T = a.shape
    return np.ascontiguousarray(a.transpose(1, 0, 2).reshape(KD * 128, T).T)


def _run(nc, in_maps):
    res = run_bass_kernel_spmd(nc, in_maps, core_ids=list(range(NCORES)))
    return res.results


def _prog(key, builder):
    if key not in _PROGS:
        P = Prog()
        PS = [P.ps(f"ps{i}", [128, 512]) for i in range(8)]
        builder(P, PS)
        _PROGS[key] = P.finish()
    return _PROGS[key]


def _slots(j):
    rh = (2 * j, 2 * j + 1) if j < 5 else (0, 1)
    sh = (2 * (j - 3), 2 * (j - 3) + 1) if j >= 3 else (0, 1)
    return rh, sh


def kernel(x, c, ctx, c_ctx, ada_w, ada_b, norm1_g, norm2_g, w_in, w_out, qk_gain, lam_q, lam_k, subln_g,
           shift_mu, w0, w_up, a0, a_up, g_up, k_k, k_a, r_k, lnx_g, lnx_b, conv_w, conv_b, dt_bias, a_log,
           d_skip, ssm_norm_g, w_ffn_in, w_ffn_out):
    f = lambda a: np.ascontiguousarray(np.asarray(a, dtype=np.float32))
    x, c, ctx, c_ctx = f(x), f(c), f(ctx), f(c_ctx)
    L = ada_w.shape[0]; SEQ = x.shape[1]; CTX = ctx.shape[1]; D = x.shape[2]; KD = D // 128
    DFF = w_ffn_out.shape[1]; FC = DFF // 128
    NC = NCORES; TCc = CTX // NC; TLc = SEQ // NC; T = TCc + TLc
    ones = np.ones((128, 128), np.float32)
    NCOL = 6 * D // NC
    ncM = _prog(("M", L, KD, NCOL), lambda P, PS: modvec_build(P, PS, L, KD, NCOL))
    cT = _fm(np.stack([c[0], c_ctx]), KD)
    ada_w = np.asarray(ada_w, np.float32); ada_b = np.asarray(ada_b, np.float32)
    ims = []
    for j in range(NC):
        cs = slice(j * NCOL, (j + 1) * NCOL)
        ims.append({"m_cT": cT, "m_w": np.ascontiguousarray(ada_w[:, :, cs].reshape(L, KD, 128, NCOL)),
                    "m_b": np.ascontiguousarray(np.repeat(ada_b[:, None, cs], 2, axis=1))})
    rs = _run(ncM, ims)
    mod = np.concatenate([r["m_o"] for r in rs], axis=-1)
    modT = [np.ascontiguousarray(mod[l].reshape(2, 6, KD, 128).transpose(3, 0, 1, 2)) for l in range(L)]

    xl = x[0].copy(); xc = ctx[0].copy()
    blk, rot = rope_consts()
    ablk, sgn, Eg = attn_consts()
    pos = np.arange(SEQ)
    ncA = _prog(("A", KD, TCc, TLc), lambda P, PS: projA_build(P, PS, KD, TCc, TLc))
    ncSR = _prog(("SR", CTX, SEQ), lambda P, PS: rwkv_build2(P, PS, CTX, SEQ))
    ncSS = _prog(("SS", CTX, SEQ), lambda P, PS: ssd_build2(P, PS, CTX, SEQ))
    ncB1 = _prog(("B1", KD, TCc, TLc, CTX, SEQ), lambda P, PS: mixB1_build(P, PS, KD, TCc, TLc, CTX, SEQ))
    ncB2 = _prog(("B2", KD, FC, TCc, TLc), lambda P, PS: ffnB2_build(P, PS, KD, FC, TCc, TLc))
    tabs = []
    for j in range(NC):
        pj = np.concatenate([np.zeros(TCc, np.int64), pos[j * TLc:(j + 1) * TLc]])
        isc = np.concatenate([np.ones(TCc, bool), np.zeros(TLc, bool)])
        tabs.append(rope_tables(pj, isc))
    for l in range(L):
        g = lambda a: np.asarray(a[l], np.float32)
        xTs = [_fm(np.concatenate([xc[j * TCc:(j + 1) * TCc], xl[j * TLc:(j + 1) * TLc]]), KD) for j in range(NC)]
        gain = np.ascontiguousarray(np.tile(g(qk_gain), (1, 2)).T)
        wA = np.ascontiguousarray(g(w_in).reshape(KD, 128, IN_COLS))
        n1 = _fm(g(norm1_g)[None], KD)[:, :, 0]
        ims = [{"a_xT": xTs[j], "a_mod": modT[l], "a_g": n1, "a_w": wA, "a_cos": tabs[j][0], "a_sin": tabs[j][1],
                "a_gain": gain, "a_blk": blk, "a_rot": rot, "a_ones": ones} for j in range(NC)]
        ra = _run(ncA, ims)
        kT_all = np.ascontiguousarray(np.concatenate([r["a_qk"][6:12, :, :TCc] for r in ra] + [r["a_qk"][6:12, :, TCc:] for r in ra], axis=2))
        vT_all = np.concatenate([r["a_v"][:, :, :TCc] for r in ra] + [r["a_v"][:, :, TCc:] for r in ra], axis=2)
        TK = CTX + SEQ
        v_all = np.ascontiguousarray(vT_all.reshape(768, TK).T.reshape(TK // 128, 128, 768).transpose(1, 0, 2))
        u_c = np.concatenate([r["a_u"][:, :TCc] for r in ra], axis=1).T
        u_l = np.concatenate([r["a_u"][:, TCc:] for r in ra], axis=1).T
        p = {"shift_mu": g(shift_mu), "w0": g(w0), "w_up": g(w_up), "a0": g(a0), "a_up": g(a_up), "g_up": g(g_up),
             "k_k": g(k_k), "k_a": g(k_a), "r_k": g(r_k), "lnx_g": g(lnx_g), "lnx_b": g(lnx_b), "conv_w": g(conv_w),
             "conv_b": g(conv_b), "dt_bias": g(dt_bias), "a_log": g(a_log), "d_skip": g(d_skip)}
        rsR = _run(ncSR, [rwkv_host_inputs(u_c[:, :2304], u_l[:, :2304], p, _slots(j)[0]) for j in range(NC)])
        rsS = _run(ncSS, [ssd_host_inputs(u_c[:, 2304:], u_l[:, 2304:], p, _slots(j)[1]) for j in range(NC)])
        if _DBG is not None:
            _DBG[f'mod{l}'] = mod[l]; _DBG[f'u_c{l}'] = u_c; _DBG[f'u_l{l}'] = u_l; _DBG[f'kT{l}'] = kT_all; _DBG[f'v{l}'] = v_all; _DBG[f'q{l}'] = [r['a_qk'][0:6] for r in ra]
        r_c = np.zeros((CTX, 640), np.float32); r_l = np.zeros((SEQ, 640), np.float32)
        y_c = np.zeros((CTX, 640), np.float32); y_l = np.zeros((SEQ, 640), np.float32)
        for h in range(10):
            j, s = h // 2, h % 2
            r_c[:, h * 64:(h + 1) * 64] = rsR[j]["rw_oc"][64 * s:64 * s + 64].T
            r_l[:, h * 64:(h + 1) * 64] = rsR[j]["rw_ol"][64 * s:64 * s + 64].T
            j = 3 + h // 2
            y_c[:, h * 64:(h + 1) * 64] = rsS[j]["sd_yc"][s].transpose(1, 0, 2).reshape(CTX, 64)
            y_l[:, h * 64:(h + 1) * 64] = rsS[j]["sd_yl"][s].transpose(1, 0, 2).reshape(SEQ, 64)
        lam_init = 0.8 - 0.6 * math.exp(-0.3 * l)
        own = lambda ac, al, j: np.concatenate([ac[j * TCc:(j + 1) * TCc], al[j * TLc:(j + 1) * TLc]])
        wO = np.ascontiguousarray(g(w_out).reshape(16, 128, D))
        common = {"b_kT": kT_all, "b_v": v_all, "b_mod": modT[l], "b_g": n1, "b_w": wO,
                  "b_subln": np.ascontiguousarray(g(subln_g)[:, None]), "b_sng": _fm(g(ssm_norm_g)[None], 5)[:, :, 0],
                  "b_lamq": np.ascontiguousarray(g(lam_q).reshape(128, 1)), "b_lamk": np.ascontiguousarray(g(lam_k).reshape(128, 1)),
                  "b_lami": np.full((128, 1), lam_init, np.float32),
                  "b_gains": np.ascontiguousarray(np.broadcast_to(g(qk_gain)[None], (128, 2, 64))),
                  "b_blk": ablk, "b_sgn": sgn, "b_E": Eg, "b_ones": ones}
        ims = []
        for j in range(NC):
            d = dict(common)
            d["b_xT"] = xTs[j]; d["b_q"] = np.ascontiguousarray(ra[j]["a_qk"][0:6])
            d["b_rT"] = _fm(own(r_c, r_l, j), 5); d["b_yT"] = _fm(own(y_c, y_l, j), 5)
            d["b_zT"] = _fm(own(u_c[:, 2304:2304 + 640], u_l[:, 2304:2304 + 640], j), 5)
            ims.append(d)
        rb1 = _run(ncB1, ims)
        if _DBG is not None:
            _DBG[f'r_l{l}'] = r_l; _DBG[f'y_l{l}'] = y_l; _DBG[f'r_c{l}'] = r_c; _DBG[f'y_c{l}'] = y_c; _DBG[f'x1_{l}'] = [_unfm(r['b_o']) for r in rb1]
        n2 = _fm(g(norm2_g)[None], KD)[:, :, 0]
        wi = np.ascontiguousarray(g(w_ffn_in).reshape(KD, 128, 2 * DFF)); wo = np.ascontiguousarray(g(w_ffn_out).reshape(FC, 128, D))
        ims = [{"f_xT": rb1[j]["b_o"], "f_mod": modT[l], "f_g": n2, "f_wi": wi, "f_wo": wo, "f_ones": ones} for j in range(NC)]
        rb2 = _run(ncB2, ims)
        for j in range(NC):
            xo = _unfm(rb2[j]["f_o"])
            xc[j * TCc:(j + 1) * TCc] = xo[:TCc]
            xl[j * TLc:(j + 1) * TLc] = xo[TCc:]
        if _DBG is not None:
            _DBG[f'xl{l}'] = xl.copy(); _DBG[f'xc{l}'] = xc.copy()
            if _DBG.get('stop_after') == l:
                return xl[None].astype(np.float32)
    return xl[None].astype(np.float32)


_NO_ILV = False


def _interleave(gens):
    gens = [g for g in gens if g is not None]
    while gens:
        for g in list(gens):
            try:
                next(g)
            except StopIteration:
                gens.remove(g)


def rwkv_build2(P, PS, Tc, Tl):
    C = RW_C; TB = 256; NCB = TB // C
    io = {}
    io["uc"] = P.dram("rw_uc", [128, 6, Tc + 2], F32, "ExternalInput")
    io["ul"] = P.dram("rw_ul", [128, 6, Tl + 2], F32, "ExternalInput")
    io["pr"] = P.dram("rw_pr", [128, PR_N], F32, "ExternalInput")
    io["wup"] = P.dram("rw_wup", [128, 128], F32, "ExternalInput")
    io["aup"] = P.dram("rw_aup", [128, 128], F32, "ExternalInput")
    io["gup"] = P.dram("rw_gup", [128, 128], F32, "ExternalInput")
    io["blk64"] = P.dram("c_blk64", [128, 128], F32, "ExternalInput")
    io["ident"] = P.dram("c_ident", [128, 128], F32, "ExternalInput")
    io["i2"] = P.dram("c_i2", [128, 64], F32, "ExternalInput")
    io["mask5"] = P.dram("c_mask5", [2, 128, 320], F32, "ExternalInput")
    io["oc"] = P.dram("rw_oc", [128, Tc], F32, "ExternalOutput")
    io["ol"] = P.dram("rw_ol", [128, Tl], F32, "ExternalOutput")
    yf = {"c": P.dram("rw_yfc", [128, Tc], F32), "l": P.dram("rw_yfl", [128, Tl], F32)}
    usrc = {"c": io["uc"], "l": io["ul"]}
    odst = {"c": io["oc"], "l": io["ol"]}

    def cst(name, shape):
        t = P.sb("k_" + name, shape)
        P.dma(t[:], io[name].ap())
        return t
    pr = cst("pr", [128, PR_N]); wup = cst("wup", [128, 128]); aup = cst("aup", [128, 128]); gup = cst("gup", [128, 128])
    blk = cst("blk64", [128, 128]); ident = cst("ident", [128, 128]); i2 = cst("i2", [128, 64])
    i24 = P.sb("k_i24", [128, NCB, 64])
    for ci in range(NCB):
        P.copy(i24[:, ci, :], i2[:])
    mask5 = []
    for d in range(2):
        t = P.sb(f"k_mask5_{d}", [128, 320])
        P.dma(t[:], io["mask5"].ap()[d])
        mask5.append(t)
    rst = P.sb("k_rst", [128, TB])
    P.memset(rst[:], 1.0)
    P.memset(rst[:].rearrange("p (c t) -> p c t", t=C)[:, :, 0:1], 0.0)
    pc = lambda i: pr[:, i:i + 1]
    P.ts(pr[:, PR_M2:PR_M2 + 6], pr[:, PR_MU0:PR_MU0 + 6], -1.0, ALU.mult, 1.0, ALU.add)
    P.tt(pr[:, PR_M2:PR_M2 + 6], pr[:, PR_M2:PR_M2 + 6], pr[:, PR_MU1:PR_MU1 + 6], ALU.subtract)
    P.ts(pr[:, PR_1MKA:PR_1MKA + 1], pr[:, PR_KA:PR_KA + 1], -1.0, ALU.mult, 1.0, ALU.add)

    P.ring("rw_u6", 2, [128, 6, TB + 2]); P.ring("rw_xs", 2, [128, 6, TB])
    for nm in ("tw", "sg", "kx", "rn", "kk", "g", "lw0", "lw1", "ic0", "ic1", "k0", "k1", "b0", "b1",
               "cw", "cwb", "ep", "en", "ea", "At", "Bt", "Kt", "Rt", "yT", "t2", "t3", "t4", "yfl"):
        P.ring("rw_" + nm, 2, [128, TB])
    for nm in ("sq", "f1", "t1"):
        P.ring("rw_" + nm, 3, [128, TB])
    P.ring("rw_tok", 2, [128, NCB, 3, 64]); P.ring("rw_M5", 2, [128, NCB, 5, 64]); P.ring("rw_TT", 2, [128, NCB, 64])
    P.ring("rw_PP", 3, [128, NCB, 128]); P.ring("rw_Xs", 3, [128, 64]); P.ring("rw_Us", 3, [128, 64])
    S = [P.sb("rw_S0", [128, 64]), P.sb("rw_S1", [128, 64])]
    Stmp = P.sb("rw_Stmp", [128, 64])
    HS = [slice(0, 64), slice(64, 128)]

    def prep_gen(seq, t0, n, o):
        u6 = P.nxt("rw_u6")
        P.dma(u6[:, :, 0:n + 2], usrc[seq].ap()[:, :, t0:t0 + n + 2])
        xs = P.nxt("rw_xs")
        for j in range(6):
            P.ts(xs[:, j, 0:n], u6[:, j, 1:n + 1], pc(PR_M2 + j), ALU.mult)
            P.stt(xs[:, j, 0:n], u6[:, j, 0:n], pc(PR_MU0 + j), xs[:, j, 0:n], ALU.mult, ALU.add)
            P.stt(xs[:, j, 0:n], u6[:, j, 2:n + 2], pc(PR_MU1 + j), xs[:, j, 0:n], ALU.mult, ALU.add)
        yield
        r, k, v, wd, ad, gd = [xs[:, j, 0:n] for j in range(6)]
        o["r"] = r; o["v"] = v
        tw = P.nxt("rw_tw")[:, 0:n]
        P.act(tw, wd, AF.Tanh)
        sg = P.nxt("rw_sg")[:, 0:n]
        P.act(sg, gd, AF.Sigmoid)
        for d in range(2):
            hs = HS[d]
            P.mm(PS[2 + d][:, 0:n], wup[hs, :], tw[hs, :])
            P.mm(PS[4 + d][:, 0:n], aup[hs, :], ad[hs, :])
        P.mm(PS[2][:, 256:256 + n], gup[:], sg)
        for d in range(2):
            lw = P.nxt(f"rw_lw{d}")[:, 0:n]
            P.act(lw, PS[2 + d][:, 0:n], AF.Sigmoid, bias=pc(PR_W0 + d))
            ic = P.nxt(f"rw_ic{d}")[:, 0:n]
            P.act(ic, PS[4 + d][:, 0:n], AF.Sigmoid, bias=pc(PR_A0 + d))
            P.ts(lw, lw, -E05, ALU.mult)
            o[f"lw{d}"] = lw; o[f"ic{d}"] = ic
        g = P.nxt("rw_g")[:, 0:n]
        P.copy(g, PS[2][:, 256:256 + n], eng="act")
        o["g"] = g
        yield
        kx = P.nxt("rw_kx")[:, 0:n]
        P.ts(kx, k, pc(PR_KK), ALU.mult)
        sq = P.nxt("rw_sq")[:, 0:n]
        P.tt(sq, kx, kx, ALU.mult)
        P.mm(PS[3][:, 256:256 + n], blk[:], sq)
        rn = P.nxt("rw_rn")[:, 0:n]
        P.act(rn, PS[3][:, 256:256 + n], AF.Sqrt, bias=1e-12)
        P.recip(rn, rn)
        kk = P.nxt("rw_kk")[:, 0:n]
        P.tt(kk, kx, rn, ALU.mult)
        o["kk"] = kk
        for d in range(2):
            f1 = P.nxt("rw_f1")[:, 0:n]
            P.ts(f1, o[f"ic{d}"], pc(PR_KA), ALU.mult, pc(PR_1MKA), ALU.add)
            kd = P.nxt(f"rw_k{d}")[:, 0:n]
            P.tt(kd, k, f1, ALU.mult)
            bd = P.nxt(f"rw_b{d}")[:, 0:n]
            P.tt(bd, kk, o[f"ic{d}"], ALU.mult)
            o[f"k{d}"] = kd; o[f"b{d}"] = bd
        yield

    def pre_gen(d, o, n, X):
        nch = n // C
        lw = o[f"lw{d}"]
        cw = P.nxt("rw_cw")[:, 0:n]
        P.scan(cw, rst[:, 0:n], lw, 0.0, ALU.mult, ALU.add)
        if d == 1:
            cwb = P.nxt("rw_cwb")[:, 0:n]
            P.tt(cwb, lw, cw, ALU.subtract)
            for c in range(nch):
                P.ts(cwb[:, c * C:(c + 1) * C], cwb[:, c * C:(c + 1) * C], cw[:, c * C + C - 1:c * C + C], ALU.add)
            cw = cwb
        ep = P.nxt("rw_ep")[:, 0:n]; en = P.nxt("rw_en")[:, 0:n]; ea = P.nxt("rw_ea")[:, 0:n]
        P.act(ep, cw, AF.Exp)
        P.act(en, cw, AF.Exp, scale=-1.0)
        t1 = P.nxt("rw_t1")[:, 0:n]
        P.tt(t1, cw, lw, ALU.subtract)
        P.act(ea, t1, AF.Exp)
        At = P.nxt("rw_At")[:, 0:n]; Bt = P.nxt("rw_Bt")[:, 0:n]; Kt = P.nxt("rw_Kt")[:, 0:n]; Rt = P.nxt("rw_Rt")[:, 0:n]
        P.stt(At, o["kk"], -1.0, ea, ALU.mult, ALU.mult)
        P.tt(Bt, o[f"b{d}"], en, ALU.mult)
        P.tt(Kt, o[f"k{d}"], en, ALU.mult)
        P.tt(Rt, o["r"], ep, ALU.mult)
        v = o["v"]
        X.update(At=At, Rt=Rt, ep=ep)
        yield
        tok = P.nxt("rw_tok")
        for ci in range(nch):
            cs = slice(ci * C, (ci + 1) * C)
            bank = PS[ci // 2]; off = (ci % 2) * 192
            for j, Xm in enumerate((Bt, Kt, v)):
                for s in range(2):
                    P.mm(bank[HS[s], off + j * 64:off + (j + 1) * 64], Xm[HS[s], cs], ident[HS[s], HS[s]])
        for b in range((nch + 1) // 2):
            k2 = min(2, nch - 2 * b)
            P.copy(tok[:, 2 * b:2 * b + k2].rearrange("p a b c -> p (a b c)"), PS[b][:, 0:192 * k2], eng="act")
        yield
        M5 = P.nxt("rw_M5")
        for ci in range(nch):
            cs = slice(ci * C, (ci + 1) * C)
            for s in range(2):
                for j, (l, rr) in enumerate(((At, Bt), (Bt, At), (Kt, At), (Bt, Rt), (Kt, Rt))):
                    P.mm(PS[2 + ci][HS[s], j * 64:(j + 1) * 64], l[HS[s], cs], rr[HS[s], cs])
            P.tt(M5[:, ci].rearrange("p a b -> p (a b)"), PS[2 + ci][:, 0:320], mask5[d][:], ALU.mult)
        yield
        TT = P.nxt("rw_TT")
        for ci in range(nch):
            P.tt(TT[:, ci, :], M5[:, ci, 1, :], i2[:], ALU.add)
        Pm = [M5[:, ci, 0, :] for ci in range(nch)]; PTm = [M5[:, ci, 1, :] for ci in range(nch)]
        def sq_mm(Pm, PTm, both):
            for ci in range(nch):
                for s in range(2):
                    P.mm(PS[6][HS[s], ci * 128:ci * 128 + 64], PTm[ci][HS[s], :], Pm[ci][HS[s], :])
                    if both:
                        P.mm(PS[6][HS[s], ci * 128 + 64:ci * 128 + 128], Pm[ci][HS[s], :], PTm[ci][HS[s], :])

        def sq_evac():
            PP = P.nxt("rw_PP")
            P.copy(PP[:, 0:nch].rearrange("p a b -> p (a b)"), PS[6][:, 0:128 * nch], eng="act")
            return [PP[:, ci, 0:64] for ci in range(nch)], [PP[:, ci, 64:128] for ci in range(nch)]

        def tt_mm(Pm):
            for ci in range(nch):
                for s in range(2):
                    P.mm(PS[ci // 2][HS[s], 384 + (ci % 2) * 64:384 + (ci % 2) * 64 + 64], Pm[ci][HS[s], :], TT[HS[s], ci, :])

        def tt_evac():
            for b in range((nch + 1) // 2):
                k2 = min(2, nch - 2 * b)
                P.tt(TT[:, 2 * b:2 * b + k2].rearrange("p a b -> p (a b)"), TT[:, 2 * b:2 * b + k2].rearrange("p a b -> p (a b)"),
                     PS[b][:, 384:384 + 64 * k2], ALU.add)
        sq_mm(Pm, PTm, True)
        Pm, PTm = sq_evac()
        yield
        for lvl in range(2, 6):
            tt_mm(Pm)
            sq_mm(Pm, PTm, lvl < 5)
            Pn, PTn = sq_evac()
            tt_evac()
            Pm, PTm = Pn, PTn
            yield
        tt_mm(Pm)
        tt_evac()
        yield
        X.update(tok=tok, M5=M5, TT=TT)

    def chain_gen(d, n, X, fin):
        nch = n // C
        Sd = S[d]
        At, Rt, ep, tok, M5, TT = X["At"], X["Rt"], X["ep"], X["tok"], X["M5"], X["TT"]
        yT = P.nxt("rw_yT")[:, 0:n]
        wcol = (C - 1) if d == 0 else 0
        order = range(nch) if d == 0 else range(nch - 1, -1, -1)
        bank = PS[7]
        for c in order:
            cs = slice(c * C, (c + 1) * C)
            Btok, Ktok, Vtok = tok[:, c, 0, :], tok[:, c, 1, :], tok[:, c, 2, :]
            AkT, LrbT, LrkT = M5[:, c, 2, :], M5[:, c, 3, :], M5[:, c, 4, :]
            for s in range(2):
                hs = HS[s]
                P.mm(bank[hs, 0:64], At[hs, cs], Sd[hs, :], start=True, stop=False)
                P.mm(bank[hs, 0:64], AkT[hs, :], Vtok[hs, :], start=False, stop=True)
            Xs = P.nxt("rw_Xs")
            P.copy(Xs[:], bank[:, 0:64], eng="act")
            yield
            for s in range(2):
                hs = HS[s]
                P.mm(bank[hs, 64:128], TT[hs, c, :], Xs[hs, :])
            Us = P.nxt("rw_Us")
            P.copy(Us[:], bank[:, 64:128])
            yield
            for s in range(2):
                hs = HS[s]
                P.mm(bank[hs, 192:256], Sd[hs, :], Rt[hs, cs], start=True, stop=False)
                P.mm(bank[hs, 192:256], Us[hs, :], LrbT[hs, :], start=False, stop=False)
                P.mm(bank[hs, 192:256], Vtok[hs, :], LrkT[hs, :], start=False, stop=True)
                P.mm(bank[hs, 128:192], Btok[hs, :], Us[hs, :], start=True, stop=False)
                P.mm(bank[hs, 128:192], Ktok[hs, :], Vtok[hs, :], start=False, stop=True)
            P.copy(yT[:, cs], bank[:, 192:256], eng="act")
            P.tt(Stmp[:], Sd[:], bank[:, 128:192], ALU.add)
            P.ts(Sd[:], Stmp[:], ep[:, c * C + wcol:c * C + wcol + 1], ALU.mult)
            yield
        fin(yT)

    def blocks(seq, T):
        return [(seq, t0, min(TB, T - t0)) for t0 in range(0, T, TB)]
    fwd = blocks("c", Tc) + blocks("l", Tl)
    bwd = blocks("c", Tc)[::-1] + blocks("l", Tl)[::-1]
    P.memset(S[0][:], 0.0); P.memset(S[1][:], 0.0)

    def fin_fwd(seq, t0, n, o):
        def f(yT):
            P.dma(yf[seq].ap()[:, t0:t0 + n], yT)
        return f

    def fin_bwd(seq, t0, n, o):
        def f(yb):
            yfl = P.nxt("rw_yfl")[:, 0:n]
            P.dma(yfl, yf[seq].ap()[:, t0:t0 + n])
            y = P.nxt("rw_t1")[:, 0:n]
            P.tt(y, yb, yfl, ALU.add)
            P.mm(PS[2][:, 0:n], blk[:], y)
            yc = P.nxt("rw_t2")[:, 0:n]
            P.stt(yc, PS[2][:, 0:n], -1.0 / 64, y, ALU.mult, ALU.add)
            sq = P.nxt("rw_sq")[:, 0:n]
            P.tt(sq, yc, yc, ALU.mult)
            P.mm(PS[3][:, 0:n], blk[:], sq)
            sd = P.nxt("rw_rn")[:, 0:n]
            P.act(sd, PS[3][:, 0:n], AF.Sqrt, bias=LN_X_EPS, scale=1.0 / 64)
            P.recip(sd, sd)
            P.tt(yc, yc, sd, ALU.mult)
            ov = P.nxt("rw_t3")[:, 0:n]
            P.ts(ov, yc, pc(PR_LG), ALU.mult, pc(PR_LB), ALU.add)
            ks = P.nxt("rw_t4")[:, 0:n]
            P.tt(ks, o["k0"], o["k1"], ALU.add)
            P.stt(ks, ks, pc(PR_RK), o["r"], ALU.mult, ALU.mult)
            P.mm(PS[4][:, 0:n], blk[:], ks)
            P.tt(ks, PS[4][:, 0:n], o["v"], ALU.mult)
            P.tt(ov, ov, ks, ALU.add)
            P.tt(ov, ov, o["g"], ALU.mult)
            P.dma(odst[seq].ap()[:, t0:t0 + n], ov, is_out=True)
        return f

    for d, blist, finf in ((0, fwd, fin_fwd), (1, bwd, fin_bwd)):
        prev = None
        for (seq, t0, n) in blist:
            o = {}; X = {}

            def both(seq=seq, t0=t0, n=n, o=o, X=X, d=d):
                yield from prep_gen(seq, t0, n, o)
                yield from pre_gen(d, o, n, X)
            if _NO_ILV:
                _interleave([prev]); _interleave([both()])
            else:
                _interleave([prev, both()])
            prev = chain_gen(d, n, X, finf(seq, t0, n, o))
        _interleave([prev])
    return io


def _ilv_gen(gens):
    gens = [g for g in gens if g is not None]
    while gens:
        for g in list(gens):
            try:
                next(g)
                yield
            except StopIteration:
                gens.remove(g)


def ssd_build2(P, PS, Tc, Tl, TB=512):
    Q = SQ
    io = {}
    T_ = {"c": Tc, "l": Tl}
    for sq in ("c", "l"):
        T = T_[sq]
        io["x" + sq] = P.dram("sd_x" + sq, [2, 64, T + 2], F32, "ExternalInput")
        io["b" + sq] = P.dram("sd_b" + sq, [2, 128, T + 2], F32, "ExternalInput")
        io["c" + sq] = P.dram("sd_c" + sq, [2, 128, T + 2], F32, "ExternalInput")
        io["dt" + sq] = P.dram("sd_dt" + sq, [2, 128, T // Q, 2], F32, "ExternalInput")
        io["y" + sq] = P.dram("sd_y" + sq, [2, 128, T // Q, 64], F32, "ExternalOutput")
    io["px"] = P.dram("sd_px", [2, 64, 4], F32, "ExternalInput")
    io["pb"] = P.dram("sd_pb", [2, 128, 4], F32, "ExternalInput")
    io["pc"] = P.dram("sd_pc", [2, 128, 4], F32, "ExternalInput")
    io["pd"] = P.dram("sd_pd", [2, 128, 6], F32, "ExternalInput")
    io["tri"] = P.dram("c_tri", [2, 128, 128], F32, "ExternalInput")
    io["ident"] = P.dram("c_identb", [128, 128], F32, "ExternalInput")
    io["ones"] = P.dram("c_ones", [128, 128], F32, "ExternalInput")
    yfd = {sq: P.dram("sd_yf" + sq, [2, 128, T_[sq] // Q, 64], F32) for sq in ("c", "l")}

    def cst(nm, src, shape):
        t = P.sb("sk_" + nm, shape)
        P.dma(t[:], src)
        return t
    tri = [cst(f"tri{d}", io["tri"].ap()[d], [128, 128]) for d in range(2)]
    ident = cst("ident", io["ident"].ap(), [128, 128])
    ones = cst("ones", io["ones"].ap(), [128, 128])
    px = [cst(f"px{s}", io["px"].ap()[s], [64, 4]) for s in range(2)]
    pb = [cst(f"pb{s}", io["pb"].ap()[s], [128, 4]) for s in range(2)]
    pcc = [cst(f"pc{s}", io["pc"].ap()[s], [128, 4]) for s in range(2)]
    pd = [cst(f"pd{s}", io["pd"].ap()[s], [128, 6]) for s in range(2)]
    Aneg = [P.sb(f"sk_A{s}", [128, 2]) for s in range(2)]
    dsum = [P.sb(f"sk_ds{s}", [128, 1]) for s in range(2)]
    for s in range(2):
        P.act(Aneg[s][:], pd[s][:, 2:4], AF.Exp)
        P.ts(Aneg[s][:], Aneg[s][:], -1.0, ALU.mult)
        P.tt(dsum[s][:], pd[s][:, 4:5], pd[s][:, 5:6], ALU.add)

    NQ = TB // Q
    P.ring("sd_xin", 2, [64, TB + 2]); P.ring("sd_bin", 2, [128, TB + 2]); P.ring("sd_cin", 2, [128, TB + 2])
    P.ring("sd_xf", 4, [64, TB]); P.ring("sd_bf", 4, [128, TB]); P.ring("sd_cf", 4, [128, TB])
    P.ring("sd_dt", 8, [128, NQ, 2]); P.ring("sd_dta", 4, [128, NQ, 2]); P.ring("sd_tmp8", 8, [128, NQ, 2])
    P.ring("sd_xtok", 16, [128, 64]); P.ring("sd_btok", 16, [128, 128]); P.ring("sd_cbm", 32, [128, 128])
    P.ring("sd_bc", 8, [128, 128]); P.ring("sd_E", 16, [128, 128]); P.ring("sd_G", 16, [128, 128]); P.ring("sd_Cs", 16, [128, 128])
    P.ring("sd_col", 24, [128, 4]); P.ring("sd_xdt", 16, [128, 64]); P.ring("sd_xw", 16, [128, 64]); P.ring("sd_st", 16, [128, 64])
    P.ring("sd_yblk", 4, [128, NQ, 64]); P.ring("sd_yfl", 4, [128, NQ, 64])
    H = [[P.sb(f"sd_h{s}{d}", [128, 64]) for d in range(2)] for s in range(2)]
    psrr = [0]

    def psn():
        b = PS[psrr[0] % 8]; psrr[0] += 1
        return b

    def prep_gen(s, seq, t0, n, o):
        nq = n // Q
        for nm, ring_in, ring_f, src, par in (("x", "sd_xin", "sd_xf", io["x" + seq], px[s]),
                                              ("b", "sd_bin", "sd_bf", io["b" + seq], pb[s]),
                                              ("c", "sd_cin", "sd_cf", io["c" + seq], pcc[s])):
            tin = P.nxt(ring_in)
            P.dma(tin[:, 0:n + 2], src.ap()[s][:, t0:t0 + n + 2])
            tf = P.nxt(ring_f)[:, 0:n]
            P.ts(tf, tin[:, 1:n + 1], par[:, 1:2], ALU.mult, par[:, 3:4], ALU.add)
            P.stt(tf, tin[:, 0:n], par[:, 0:1], tf, ALU.mult, ALU.add)
            P.stt(tf, tin[:, 2:n + 2], par[:, 2:3], tf, ALU.mult, ALU.add)
            P.act(tf, tf, AF.Silu)
            o[nm] = tf
            yield
        dtr = P.nxt("sd_dt")[:, 0:nq, :]
        P.dma(dtr, io["dt" + seq].ap()[s][:, t0 // Q:t0 // Q + nq, :])
        dt = P.nxt("sd_dt")[:, 0:nq, :]
        dta = P.nxt("sd_dta")[:, 0:nq, :]
        xb = P.nxt("sd_tmp8")[:, 0:nq, :]
        for d in range(2):
            P.ts(xb[:, :, d:d + 1], dtr[:, :, d:d + 1], pd[s][:, d:d + 1], ALU.add)
        ab = P.nxt("sd_tmp8")[:, 0:nq, :]
        P.act(ab, xb, AF.Abs)
        P.act(ab, ab, AF.Exp, scale=-1.0)
        P.act(ab, ab, AF.Ln, bias=1.0)
        P.ts(xb, xb, 0.0, ALU.max)
        P.tt(dt, xb, ab, ALU.add)
        for d in range(2):
            P.ts(dta[:, :, d:d + 1], dt[:, :, d:d + 1], Aneg[s][:, d:d + 1], ALU.mult)
        o["dt"] = dt; o["dta"] = dta
        yield
        o["xtok"] = []; o["btok"] = []; o["cb"] = []
        for c in range(nq):
            cs = slice(c * Q, (c + 1) * Q)
            pa = psn()
            P.mm(pa[:, 0:64], o["x"][:, cs], ident[0:64, 0:64])
            P.mm(pa[:, 128:256], o["b"][:, cs], ident[:])
            P.mm(pa[:, 256:384], o["b"][:, cs], o["c"][:, cs])
            xt = P.nxt("sd_xtok"); bt = P.nxt("sd_btok")
            P.copy(xt[:], pa[:, 0:64], eng="act")
            P.copy(bt[:], pa[:, 128:256], eng="act")
            cbm = []
            for d in range(2):
                m = P.nxt("sd_cbm")
                P.tt(m[:], pa[:, 256:384], tri[d][:], ALU.mult)
                cbm.append(m)
            o["xtok"].append(xt); o["btok"].append(bt); o["cb"].append(cbm)
            yield

    def chunk_gen(s, d, o, c, R):
        cs = slice(c * Q, (c + 1) * Q)
        dta = o["dta"][:, c, d:d + 1]; dt = o["dt"][:, c, d:d + 1]
        last = Q - 1 if d == 0 else 0
        bc = P.nxt("sd_bc")
        P.ts(bc[:], ones[:], dta, ALU.mult)
        pd_ = psn()
        P.mm(pd_[:, 0:128], bc[:], tri[d][:])
        P.mm(pd_[:, 128:129], tri[d][:], dta)
        col = P.nxt("sd_col")
        P.copy(col[:, 0:1], pd_[:, 128:129])
        yield
        E = P.nxt("sd_E")
        P.ts(E[:], pd_[:, 0:128], col[:, 0:1], ALU.subtract, 0.0, ALU.min)
        P.ts(col[:, 1:2], col[:, 0:1], -1.0, ALU.mult, pd_[:, last:last + 1], ALU.add)
        Cs = P.nxt("sd_Cs")
        P.act(Cs[:], pd_[:, 0:128], AF.Exp)
        P.act(col[:, 2:3], pd_[:, last:last + 1], AF.Exp)
        yield
        P.act(E[:], E[:], AF.Exp)
        P.act(col[:, 1:2], col[:, 1:2], AF.Exp)
        P.tt(Cs[:], Cs[:], o["c"][:, cs], ALU.mult)
        xdt = P.nxt("sd_xdt")
        P.ts(xdt[:], o["xtok"][c][:], dt, ALU.mult)
        yield
        G = P.nxt("sd_G")
        P.tt(G[:], E[:], o["cb"][c][d][:], ALU.mult)
        P.tt(col[:, 1:2], col[:, 1:2], dt, ALU.mult)
        xw = P.nxt("sd_xw")
        P.ts(xw[:], o["xtok"][c][:], col[:, 1:2], ALU.mult)
        yield
        pg_ = psn()
        P.mm(pg_[:, 0:64], o["btok"][c][:], xw[:])
        st = P.nxt("sd_st")
        P.copy(st[:], pg_[:, 0:64], eng="act")
        R[c] = (G, xdt, Cs, col, st)

    def slot_gen(s, d, seq, t0, n):
        nq = n // Q
        o = {}; R = {}
        yield from prep_gen(s, seq, t0, n, o)
        yield from _ilv_gen([chunk_gen(s, d, o, c, R) for c in range(nq)])
        yb = P.nxt("sd_yblk")
        h = H[s][d]
        order = range(nq) if d == 0 else range(nq - 1, -1, -1)
        for c in order:
            G, xdt, Cs, col, st = R[c]
            pf = psn()
            P.mm(pf[:, 0:64], G[:], xdt[:], start=True, stop=False)
            P.mm(pf[:, 0:64], Cs[:], h[:], start=False, stop=True)
            P.copy(yb[:, c, :], pf[:, 0:64], eng="act")
            P.stt(h[:], h[:], col[:, 2:3], st[:], ALU.mult, ALU.add)
            yield
        if d == 0:
            P.dma(yfd[seq].ap()[s][:, t0 // Q:t0 // Q + nq, :], yb[:, 0:nq, :])
        else:
            yfl = P.nxt("sd_yfl")
            P.dma(yfl[:, 0:nq, :], yfd[seq].ap()[s][:, t0 // Q:t0 // Q + nq, :])
            P.tt(yb[:, 0:nq, :], yb[:, 0:nq, :], yfl[:, 0:nq, :], ALU.add)
            for c in range(nq):
                P.stt(yb[:, c, :], o["xtok"][c][:], dsum[s][:, 0:1], yb[:, c, :], ALU.mult, ALU.add)
            P.dma(io["y" + seq].ap()[s][:, t0 // Q:t0 // Q + nq, :], yb[:, 0:nq, :], is_out=True)

    def blocks(seq, T):
        return [(seq, t0, min(TB, T - t0)) for t0 in range(0, T, TB)]
    fwd = blocks("c", Tc) + blocks("l", Tl)
    bwd = blocks("c", Tc)[::-1] + blocks("l", Tl)[::-1]
    for s in range(2):
        for d in range(2):
            P.memset(H[s][d][:], 0.0)
    for d, blist in ((0, fwd), (1, bwd)):
        for (seq, t0, n) in blist:
            _interleave([slot_gen(0, d, seq, t0, n), slot_gen(1, d, seq, t0, n)])
    return io
```

```python
import math
from contextlib import ExitStack
import numpy as np
import ml_dtypes
import concourse.bass as bass
import concourse.mybir as mybir
from concourse.bass_utils import run_bass_kernel_spmd

F32 = mybir.dt.float32
BF16 = mybir.dt.bfloat16
AF = mybir.ActivationFunctionType
ALU = mybir.AluOpType
AX = mybir.AxisListType

COMPUTE = ("pe", "act", "dve", "pool")
STRICT_SYNC = False


class _Ins:
    __slots__ = ("eng", "fn", "waits", "is_dma", "tok", "idx", "need_inc")

    def __init__(self, eng, fn, is_dma):
        self.eng, self.fn, self.is_dma = eng, fn, is_dma
        self.waits = []
        self.tok = None
        self.need_inc = False


class _Trk:
    __slots__ = ("writers", "readers", "const")

    def __init__(self):
        self.writers = []
        self.readers = {}
        self.const = False


class Prog:
    def __init__(self, n_dma_sems=24):
        self.nc = bass.Bass("TRN2", target_bir_lowering=False)
        self.es = ExitStack()
        self.ins = []
        self.trk = {}
        self.n_dma_sems = n_dma_sems
        self.dma_rr = {"sp": 0, "pool": 0, "act": 0}
        self.dma_prev = {}
        self.out_dmas = []
        self._uid = 0
        self.rings = {}
        self.psum_names = set()
        self.debug = False
        self.dbg_out = {}

    def sb(self, name, shape, dtype=F32):
        t = self.es.enter_context(self.nc.sbuf_tensor(name, list(shape), dtype))
        return t

    def ps(self, name, shape, dtype=F32):
        t = self.es.enter_context(self.nc.psum_tensor(name, list(shape), dtype))
        self.psum_names.add(name)
        return t

    def dram(self, name, shape, dtype=F32, kind="Internal"):
        t = self.nc.dram_tensor(name, list(shape), dtype, kind=kind)
        if kind == "ExternalInput":
            self._t(name).const = True
        return t

    def ring(self, name, n, shape, dtype=F32, psum=False):
        tiles = [(self.ps if psum else self.sb)(f"{name}{i}", shape, dtype) for i in range(n)]
        self.rings[name] = [tiles, 0]
        return name

    def nxt(self, name):
        r = self.rings[name]
        t = r[0][r[1] % len(r[0])]
        r[1] += 1
        return t

    def _t(self, name):
        t = self.trk.get(name)
        if t is None:
            t = self.trk[name] = _Trk()
        return t

    def _rec(self, eng, fn, reads, writes, is_dma=False):
        x = _Ins(eng, fn, is_dma)
        x.idx = len(self.ins)
        deps = []
        rn = []
        for a in reads:
            if a is None or isinstance(a, (int, float)):
                continue
            rn.append(a.tensor.name)
        wn = [a.tensor.name for a in writes]
        raw = set()
        for n in rn:
            deps.extend(self._t(n).writers)
            for w_ in self._t(n).writers:
                raw.add(w_.idx)
            if n in self.psum_names:
                deps.extend(r_ for k_, r_ in self._t(n).readers.items() if k_ != eng)
        for n in wn:
            t = self._t(n)
            deps.extend(t.writers)
            deps.extend(t.readers.values())
        seen = set()
        for d in deps:
            if d.idx in seen:
                continue
            seen.add(d.idx)
            if (not d.is_dma) and (not is_dma) and d.eng == eng:
                if eng == "pe" or (d.idx not in raw and not STRICT_SYNC):
                    continue
            x.waits.append(d)
            d.need_inc = True
        for n in rn:
            t = self._t(n)
            if t.const:
                continue
            key = ("dma", x.idx) if is_dma else eng
            t.readers[key] = x
            if len(t.readers) > 48:
                ks = [k for k in t.readers if isinstance(k, tuple)]
                pass
        for n in wn:
            t = self._t(n)
            t.writers = [x]
            t.readers = {}
        self.ins.append(x)
        return x

    def mm(self, out, lhsT, rhs, start=True, stop=True):
        rd = [lhsT, rhs] + ([] if start else [out])
        return self._rec("pe", lambda e: e.matmul(out, lhsT, rhs, start=start, stop=stop), rd, [out])

    def transpose(self, out, in_, ident):
        return self._rec("pe", lambda e: e.transpose(out, in_, ident), [in_, ident], [out])

    def act(self, out, in_, func, bias=None, scale=None, accum=None, eng="act"):
        kw = {}
        if bias is not None:
            kw["bias"] = bias
        if scale is not None:
            kw["scale"] = scale
        if accum is not None:
            kw["accum_out"] = accum
        rd = [in_, bias if hasattr(bias, "tensor") else None, scale if hasattr(scale, "tensor") else None]
        wr = [out] + ([accum] if accum is not None else [])
        return self._rec("act", lambda e: e.activation(out, in_, func, **kw), rd, wr)

    def tt(self, out, a, b, op, eng="dve"):
        return self._rec(eng, lambda e: e.tensor_tensor(out, a, b, op), [a, b], [out])

    def ts(self, out, a, s1, op0, s2=None, op1=None, eng="dve", accum=None):
        rd = [a, s1 if hasattr(s1, "tensor") else None, s2 if hasattr(s2, "tensor") else None]
        if op1 is None:
            return self._rec(eng, lambda e: e.tensor_scalar(out, a, s1, None, op0), rd, [out])
        kw = {}
        wr = [out]
        if accum is not None:
            kw["accum_out"] = accum
            wr.append(accum)
        return self._rec(eng, lambda e: e.tensor_scalar(out, a, s1, s2, op0, op1, **kw), rd, wr)

    def stt(self, out, in0, scalar, in1, op0, op1, eng="dve"):
        rd = [in0, in1, scalar if hasattr(scalar, "tensor") else None]
        return self._rec(eng, lambda e: e.scalar_tensor_tensor(out, in0, scalar, in1, op0, op1), rd, [out])

    def copy(self, out, in_, eng="dve"):
        if eng == "act":
            return self._rec("act", lambda e: e.copy(out, in_), [in_], [out])
        return self._rec(eng, lambda e: e.tensor_copy(out, in_), [in_], [out])

    def memset(self, out, val, eng="dve"):
        return self._rec(eng, lambda e: e.memset(out, val), [], [out])

    def reduce(self, out, in_, op, axis=AX.X, eng="dve"):
        return self._rec(eng, lambda e: e.tensor_reduce(out, in_, axis, op), [in_], [out])

    def scan(self, out, d0, d1, init, op0, op1):
        return self._rec("dve", lambda e: e.tensor_tensor_scan(out, d0, d1, init, op0, op1), [d0, d1], [out])

    def recip(self, out, in_):
        return self._rec("dve", lambda e: e.reciprocal(out, in_), [in_], [out])

    def dma(self, out, in_, q="sp", is_out=False):
        x = self._rec(q, lambda e: e.dma_start(out=out, in_=in_), [in_], [out], is_dma=True)
        k = self.dma_rr[q] % self.n_dma_sems
        self.dma_rr[q] += 1
        key = (q, k)
        prev = self.dma_prev.get(key)
        val = (prev.tok[1] if prev is not None else 0) + 16
        x.tok = (key, val)
        if prev is not None and prev not in x.waits:
            x.waits.append(prev)
        self.dma_prev[key] = x
        if is_out:
            self.out_dmas.append(x)
        return x

    def dbg(self, name, ap):
        if not getattr(self, "debug", False):
            return
        if name in self.dbg_out:
            return
        t = self.dram("dbg_" + name, list(ap.shape), F32, "ExternalOutput")
        self.dbg_out[name] = t
        self.dma(t.ap(), ap, is_out=True)

    def finish(self):
        nc = self.nc
        engs = {"pe": [], "act": [], "dve": [], "pool": [], "sp": []}
        for x in self.ins:
            engs[x.eng].append(x)
        sems = {}
        for e in COMPUTE:
            sems[e] = self.es.enter_context(nc.semaphore(f"s_{e}"))
            c = 0
            for x in engs[e]:
                if x.is_dma:
                    continue
                if x.need_inc:
                    c += 1
                    x.tok = (e, c)
        for q in ("sp", "pool", "act"):
            for k in range(self.n_dma_sems):
                if (q, k) in self.dma_prev:
                    sems[(q, k)] = self.es.enter_context(nc.semaphore(f"d_{q}{k}"))
        block = self.es.enter_context(nc.Block())
        out_dmas = self.out_dmas

        def emit(engname):
            def body(e):
                waited = {}
                for x in engs[engname]:
                    for d in x.waits:
                        s, v = d.tok
                        if waited.get(s, 0) >= v:
                            continue
                        waited[s] = v
                        e.wait_ge(sems[s], v)
                    r = x.fn(e)
                    if x.is_dma:
                        r.then_inc(sems[x.tok[0]], 16)
                    elif x.need_inc:
                        r.then_inc(sems[x.tok[0]], 1)
                if engname == "sp":
                    for d in out_dmas:
                        s, v = d.tok
                        if waited.get(s, 0) >= v:
                            continue
                        waited[s] = v
                        e.wait_ge(sems[s], v)
            return body

        block.sync(emit("sp"))
        block.tensor(emit("pe"))
        block.scalar(emit("act"))
        block.vector(emit("dve"))
        block.gpsimd(emit("pool"))
        self.es.close()
        return nc


RW_C = 64
E05 = math.exp(-0.5)
LN_X_EPS = 64e-5
PR_MU0, PR_MU1, PR_M2, PR_W0, PR_A0, PR_KK, PR_KA, PR_1MKA, PR_RK, PR_LG, PR_LB = 0, 6, 12, 18, 20, 22, 23, 24, 25, 26, 27
PR_N = 28


def rwkv_consts():
    c = {}
    blk = np.zeros((128, 128), np.float32); blk[:64, :64] = 1; blk[64:, 64:] = 1
    c["blk64"] = blk
    c["ident"] = np.eye(128, dtype=np.float32)
    i2 = np.zeros((128, 64), np.float32); i2[:64] = np.eye(64); i2[64:] = np.eye(64)
    c["i2"] = i2
    p = np.arange(64)[:, None]; f = np.arange(64)[None, :]
    m = np.zeros((2, 64, 5, 64), np.float32)
    m[0, :, 0] = f < p; m[0, :, 1] = f > p; m[0, :, 2] = f > p; m[0, :, 3] = f >= p; m[0, :, 4] = f >= p
    m[1, :, 0] = f > p; m[1, :, 1] = f < p; m[1, :, 2] = f < p; m[1, :, 3] = f <= p; m[1, :, 4] = f <= p
    c["mask5"] = np.concatenate([m, m], axis=1).reshape(2, 128, 320)
    return c


def rwkv_build(P, PS, Tc, Tl, TB=512):
    C = RW_C
    io = {}
    io["uc"] = P.dram("rw_uc", [128, 6, Tc + 2], F32, "ExternalInput")
    io["ul"] = P.dram("rw_ul", [128, 6, Tl + 2], F32, "ExternalInput")
    io["pr"] = P.dram("rw_pr", [128, PR_N], F32, "ExternalInput")
    io["wup"] = P.dram("rw_wup", [128, 128], F32, "ExternalInput")
    io["aup"] = P.dram("rw_aup", [128, 128], F32, "ExternalInput")
    io["gup"] = P.dram("rw_gup", [128, 128], F32, "ExternalInput")
    io["blk64"] = P.dram("c_blk64", [128, 128], F32, "ExternalInput")
    io["ident"] = P.dram("c_ident", [128, 128], F32, "ExternalInput")
    io["i2"] = P.dram("c_i2", [128, 64], F32, "ExternalInput")
    io["mask5"] = P.dram("c_mask5", [2, 128, 320], F32, "ExternalInput")
    io["oc"] = P.dram("rw_oc", [128, Tc], F32, "ExternalOutput")
    io["ol"] = P.dram("rw_ol", [128, Tl], F32, "ExternalOutput")
    yf = {"c": P.dram("rw_yfc", [128, Tc], F32), "l": P.dram("rw_yfl", [128, Tl], F32)}
    usrc = {"c": io["uc"], "l": io["ul"]}
    odst = {"c": io["oc"], "l": io["ol"]}

    def cst(name, shape):
        t = P.sb("k_" + name, shape)
        P.dma(t[:], io[name].ap())
        return t
    pr = cst("pr", [128, PR_N]); wup = cst("wup", [128, 128]); aup = cst("aup", [128, 128]); gup = cst("gup", [128, 128])
    blk = cst("blk64", [128, 128]); ident = cst("ident", [128, 128]); i2 = cst("i2", [128, 64])
    mask5 = []
    for d in range(2):
        t = P.sb(f"k_mask5_{d}", [128, 320])
        P.dma(t[:], io["mask5"].ap()[d])
        mask5.append(t)
    rst = P.sb("k_rst", [128, TB])
    P.memset(rst[:], 1.0)
    P.memset(rst[:].rearrange("p (c t) -> p c t", t=C)[:, :, 0:1], 0.0)
    pc = lambda i: pr[:, i:i + 1]
    P.ts(pr[:, PR_M2:PR_M2 + 6], pr[:, PR_MU0:PR_MU0 + 6], -1.0, ALU.mult, 1.0, ALU.add)
    P.tt(pr[:, PR_M2:PR_M2 + 6], pr[:, PR_M2:PR_M2 + 6], pr[:, PR_MU1:PR_MU1 + 6], ALU.subtract)
    P.ts(pr[:, PR_1MKA:PR_1MKA + 1], pr[:, PR_KA:PR_KA + 1], -1.0, ALU.mult, 1.0, ALU.add)

    for nm in ("u6",):
        P.ring("rw_" + nm, 1, [128, 6, TB + 2])
    for nm in ("xs", ):
        P.ring("rw_" + nm, 1, [128, 6, TB])
    for nm in ("tw", "sg", "kx", "rn", "kk", "g", "lw0", "lw1", "ic0", "ic1", "k0", "k1", "b0", "b1"):
        P.ring("rw_" + nm, 1, [128, TB])
    for nm in ("sq", "f1", "t1"):
        P.ring("rw_" + nm, 2, [128, TB])
    for nm in ("cw", "cwb", "ep", "en", "ea", "At", "Bt", "Kt", "Rt", "yT", "t2", "t3", "t4", "yfl"):
        P.ring("rw_" + nm, 1, [128, TB])
    P.ring("rw_tok", 10, [128, 3, 64])
    P.ring("rw_M5", 10, [128, 5, 64])
    P.ring("rw_TT", 10, [128, 64])
    P.ring("rw_PP", 4, [128, 128])
    P.ring("rw_Xs", 3, [128, 64])
    P.ring("rw_Us", 3, [128, 64])
    S = [P.sb("rw_S0", [128, 64]), P.sb("rw_S1", [128, 64])]
    Stmp = P.sb("rw_Stmp", [128, 64])
    ps_tok, ps5, psL, psT, psX, psU, psY, psS = PS

    def prep(seq, t0, n):
        u6 = P.nxt("rw_u6")
        P.dma(u6[:, :, 0:n + 2], usrc[seq].ap()[:, :, t0:t0 + n + 2])
        xs = P.nxt("rw_xs")
        P.dbg("u6", u6[:, 0, 0:n + 2]); P.dbg("pr", pr[:])
        for j in range(6):
            P.ts(xs[:, j, 0:n], u6[:, j, 1:n + 1], pc(PR_M2 + j), ALU.mult)
            P.stt(xs[:, j, 0:n], u6[:, j, 0:n], pc(PR_MU0 + j), xs[:, j, 0:n], ALU.mult, ALU.add)
            P.stt(xs[:, j, 0:n], u6[:, j, 2:n + 2], pc(PR_MU1 + j), xs[:, j, 0:n], ALU.mult, ALU.add)
        r, k, v, wd, ad, gd = [xs[:, j, 0:n] for j in range(6)]
        o = {"r": r, "v": v}
        tw = P.nxt("rw_tw")[:, 0:n]
        P.act(tw, wd, AF.Tanh)
        for d in range(2):
            hs = slice(64 * d, 64 * d + 64)
            P.mm(psL[:, 0:n], wup[hs, :], tw[hs, :])
            lw = P.nxt(f"rw_lw{d}")[:, 0:n]
            P.act(lw, psL[:, 0:n], AF.Sigmoid, bias=pc(PR_W0 + d))
            P.ts(lw, lw, -E05, ALU.mult)
            o[f"lw{d}"] = lw
            P.mm(psT[:, 0:n], aup[hs, :], ad[hs, :])
            ic = P.nxt(f"rw_ic{d}")[:, 0:n]
            P.act(ic, psT[:, 0:n], AF.Sigmoid, bias=pc(PR_A0 + d))
            o[f"ic{d}"] = ic
        kx = P.nxt("rw_kx")[:, 0:n]
        P.ts(kx, k, pc(PR_KK), ALU.mult)
        sq = P.nxt("rw_sq")[:, 0:n]
        P.tt(sq, kx, kx, ALU.mult)
        P.mm(psX[:, 0:n], blk[:], sq)
        rn = P.nxt("rw_rn")[:, 0:n]
        P.act(rn, psX[:, 0:n], AF.Sqrt, bias=1e-12)
        P.recip(rn, rn)
        kk = P.nxt("rw_kk")[:, 0:n]
        P.tt(kk, kx, rn, ALU.mult)
        o["kk"] = kk
        for d in range(2):
            f1 = P.nxt("rw_f1")[:, 0:n]
            P.ts(f1, o[f"ic{d}"], pc(PR_KA), ALU.mult, pc(PR_1MKA), ALU.add)
            kd = P.nxt(f"rw_k{d}")[:, 0:n]
            P.tt(kd, k, f1, ALU.mult)
            bd = P.nxt(f"rw_b{d}")[:, 0:n]
            P.tt(bd, kk, o[f"ic{d}"], ALU.mult)
            o[f"k{d}"] = kd
            o[f"b{d}"] = bd
        sg = P.nxt("rw_sg")[:, 0:n]
        P.act(sg, gd, AF.Sigmoid)
        P.mm(psU[:, 0:n], gup[:], sg)
        g = P.nxt("rw_g")[:, 0:n]
        P.copy(g, psU[:, 0:n], eng="act")
        o["g"] = g
        for kname in ("r", "v", "kk", "lw0", "lw1", "ic0", "k0", "b0", "g"):
            P.dbg("p_" + kname, o[kname])
        return o

    def scan_block(d, o, n):
        nch = n // C
        lw = o[f"lw{d}"]
        cw = P.nxt("rw_cw")[:, 0:n]
        P.scan(cw, rst[:, 0:n], lw, 0.0, ALU.mult, ALU.add)
        v3 = lambda a: a.rearrange("p (c t) -> p c t", t=C)
        if d == 1:
            cwb = P.nxt("rw_cwb")[:, 0:n]
            P.tt(cwb, lw, cw, ALU.subtract)
            for c in range(nch):
                P.ts(cwb[:, c * C:(c + 1) * C], cwb[:, c * C:(c + 1) * C], cw[:, c * C + C - 1:c * C + C], ALU.add)
            cw = cwb
        ep = P.nxt("rw_ep")[:, 0:n]; en = P.nxt("rw_en")[:, 0:n]; ea = P.nxt("rw_ea")[:, 0:n]
        P.act(ep, cw, AF.Exp)
        P.act(en, cw, AF.Exp, scale=-1.0)
        t1 = P.nxt("rw_t1")[:, 0:n]
        P.tt(t1, cw, lw, ALU.subtract)
        P.act(ea, t1, AF.Exp)
        At = P.nxt("rw_At")[:, 0:n]; Bt = P.nxt("rw_Bt")[:, 0:n]; Kt = P.nxt("rw_Kt")[:, 0:n]; Rt = P.nxt("rw_Rt")[:, 0:n]
        P.stt(At, o["kk"], -1.0, ea, ALU.mult, ALU.mult)
        P.tt(Bt, o[f"b{d}"], en, ALU.mult)
        P.tt(Kt, o[f"k{d}"], en, ALU.mult)
        P.tt(Rt, o["r"], ep, ALU.mult)
        wcol = (C - 1) if d == 0 else 0
        v = o["v"]
        P.dbg(f"cw{d}", cw); P.dbg(f"At{d}", At); P.dbg(f"Bt{d}", Bt); P.dbg(f"Rt{d}", Rt)
        pre = []
        for c in range(nch):
            cs = slice(c * C, (c + 1) * C)
            tok = P.nxt("rw_tok")
            for j, X in enumerate((Bt, Kt, v)):
                for s in range(2):
                    hs = slice(64 * s, 64 * s + 64)
                    P.mm(ps_tok[hs, j * 64:(j + 1) * 64], X[hs, cs], ident[hs, hs])
            P.copy(tok[:].rearrange("p a b -> p (a b)"), ps_tok[:, 0:192], eng="act")
            M5 = P.nxt("rw_M5")
            for s in range(2):
                hs = slice(64 * s, 64 * s + 64)
                for j, (l, rr) in enumerate(((At, Bt), (Bt, At), (Kt, At), (Bt, Rt), (Kt, Rt))):
                    P.mm(ps5[hs, j * 64:(j + 1) * 64], l[hs, cs], rr[hs, cs])
            P.tt(M5[:].rearrange("p a b -> p (a b)"), ps5[:, 0:320], mask5[d][:], ALU.mult)
            TT = P.nxt("rw_TT")
            P.tt(TT[:], M5[:, 1, :], i2[:], ALU.add)
            Pm, PTm = M5[:, 0, :], M5[:, 1, :]
            for lvl in range(1, 6):
                for s in range(2):
                    hs = slice(64 * s, 64 * s + 64)
                    P.mm(psL[hs, 0:64], PTm[hs, :], Pm[hs, :])
                    if lvl < 5:
                        P.mm(psL[hs, 64:128], Pm[hs, :], PTm[hs, :])
                PP = P.nxt("rw_PP")
                w = 128 if lvl < 5 else 64
                P.copy(PP[:, 0:w], psL[:, 0:w], eng="act")
                Pm, PTm = PP[:, 0:64], PP[:, 64:128]
                for s in range(2):
                    hs = slice(64 * s, 64 * s + 64)
                    P.mm(psT[hs, 0:64], Pm[hs, :], TT[hs, :])
                P.tt(TT[:], TT[:], psT[:, 0:64], ALU.add)
            pre.append((tok, M5, TT))
            if c == 0:
                P.dbg(f"tok{d}", tok[:].rearrange("p a b -> p (a b)")); P.dbg(f"M5{d}", M5[:].rearrange("p a b -> p (a b)")); P.dbg(f"TT{d}", TT[:])
        Sd = S[d]
        order = range(nch) if d == 0 else range(nch - 1, -1, -1)
        for c in order:
            cs = slice(c * C, (c + 1) * C)
            tok, M5, TT = pre[c]
            Btok, Ktok, Vtok = tok[:, 0, :], tok[:, 1, :], tok[:, 2, :]
            AkT, LrbT, LrkT = M5[:, 2, :], M5[:, 3, :], M5[:, 4, :]
            for s in range(2):
                hs = slice(64 * s, 64 * s + 64)
                P.mm(psX[hs, 0:64], At[hs, cs], Sd[hs, :], start=True, stop=False)
                P.mm(psX[hs, 0:64], AkT[hs, :], Vtok[hs, :], start=False, stop=True)
            Xs = P.nxt("rw_Xs")
            P.copy(Xs[:], psX[:, 0:64], eng="act")
            for s in range(2):
                hs = slice(64 * s, 64 * s + 64)
                P.mm(psU[hs, 0:64], TT[hs, :], Xs[hs, :])
            Us = P.nxt("rw_Us")
            P.copy(Us[:], psU[:, 0:64])
            for s in range(2):
                hs = slice(64 * s, 64 * s + 64)
                P.mm(psY[hs, cs], Sd[hs, :], Rt[hs, cs], start=True, stop=False)
                P.mm(psY[hs, cs], Us[hs, :], LrbT[hs, :], start=False, stop=False)
                P.mm(psY[hs, cs], Vtok[hs, :], LrkT[hs, :], start=False, stop=True)
                P.mm(psS[hs, 0:64], Btok[hs, :], Us[hs, :], start=True, stop=False)
                P.mm(psS[hs, 0:64], Ktok[hs, :], Vtok[hs, :], start=False, stop=True)
            P.tt(Stmp[:], Sd[:], psS[:, 0:64], ALU.add)
            P.ts(Sd[:], Stmp[:], ep[:, c * C + wcol:c * C + wcol + 1], ALU.mult)
        yT = P.nxt("rw_yT")[:, 0:n]
        P.copy(yT, psY[:, 0:n], eng="act")
        P.dbg(f"yT{d}", yT)
        return yT

    def blocks(seq, T):
        return [(seq, t0, min(TB, T - t0)) for t0 in range(0, T, TB)]
    fwd = blocks("c", Tc) + blocks("l", Tl)
    bwd = blocks("c", Tc)[::-1] + blocks("l", Tl)[::-1]
    P.memset(S[0][:], 0.0); P.memset(S[1][:], 0.0)
    for (seq, t0, n) in fwd:
        o = prep(seq, t0, n)
        yT = scan_block(0, o, n)
        P.dma(yf[seq].ap()[:, t0:t0 + n], yT)
    for (seq, t0, n) in bwd:
        o = prep(seq, t0, n)
        yb = scan_block(1, o, n)
        yfl = P.nxt("rw_yfl")[:, 0:n]
        P.dma(yfl, yf[seq].ap()[:, t0:t0 + n])
        y = P.nxt("rw_t1")[:, 0:n]
        P.tt(y, yb, yfl, ALU.add)
        P.mm(psL[:, 0:n], blk[:], y)
        yc = P.nxt("rw_t2")[:, 0:n]
        P.stt(yc, psL[:, 0:n], -1.0 / 64, y, ALU.mult, ALU.add)
        sq = P.nxt("rw_sq")[:, 0:n]
        P.tt(sq, yc, yc, ALU.mult)
        P.mm(psT[:, 0:n], blk[:], sq)
        sd = P.nxt("rw_rn")[:, 0:n]
        P.act(sd, psT[:, 0:n], AF.Sqrt, bias=LN_X_EPS, scale=1.0 / 64)
        P.recip(sd, sd)
        P.tt(yc, yc, sd, ALU.mult)
        ov = P.nxt("rw_t3")[:, 0:n]
        P.ts(ov, yc, pc(PR_LG), ALU.mult, pc(PR_LB), ALU.add)
        ks = P.nxt("rw_t4")[:, 0:n]
        P.tt(ks, o["k0"], o["k1"], ALU.add)
        P.stt(ks, ks, pc(PR_RK), o["r"], ALU.mult, ALU.mult)
        P.mm(psX[:, 0:n], blk[:], ks)
        P.tt(ks, psX[:, 0:n], o["v"], ALU.mult)
        P.tt(ov, ov, ks, ALU.add)
        P.tt(ov, ov, o["g"], ALU.mult)
        P.dma(odst[seq].ap()[:, t0:t0 + n], ov, is_out=True)
    return io


def rwkv_host_inputs(rw_c, rw_l, p, heads):
    W = 640
    cols = []
    for base in (0, W, 2 * W):
        cols.append(np.concatenate([np.arange(base + h * 64, base + h * 64 + 64) for h in heads]))
    cols.append(np.arange(3 * W, 3 * W + 128)); cols.append(np.arange(3 * W + 128, 3 * W + 256)); cols.append(np.arange(3 * W + 256, 3 * W + 384))
    cols = np.stack(cols)

    def lay(u):
        T = u.shape[0]
        o = np.zeros((128, 6, T + 2), np.float32)
        o[:, :, 1:T + 1] = u[:, cols].transpose(2, 1, 0)
        return o
    hc = np.concatenate([np.arange(h * 64, h * 64 + 64) for h in heads])
    pr = np.zeros((128, PR_N), np.float32)
    mu = p["shift_mu"]
    pr[:, PR_MU0:PR_MU0 + 6] = mu[0][cols].T
    pr[:, PR_MU1:PR_MU1 + 6] = mu[1][cols].T
    pr[:, PR_W0:PR_W0 + 2] = p["w0"][:, hc].T
    pr[:, PR_A0:PR_A0 + 2] = p["a0"][:, hc].T
    pr[:, PR_KK] = p["k_k"][hc]; pr[:, PR_KA] = p["k_a"][hc]
    pr[:, PR_RK] = p["r_k"].reshape(-1)[hc]; pr[:, PR_LG] = p["lnx_g"][hc]; pr[:, PR_LB] = p["lnx_b"][hc]
    d = {"rw_uc": lay(rw_c), "rw_ul": lay(rw_l), "rw_pr": pr,
         "rw_wup": np.ascontiguousarray(p["w_up"][:, :, hc].reshape(128, 128)),
         "rw_aup": np.ascontiguousarray(p["a_up"][:, :, hc].reshape(128, 128)),
         "rw_gup": np.ascontiguousarray(p["g_up"][:, hc])}
    for k, v in rwkv_consts().items():
        d["c_" + k] = v
    return d


SQ = 128


def ssd_consts():
    p = np.arange(128)[:, None]; f = np.arange(128)[None, :]
    tri = np.stack([(f >= p), (f <= p)]).astype(np.float32)
    return {"tri": tri, "ident": np.eye(128, dtype=np.float32), "ones": np.ones((128, 128), np.float32)}


def ssd_build(P, PS, Tc, Tl, TB=512):
    Q = SQ
    io = {}
    T_ = {"c": Tc, "l": Tl}
    for sq in ("c", "l"):
        T = T_[sq]
        io["x" + sq] = P.dram("sd_x" + sq, [2, 64, T + 2], F32, "ExternalInput")
        io["b" + sq] = P.dram("sd_b" + sq, [2, 128, T + 2], F32, "ExternalInput")
        io["c" + sq] = P.dram("sd_c" + sq, [2, 128, T + 2], F32, "ExternalInput")
        io["dt" + sq] = P.dram("sd_dt" + sq, [2, 128, T // Q, 2], F32, "ExternalInput")
        io["y" + sq] = P.dram("sd_y" + sq, [2, 128, T // Q, 64], F32, "ExternalOutput")
    io["px"] = P.dram("sd_px", [2, 64, 4], F32, "ExternalInput")
    io["pb"] = P.dram("sd_pb", [2, 128, 4], F32, "ExternalInput")
    io["pc"] = P.dram("sd_pc", [2, 128, 4], F32, "ExternalInput")
    io["pd"] = P.dram("sd_pd", [2, 128, 6], F32, "ExternalInput")
    io["tri"] = P.dram("c_tri", [2, 128, 128], F32, "ExternalInput")
    io["ident"] = P.dram("c_identb", [128, 128], F32, "ExternalInput")
    io["ones"] = P.dram("c_ones", [128, 128], F32, "ExternalInput")
    yfd = {sq: P.dram("sd_yf" + sq, [2, 128, T_[sq] // Q, 64], F32) for sq in ("c", "l")}

    def cst(nm, src, shape):
        t = P.sb("sk_" + nm, shape)
        P.dma(t[:], src)
        return t
    tri = [cst(f"tri{d}", io["tri"].ap()[d], [128, 128]) for d in range(2)]
    ident = cst("ident", io["ident"].ap(), [128, 128])
    ones = cst("ones", io["ones"].ap(), [128, 128])
    px = [cst(f"px{s}", io["px"].ap()[s], [64, 4]) for s in range(2)]
    pb = [cst(f"pb{s}", io["pb"].ap()[s], [128, 4]) for s in range(2)]
    pcc = [cst(f"pc{s}", io["pc"].ap()[s], [128, 4]) for s in range(2)]
    pd = [cst(f"pd{s}", io["pd"].ap()[s], [128, 6]) for s in range(2)]
    Aneg = [P.sb(f"sk_A{s}", [128, 2]) for s in range(2)]
    dsum = [P.sb(f"sk_ds{s}", [128, 1]) for s in range(2)]
    for s in range(2):
        P.act(Aneg[s][:], pd[s][:, 2:4], AF.Exp)
        P.ts(Aneg[s][:], Aneg[s][:], -1.0, ALU.mult)
        P.tt(dsum[s][:], pd[s][:, 4:5], pd[s][:, 5:6], ALU.add)

    P.ring("sd_xin", 2, [64, TB + 2]); P.ring("sd_bin", 2, [128, TB + 2]); P.ring("sd_cin", 2, [128, TB + 2])
    P.ring("sd_xf", 4, [64, TB]); P.ring("sd_bf", 4, [128, TB]); P.ring("sd_cf", 4, [128, TB])
    P.ring("sd_dt", 4, [128, TB // Q, 2]); P.ring("sd_dta", 4, [128, TB // Q, 2]); P.ring("sd_tmp8", 6, [128, TB // Q, 2])
    P.ring("sd_xtok", 12, [128, 64]); P.ring("sd_btok", 12, [128, 128]); P.ring("sd_cbm", 12, [128, 128])
    P.ring("sd_bc", 3, [128, 128]); P.ring("sd_E", 3, [128, 128]); P.ring("sd_G", 3, [128, 128]); P.ring("sd_Cs", 3, [128, 128])
    P.ring("sd_col", 12, [128, 4]); P.ring("sd_xdt", 3, [128, 64]); P.ring("sd_xw", 3, [128, 64])
    P.ring("sd_yblk", 3, [128, TB // Q, 64]); P.ring("sd_yfl", 3, [128, TB // Q, 64])
    H = [[P.sb(f"sd_h{s}{d}", [128, 64]) for d in range(2)] for s in range(2)]
    ps_a, ps_b, ps_c, ps_d, ps_e, ps_f, ps_g, ps_h = PS

    def prep(s, seq, t0, n):
        nq = n // Q
        o = {}
        for nm, ring_in, ring_f, src, par, rows in (("x", "sd_xin", "sd_xf", io["x" + seq], px[s], 64),
                                                    ("b", "sd_bin", "sd_bf", io["b" + seq], pb[s], 128),
                                                    ("c", "sd_cin", "sd_cf", io["c" + seq], pcc[s], 128)):
            tin = P.nxt(ring_in)
            P.dma(tin[:, 0:n + 2], src.ap()[s][:, t0:t0 + n + 2])
            tf = P.nxt(ring_f)[:, 0:n]
            P.ts(tf, tin[:, 1:n + 1], par[:, 1:2], ALU.mult, par[:, 3:4], ALU.add)
            P.stt(tf, tin[:, 0:n], par[:, 0:1], tf, ALU.mult, ALU.add)
            P.stt(tf, tin[:, 2:n + 2], par[:, 2:3], tf, ALU.mult, ALU.add)
            P.act(tf, tf, AF.Silu)
            o[nm] = tf
        dtr = P.nxt("sd_dt")[:, 0:nq, :]
        P.dma(dtr, io["dt" + seq].ap()[s][:, t0 // Q:t0 // Q + nq, :])
        dt = P.nxt("sd_dt")[:, 0:nq, :]
        dta = P.nxt("sd_dta")[:, 0:nq, :]
        for d in range(2):
            xb = P.nxt("sd_tmp8")[:, 0:nq, d:d + 1]
            P.ts(xb, dtr[:, :, d:d + 1], pd[s][:, d:d + 1], ALU.add)
            ab = P.nxt("sd_tmp8")[:, 0:nq, d:d + 1]
            P.act(ab, xb, AF.Abs)
            P.act(ab, ab, AF.Exp, scale=-1.0)
            P.act(ab, ab, AF.Ln, bias=1.0)
            P.ts(xb, xb, 0.0, ALU.max)
            P.tt(dt[:, :, d:d + 1], xb, ab, ALU.add)
            P.ts(dta[:, :, d:d + 1], dt[:, :, d:d + 1], Aneg[s][:, d:d + 1], ALU.mult)
        o["dt"] = dt; o["dta"] = dta
        o["xtok"] = []; o["btok"] = []; o["cb"] = []
        for c in range(nq):
            cs = slice(c * Q, (c + 1) * Q)
            P.mm(ps_a[:, 0:64], o["x"][:, cs], ident[0:64, 0:64])
            xt = P.nxt("sd_xtok")
            P.copy(xt[:], ps_a[:, 0:64], eng="act")
            P.mm(ps_b[:, 0:128], o["b"][:, cs], ident[:])
            bt = P.nxt("sd_btok")
            P.copy(bt[:], ps_b[:, 0:128], eng="act")
            P.mm(ps_c[:, 0:128], o["b"][:, cs], o["c"][:, cs])
            o["xtok"].append(xt); o["btok"].append(bt); o["cb"].append(ps_c)
            cbm = []
            for d in range(2):
                m = P.nxt("sd_cbm")
                P.tt(m[:], ps_c[:, 0:128], tri[d][:], ALU.mult)
                cbm.append(m)
            o["cb"][-1] = cbm
        return o

    def scan(s, d, o, n):
        nq = n // Q
        yb = P.nxt("sd_yblk")
        h = H[s][d]
        order = range(nq) if d == 0 else range(nq - 1, -1, -1)
        last = Q - 1 if d == 0 else 0
        for c in order:
            cs = slice(c * Q, (c + 1) * Q)
            dta = o["dta"][:, c, d:d + 1]
            dt = o["dt"][:, c, d:d + 1]
            bc = P.nxt("sd_bc")
            P.ts(bc[:], ones[:], dta, ALU.mult)
            P.mm(ps_d[:, 0:128], bc[:], tri[d][:])
            P.mm(ps_e[:, 0:1], tri[d][:], dta)
            col = P.nxt("sd_col")
            P.copy(col[:, 0:1], ps_e[:, 0:1])
            E = P.nxt("sd_E")
            P.ts(E[:], ps_d[:, 0:128], col[:, 0:1], ALU.subtract, 0.0, ALU.min)
            P.act(E[:], E[:], AF.Exp)
            G = P.nxt("sd_G")
            P.tt(G[:], E[:], o["cb"][c][d][:], ALU.mult)
            Cs = P.nxt("sd_Cs")
            P.act(Cs[:], ps_d[:, 0:128], AF.Exp)
            P.tt(Cs[:], Cs[:], o["c"][:, cs], ALU.mult)
            xdt = P.nxt("sd_xdt")
            P.ts(xdt[:], o["xtok"][c][:], dt, ALU.mult)
            P.mm(ps_f[:, 0:64], G[:], xdt[:], start=True, stop=False)
            P.mm(ps_f[:, 0:64], Cs[:], h[:], start=False, stop=True)
            P.copy(yb[:, c, :], ps_f[:, 0:64], eng="act")
            P.ts(col[:, 1:2], col[:, 0:1], -1.0, ALU.mult, ps_d[:, last:last + 1], ALU.add)
            P.act(col[:, 1:2], col[:, 1:2], AF.Exp)
            P.tt(col[:, 1:2], col[:, 1:2], dt, ALU.mult)
            P.act(col[:, 2:3], ps_d[:, last:last + 1], AF.Exp)
            xw = P.nxt("sd_xw")
            P.ts(xw[:], o["xtok"][c][:], col[:, 1:2], ALU.mult)
            P.mm(ps_g[:, 0:64], o["btok"][c][:], xw[:])
            P.stt(h[:], h[:], col[:, 2:3], ps_g[:, 0:64], ALU.mult, ALU.add)
        return yb

    def blocks(seq, T):
        return [(seq, t0, min(TB, T - t0)) for t0 in range(0, T, TB)]
    fwd = blocks("c", Tc) + blocks("l", Tl)
    bwd = blocks("c", Tc)[::-1] + blocks("l", Tl)[::-1]
    for s in range(2):
        for d in range(2):
            P.memset(H[s][d][:], 0.0)
    for (seq, t0, n) in fwd:
        for s in range(2):
            o = prep(s, seq, t0, n)
            yb = scan(s, 0, o, n)
            nq = n // Q
            P.dma(yfd[seq].ap()[s][:, t0 // Q:t0 // Q + nq, :], yb[:, 0:nq, :])
    for (seq, t0, n) in bwd:
        for s in range(2):
            o = prep(s, seq, t0, n)
            yb = scan(s, 1, o, n)
            nq = n // Q
            yfl = P.nxt("sd_yfl")
            P.dma(yfl[:, 0:nq, :], yfd[seq].ap()[s][:, t0 // Q:t0 // Q + nq, :])
            P.tt(yb[:, 0:nq, :], yb[:, 0:nq, :], yfl[:, 0:nq, :], ALU.add)
            for c in range(nq):
                P.stt(yb[:, c, :], o["xtok"][c][:], dsum[s][:, 0:1], yb[:, c, :], ALU.mult, ALU.add)
            P.dma(io["y" + seq].ap()[s][:, t0 // Q:t0 // Q + nq, :], yb[:, 0:nq, :], is_out=True)
    return io


def ssd_host_inputs(ssm_c, ssm_l, p, heads):
    d = {}
    for sq, u in (("c", ssm_c), ("l", ssm_l)):
        T = u.shape[0]
        xbc = u[:, 640:640 + 1152]
        X = np.zeros((2, 64, T + 2), np.float32); B = np.zeros((2, 128, T + 2), np.float32); Cc = np.zeros((2, 128, T + 2), np.float32)
        DT = np.zeros((2, 128, T // SQ, 2), np.float32)
        for s, h in enumerate(heads):
            g = h // 5
            X[s, :, 1:T + 1] = xbc[:, h * 64:h * 64 + 64].T
            B[s, :, 1:T + 1] = xbc[:, 640 + g * 128:640 + g * 128 + 128].T
            Cc[s, :, 1:T + 1] = xbc[:, 896 + g * 128:896 + g * 128 + 128].T
            DT[s, :, :, 0] = u[:, 1792 + h].reshape(T // SQ, SQ).T
            DT[s, :, :, 1] = u[:, 1802 + h].reshape(T // SQ, SQ).T
        d["sd_x" + sq] = X; d["sd_b" + sq] = B; d["sd_c" + sq] = Cc; d["sd_dt" + sq] = DT
    cw = np.concatenate([p["conv_w"], p["conv_b"][None]], 0).T
    px = np.zeros((2, 64, 4), np.float32); pb = np.zeros((2, 128, 4), np.float32); pc = np.zeros((2, 128, 4), np.float32)
    pd = np.zeros((2, 128, 6), np.float32)
    for s, h in enumerate(heads):
        g = h // 5
        px[s] = cw[h * 64:h * 64 + 64]; pb[s] = cw[640 + g * 128:640 + g * 128 + 128]; pc[s] = cw[896 + g * 128:896 + g * 128 + 128]
        pd[s, :, 0] = p["dt_bias"][0, h]; pd[s, :, 1] = p["dt_bias"][1, h]
        pd[s, :, 2] = p["a_log"][0, h]; pd[s, :, 3] = p["a_log"][1, h]
        pd[s, :, 4] = p["d_skip"][0, h]; pd[s, :, 5] = p["d_skip"][1, h]
    d.update({"sd_px": px, "sd_pb": pb, "sd_pc": pc, "sd_pd": pd})
    c = ssd_consts()
    d["c_tri"] = c["tri"]; d["c_identb"] = c["ident"]; d["c_ones"] = c["ones"]
    return d


def modvec_build(P, PS, L, KD, NCOL):
    D = KD * 128
    io = {}
    io["cT"] = P.dram("m_cT", [128, KD, 2], F32, "ExternalInput")
    io["w"] = P.dram("m_w", [L, KD, 128, NCOL], F32, "ExternalInput")
    io["b"] = P.dram("m_b", [L, 2, NCOL], F32, "ExternalInput")
    io["o"] = P.dram("m_o", [L, 2, NCOL], F32, "ExternalOutput")
    cT = P.sb("m_cTs", [128, KD, 2])
    P.dma(cT[:], io["cT"].ap())
    P.act(cT[:], cT[:], AF.Silu)
    P.ring("m_wt", 4, [128, NCOL]); P.ring("m_bt", 2, [2, NCOL]); P.ring("m_ot", 2, [2, NCOL])
    nt = (NCOL + 511) // 512
    for l in range(L):
        bt = P.nxt("m_bt")
        P.dma(bt[:], io["b"].ap()[l])
        for k in range(KD):
            wt = P.nxt("m_wt")
            P.dma(wt[:], io["w"].ap()[l, k])
            for j in range(nt):
                cs = slice(j * 512, min(NCOL, (j + 1) * 512))
                P.mm(PS[j][0:2, 0:cs.stop - cs.start], cT[:, k, :], wt[:, cs], start=(k == 0), stop=(k == KD - 1))
        ot = P.nxt("m_ot")
        for j in range(nt):
            cs = slice(j * 512, min(NCOL, (j + 1) * 512))
            P.tt(ot[:, cs], PS[j][0:2, 0:cs.stop - cs.start], bt[:, cs], ALU.add)
        P.dma(io["o"].ap()[l], ot[:], is_out=True)
    return io


EPS = 1e-6


def norm_mod(P, ps, ones, xT, hT, KD, segs, ab, tmpring, x_local=False, mkring=True):
    D = KD * 128
    if mkring:
        P.ring(tmpring + "_r", 2, [128, 512])
    for (t0, n, si) in segs:
        ts_ = slice(t0, t0 + n)
        xs_ = slice(0, n) if x_local else ts_
        for k in range(KD):
            sq = P.nxt(tmpring)[:, 0:n]
            P.act(sq, xT[:, k, xs_], AF.Square)
            P.mm(ps[:, 0:n], ones[:], sq, start=(k == 0), stop=(k == KD - 1))
        rstd = P.nxt(tmpring + "_r")[:, 0:n]
        P.act(rstd, ps[:, 0:n], AF.Sqrt, bias=EPS, scale=1.0 / D)
        P.recip(rstd, rstd)
        a, b = ab[si]
        for k in range(KD):
            t = P.nxt(tmpring)[:, 0:n]
            P.tt(t, xT[:, k, xs_], rstd, ALU.mult)
            P.ts(hT[:, k, ts_], t, a[:, k:k + 1], ALU.mult, b[:, k:k + 1], ALU.add)


def load_mods(P, io_mod, io_g, KD, which):
    mod = P.sb("mod_t", [128, 2, 6, KD])
    P.dma(mod[:], io_mod.ap())
    g = P.sb("mod_g", [128, KD])
    P.dma(g[:], io_g.ap())
    out = []
    for si in range(2):
        a = P.sb(f"mod_a{si}", [128, KD])
        P.ts(a[:], mod[:, si, 3 * which + 1, :], 1.0, ALU.add)
        P.tt(a[:], a[:], g[:], ALU.mult)
        out.append((a, mod[:, si, 3 * which + 0, :], mod[:, si, 3 * which + 2, :]))
    return out


IN_COLS = 6420
QK_EPS = 1e-6


def rope_consts():
    blk = np.zeros((128, 128), np.float32); blk[:64, :64] = 1; blk[64:, 64:] = 1
    rot = np.zeros((128, 128), np.float32)
    for base in (0, 64):
        for m in range(32):
            rot[base + m + 32, base + m] = -1.0
            rot[base + m, base + m + 32] = 1.0
    return blk, rot


def rope_tables(positions, is_ctx):
    n_freq = 16
    inv = (10000.0 ** (-np.arange(n_freq, dtype=np.float32) / n_freq)).astype(np.float32)
    row = (positions // 64).astype(np.float32); col = (positions % 64).astype(np.float32)
    ang = np.concatenate([row[:, None] * inv, col[:, None] * inv], axis=-1).astype(np.float32)
    cos = np.cos(ang).astype(np.float32); sin = np.sin(ang).astype(np.float32)
    cos = np.where(is_ctx[:, None], 1.0, cos).astype(np.float32); sin = np.where(is_ctx[:, None], 0.0, sin).astype(np.float32)
    cosT = np.tile(cos.T, (4, 1)); sinT = np.tile(sin.T, (4, 1))
    return np.ascontiguousarray(cosT), np.ascontiguousarray(sinT)


def projA_build(P, PS, KD, TCc, TLc):
    T = TCc + TLc
    io = {}
    io["xT"] = P.dram("a_xT", [128, KD, T], F32, "ExternalInput")
    io["mod"] = P.dram("a_mod", [128, 2, 6, KD], F32, "ExternalInput")
    io["g"] = P.dram("a_g", [128, KD], F32, "ExternalInput")
    io["w"] = P.dram("a_w", [KD, 128, IN_COLS], F32, "ExternalInput")
    io["cos"] = P.dram("a_cos", [128, T], F32, "ExternalInput")
    io["sin"] = P.dram("a_sin", [128, T], F32, "ExternalInput")
    io["gain"] = P.dram("a_gain", [128, 2], F32, "ExternalInput")
    io["blk"] = P.dram("a_blk", [128, 128], F32, "ExternalInput")
    io["rot"] = P.dram("a_rot", [128, 128], F32, "ExternalInput")
    io["ones"] = P.dram("a_ones", [128, 128], F32, "ExternalInput")
    io["qk"] = P.dram("a_qk", [12, 128, T], BF16, "ExternalOutput")
    io["v"] = P.dram("a_v", [6, 128, T], BF16, "ExternalOutput")
    io["u"] = P.dram("a_u", [IN_COLS - 2304, T], F32, "ExternalOutput")
    xT = P.sb("a_xTs", [128, KD, T]); hT = P.sb("a_hT", [128, KD, T], BF16)
    P.dma(xT[:], io["xT"].ap())
    cos = P.sb("a_cos_s", [128, T]); sin = P.sb("a_sin_s", [128, T])
    P.dma(cos[:], io["cos"].ap()); P.dma(sin[:], io["sin"].ap())
    gain = P.sb("a_gain_s", [128, 2]); P.dma(gain[:], io["gain"].ap())
    blk = P.sb("a_blk_s", [128, 128]); P.dma(blk[:], io["blk"].ap())
    rot = P.sb("a_rot_s", [128, 128]); P.dma(rot[:], io["rot"].ap())
    ones = P.sb("a_ones_s", [128, 128]); P.dma(ones[:], io["ones"].ap())
    mods = load_mods(P, io["mod"], io["g"], KD, 0)
    P.ring("a_tmp", 6, [128, 512])
    segs = [(0, TCc, 1)] + [(TCc + t0, min(512, TLc - t0), 0) for t0 in range(0, TLc, 512)]
    norm_mod(P, PS[7], ones, xT, hT, KD, segs, [(m[0], m[1]) for m in mods], "a_tmp")
    P.ring("a_w", 3, [128, KD, 128], BF16)
    P.ring("a_o", 4, [128, 512]); P.ring("a_ob", 4, [128, 512], BF16)
    nchunk = (IN_COLS + 127) // 128
    pi = 0
    for m in range(nchunk):
        c0 = m * 128; mc = min(128, IN_COLS - c0)
        wt = P.nxt("a_w")
        P.dma(wt[:, :, 0:mc], io["w"].ap()[:, :, c0:c0 + mc].rearrange("k p c -> p k c"), q="pool")
        for (t0, n, si) in segs:
            ts_ = slice(t0, t0 + n)
            ps = PS[pi % 4]; pi += 1
            for k in range(KD):
                P.mm(ps[0:mc, 0:n], wt[:, k, 0:mc], hT[:, k, ts_], start=(k == 0), stop=(k == KD - 1))
            if m < 12:
                x = P.nxt("a_o")[:, 0:n]
                P.copy(x, ps[:, 0:n], eng="act")
                sq = P.nxt("a_o")[:, 0:n]
                P.tt(sq, x, x, ALU.mult)
                P.mm(PS[4][:, 0:n], blk[:], sq)
                rs = P.nxt("a_o")[:, 0:n]
                P.act(rs, PS[4][:, 0:n], AF.Sqrt, bias=QK_EPS, scale=1.0 / 64)
                P.recip(rs, rs)
                P.stt(x, x, gain[:, (0 if m < 6 else 1):(1 if m < 6 else 2)], rs, ALU.mult, ALU.mult)
                P.mm(PS[5][:, 0:n], rot[:], x)
                r2 = P.nxt("a_o")[:, 0:n]
                P.tt(r2, PS[5][:, 0:n], sin[:, ts_], ALU.mult)
                P.tt(x, x, cos[:, ts_], ALU.mult)
                ob = P.nxt("a_ob")[:, 0:n]
                P.tt(ob, x, r2, ALU.add)
                P.dma(io["qk"].ap()[m][:, ts_], ob, is_out=True)
            elif m < 18:
                ob = P.nxt("a_ob")[:, 0:n]
                P.copy(ob, ps[:, 0:n], eng="act")
                P.dma(io["v"].ap()[m - 12][:, ts_], ob, is_out=True)
            else:
                o = P.nxt("a_o")
                P.copy(o[0:mc, 0:n], ps[0:mc, 0:n], eng="act")
                P.dma(io["u"].ap()[c0 - 2304:c0 - 2304 + mc, ts_], o[0:mc, 0:n], is_out=True)
    return io


def attn_consts():
    blk = np.zeros((128, 128), np.float32); blk[:64, :64] = 1; blk[64:, 64:] = 1
    sgn = np.zeros((128, 128), np.float32); sgn[:64, :] = 1.0 / 64; sgn[64:, :] = -1.0 / 64
    E = np.zeros((640, 640), np.float32); E[:320, :320] = 1; E[320:, 320:] = 1
    return blk, sgn, np.ascontiguousarray(E.reshape(5, 128, 640))


def mixB1_build(P, PS, KD, TCc, TLc, TCX, TLAT):
    T = TCc + TLc; TK = TCX + TLAT; KT = TK // 128; D = KD * 128
    io = {}
    dr = lambda nm, shp, dt=F32, kind="ExternalInput": P.dram("b_" + nm, shp, dt, kind)
    io["xT"] = dr("xT", [128, KD, T]); io["q"] = dr("q", [6, 128, T], BF16); io["kT"] = dr("kT", [6, 128, TK], BF16)
    io["v"] = dr("v", [128, KT, 768], BF16)
    io["rT"] = dr("rT", [128, 5, T]); io["yT"] = dr("yT", [128, 5, T]); io["zT"] = dr("zT", [128, 5, T])
    io["mod"] = dr("mod", [128, 2, 6, KD]); io["g"] = dr("g", [128, KD])
    io["w"] = dr("w", [16, 128, D])
    io["subln"] = dr("subln", [128, 1]); io["sng"] = dr("sng", [128, 5]); io["lamq"] = dr("lamq", [128, 1]); io["lamk"] = dr("lamk", [128, 1])
    io["lami"] = dr("lami", [128, 1]); io["gains"] = dr("gains", [128, 2, 64])
    io["blk"] = dr("blk", [128, 128]); io["sgn"] = dr("sgn", [128, 128]); io["E"] = dr("E", [5, 128, 640]); io["ones"] = dr("ones", [128, 128])
    io["o"] = dr("o", [128, KD, T], F32, "ExternalOutput")

    def cst(nm, shape, src=None, dt=F32, q="sp"):
        t = P.sb("bk_" + nm, shape, dt)
        P.dma(t[:], io[nm].ap() if src is None else src, q=q)
        return t
    subln = cst("subln", [128, 1]); sng = cst("sng", [128, 5]); lamq = cst("lamq", [128, 1]); lamk = cst("lamk", [128, 1])
    lami = cst("lami", [128, 1]); gains = cst("gains", [128, 2, 64]); blk = cst("blk", [128, 128]); sgn = cst("sgn", [128, 128])
    ones = cst("ones", [128, 128])
    onesb = P.sb("bk_onesb", [128, 128], BF16)
    P.copy(onesb[:], ones[:])
    Eg = [cst(f"E{k}", [128, 640], io["E"].ap()[k]) for k in range(5)]
    mods = load_mods(P, io["mod"], io["g"], KD, 0)
    qs = P.sb("bk_q", [128, 6, T], BF16)
    P.dma(qs[:], io["q"].ap().rearrange("h p t -> p h t"))
    sm = P.sb("bk_sm", [128, 8])
    P.tt(sm[:, 0:1], lamq[:], lamk[:], ALU.mult)
    P.mm(PS[0][:, 0:1], blk[:], sm[:, 0:1])
    P.act(sm[:, 1:2], PS[0][:, 0:1], AF.Exp)
    P.mm(PS[1][:, 0:1], sgn[:], sm[:, 1:2])
    P.tt(sm[:, 2:3], PS[1][:, 0:1], lami[:], ALU.add)
    P.ts(sm[:, 3:4], sm[:, 2:3], -1.0, ALU.mult)
    P.ts(sm[:, 4:5], lami[:], -1.0, ALU.mult, 1.0, ALU.add)
    P.tt(sm[:, 4:5], sm[:, 4:5], subln[:], ALU.mult)
    ga = P.sb("bk_ga", [128, 2, 64])
    P.act(ga[:], gains[:], AF.Abs)
    P.reduce(sm[:, 5:7], ga[:], ALU.max)
    P.tt(sm[:, 7:8], sm[:, 5:6], sm[:, 6:7], ALU.mult)
    P.ts(sm[:, 7:8], sm[:, 7:8], -8.0, ALU.mult)
    neg_lam, sgl, ebias = sm[:, 3:4], sm[:, 4:5], sm[:, 7:8]

    mixT = P.sb("bk_mixT", [128, 16, T], BF16)
    P.ring("b_kT", 2, [128, TK], BF16); P.ring("b_vh", 2, [128, KT, 128], BF16)
    P.ring("b_E", 8, [128, 512], BF16); P.ring("b_t", 8, [128, 512])
    qtiles = [(0, TCc, TCX)] + [(TCc + t0, min(512, TLc - t0), TK) for t0 in range(0, TLc, 512)]
    si_ = 0
    for h in range(6):
        kh = P.nxt("b_kT"); vh = P.nxt("b_vh")
        P.dma(kh[:], io["kT"].ap()[h])
        P.dma(vh[:], io["v"].ap()[:, :, h * 128:(h + 1) * 128])
        for (t0, n, nkeys) in qtiles:
            ts_ = slice(t0, t0 + n)
            nkt = nkeys // 128
            Eq = []
            LA = 2
            for it in range(nkt + LA):
                if it < nkt:
                    ks = slice(it * 128, (it + 1) * 128)
                    Es = []
                    for c in range(2):
                        hs = slice(64 * c, 64 * c + 64)
                        Sp = PS[si_ % 4]; si_ += 1
                        P.mm(Sp[:, 0:n], kh[hs, ks], qs[hs, h, ts_])
                        E = P.nxt("b_E")[:, 0:n]
                        P.act(E, Sp[:, 0:n], AF.Exp, bias=ebias, scale=0.125)
                        Es.append(E)
                    Eq.append(Es)
                if it >= LA:
                    kt = it - LA
                    for c in range(2):
                        E = Eq[kt][c]
                        P.mm(PS[4 + c][:, 0:n], vh[:, kt, :], E, start=(kt == 0), stop=(kt == nkt - 1))
                        P.mm(PS[6 + c][:, 0:n], onesb[:], E, start=(kt == 0), stop=(kt == nkt - 1))
            rl0 = P.nxt("b_t")[:, 0:n]; rl1 = P.nxt("b_t")[:, 0:n]
            P.recip(rl0, PS[6][:, 0:n]); P.recip(rl1, PS[7][:, 0:n])
            o0 = P.nxt("b_t")[:, 0:n]; o1 = P.nxt("b_t")[:, 0:n]
            P.tt(o0, PS[4][:, 0:n], rl0, ALU.mult)
            P.tt(o1, PS[5][:, 0:n], rl1, ALU.mult)
            P.stt(o0, o1, neg_lam, o0, ALU.mult, ALU.add)
            sq = P.nxt("b_t")[:, 0:n]
            P.tt(sq, o0, o0, ALU.mult)
            P.mm(PS[0][:, 0:n], ones[:], sq)
            rs = P.nxt("b_t")[:, 0:n]
            P.act(rs, PS[0][:, 0:n], AF.Sqrt, bias=EPS, scale=1.0 / 128)
            P.recip(rs, rs)
            P.stt(mixT[:, h, ts_], o0, sgl, rs, ALU.mult, ALU.mult)
    P.ring("b_in5", 3, [128, 5, 512])
    toks = [(0, TCc, 1)] + [(TCc + t0, min(512, TLc - t0), 0) for t0 in range(0, TLc, 512)]
    for (t0, n, si) in toks:
        ts_ = slice(t0, t0 + n)
        rt = P.nxt("b_in5"); yt = P.nxt("b_in5"); zt = P.nxt("b_in5")
        P.dma(rt[:, :, 0:n], io["rT"].ap()[:, :, ts_]); P.dma(yt[:, :, 0:n], io["yT"].ap()[:, :, ts_]); P.dma(zt[:, :, 0:n], io["zT"].ap()[:, :, ts_])
        P.copy(mixT[:, 6:11, ts_], rt[:, :, 0:n])
        P.act(zt[:, :, 0:n], zt[:, :, 0:n], AF.Silu)
        P.tt(yt[:, :, 0:n], yt[:, :, 0:n], zt[:, :, 0:n], ALU.mult)
        P.tt(zt[:, :, 0:n], yt[:, :, 0:n], yt[:, :, 0:n], ALU.mult)
        for m in range(5):
            for k in range(5):
                P.mm(PS[m % 4][:, 0:n], Eg[k][:, m * 128:(m + 1) * 128], zt[:, k, 0:n], start=(k == 0), stop=(k == 4))
            rs = P.nxt("b_t")[:, 0:n]
            P.act(rs, PS[m % 4][:, 0:n], AF.Sqrt, bias=EPS, scale=1.0 / 320)
            P.recip(rs, rs)
            P.stt(mixT[:, 11 + m, ts_], yt[:, m, 0:n], sng[:, m:m + 1], rs, ALU.mult, ALU.mult)
    P.ring("b_w", 3, [128, 16, 128], BF16); P.ring("b_x", 4, [128, 512])
    pi = 0
    for m in range(KD):
        wt = P.nxt("b_w")
        P.dma(wt[:], io["w"].ap()[:, :, m * 128:(m + 1) * 128].rearrange("k p c -> p k c"), q="pool")
        for (t0, n, si) in toks:
            ts_ = slice(t0, t0 + n)
            ps = PS[pi % 4]; pi += 1
            for k in range(16):
                P.mm(ps[:, 0:n], wt[:, k, :], mixT[:, k, ts_], start=(k == 0), stop=(k == 15))
            xt = P.nxt("b_x")[:, 0:n]
            P.dma(xt, io["xT"].ap()[:, m, ts_])
            P.stt(xt, ps[:, 0:n], mods[si][2][:, m:m + 1], xt, ALU.mult, ALU.add)
            P.dma(io["o"].ap()[:, m, ts_], xt, is_out=True)
    return io


def ffnB2_build(P, PS, KD, FC, TCc, TLc):
    T = TCc + TLc; D = KD * 128
    io = {}
    dr = lambda nm, shp, dt=F32, kind="ExternalInput": P.dram("f_" + nm, shp, dt, kind)
    io["xT"] = dr("xT", [128, KD, T]); io["mod"] = dr("mod", [128, 2, 6, KD]); io["g"] = dr("g", [128, KD])
    io["wi"] = dr("wi", [KD, 128, 2 * FC * 128]); io["wo"] = dr("wo", [FC, 128, D]); io["ones"] = dr("ones", [128, 128])
    io["o"] = dr("o", [128, KD, T], F32, "ExternalOutput")
    hT = P.sb("f_hT", [128, KD, T], BF16)
    ones = P.sb("f_ones_s", [128, 128]); P.dma(ones[:], io["ones"].ap())
    mods = load_mods(P, io["mod"], io["g"], KD, 1)
    P.ring("f_tmp", 6, [128, 512]); P.ring("f_xt", 1, [128, KD, 256])
    toks = [(0, TCc, 1)] + [(TCc + t0, min(512, TLc - t0), 0) for t0 in range(0, TLc, 512)]
    nsegs = [(0, TCc, 1)] + [(TCc + t0, min(256, TLc - t0), 0) for t0 in range(0, TLc, 256)]
    ab = [(m[0], m[1]) for m in mods]
    for i, (t0, n, si) in enumerate(nsegs):
        xt = P.nxt("f_xt")
        P.dma(xt[:, :, 0:n], io["xT"].ap()[:, :, t0:t0 + n])
        norm_mod(P, PS[7], ones, xt, hT, KD, [(t0, n, si)], ab, "f_tmp", x_local=True, mkring=(i == 0))
    actT = P.sb("f_actT", [128, FC, T], BF16)
    P.ring("f_wi", 2, [128, KD, 256], BF16); P.ring("f_wo", 2, [128, FC, 128], BF16); P.ring("f_o", 3, [128, 512])
    pi = 0
    for m in range(FC):
        wt = P.nxt("f_wi")
        P.dma(wt[:, :, 0:128], io["wi"].ap()[:, :, m * 128:(m + 1) * 128].rearrange("k p c -> p k c"), q="pool")
        P.dma(wt[:, :, 128:256], io["wi"].ap()[:, :, (FC + m) * 128:(FC + m + 1) * 128].rearrange("k p c -> p k c"), q="pool")
        for (t0, n, si) in toks:
            ts_ = slice(t0, t0 + n)
            pg = PS[pi % 6]; pu = PS[(pi + 1) % 6]; pi += 2
            for k in range(KD):
                P.mm(pg[:, 0:n], wt[:, k, 0:128], hT[:, k, ts_], start=(k == 0), stop=(k == KD - 1))
            for k in range(KD):
                P.mm(pu[:, 0:n], wt[:, k, 128:256], hT[:, k, ts_], start=(k == 0), stop=(k == KD - 1))
            gs = P.nxt("f_tmp")[:, 0:n]
            P.act(gs, pg[:, 0:n], AF.Silu)
            P.tt(actT[:, m, ts_], gs, pu[:, 0:n], ALU.mult)
    for mo in range(KD):
        wt = P.nxt("f_wo")
        P.dma(wt[:], io["wo"].ap()[:, :, mo * 128:(mo + 1) * 128].rearrange("m p c -> p m c"), q="pool")
        for (t0, n, si) in toks:
            ts_ = slice(t0, t0 + n)
            ps = PS[pi % 6]; pi += 1
            for m in range(FC):
                P.mm(ps[:, 0:n], wt[:, m, :], actT[:, m, ts_], start=(m == 0), stop=(m == FC - 1))
            ot = P.nxt("f_o")[:, 0:n]
            P.dma(ot, io["xT"].ap()[:, mo, ts_])
            P.stt(ot, ps[:, 0:n], mods[si][2][:, mo:mo + 1], ot, ALU.mult, ALU.add)
            P.dma(io["o"].ap()[:, mo, ts_], ot, is_out=True)
    return io


NCORES = 8
_PROGS = {}
_DBG = None


def _fm(x2d, KD):
    T = x2d.shape[0]
    return np.ascontiguousarray(x2d.T.reshape(KD, 128, T).transpose(1, 0, 2))


def _unfm(a):
    p, KD, T = a.shape
    return np.ascontiguousarray(a.transpose(1, 0, 2).reshape(KD * 128, T).T)


def _run(nc, in_maps):
    res = run_bass_kernel_spmd(nc, in_maps, core_ids=list(range(NCORES)))
    return res.results


def _prog(key, builder):
    if key not in _PROGS:
        P = Prog()
        PS = [P.ps(f"ps{i}", [128, 512]) for i in range(8)]
        builder(P, PS)
        _PROGS[key] = P.finish()
    return _PROGS[key]


def _slots(j):
    rh = (2 * j, 2 * j + 1) if j < 5 else (0, 1)
    sh = (2 * (j - 3), 2 * (j - 3) + 1) if j >= 3 else (0, 1)
    return rh, sh


def kernel(x, c, ctx, c_ctx, ada_w, ada_b, norm1_g, norm2_g, w_in, w_out, qk_gain, lam_q, lam_k, subln_g,
           shift_mu, w0, w_up, a0, a_up, g_up, k_k, k_a, r_k, lnx_g, lnx_b, conv_w, conv_b, dt_bias, a_log,
           d_skip, ssm_norm_g, w_ffn_in, w_ffn_out):
    f = lambda a: np.ascontiguousarray(np.asarray(a, dtype=np.float32))
    x, c, ctx, c_ctx = f(x), f(c), f(ctx), f(c_ctx)
    L = ada_w.shape[0]; SEQ = x.shape[1]; CTX = ctx.shape[1]; D = x.shape[2]; KD = D // 128
    DFF = w_ffn_out.shape[1]; FC = DFF // 128
    NC = NCORES; TCc = CTX // NC; TLc = SEQ // NC; T = TCc + TLc
    ones = np.ones((128, 128), np.float32)
    NCOL = 6 * D // NC
    ncM = _prog(("M", L, KD, NCOL), lambda P, PS: modvec_build(P, PS, L, KD, NCOL))
    cT = _fm(np.stack([c[0], c_ctx]), KD)
    ada_w = np.asarray(ada_w, np.float32); ada_b = np.asarray(ada_b, np.float32)
    ims = []
    for j in range(NC):
        cs = slice(j * NCOL, (j + 1) * NCOL)
        ims.append({"m_cT": cT, "m_w": np.ascontiguousarray(ada_w[:, :, cs].reshape(L, KD, 128, NCOL)),
                    "m_b": np.ascontiguousarray(np.repeat(ada_b[:, None, cs], 2, axis=1))})
    rs = _run(ncM, ims)
    mod = np.concatenate([r["m_o"] for r in rs], axis=-1)
    modT = [np.ascontiguousarray(mod[l].reshape(2, 6, KD, 128).transpose(3, 0, 1, 2)) for l in range(L)]

    xl = x[0].copy(); xc = ctx[0].copy()
    blk, rot = rope_consts()
    ablk, sgn, Eg = attn_consts()
    pos = np.arange(SEQ)
    ncA = _prog(("A", KD, TCc, TLc), lambda P, PS: projA_build(P, PS, KD, TCc, TLc))
    ncSR = _prog(("SR", CTX, SEQ), lambda P, PS: rwkv_build2(P, PS, CTX, SEQ))
    ncSS = _prog(("SS", CTX, SEQ), lambda P, PS: ssd_build2(P, PS, CTX, SEQ))
    ncB1 = _prog(("B1", KD, TCc, TLc, CTX, SEQ), lambda P, PS: mixB1_build(P, PS, KD, TCc, TLc, CTX, SEQ))
    ncB2 = _prog(("B2", KD, FC, TCc, TLc), lambda P, PS: ffnB2_build(P, PS, KD, FC, TCc, TLc))
    tabs = []
    for j in range(NC):
        pj = np.concatenate([np.zeros(TCc, np.int64), pos[j * TLc:(j + 1) * TLc]])
        isc = np.concatenate([np.ones(TCc, bool), np.zeros(TLc, bool)])
        tabs.append(rope_tables(pj, isc))
    for l in range(L):
        g = lambda a: np.asarray(a[l], np.float32)
        xTs = [_fm(np.concatenate([xc[j * TCc:(j + 1) * TCc], xl[j * TLc:(j + 1) * TLc]]), KD) for j in range(NC)]
        gain = np.ascontiguousarray(np.tile(g(qk_gain), (1, 2)).T)
        wA = np.ascontiguousarray(g(w_in).reshape(KD, 128, IN_COLS))
        n1 = _fm(g(norm1_g)[None], KD)[:, :, 0]
        ims = [{"a_xT": xTs[j], "a_mod": modT[l], "a_g": n1, "a_w": wA, "a_cos": tabs[j][0], "a_sin": tabs[j][1],
                "a_gain": gain, "a_blk": blk, "a_rot": rot, "a_ones": ones} for j in range(NC)]
        ra = _run(ncA, ims)
        kT_all = np.ascontiguousarray(np.concatenate([r["a_qk"][6:12, :, :TCc] for r in ra] + [r["a_qk"][6:12, :, TCc:] for r in ra], axis=2))
        vT_all = np.concatenate([r["a_v"][:, :, :TCc] for r in ra] + [r["a_v"][:, :, TCc:] for r in ra], axis=2)
        TK = CTX + SEQ
        v_all = np.ascontiguousarray(vT_all.reshape(768, TK).T.reshape(TK // 128, 128, 768).transpose(1, 0, 2))
        u_c = np.concatenate([r["a_u"][:, :TCc] for r in ra], axis=1).T
        u_l = np.concatenate([r["a_u"][:, TCc:] for r in ra], axis=1).T
        p = {"shift_mu": g(shift_mu), "w0": g(w0), "w_up": g(w_up), "a0": g(a0), "a_up": g(a_up), "g_up": g(g_up),
             "k_k": g(k_k), "k_a": g(k_a), "r_k": g(r_k), "lnx_g": g(lnx_g), "lnx_b": g(lnx_b), "conv_w": g(conv_w),
             "conv_b": g(conv_b), "dt_bias": g(dt_bias), "a_log": g(a_log), "d_skip": g(d_skip)}
        rsR = _run(ncSR, [rwkv_host_inputs(u_c[:, :2304], u_l[:, :2304], p, _slots(j)[0]) for j in range(NC)])
        rsS = _run(ncSS, [ssd_host_inputs(u_c[:, 2304:], u_l[:, 2304:], p, _slots(j)[1]) for j in range(NC)])
        if _DBG is not None:
            _DBG[f'mod{l}'] = mod[l]; _DBG[f'u_c{l}'] = u_c; _DBG[f'u_l{l}'] = u_l; _DBG[f'kT{l}'] = kT_all; _DBG[f'v{l}'] = v_all; _DBG[f'q{l}'] = [r['a_qk'][0:6] for r in ra]
        r_c = np.zeros((CTX, 640), np.float32); r_l = np.zeros((SEQ, 640), np.float32)
        y_c = np.zeros((CTX, 640), np.float32); y_l = np.zeros((SEQ, 640), np.float32)
        for h in range(10):
            j, s = h // 2, h % 2
            r_c[:, h * 64:(h + 1) * 64] = rsR[j]["rw_oc"][64 * s:64 * s + 64].T
            r_l[:, h * 64:(h + 1) * 64] = rsR[j]["rw_ol"][64 * s:64 * s + 64].T
            j = 3 + h // 2
            y_c[:, h * 64:(h + 1) * 64] = rsS[j]["sd_yc"][s].transpose(1, 0, 2).reshape(CTX, 64)
            y_l[:, h * 64:(h + 1) * 64] = rsS[j]["sd_yl"][s].transpose(1, 0, 2).reshape(SEQ, 64)
        lam_init = 0.8 - 0.6 * math.exp(-0.3 * l)
        own = lambda ac, al, j: np.concatenate([ac[j * TCc:(j + 1) * TCc], al[j * TLc:(j + 1) * TLc]])
        wO = np.ascontiguousarray(g(w_out).reshape(16, 128, D))
        common = {"b_kT": kT_all, "b_v": v_all, "b_mod": modT[l], "b_g": n1, "b_w": wO,
                  "b_subln": np.ascontiguousarray(g(subln_g)[:, None]), "b_sng": _fm(g(ssm_norm_g)[None], 5)[:, :, 0],
                  "b_lamq": np.ascontiguousarray(g(lam_q).reshape(128, 1)), "b_lamk": np.ascontiguousarray(g(lam_k).reshape(128, 1)),
                  "b_lami": np.full((128, 1), lam_init, np.float32),
                  "b_gains": np.ascontiguousarray(np.broadcast_to(g(qk_gain)[None], (128, 2, 64))),
                  "b_blk": ablk, "b_sgn": sgn, "b_E": Eg, "b_ones": ones}
        ims = []
        for j in range(NC):
            d = dict(common)
            d["b_xT"] = xTs[j]; d["b_q"] = np.ascontiguousarray(ra[j]["a_qk"][0:6])
            d["b_rT"] = _fm(own(r_c, r_l, j), 5); d["b_yT"] = _fm(own(y_c, y_l, j), 5)
            d["b_zT"] = _fm(own(u_c[:, 2304:2304 + 640], u_l[:, 2304:2304 + 640], j), 5)
            ims.append(d)
        rb1 = _run(ncB1, ims)
        if _DBG is not None:
            _DBG[f'r_l{l}'] = r_l; _DBG[f'y_l{l}'] = y_l; _DBG[f'r_c{l}'] = r_c; _DBG[f'y_c{l}'] = y_c; _DBG[f'x1_{l}'] = [_unfm(r['b_o']) for r in rb1]
        n2 = _fm(g(norm2_g)[None], KD)[:, :, 0]
        wi = np.ascontiguousarray(g(w_ffn_in).reshape(KD, 128, 2 * DFF)); wo = np.ascontiguousarray(g(w_ffn_out).reshape(FC, 128, D))
        ims = [{"f_xT": rb1[j]["b_o"], "f_mod": modT[l], "f_g": n2, "f_wi": wi, "f_wo": wo, "f_ones": ones} for j in range(NC)]
        rb2 = _run(ncB2, ims)
        for j in range(NC):
            xo = _unfm(rb2[j]["f_o"])
            xc[j * TCc:(j + 1) * TCc] = xo[:TCc]
            xl[j * TLc:(j + 1) * TLc] = xo[TCc:]
        if _DBG is not None:
            _DBG[f'xl{l}'] = xl.copy(); _DBG[f'xc{l}'] = xc.copy()
            if _DBG.get('stop_after') == l:
                return xl[None].astype(np.float32)
    return xl[None].astype(np.float32)


_NO_ILV = False


def _interleave(gens):
    gens = [g for g in gens if g is not None]
    while gens:
        for g in list(gens):
            try:
                next(g)
            except StopIteration:
                gens.remove(g)


def rwkv_build2(P, PS, Tc, Tl):
    C = RW_C; TB = 256; NCB = TB // C
    io = {}
    io["uc"] = P.dram("rw_uc", [128, 6, Tc + 2], F32, "ExternalInput")
    io["ul"] = P.dram("rw_ul", [128, 6, Tl + 2], F32, "ExternalInput")
    io["pr"] = P.dram("rw_pr", [128, PR_N], F32, "ExternalInput")
    io["wup"] = P.dram("rw_wup", [128, 128], F32, "ExternalInput")
    io["aup"] = P.dram("rw_aup", [128, 128], F32, "ExternalInput")
    io["gup"] = P.dram("rw_gup", [128, 128], F32, "ExternalInput")
    io["blk64"] = P.dram("c_blk64", [128, 128], F32, "ExternalInput")
    io["ident"] = P.dram("c_ident", [128, 128], F32, "ExternalInput")
    io["i2"] = P.dram("c_i2", [128, 64], F32, "ExternalInput")
    io["mask5"] = P.dram("c_mask5", [2, 128, 320], F32, "ExternalInput")
    io["oc"] = P.dram("rw_oc", [128, Tc], F32, "ExternalOutput")
    io["ol"] = P.dram("rw_ol", [128, Tl], F32, "ExternalOutput")
    yf = {"c": P.dram("rw_yfc", [128, Tc], F32), "l": P.dram("rw_yfl", [128, Tl], F32)}
    usrc = {"c": io["uc"], "l": io["ul"]}
    odst = {"c": io["oc"], "l": io["ol"]}

    def cst(name, shape):
        t = P.sb("k_" + name, shape)
        P.dma(t[:], io[name].ap())
        return t
    pr = cst("pr", [128, PR_N]); wup = cst("wup", [128, 128]); aup = cst("aup", [128, 128]); gup = cst("gup", [128, 128])
    blk = cst("blk64", [128, 128]); ident = cst("ident", [128, 128]); i2 = cst("i2", [128, 64])
    i24 = P.sb("k_i24", [128, NCB, 64])
    for ci in range(NCB):
        P.copy(i24[:, ci, :], i2[:])
    mask5 = []
    for d in range(2):
        t = P.sb(f"k_mask5_{d}", [128, 320])
        P.dma(t[:], io["mask5"].ap()[d])
        mask5.append(t)
    rst = P.sb("k_rst", [128, TB])
    P.memset(rst[:], 1.0)
    P.memset(rst[:].rearrange("p (c t) -> p c t", t=C)[:, :, 0:1], 0.0)
    pc = lambda i: pr[:, i:i + 1]
    P.ts(pr[:, PR_M2:PR_M2 + 6], pr[:, PR_MU0:PR_MU0 + 6], -1.0, ALU.mult, 1.0, ALU.add)
    P.tt(pr[:, PR_M2:PR_M2 + 6], pr[:, PR_M2:PR_M2 + 6], pr[:, PR_MU1:PR_MU1 + 6], ALU.subtract)
    P.ts(pr[:, PR_1MKA:PR_1MKA + 1], pr[:, PR_KA:PR_KA + 1], -1.0, ALU.mult, 1.0, ALU.add)

    P.ring("rw_u6", 2, [128, 6, TB + 2]); P.ring("rw_xs", 2, [128, 6, TB])
    for nm in ("tw", "sg", "kx", "rn", "kk", "g", "lw0", "lw1", "ic0", "ic1", "k0", "k1", "b0", "b1",
               "cw", "cwb", "ep", "en", "ea", "At", "Bt", "Kt", "Rt", "yT", "t2", "t3", "t4", "yfl"):
        P.ring("rw_" + nm, 2, [128, TB])
    for nm in ("sq", "f1", "t1"):
        P.ring("rw_" + nm, 3, [128, TB])
    P.ring("rw_tok", 2, [128, NCB, 3, 64]); P.ring("rw_M5", 2, [128, NCB, 5, 64]); P.ring("rw_TT", 2, [128, NCB, 64])
    P.ring("rw_PP", 3, [128, NCB, 128]); P.ring("rw_Xs", 3, [128, 64]); P.ring("rw_Us", 3, [128, 64])
    S = [P.sb("rw_S0", [128, 64]), P.sb("rw_S1", [128, 64])]
    Stmp = P.sb("rw_Stmp", [128, 64])
    HS = [slice(0, 64), slice(64, 128)]

    def prep_gen(seq, t0, n, o):
        u6 = P.nxt("rw_u6")
        P.dma(u6[:, :, 0:n + 2], usrc[seq].ap()[:, :, t0:t0 + n + 2])
        xs = P.nxt("rw_xs")
        for j in range(6):
            P.ts(xs[:, j, 0:n], u6[:, j, 1:n + 1], pc(PR_M2 + j), ALU.mult)
            P.stt(xs[:, j, 0:n], u6[:, j, 0:n], pc(PR_MU0 + j), xs[:, j, 0:n], ALU.mult, ALU.add)
            P.stt(xs[:, j, 0:n], u6[:, j, 2:n + 2], pc(PR_MU1 + j), xs[:, j, 0:n], ALU.mult, ALU.add)
        yield
        r, k, v, wd, ad, gd = [xs[:, j, 0:n] for j in range(6)]
        o["r"] = r; o["v"] = v
        tw = P.nxt("rw_tw")[:, 0:n]
        P.act(tw, wd, AF.Tanh)
        sg = P.nxt("rw_sg")[:, 0:n]
        P.act(sg, gd, AF.Sigmoid)
        for d in range(2):
            hs = HS[d]
            P.mm(PS[2 + d][:, 0:n], wup[hs, :], tw[hs, :])
            P.mm(PS[4 + d][:, 0:n], aup[hs, :], ad[hs, :])
        P.mm(PS[2][:, 256:256 + n], gup[:], sg)
        for d in range(2):
            lw = P.nxt(f"rw_lw{d}")[:, 0:n]
            P.act(lw, PS[2 + d][:, 0:n], AF.Sigmoid, bias=pc(PR_W0 + d))
            ic = P.nxt(f"rw_ic{d}")[:, 0:n]
            P.act(ic, PS[4 + d][:, 0:n], AF.Sigmoid, bias=pc(PR_A0 + d))
            P.ts(lw, lw, -E05, ALU.mult)
            o[f"lw{d}"] = lw; o[f"ic{d}"] = ic
        g = P.nxt("rw_g")[:, 0:n]
        P.copy(g, PS[2][:, 256:256 + n], eng="act")
        o["g"] = g
        yield
        kx = P.nxt("rw_kx")[:, 0:n]
        P.ts(kx, k, pc(PR_KK), ALU.mult)
        sq = P.nxt("rw_sq")[:, 0:n]
        P.tt(sq, kx, kx, ALU.mult)
        P.mm(PS[3][:, 256:256 + n], blk[:], sq)
        rn = P.nxt("rw_rn")[:, 0:n]
        P.act(rn, PS[3][:, 256:256 + n], AF.Sqrt, bias=1e-12)
        P.recip(rn, rn)
        kk = P.nxt("rw_kk")[:, 0:n]
        P.tt(kk, kx, rn, ALU.mult)
        o["kk"] = kk
        for d in range(2):
            f1 = P.nxt("rw_f1")[:, 0:n]
            P.ts(f1, o[f"ic{d}"], pc(PR_KA), ALU.mult, pc(PR_1MKA), ALU.add)
            kd = P.nxt(f"rw_k{d}")[:, 0:n]
            P.tt(kd, k, f1, ALU.mult)
            bd = P.nxt(f"rw_b{d}")[:, 0:n]
            P.tt(bd, kk, o[f"ic{d}"], ALU.mult)
            o[f"k{d}"] = kd; o[f"b{d}"] = bd
        yield

    def pre_gen(d, o, n, X):
        nch = n // C
        lw = o[f"lw{d}"]
        cw = P.nxt("rw_cw")[:, 0:n]
        P.scan(cw, rst[:, 0:n], lw, 0.0, ALU.mult, ALU.add)
        if d == 1:
            cwb = P.nxt("rw_cwb")[:, 0:n]
            P.tt(cwb, lw, cw, ALU.subtract)
            for c in range(nch):
                P.ts(cwb[:, c * C:(c + 1) * C], cwb[:, c * C:(c + 1) * C], cw[:, c * C + C - 1:c * C + C], ALU.add)
            cw = cwb
        ep = P.nxt("rw_ep")[:, 0:n]; en = P.nxt("rw_en")[:, 0:n]; ea = P.nxt("rw_ea")[:, 0:n]
        P.act(ep, cw, AF.Exp)
        P.act(en, cw, AF.Exp, scale=-1.0)
        t1 = P.nxt("rw_t1")[:, 0:n]
        P.tt(t1, cw, lw, ALU.subtract)
        P.act(ea, t1, AF.Exp)
        At = P.nxt("rw_At")[:, 0:n]; Bt = P.nxt("rw_Bt")[:, 0:n]; Kt = P.nxt("rw_Kt")[:, 0:n]; Rt = P.nxt("rw_Rt")[:, 0:n]
        P.stt(At, o["kk"], -1.0, ea, ALU.mult, ALU.mult)
        P.tt(Bt, o[f"b{d}"], en, ALU.mult)
        P.tt(Kt, o[f"k{d}"], en, ALU.mult)
        P.tt(Rt, o["r"], ep, ALU.mult)
        v = o["v"]
        X.update(At=At, Rt=Rt, ep=ep)
        yield
        tok = P.nxt("rw_tok")
        for ci in range(nch):
            cs = slice(ci * C, (ci + 1) * C)
            bank = PS[ci // 2]; off = (ci % 2) * 192
            for j, Xm in enumerate((Bt, Kt, v)):
                for s in range(2):
                    P.mm(bank[HS[s], off + j * 64:off + (j + 1) * 64], Xm[HS[s], cs], ident[HS[s], HS[s]])
        for b in range((nch + 1) // 2):
            k2 = min(2, nch - 2 * b)
            P.copy(tok[:, 2 * b:2 * b + k2].rearrange("p a b c -> p (a b c)"), PS[b][:, 0:192 * k2], eng="act")
        yield
        M5 = P.nxt("rw_M5")
        for ci in range(nch):
            cs = slice(ci * C, (ci + 1) * C)
            for s in range(2):
                for j, (l, rr) in enumerate(((At, Bt), (Bt, At), (Kt, At), (Bt, Rt), (Kt, Rt))):
                    P.mm(PS[2 + ci][HS[s], j * 64:(j + 1) * 64], l[HS[s], cs], rr[HS[s], cs])
            P.tt(M5[:, ci].rearrange("p a b -> p (a b)"), PS[2 + ci][:, 0:320], mask5[d][:], ALU.mult)
        yield
        TT = P.nxt("rw_TT")
        for ci in range(nch):
            P.tt(TT[:, ci, :], M5[:, ci, 1, :], i2[:], ALU.add)
        Pm = [M5[:, ci, 0, :] for ci in range(nch)]; PTm = [M5[:, ci, 1, :] for ci in range(nch)]
        def sq_mm(Pm, PTm, both):
            for ci in range(nch):
                for s in range(2):
                    P.mm(PS[6][HS[s], ci * 128:ci * 128 + 64], PTm[ci][HS[s], :], Pm[ci][HS[s], :])
                    if both:
                        P.mm(PS[6][HS[s], ci * 128 + 64:ci * 128 + 128], Pm[ci][HS[s], :], PTm[ci][HS[s], :])

        def sq_evac():
            PP = P.nxt("rw_PP")
            P.copy(PP[:, 0:nch].rearrange("p a b -> p (a b)"), PS[6][:, 0:128 * nch], eng="act")
            return [PP[:, ci, 0:64] for ci in range(nch)], [PP[:, ci, 64:128] for ci in range(nch)]

        def tt_mm(Pm):
            for ci in range(nch):
                for s in range(2):
                    P.mm(PS[ci // 2][HS[s], 384 + (ci % 2) * 64:384 + (ci % 2) * 64 + 64], Pm[ci][HS[s], :], TT[HS[s], ci, :])

        def tt_evac():
            for b in range((nch + 1) // 2):
                k2 = min(2, nch - 2 * b)
                P.tt(TT[:, 2 * b:2 * b + k2].rearrange("p a b -> p (a b)"), TT[:, 2 * b:2 * b + k2].rearrange("p a b -> p (a b)"),
                     PS[b][:, 384:384 + 64 * k2], ALU.add)
        sq_mm(Pm, PTm, True)
        Pm, PTm = sq_evac()
        yield
        for lvl in range(2, 6):
            tt_mm(Pm)
            sq_mm(Pm, PTm, lvl < 5)
            Pn, PTn = sq_evac()
            tt_evac()
            Pm, PTm = Pn, PTn
            yield
        tt_mm(Pm)
        tt_evac()
        yield
        X.update(tok=tok, M5=M5, TT=TT)

    def chain_gen(d, n, X, fin):
        nch = n // C
        Sd = S[d]
        At, Rt, ep, tok, M5, TT = X["At"], X["Rt"], X["ep"], X["tok"], X["M5"], X["TT"]
        yT = P.nxt("rw_yT")[:, 0:n]
        wcol = (C - 1) if d == 0 else 0
        order = range(nch) if d == 0 else range(nch - 1, -1, -1)
        bank = PS[7]
        for c in order:
            cs = slice(c * C, (c + 1) * C)
            Btok, Ktok, Vtok = tok[:, c, 0, :], tok[:, c, 1, :], tok[:, c, 2, :]
            AkT, LrbT, LrkT = M5[:, c, 2, :], M5[:, c, 3, :], M5[:, c, 4, :]
            for s in range(2):
                hs = HS[s]
                P.mm(bank[hs, 0:64], At[hs, cs], Sd[hs, :], start=True, stop=False)
                P.mm(bank[hs, 0:64], AkT[hs, :], Vtok[hs, :], start=False, stop=True)
            Xs = P.nxt("rw_Xs")
            P.copy(Xs[:], bank[:, 0:64], eng="act")
            yield
            for s in range(2):
                hs = HS[s]
                P.mm(bank[hs, 64:128], TT[hs, c, :], Xs[hs, :])
            Us = P.nxt("rw_Us")
            P.copy(Us[:], bank[:, 64:128])
            yield
            for s in range(2):
                hs = HS[s]
                P.mm(bank[hs, 192:256], Sd[hs, :], Rt[hs, cs], start=True, stop=False)
                P.mm(bank[hs, 192:256], Us[hs, :], LrbT[hs, :], start=False, stop=False)
                P.mm(bank[hs, 192:256], Vtok[hs, :], LrkT[hs, :], start=False, stop=True)
                P.mm(bank[hs, 128:192], Btok[hs, :], Us[hs, :], start=True, stop=False)
                P.mm(bank[hs, 128:192], Ktok[hs, :], Vtok[hs, :], start=False, stop=True)
            P.copy(yT[:, cs], bank[:, 192:256], eng="act")
            P.tt(Stmp[:], Sd[:], bank[:, 128:192], ALU.add)
            P.ts(Sd[:], Stmp[:], ep[:, c * C + wcol:c * C + wcol + 1], ALU.mult)
            yield
        fin(yT)

    def blocks(seq, T):
        return [(seq, t0, min(TB, T - t0)) for t0 in range(0, T, TB)]
    fwd = blocks("c", Tc) + blocks("l", Tl)
    bwd = blocks("c", Tc)[::-1] + blocks("l", Tl)[::-1]
    P.memset(S[0][:], 0.0); P.memset(S[1][:], 0.0)

    def fin_fwd(seq, t0, n, o):
        def f(yT):
            P.dma(yf[seq].ap()[:, t0:t0 + n], yT)
        return f

    def fin_bwd(seq, t0, n, o):
        def f(yb):
            yfl = P.nxt("rw_yfl")[:, 0:n]
            P.dma(yfl, yf[seq].ap()[:, t0:t0 + n])
            y = P.nxt("rw_t1")[:, 0:n]
            P.tt(y, yb, yfl, ALU.add)
            P.mm(PS[2][:, 0:n], blk[:], y)
            yc = P.nxt("rw_t2")[:, 0:n]
            P.stt(yc, PS[2][:, 0:n], -1.0 / 64, y, ALU.mult, ALU.add)
            sq = P.nxt("rw_sq")[:, 0:n]
            P.tt(sq, yc, yc, ALU.mult)
            P.mm(PS[3][:, 0:n], blk[:], sq)
            sd = P.nxt("rw_rn")[:, 0:n]
            P.act(sd, PS[3][:, 0:n], AF.Sqrt, bias=LN_X_EPS, scale=1.0 / 64)
            P.recip(sd, sd)
            P.tt(yc, yc, sd, ALU.mult)
            ov = P.nxt("rw_t3")[:, 0:n]
            P.ts(ov, yc, pc(PR_LG), ALU.mult, pc(PR_LB), ALU.add)
            ks = P.nxt("rw_t4")[:, 0:n]
            P.tt(ks, o["k0"], o["k1"], ALU.add)
            P.stt(ks, ks, pc(PR_RK), o["r"], ALU.mult, ALU.mult)
            P.mm(PS[4][:, 0:n], blk[:], ks)
            P.tt(ks, PS[4][:, 0:n], o["v"], ALU.mult)
            P.tt(ov, ov, ks, ALU.add)
            P.tt(ov, ov, o["g"], ALU.mult)
            P.dma(odst[seq].ap()[:, t0:t0 + n], ov, is_out=True)
        return f

    for d, blist, finf in ((0, fwd, fin_fwd), (1, bwd, fin_bwd)):
        prev = None
        for (seq, t0, n) in blist:
            o = {}; X = {}

            def both(seq=seq, t0=t0, n=n, o=o, X=X, d=d):
                yield from prep_gen(seq, t0, n, o)
                yield from pre_gen(d, o, n, X)
            if _NO_ILV:
                _interleave([prev]); _interleave([both()])
            else:
                _interleave([prev, both()])
            prev = chain_gen(d, n, X, finf(seq, t0, n, o))
        _interleave([prev])
    return io


def _ilv_gen(gens):
    gens = [g for g in gens if g is not None]
    while gens:
        for g in list(gens):
            try:
                next(g)
                yield
            except StopIteration:
                gens.remove(g)


def ssd_build2(P, PS, Tc, Tl, TB=512):
    Q = SQ
    io = {}
    T_ = {"c": Tc, "l": Tl}
    for sq in ("c", "l"):
        T = T_[sq]
        io["x" + sq] = P.dram("sd_x" + sq, [2, 64, T + 2], F32, "ExternalInput")
        io["b" + sq] = P.dram("sd_b" + sq, [2, 128, T + 2], F32, "ExternalInput")
        io["c" + sq] = P.dram("sd_c" + sq, [2, 128, T + 2], F32, "ExternalInput")
        io["dt" + sq] = P.dram("sd_dt" + sq, [2, 128, T // Q, 2], F32, "ExternalInput")
        io["y" + sq] = P.dram("sd_y" + sq, [2, 128, T // Q, 64], F32, "ExternalOutput")
    io["px"] = P.dram("sd_px", [2, 64, 4], F32, "ExternalInput")
    io["pb"] = P.dram("sd_pb", [2, 128, 4], F32, "ExternalInput")
    io["pc"] = P.dram("sd_pc", [2, 128, 4], F32, "ExternalInput")
    io["pd"] = P.dram("sd_pd", [2, 128, 6], F32, "ExternalInput")
    io["tri"] = P.dram("c_tri", [2, 128, 128], F32, "ExternalInput")
    io["ident"] = P.dram("c_identb", [128, 128], F32, "ExternalInput")
    io["ones"] = P.dram("c_ones", [128, 128], F32, "ExternalInput")
    yfd = {sq: P.dram("sd_yf" + sq, [2, 128, T_[sq] // Q, 64], F32) for sq in ("c", "l")}

    def cst(nm, src, shape):
        t = P.sb("sk_" + nm, shape)
        P.dma(t[:], src)
        return t
    tri = [cst(f"tri{d}", io["tri"].ap()[d], [128, 128]) for d in range(2)]
    ident = cst("ident", io["ident"].ap(), [128, 128])
    ones = cst("ones", io["ones"].ap(), [128, 128])
    px = [cst(f"px{s}", io["px"].ap()[s], [64, 4]) for s in range(2)]
    pb = [cst(f"pb{s}", io["pb"].ap()[s], [128, 4]) for s in range(2)]
    pcc = [cst(f"pc{s}", io["pc"].ap()[s], [128, 4]) for s in range(2)]
    pd = [cst(f"pd{s}", io["pd"].ap()[s], [128, 6]) for s in range(2)]
    Aneg = [P.sb(f"sk_A{s}", [128, 2]) for s in range(2)]
    dsum = [P.sb(f"sk_ds{s}", [128, 1]) for s in range(2)]
    for s in range(2):
        P.act(Aneg[s][:], pd[s][:, 2:4], AF.Exp)
        P.ts(Aneg[s][:], Aneg[s][:], -1.0, ALU.mult)
        P.tt(dsum[s][:], pd[s][:, 4:5], pd[s][:, 5:6], ALU.add)

    NQ = TB // Q
    P.ring("sd_xin", 2, [64, TB + 2]); P.ring("sd_bin", 2, [128, TB + 2]); P.ring("sd_cin", 2, [128, TB + 2])
    P.ring("sd_xf", 4, [64, TB]); P.ring("sd_bf", 4, [128, TB]); P.ring("sd_cf", 4, [128, TB])
    P.ring("sd_dt", 8, [128, NQ, 2]); P.ring("sd_dta", 4, [128, NQ, 2]); P.ring("sd_tmp8", 8, [128, NQ, 2])
    P.ring("sd_xtok", 16, [128, 64]); P.ring("sd_btok", 16, [128, 128]); P.ring("sd_cbm", 32, [128, 128])
    P.ring("sd_bc", 8, [128, 128]); P.ring("sd_E", 16, [128, 128]); P.ring("sd_G", 16, [128, 128]); P.ring("sd_Cs", 16, [128, 128])
    P.ring("sd_col", 24, [128, 4]); P.ring("sd_xdt", 16, [128, 64]); P.ring("sd_xw", 16, [128, 64]); P.ring("sd_st", 16, [128, 64])
    P.ring("sd_yblk", 4, [128, NQ, 64]); P.ring("sd_yfl", 4, [128, NQ, 64])
    H = [[P.sb(f"sd_h{s}{d}", [128, 64]) for d in range(2)] for s in range(2)]
    psrr = [0]

    def psn():
        b = PS[psrr[0] % 8]; psrr[0] += 1
        return b

    def prep_gen(s, seq, t0, n, o):
        nq = n // Q
        for nm, ring_in, ring_f, src, par in (("x", "sd_xin", "sd_xf", io["x" + seq], px[s]),
                                              ("b", "sd_bin", "sd_bf", io["b" + seq], pb[s]),
                                              ("c", "sd_cin", "sd_cf", io["c" + seq], pcc[s])):
            tin = P.nxt(ring_in)
            P.dma(tin[:, 0:n + 2], src.ap()[s][:, t0:t0 + n + 2])
            tf = P.nxt(ring_f)[:, 0:n]
            P.ts(tf, tin[:, 1:n + 1], par[:, 1:2], ALU.mult, par[:, 3:4], ALU.add)
            P.stt(tf, tin[:, 0:n], par[:, 0:1], tf, ALU.mult, ALU.add)
            P.stt(tf, tin[:, 2:n + 2], par[:, 2:3], tf, ALU.mult, ALU.add)
            P.act(tf, tf, AF.Silu)
            o[nm] = tf
            yield
        dtr = P.nxt("sd_dt")[:, 0:nq, :]
        P.dma(dtr, io["dt" + seq].ap()[s][:, t0 // Q:t0 // Q + nq, :])
        dt = P.nxt("sd_dt")[:, 0:nq, :]
        dta = P.nxt("sd_dta")[:, 0:nq, :]
        xb = P.nxt("sd_tmp8")[:, 0:nq, :]
        for d in range(2):
            P.ts(xb[:, :, d:d + 1], dtr[:, :, d:d + 1], pd[s][:, d:d + 1], ALU.add)
        ab = P.nxt("sd_tmp8")[:, 0:nq, :]
        P.act(ab, xb, AF.Abs)
        P.act(ab, ab, AF.Exp, scale=-1.0)
        P.act(ab, ab, AF.Ln, bias=1.0)
        P.ts(xb, xb, 0.0, ALU.max)
        P.tt(dt, xb, ab, ALU.add)
        for d in range(2):
            P.ts(dta[:, :, d:d + 1], dt[:, :, d:d + 1], Aneg[s][:, d:d + 1], ALU.mult)
        o["dt"] = dt; o["dta"] = dta
        yield
        o["xtok"] = []; o["btok"] = []; o["cb"] = []
        for c in range(nq):
            cs = slice(c * Q, (c + 1) * Q)
            pa = psn()
            P.mm(pa[:, 0:64], o["x"][:, cs], ident[0:64, 0:64])
            P.mm(pa[:, 128:256], o["b"][:, cs], ident[:])
            P.mm(pa[:, 256:384], o["b"][:, cs], o["c"][:, cs])
            xt = P.nxt("sd_xtok"); bt = P.nxt("sd_btok")
            P.copy(xt[:], pa[:, 0:64], eng="act")
            P.copy(bt[:], pa[:, 128:256], eng="act")
            cbm = []
            for d in range(2):
                m = P.nxt("sd_cbm")
                P.tt(m[:], pa[:, 256:384], tri[d][:], ALU.mult)
                cbm.append(m)
            o["xtok"].append(xt); o["btok"].append(bt); o["cb"].append(cbm)
            yield

    def chunk_gen(s, d, o, c, R):
        cs = slice(c * Q, (c + 1) * Q)
        dta = o["dta"][:, c, d:d + 1]; dt = o["dt"][:, c, d:d + 1]
        last = Q - 1 if d == 0 else 0
        bc = P.nxt("sd_bc")
        P.ts(bc[:], ones[:], dta, ALU.mult)
        pd_ = psn()
        P.mm(pd_[:, 0:128], bc[:], tri[d][:])
        P.mm(pd_[:, 128:129], tri[d][:], dta)
        col = P.nxt("sd_col")
        P.copy(col[:, 0:1], pd_[:, 128:129])
        yield
        E = P.nxt("sd_E")
        P.ts(E[:], pd_[:, 0:128], col[:, 0:1], ALU.subtract, 0.0, ALU.min)
        P.ts(col[:, 1:2], col[:, 0:1], -1.0, ALU.mult, pd_[:, last:last + 1], ALU.add)
        Cs = P.nxt("sd_Cs")
        P.act(Cs[:], pd_[:, 0:128], AF.Exp)
        P.act(col[:, 2:3], pd_[:, last:last + 1], AF.Exp)
        yield
        P.act(E[:], E[:], AF.Exp)
        P.act(col[:, 1:2], col[:, 1:2], AF.Exp)
        P.tt(Cs[:], Cs[:], o["c"][:, cs], ALU.mult)
        xdt = P.nxt("sd_xdt")
        P.ts(xdt[:], o["xtok"][c][:], dt, ALU.mult)
        yield
        G = P.nxt("sd_G")
        P.tt(G[:], E[:], o["cb"][c][d][:], ALU.mult)
        P.tt(col[:, 1:2], col[:, 1:2], dt, ALU.mult)
        xw = P.nxt("sd_xw")
        P.ts(xw[:], o["xtok"][c][:], col[:, 1:2], ALU.mult)
        yield
        pg_ = psn()
        P.mm(pg_[:, 0:64], o["btok"][c][:], xw[:])
        st = P.nxt("sd_st")
        P.copy(st[:], pg_[:, 0:64], eng="act")
        R[c] = (G, xdt, Cs, col, st)

    def slot_gen(s, d, seq, t0, n):
        nq = n // Q
        o = {}; R = {}
        yield from prep_gen(s, seq, t0, n, o)
        yield from _ilv_gen([chunk_gen(s, d, o, c, R) for c in range(nq)])
        yb = P.nxt("sd_yblk")
        h = H[s][d]
        order = range(nq) if d == 0 else range(nq - 1, -1, -1)
        for c in order:
            G, xdt, Cs, col, st = R[c]
            pf = psn()
            P.mm(pf[:, 0:64], G[:], xdt[:], start=True, stop=False)
            P.mm(pf[:, 0:64], Cs[:], h[:], start=False, stop=True)
            P.copy(yb[:, c, :], pf[:, 0:64], eng="act")
            P.stt(h[:], h[:], col[:, 2:3], st[:], ALU.mult, ALU.add)
            yield
        if d == 0:
            P.dma(yfd[seq].ap()[s][:, t0 // Q:t0 // Q + nq, :], yb[:, 0:nq, :])
        else:
            yfl = P.nxt("sd_yfl")
            P.dma(yfl[:, 0:nq, :], yfd[seq].ap()[s][:, t0 // Q:t0 // Q + nq, :])
            P.tt(yb[:, 0:nq, :], yb[:, 0:nq, :], yfl[:, 0:nq, :], ALU.add)
            for c in range(nq):
                P.stt(yb[:, c, :], o["xtok"][c][:], dsum[s][:, 0:1], yb[:, c, :], ALU.mult, ALU.add)
            P.dma(io["y" + seq].ap()[s][:, t0 // Q:t0 // Q + nq, :], yb[:, 0:nq, :], is_out=True)

    def blocks(seq, T):
        return [(seq, t0, min(TB, T - t0)) for t0 in range(0, T, TB)]
    fwd = blocks("c", Tc) + blocks("l", Tl)
    bwd = blocks("c", Tc)[::-1] + blocks("l", Tl)[::-1]
    for s in range(2):
        for d in range(2):
            P.memset(H[s][d][:], 0.0)
    for d, blist in ((0, fwd), (1, bwd)):
        for (seq, t0, n) in blist:
            _interleave([slot_gen(0, d, seq, t0, n), slot_gen(1, d, seq, t0, n)])
    return io
```

```python
import math
from contextlib import ExitStack
import numpy as np
import ml_dtypes
import concourse.bass as bass
import concourse.mybir as mybir
from concourse.bass_utils import run_bass_kernel_spmd

F32 = mybir.dt.float32
BF16 = mybir.dt.bfloat16
AF = mybir.ActivationFunctionType
ALU = mybir.AluOpType
AX = mybir.AxisListType

COMPUTE = ("pe", "act", "dve", "pool")
STRICT_SYNC = False


class _Ins:
    __slots__ = ("eng", "fn", "waits", "is_dma", "tok", "idx", "need_inc")

    def __init__(self, eng, fn, is_dma):
        self.eng, self.fn, self.is_dma = eng, fn, is_dma
        self.waits = []
        self.tok = None
        self.need_inc = False


class _Trk:
    __slots__ = ("writers", "readers", "const")

    def __init__(self):
        self.writers = []
        self.readers = {}
        self.const = False


class Prog:
    def __init__(self, n_dma_sems=24):
        self.nc = bass.Bass("TRN2", target_bir_lowering=False)
        self.es = ExitStack()
        self.ins = []
        self.trk = {}
        self.n_dma_sems = n_dma_sems
        self.dma_rr = {"sp": 0, "pool": 0, "act": 0}
        self.dma_prev = {}
        self.out_dmas = []
        self._uid = 0
        self.rings = {}
        self.psum_names = set()
        self.debug = False
        self.dbg_out = {}

    def sb(self, name, shape, dtype=F32):
        t = self.es.enter_context(self.nc.sbuf_tensor(name, list(shape), dtype))
        return t

    def ps(self, name, shape, dtype=F32):
        t = self.es.enter_context(self.nc.psum_tensor(name, list(shape), dtype))
        self.psum_names.add(name)
        return t

    def dram(self, name, shape, dtype=F32, kind="Internal"):
        t = self.nc.dram_tensor(name, list(shape), dtype, kind=kind)
        if kind == "ExternalInput":
            self._t(name).const = True
        return t

    def ring(self, name, n, shape, dtype=F32, psum=False):
        tiles = [(self.ps if psum else self.sb)(f"{name}{i}", shape, dtype) for i in range(n)]
        self.rings[name] = [tiles, 0]
        return name

    def nxt(self, name):
        r = self.rings[name]
        t = r[0][r[1] % len(r[0])]
        r[1] += 1
        return t

    def _t(self, name):
        t = self.trk.get(name)
        if t is None:
            t = self.trk[name] = _Trk()
        return t

    def _rec(self, eng, fn, reads, writes, is_dma=False):
        x = _Ins(eng, fn, is_dma)
        x.idx = len(self.ins)
        deps = []
        rn = []
        for a in reads:
            if a is None or isinstance(a, (int, float)):
                continue
            rn.append(a.tensor.name)
        wn = [a.tensor.name for a in writes]
        raw = set()
        for n in rn:
            deps.extend(self._t(n).writers)
            for w_ in self._t(n).writers:
                raw.add(w_.idx)
            if n in self.psum_names:
                deps.extend(r_ for k_, r_ in self._t(n).readers.items() if k_ != eng)
        for n in wn:
            t = self._t(n)
            deps.extend(t.writers)
            deps.extend(t.readers.values())
        seen = set()
        for d in deps:
            if d.idx in seen:
                continue
            seen.add(d.idx)
            if (not d.is_dma) and (not is_dma) and d.eng == eng:
                if eng == "pe" or (d.idx not in raw and not STRICT_SYNC):
                    continue
            x.waits.append(d)
            d.need_inc = True
        for n in rn:
            t = self._t(n)
            if t.const:
                continue
            key = ("dma", x.idx) if is_dma else eng
            t.readers[key] = x
            if len(t.readers) > 48:
                ks = [k for k in t.readers if isinstance(k, tuple)]
                pass
        for n in wn:
            t = self._t(n)
            t.writers = [x]
            t.readers = {}
        self.ins.append(x)
        return x

    def mm(self, out, lhsT, rhs, start=True, stop=True):
        rd = [lhsT, rhs] + ([] if start else [out])
        return self._rec("pe", lambda e: e.matmul(out, lhsT, rhs, start=start, stop=stop), rd, [out])

    def transpose(self, out, in_, ident):
        return self._rec("pe", lambda e: e.transpose(out, in_, ident), [in_, ident], [out])

    def act(self, out, in_, func, bias=None, scale=None, accum=None, eng="act"):
        kw = {}
        if bias is not None:
            kw["bias"] = bias
        if scale is not None:
            kw["scale"] = scale
        if accum is not None:
            kw["accum_out"] = accum
        rd = [in_, bias if hasattr(bias, "tensor") else None, scale if hasattr(scale, "tensor") else None]
        wr = [out] + ([accum] if accum is not None else [])
        return self._rec("act", lambda e: e.activation(out, in_, func, **kw), rd, wr)

    def tt(self, out, a, b, op, eng="dve"):
        return self._rec(eng, lambda e: e.tensor_tensor(out, a, b, op), [a, b], [out])

    def ts(self, out, a, s1, op0, s2=None, op1=None, eng="dve", accum=None):
        rd = [a, s1 if hasattr(s1, "tensor") else None, s2 if hasattr(s2, "tensor") else None]
        if op1 is None:
            return self._rec(eng, lambda e: e.tensor_scalar(out, a, s1, None, op0), rd, [out])
        kw = {}
        wr = [out]
        if accum is not None:
            kw["accum_out"] = accum
            wr.append(accum)
        return self._rec(eng, lambda e: e.tensor_scalar(out, a, s1, s2, op0, op1, **kw), rd, wr)

    def stt(self, out, in0, scalar, in1, op0, op1, eng="dve"):
        rd = [in0, in1, scalar if hasattr(scalar, "tensor") else None]
        return self._rec(eng, lambda e: e.scalar_tensor_tensor(out, in0, scalar, in1, op0, op1), rd, [out])

    def copy(self, out, in_, eng="dve"):
        if eng == "act":
            return self._rec("act", lambda e: e.copy(out, in_), [in_], [out])
        return self._rec(eng, lambda e: e.tensor_copy(out, in_), [in_], [out])

    def memset(self, out, val, eng="dve"):
        return self._rec(eng, lambda e: e.memset(out, val), [], [out])

    def reduce(self, out, in_, op, axis=AX.X, eng="dve"):
        return self._rec(eng, lambda e: e.tensor_reduce(out, in_, axis, op), [in_], [out])

    def scan(self, out, d0, d1, init, op0, op1):
        return self._rec("dve", lambda e: e.tensor_tensor_scan(out, d0, d1, init, op0, op1), [d0, d1], [out])

    def recip(self, out, in_):
        return self._rec("dve", lambda e: e.reciprocal(out, in_), [in_], [out])

    def dma(self, out, in_, q="sp", is_out=False):
        x = self._rec(q, lambda e: e.dma_start(out=out, in_=in_), [in_], [out], is_dma=True)
        k = self.dma_rr[q] % self.n_dma_sems
        self.dma_rr[q] += 1
        key = (q, k)
        prev = self.dma_prev.get(key)
        val = (prev.tok[1] if prev is not None else 0) + 16
        x.tok = (key, val)
        if prev is not None and prev not in x.waits:
            x.waits.append(prev)
        self.dma_prev[key] = x
        if is_out:
            self.out_dmas.append(x)
        return x

    def dbg(self, name, ap):
        if not getattr(self, "debug", False):
            return
        if name in self.dbg_out:
            return
        t = self.dram("dbg_" + name, list(ap.shape), F32, "ExternalOutput")
        self.dbg_out[name] = t
        self.dma(t.ap(), ap, is_out=True)

    def finish(self):
        nc = self.nc
        engs = {"pe": [], "act": [], "dve": [], "pool": [], "sp": []}
        for x in self.ins:
            engs[x.eng].append(x)
        sems = {}
        for e in COMPUTE:
            sems[e] = self.es.enter_context(nc.semaphore(f"s_{e}"))
            c = 0
            for x in engs[e]:
                if x.is_dma:
                    continue
                if x.need_inc:
                    c += 1
                    x.tok = (e, c)
        for q in ("sp", "pool", "act"):
            for k in range(self.n_dma_sems):
                if (q, k) in self.dma_prev:
                    sems[(q, k)] = self.es.enter_context(nc.semaphore(f"d_{q}{k}"))
        block = self.es.enter_context(nc.Block())
        out_dmas = self.out_dmas

        def emit(engname):
            def body(e):
                waited = {}
                for x in engs[engname]:
                    for d in x.waits:
                        s, v = d.tok
                        if waited.get(s, 0) >= v:
                            continue
                        waited[s] = v
                        e.wait_ge(sems[s], v)
                    r = x.fn(e)
                    if x.is_dma:
                        r.then_inc(sems[x.tok[0]], 16)
                    elif x.need_inc:
                        r.then_inc(sems[x.tok[0]], 1)
                if engname == "sp":
                    for d in out_dmas:
                        s, v = d.tok
                        if waited.get(s, 0) >= v:
                            continue
                        waited[s] = v
                        e.wait_ge(sems[s], v)
            return body

        block.sync(emit("sp"))
        block.tensor(emit("pe"))
        block.scalar(emit("act"))
        block.vector(emit("dve"))
        block.gpsimd(emit("pool"))
        self.es.close()
        return nc


RW_C = 64
E05 = math.exp(-0.5)
LN_X_EPS = 64e-5
PR_MU0, PR_MU1, PR_M2, PR_W0, PR_A0, PR_KK, PR_KA, PR_1MKA, PR_RK, PR_LG, PR_LB = 0, 6, 12, 18, 20, 22, 23, 24, 25, 26, 27
PR_N = 28


def rwkv_consts():
    c = {}
    blk = np.zeros((128, 128), np.float32); blk[:64, :64] = 1; blk[64:, 64:] = 1
    c["blk64"] = blk
    c["ident"] = np.eye(128, dtype=np.float32)
    i2 = np.zeros((128, 64), np.float32); i2[:64] = np.eye(64); i2[64:] = np.eye(64)
    c["i2"] = i2
    p = np.arange(64)[:, None]; f = np.arange(64)[None, :]
    m = np.zeros((2, 64, 5, 64), np.float32)
    m[0, :, 0] = f < p; m[0, :, 1] = f > p; m[0, :, 2] = f > p; m[0, :, 3] = f >= p; m[0, :, 4] = f >= p
    m[1, :, 0] = f > p; m[1, :, 1] = f < p; m[1, :, 2] = f < p; m[1, :, 3] = f <= p; m[1, :, 4] = f <= p
    c["mask5"] = np.concatenate([m, m], axis=1).reshape(2, 128, 320)
    return c


def rwkv_build(P, PS, Tc, Tl, TB=512):
    C = RW_C
    io = {}
    io["uc"] = P.dram("rw_uc", [128, 6, Tc + 2], F32, "ExternalInput")
    io["ul"] = P.dram("rw_ul", [128, 6, Tl + 2], F32, "ExternalInput")
    io["pr"] = P.dram("rw_pr", [128, PR_N], F32, "ExternalInput")
    io["wup"] = P.dram("rw_wup", [128, 128], F32, "ExternalInput")
    io["aup"] = P.dram("rw_aup", [128, 128], F32, "ExternalInput")
    io["gup"] = P.dram("rw_gup", [128, 128], F32, "ExternalInput")
    io["blk64"] = P.dram("c_blk64", [128, 128], F32, "ExternalInput")
    io["ident"] = P.dram("c_ident", [128, 128], F32, "ExternalInput")
    io["i2"] = P.dram("c_i2", [128, 64], F32, "ExternalInput")
    io["mask5"] = P.dram("c_mask5", [2, 128, 320], F32, "ExternalInput")
    io["oc"] = P.dram("rw_oc", [128, Tc], F32, "ExternalOutput")
    io["ol"] = P.dram("rw_ol", [128, Tl], F32, "ExternalOutput")
    yf = {"c": P.dram("rw_yfc", [128, Tc], F32), "l": P.dram("rw_yfl", [128, Tl], F32)}
    usrc = {"c": io["uc"], "l": io["ul"]}
    odst = {"c": io["oc"], "l": io["ol"]}

    def cst(name, shape):
        t = P.sb("k_" + name, shape)
        P.dma(t[:], io[name].ap())
        return t
    pr = cst("pr", [128, PR_N]); wup = cst("wup", [128, 128]); aup = cst("aup", [128, 128]); gup = cst("gup", [128, 128])
    blk = cst("blk64", [128, 128]); ident = cst("ident", [128, 128]); i2 = cst("i2", [128, 64])
    mask5 = []
    for d in range(2):
        t = P.sb(f"k_mask5_{d}", [128, 320])
        P.dma(t[:], io["mask5"].ap()[d])
        mask5.append(t)
    rst = P.sb("k_rst", [128, TB])
    P.memset(rst[:], 1.0)
    P.memset(rst[:].rearrange("p (c t) -> p c t", t=C)[:, :, 0:1], 0.0)
    pc = lambda i: pr[:, i:i + 1]
    P.ts(pr[:, PR_M2:PR_M2 + 6], pr[:, PR_MU0:PR_MU0 + 6], -1.0, ALU.mult, 1.0, ALU.add)
    P.tt(pr[:, PR_M2:PR_M2 + 6], pr[:, PR_M2:PR_M2 + 6], pr[:, PR_MU1:PR_MU1 + 6], ALU.subtract)
    P.ts(pr[:, PR_1MKA:PR_1MKA + 1], pr[:, PR_KA:PR_KA + 1], -1.0, ALU.mult, 1.0, ALU.add)

    for nm in ("u6",):
        P.ring("rw_" + nm, 1, [128, 6, TB + 2])
    for nm in ("xs", ):
        P.ring("rw_" + nm, 1, [128, 6, TB])
    for nm in ("tw", "sg", "kx", "rn", "kk", "g", "lw0", "lw1", "ic0", "ic1", "k0", "k1", "b0", "b1"):
        P.ring("rw_" + nm, 1, [128, TB])
    for nm in ("sq", "f1", "t1"):
        P.ring("rw_" + nm, 2, [128, TB])
    for nm in ("cw", "cwb", "ep", "en", "ea", "At", "Bt", "Kt", "Rt", "yT", "t2", "t3", "t4", "yfl"):
        P.ring("rw_" + nm, 1, [128, TB])
    P.ring("rw_tok", 10, [128, 3, 64])
    P.ring("rw_M5", 10, [128, 5, 64])
    P.ring("rw_TT", 10, [128, 64])
    P.ring("rw_PP", 4, [128, 128])
    P.ring("rw_Xs", 3, [128, 64])
    P.ring("rw_Us", 3, [128, 64])
    S = [P.sb("rw_S0", [128, 64]), P.sb("rw_S1", [128, 64])]
    Stmp = P.sb("rw_Stmp", [128, 64])
    ps_tok, ps5, psL, psT, psX, psU, psY, psS = PS

    def prep(seq, t0, n):
        u6 = P.nxt("rw_u6")
        P.dma(u6[:, :, 0:n + 2], usrc[seq].ap()[:, :, t0:t0 + n + 2])
        xs = P.nxt("rw_xs")
        P.dbg("u6", u6[:, 0, 0:n + 2]); P.dbg("pr", pr[:])
        for j in range(6):
            P.ts(xs[:, j, 0:n], u6[:, j, 1:n + 1], pc(PR_M2 + j), ALU.mult)
            P.stt(xs[:, j, 0:n], u6[:, j, 0:n], pc(PR_MU0 + j), xs[:, j, 0:n], ALU.mult, ALU.add)
            P.stt(xs[:, j, 0:n], u6[:, j, 2:n + 2], pc(PR_MU1 + j), xs[:, j, 0:n], ALU.mult, ALU.add)
        r, k, v, wd, ad, gd = [xs[:, j, 0:n] for j in range(6)]
        o = {"r": r, "v": v}
        tw = P.nxt("rw_tw")[:, 0:n]
        P.act(tw, wd, AF.Tanh)
        for d in range(2):
            hs = slice(64 * d, 64 * d + 64)
            P.mm(psL[:, 0:n], wup[hs, :], tw[hs, :])
            lw = P.nxt(f"rw_lw{d}")[:, 0:n]
            P.act(lw, psL[:, 0:n], AF.Sigmoid, bias=pc(PR_W0 + d))
            P.ts(lw, lw, -E05, ALU.mult)
            o[f"lw{d}"] = lw
            P.mm(psT[:, 0:n], aup[hs, :], ad[hs, :])
            ic = P.nxt(f"rw_ic{d}")[:, 0:n]
            P.act(ic, psT[:, 0:n], AF.Sigmoid, bias=pc(PR_A0 + d))
            o[f"ic{d}"] = ic
        kx = P.nxt("rw_kx")[:, 0:n]
        P.ts(kx, k, pc(PR_KK), ALU.mult)
        sq = P.nxt("rw_sq")[:, 0:n]
        P.tt(sq, kx, kx, ALU.mult)
        P.mm(psX[:, 0:n], blk[:], sq)
        rn = P.nxt("rw_rn")[:, 0:n]
        P.act(rn, psX[:, 0:n], AF.Sqrt, bias=1e-12)
        P.recip(rn, rn)
        kk = P.nxt("rw_kk")[:, 0:n]
        P.tt(kk, kx, rn, ALU.mult)
        o["kk"] = kk
        for d in range(2):
            f1 = P.nxt("rw_f1")[:, 0:n]
            P.ts(f1, o[f"ic{d}"], pc(PR_KA), ALU.mult, pc(PR_1MKA), ALU.add)
            kd = P.nxt(f"rw_k{d}")[:, 0:n]
            P.tt(kd, k, f1, ALU.mult)
            bd = P.nxt(f"rw_b{d}")[:, 0:n]
            P.tt(bd, kk, o[f"ic{d}"], ALU.mult)
            o[f"k{d}"] = kd
            o[f"b{d}"] = bd
        sg = P.nxt("rw_sg")[:, 0:n]
        P.act(sg, gd, AF.Sigmoid)
        P.mm(psU[:, 0:n], gup[:], sg)
        g = P.nxt("rw_g")[:, 0:n]
        P.copy(g, psU[:, 0:n], eng="act")
        o["g"] = g
        for kname in ("r", "v", "kk", "lw0", "lw1", "ic0", "k0", "b0", "g"):
            P.dbg("p_" + kname, o[kname])
        return o

    def scan_block(d, o, n):
        nch = n // C
        lw = o[f"lw{d}"]
        cw = P.nxt("rw_cw")[:, 0:n]
        P.scan(cw, rst[:, 0:n], lw, 0.0, ALU.mult, ALU.add)
        v3 = lambda a: a.rearrange("p (c t) -> p c t", t=C)
        if d == 1:
            cwb = P.nxt("rw_cwb")[:, 0:n]
            P.tt(cwb, lw, cw, ALU.subtract)
            for c in range(nch):
                P.ts(cwb[:, c * C:(c + 1) * C], cwb[:, c * C:(c + 1) * C], cw[:, c * C + C - 1:c * C + C], ALU.add)
            cw = cwb
        ep = P.nxt("rw_ep")[:, 0:n]; en = P.nxt("rw_en")[:, 0:n]; ea = P.nxt("rw_ea")[:, 0:n]
        P.act(ep, cw, AF.Exp)
        P.act(en, cw, AF.Exp, scale=-1.0)
        t1 = P.nxt("rw_t1")[:, 0:n]
        P.tt(t1, cw, lw, ALU.subtract)
        P.act(ea, t1, AF.Exp)
        At = P.nxt("rw_At")[:, 0:n]; Bt = P.nxt("rw_Bt")[:, 0:n]; Kt = P.nxt("rw_Kt")[:, 0:n]; Rt = P.nxt("rw_Rt")[:, 0:n]
        P.stt(At, o["kk"], -1.0, ea, ALU.mult, ALU.mult)
        P.tt(Bt, o[f"b{d}"], en, ALU.mult)
        P.tt(Kt, o[f"k{d}"], en, ALU.mult)
        P.tt(Rt, o["r"], ep, ALU.mult)
        wcol = (C - 1) if d == 0 else 0
        v = o["v"]
        P.dbg(f"cw{d}", cw); P.dbg(f"At{d}", At); P.dbg(f"Bt{d}", Bt); P.dbg(f"Rt{d}", Rt)
        pre = []
        for c in range(nch):
            cs = slice(c * C, (c + 1) * C)
            tok = P.nxt("rw_tok")
            for j, X in enumerate((Bt, Kt, v)):
                for s in range(2):
                    hs = slice(64 * s, 64 * s + 64)
                    P.mm(ps_tok[hs, j * 64:(j + 1) * 64], X[hs, cs], ident[hs, hs])
            P.copy(tok[:].rearrange("p a b -> p (a b)"), ps_tok[:, 0:192], eng="act")
            M5 = P.nxt("rw_M5")
            for s in range(2):
                hs = slice(64 * s, 64 * s + 64)
                for j, (l, rr) in enumerate(((At, Bt), (Bt, At), (Kt, At), (Bt, Rt), (Kt, Rt))):
                    P.mm(ps5[hs, j * 64:(j + 1) * 64], l[hs, cs], rr[hs, cs])
            P.tt(M5[:].rearrange("p a b -> p (a b)"), ps5[:, 0:320], mask5[d][:], ALU.mult)
            TT = P.nxt("rw_TT")
            P.tt(TT[:], M5[:, 1, :], i2[:], ALU.add)
            Pm, PTm = M5[:, 0, :], M5[:, 1, :]
            for lvl in range(1, 6):
                for s in range(2):
                    hs = slice(64 * s, 64 * s + 64)
                    P.mm(psL[hs, 0:64], PTm[hs, :], Pm[hs, :])
                    if lvl < 5:
                        P.mm(psL[hs, 64:128], Pm[hs, :], PTm[hs, :])
                PP = P.nxt("rw_PP")
                w = 128 if lvl < 5 else 64
                P.copy(PP[:, 0:w], psL[:, 0:w], eng="act")
                Pm, PTm = PP[:, 0:64], PP[:, 64:128]
                for s in range(2):
                    hs = slice(64 * s, 64 * s + 64)
                    P.mm(psT[hs, 0:64], Pm[hs, :], TT[hs, :])
                P.tt(TT[:], TT[:], psT[:, 0:64], ALU.add)
            pre.append((tok, M5, TT))
            if c == 0:
                P.dbg(f"tok{d}", tok[:].rearrange("p a b -> p (a b)")); P.dbg(f"M5{d}", M5[:].rearrange("p a b -> p (a b)")); P.dbg(f"TT{d}", TT[:])
        Sd = S[d]
        order = range(nch) if d == 0 else range(nch - 1, -1, -1)
        for c in order:
            cs = slice(c * C, (c + 1) * C)
            tok, M5, TT = pre[c]
            Btok, Ktok, Vtok = tok[:, 0, :], tok[:, 1, :], tok[:, 2, :]
            AkT, LrbT, LrkT = M5[:, 2, :], M5[:, 3, :], M5[:, 4, :]
            for s in range(2):
                hs = slice(64 * s, 64 * s + 64)
                P.mm(psX[hs, 0:64], At[hs, cs], Sd[hs, :], start=True, stop=False)
                P.mm(psX[hs, 0:64], AkT[hs, :], Vtok[hs, :], start=False, stop=True)
            Xs = P.nxt("rw_Xs")
            P.copy(Xs[:], psX[:, 0:64], eng="act")
            for s in range(2):
                hs = slice(64 * s, 64 * s + 64)
                P.mm(psU[hs, 0:64], TT[hs, :], Xs[hs, :])
            Us = P.nxt("rw_Us")
            P.copy(Us[:], psU[:, 0:64])
            for s in range(2):
                hs = slice(64 * s, 64 * s + 64)
                P.mm(psY[hs, cs], Sd[hs, :], Rt[hs, cs], start=True, stop=False)
                P.mm(psY[hs, cs], Us[hs, :], LrbT[hs, :], start=False, stop=False)
                P.mm(psY[hs, cs], Vtok[hs, :], LrkT[hs, :], start=False, stop=True)
                P.mm(psS[hs, 0:64], Btok[hs, :], Us[hs, :], start=True, stop=False)
                P.mm(psS[hs, 0:64], Ktok[hs, :], Vtok[hs, :], start=False, stop=True)
            P.tt(Stmp[:], Sd[:], psS[:, 0:64], ALU.add)
            P.ts(Sd[:], Stmp[:], ep[:, c * C + wcol:c * C + wcol + 1], ALU.mult)
        yT = P.nxt("rw_yT")[:, 0:n]
        P.copy(yT, psY[:, 0:n], eng="act")
        P.dbg(f"yT{d}", yT)
        return yT

    def blocks(seq, T):
        return [(seq, t0, min(TB, T - t0)) for t0 in range(0, T, TB)]
    fwd = blocks("c", Tc) + blocks("l", Tl)
    bwd = blocks("c", Tc)[::-1] + blocks("l", Tl)[::-1]
    P.memset(S[0][:], 0.0); P.memset(S[1][:], 0.0)
    for (seq, t0, n) in fwd:
        o = prep(seq, t0, n)
        yT = scan_block(0, o, n)
        P.dma(yf[seq].ap()[:, t0:t0 + n], yT)
    for (seq, t0, n) in bwd:
        o = prep(seq, t0, n)
        yb = scan_block(1, o, n)
        yfl = P.nxt("rw_yfl")[:, 0:n]
        P.dma(yfl, yf[seq].ap()[:, t0:t0 + n])
        y = P.nxt("rw_t1")[:, 0:n]
        P.tt(y, yb, yfl, ALU.add)
        P.mm(psL[:, 0:n], blk[:], y)
        yc = P.nxt("rw_t2")[:, 0:n]
        P.stt(yc, psL[:, 0:n], -1.0 / 64, y, ALU.mult, ALU.add)
        sq = P.nxt("rw_sq")[:, 0:n]
        P.tt(sq, yc, yc, ALU.mult)
        P.mm(psT[:, 0:n], blk[:], sq)
        sd = P.nxt("rw_rn")[:, 0:n]
        P.act(sd, psT[:, 0:n], AF.Sqrt, bias=LN_X_EPS, scale=1.0 / 64)
        P.recip(sd, sd)
        P.tt(yc, yc, sd, ALU.mult)
        ov = P.nxt("rw_t3")[:, 0:n]
        P.ts(ov, yc, pc(PR_LG), ALU.mult, pc(PR_LB), ALU.add)
        ks = P.nxt("rw_t4")[:, 0:n]
        P.tt(ks, o["k0"], o["k1"], ALU.add)
        P.stt(ks, ks, pc(PR_RK), o["r"], ALU.mult, ALU.mult)
        P.mm(psX[:, 0:n], blk[:], ks)
        P.tt(ks, psX[:, 0:n], o["v"], ALU.mult)
        P.tt(ov, ov, ks, ALU.add)
        P.tt(ov, ov, o["g"], ALU.mult)
        P.dma(odst[seq].ap()[:, t0:t0 + n], ov, is_out=True)
    return io


def rwkv_host_inputs(rw_c, rw_l, p, heads):
    W = 640
    cols = []
    for base in (0, W, 2 * W):
        cols.append(np.concatenate([np.arange(base + h * 64, base + h * 64 + 64) for h in heads]))
    cols.append(np.arange(3 * W, 3 * W + 128)); cols.append(np.arange(3 * W + 128, 3 * W + 256)); cols.append(np.arange(3 * W + 256, 3 * W + 384))
    cols = np.stack(cols)

    def lay(u):
        T = u.shape[0]
        o = np.zeros((128, 6, T + 2), np.float32)
        o[:, :, 1:T + 1] = u[:, cols].transpose(2, 1, 0)
        return o
    hc = np.concatenate([np.arange(h * 64, h * 64 + 64) for h in heads])
    pr = np.zeros((128, PR_N), np.float32)
    mu = p["shift_mu"]
    pr[:, PR_MU0:PR_MU0 + 6] = mu[0][cols].T
    pr[:, PR_MU1:PR_MU1 + 6] = mu[1][cols].T
    pr[:, PR_W0:PR_W0 + 2] = p["w0"][:, hc].T
    pr[:, PR_A0:PR_A0 + 2] = p["a0"][:, hc].T
    pr[:, PR_KK] = p["k_k"][hc]; pr[:, PR_KA] = p["k_a"][hc]
    pr[:, PR_RK] = p["r_k"].reshape(-1)[hc]; pr[:, PR_LG] = p["lnx_g"][hc]; pr[:, PR_LB] = p["lnx_b"][hc]
    d = {"rw_uc": lay(rw_c), "rw_ul": lay(rw_l), "rw_pr": pr,
         "rw_wup": np.ascontiguousarray(p["w_up"][:, :, hc].reshape(128, 128)),
         "rw_aup": np.ascontiguousarray(p["a_up"][:, :, hc].reshape(128, 128)),
         "rw_gup": np.ascontiguousarray(p["g_up"][:, hc])}
    for k, v in rwkv_consts().items():
        d["c_" + k] = v
    return d


SQ = 128


def ssd_consts():
    p = np.arange(128)[:, None]; f = np.arange(128)[None, :]
    tri = np.stack([(f >= p), (f <= p)]).astype(np.float32)
    return {"tri": tri, "ident": np.eye(128, dtype=np.float32), "ones": np.ones((128, 128), np.float32)}


def ssd_build(P, PS, Tc, Tl, TB=512):
    Q = SQ
    io = {}
    T_ = {"c": Tc, "l": Tl}
    for sq in ("c", "l"):
        T = T_[sq]
        io["x" + sq] = P.dram("sd_x" + sq, [2, 64, T + 2], F32, "ExternalInput")
        io["b" + sq] = P.dram("sd_b" + sq, [2, 128, T + 2], F32, "ExternalInput")
        io["c" + sq] = P.dram("sd_c" + sq, [2, 128, T + 2], F32, "ExternalInput")
        io["dt" + sq] = P.dram("sd_dt" + sq, [2, 128, T // Q, 2], F32, "ExternalInput")
        io["y" + sq] = P.dram("sd_y" + sq, [2, 128, T // Q, 64], F32, "ExternalOutput")
    io["px"] = P.dram("sd_px", [2, 64, 4], F32, "ExternalInput")
    io["pb"] = P.dram("sd_pb", [2, 128, 4], F32, "ExternalInput")
    io["pc"] = P.dram("sd_pc", [2, 128, 4], F32, "ExternalInput")
    io["pd"] = P.dram("sd_pd", [2, 128, 6], F32, "ExternalInput")
    io["tri"] = P.dram("c_tri", [2, 128, 128], F32, "ExternalInput")
    io["ident"] = P.dram("c_identb", [128, 128], F32, "ExternalInput")
    io["ones"] = P.dram("c_ones", [128, 128], F32, "ExternalInput")
    yfd = {sq: P.dram("sd_yf" + sq, [2, 128, T_[sq] // Q, 64], F32) for sq in ("c", "l")}

    def cst(nm, src, shape):
        t = P.sb("sk_" + nm, shape)
        P.dma(t[:], src)
        return t
    tri = [cst(f"tri{d}", io["tri"].ap()[d], [128, 128]) for d in range(2)]
    ident = cst("ident", io["ident"].ap(), [128, 128])
    ones = cst("ones", io["ones"].ap(), [128, 128])
    px = [cst(f"px{s}", io["px"].ap()[s], [64, 4]) for s in range(2)]
    pb = [cst(f"pb{s}", io["pb"].ap()[s], [128, 4]) for s in range(2)]
    pcc = [cst(f"pc{s}", io["pc"].ap()[s], [128, 4]) for s in range(2)]
    pd = [cst(f"pd{s}", io["pd"].ap()[s], [128, 6]) for s in range(2)]
    Aneg = [P.sb(f"sk_A{s}", [128, 2]) for s in range(2)]
    dsum = [P.sb(f"sk_ds{s}", [128, 1]) for s in range(2)]
    for s in range(2):
        P.act(Aneg[s][:], pd[s][:, 2:4], AF.Exp)
        P.ts(Aneg[s][:], Aneg[s][:], -1.0, ALU.mult)
        P.tt(dsum[s][:], pd[s][:, 4:5], pd[s][:, 5:6], ALU.add)

    P.ring("sd_xin", 2, [64, TB + 2]); P.ring("sd_bin", 2, [128, TB + 2]); P.ring("sd_cin", 2, [128, TB + 2])
    P.ring("sd_xf", 4, [64, TB]); P.ring("sd_bf", 4, [128, TB]); P.ring("sd_cf", 4, [128, TB])
    P.ring("sd_dt", 4, [128, TB // Q, 2]); P.ring("sd_dta", 4, [128, TB // Q, 2]); P.ring("sd_tmp8", 6, [128, TB // Q, 2])
    P.ring("sd_xtok", 12, [128, 64]); P.ring("sd_btok", 12, [128, 128]); P.ring("sd_cbm", 12, [128, 128])
    P.ring("sd_bc", 3, [128, 128]); P.ring("sd_E", 3, [128, 128]); P.ring("sd_G", 3, [128, 128]); P.ring("sd_Cs", 3, [128, 128])
    P.ring("sd_col", 12, [128, 4]); P.ring("sd_xdt", 3, [128, 64]); P.ring("sd_xw", 3, [128, 64])
    P.ring("sd_yblk", 3, [128, TB // Q, 64]); P.ring("sd_yfl", 3, [128, TB // Q, 64])
    H = [[P.sb(f"sd_h{s}{d}", [128, 64]) for d in range(2)] for s in range(2)]
    ps_a, ps_b, ps_c, ps_d, ps_e, ps_f, ps_g, ps_h = PS

    def prep(s, seq, t0, n):
        nq = n // Q
        o = {}
        for nm, ring_in, ring_f, src, par, rows in (("x", "sd_xin", "sd_xf", io["x" + seq], px[s], 64),
                                                    ("b", "sd_bin", "sd_bf", io["b" + seq], pb[s], 128),
                                                    ("c", "sd_cin", "sd_cf", io["c" + seq], pcc[s], 128)):
            tin = P.nxt(ring_in)
            P.dma(tin[:, 0:n + 2], src.ap()[s][:, t0:t0 + n + 2])
            tf = P.nxt(ring_f)[:, 0:n]
            P.ts(tf, tin[:, 1:n + 1], par[:, 1:2], ALU.mult, par[:, 3:4], ALU.add)
            P.stt(tf, tin[:, 0:n], par[:, 0:1], tf, ALU.mult, ALU.add)
            P.stt(tf, tin[:, 2:n + 2], par[:, 2:3], tf, ALU.mult, ALU.add)
            P.act(tf, tf, AF.Silu)
            o[nm] = tf
        dtr = P.nxt("sd_dt")[:, 0:nq, :]
        P.dma(dtr, io["dt" + seq].ap()[s][:, t0 // Q:t0 // Q + nq, :])
        dt = P.nxt("sd_dt")[:, 0:nq, :]
        dta = P.nxt("sd_dta")[:, 0:nq, :]
        for d in range(2):
            xb = P.nxt("sd_tmp8")[:, 0:nq, d:d + 1]
            P.ts(xb, dtr[:, :, d:d + 1], pd[s][:, d:d + 1], ALU.add)
            ab = P.nxt("sd_tmp8")[:, 0:nq, d:d + 1]
            P.act(ab, xb, AF.Abs)
            P.act(ab, ab, AF.Exp, scale=-1.0)
            P.act(ab, ab, AF.Ln, bias=1.0)
            P.ts(xb, xb, 0.0, ALU.max)
            P.tt(dt[:, :, d:d + 1], xb, ab, ALU.add)
            P.ts(dta[:, :, d:d + 1], dt[:, :, d:d + 1], Aneg[s][:, d:d + 1], ALU.mult)
        o["dt"] = dt; o["dta"] = dta
        o["xtok"] = []; o["btok"] = []; o["cb"] = []
        for c in range(nq):
            cs = slice(c * Q, (c + 1) * Q)
            P.mm(ps_a[:, 0:64], o["x"][:, cs], ident[0:64, 0:64])
            xt = P.nxt("sd_xtok")
            P.copy(xt[:], ps_a[:, 0:64], eng="act")
            P.mm(ps_b[:, 0:128], o["b"][:, cs], ident[:])
            bt = P.nxt("sd_btok")
            P.copy(bt[:], ps_b[:, 0:128], eng="act")
            P.mm(ps_c[:, 0:128], o["b"][:, cs], o["c"][:, cs])
            o["xtok"].append(xt); o["btok"].append(bt); o["cb"].append(ps_c)
            cbm = []
            for d in range(2):
                m = P.nxt("sd_cbm")
                P.tt(m[:], ps_c[:, 0:128], tri[d][:], ALU.mult)
                cbm.append(m)
            o["cb"][-1] = cbm
        return o

    def scan(s, d, o, n):
        nq = n // Q
        yb = P.nxt("sd_yblk")
        h = H[s][d]
        order = range(nq) if d == 0 else range(nq - 1, -1, -1)
        last = Q - 1 if d == 0 else 0
        for c in order:
            cs = slice(c * Q, (c + 1) * Q)
            dta = o["dta"][:, c, d:d + 1]
            dt = o["dt"][:, c, d:d + 1]
            bc = P.nxt("sd_bc")
            P.ts(bc[:], ones[:], dta, ALU.mult)
            P.mm(ps_d[:, 0:128], bc[:], tri[d][:])
            P.mm(ps_e[:, 0:1], tri[d][:], dta)
            col = P.nxt("sd_col")
            P.copy(col[:, 0:1], ps_e[:, 0:1])
            E = P.nxt("sd_E")
            P.ts(E[:], ps_d[:, 0:128], col[:, 0:1], ALU.subtract, 0.0, ALU.min)
            P.act(E[:], E[:], AF.Exp)
            G = P.nxt("sd_G")
            P.tt(G[:], E[:], o["cb"][c][d][:], ALU.mult)
            Cs = P.nxt("sd_Cs")
            P.act(Cs[:], ps_d[:, 0:128], AF.Exp)
            P.tt(Cs[:], Cs[:], o["c"][:, cs], ALU.mult)
            xdt = P.nxt("sd_xdt")
            P.ts(xdt[:], o["xtok"][c][:], dt, ALU.mult)
            P.mm(ps_f[:, 0:64], G[:], xdt[:], start=True, stop=False)
            P.mm(ps_f[:, 0:64], Cs[:], h[:], start=False, stop=True)
            P.copy(yb[:, c, :], ps_f[:, 0:64], eng="act")
            P.ts(col[:, 1:2], col[:, 0:1], -1.0, ALU.mult, ps_d[:, last:last + 1], ALU.add)
            P.act(col[:, 1:2], col[:, 1:2], AF.Exp)
            P.tt(col[:, 1:2], col[:, 1:2], dt, ALU.mult)
            P.act(col[:, 2:3], ps_d[:, last:last + 1], AF.Exp)
            xw = P.nxt("sd_xw")
            P.ts(xw[:], o["xtok"][c][:], col[:, 1:2], ALU.mult)
            P.mm(ps_g[:, 0:64], o["btok"][c][:], xw[:])
            P.stt(h[:], h[:], col[:, 2:3], ps_g[:, 0:64], ALU.mult, ALU.add)
        return yb

    def blocks(seq, T):
        return [(seq, t0, min(TB, T - t0)) for t0 in range(0, T, TB)]
    fwd = blocks("c", Tc) + blocks("l", Tl)
    bwd = blocks("c", Tc)[::-1] + blocks("l", Tl)[::-1]
    for s in range(2):
        for d in range(2):
            P.memset(H[s][d][:], 0.0)
    for (seq, t0, n) in fwd:
        for s in range(2):
            o = prep(s, seq, t0, n)
            yb = scan(s, 0, o, n)
            nq = n // Q
            P.dma(yfd[seq].ap()[s][:, t0 // Q:t0 // Q + nq, :], yb[:, 0:nq, :])
    for (seq, t0, n) in bwd:
        for s in range(2):
            o = prep(s, seq, t0, n)
            yb = scan(s, 1, o, n)
            nq = n // Q
            yfl = P.nxt("sd_yfl")
            P.dma(yfl[:, 0:nq, :], yfd[seq].ap()[s][:, t0 // Q:t0 // Q + nq, :])
            P.tt(yb[:, 0:nq, :], yb[:, 0:nq, :], yfl[:, 0:nq, :], ALU.add)
            for c in range(nq):
                P.stt(yb[:, c, :], o["xtok"][c][:], dsum[s][:, 0:1], yb[:, c, :], ALU.mult, ALU.add)
            P.dma(io["y" + seq].ap()[s][:, t0 // Q:t0 // Q + nq, :], yb[:, 0:nq, :], is_out=True)
    return io


def ssd_host_inputs(ssm_c, ssm_l, p, heads):
    d = {}
    for sq, u in (("c", ssm_c), ("l", ssm_l)):
        T = u.shape[0]
        xbc = u[:, 640:640 + 1152]
        X = np.zeros((2, 64, T + 2), np.float32); B = np.zeros((2, 128, T + 2), np.float32); Cc = np.zeros((2, 128, T + 2), np.float32)
        DT = np.zeros((2, 128, T // SQ, 2), np.float32)
        for s, h in enumerate(heads):
            g = h // 5
            X[s, :, 1:T + 1] = xbc[:, h * 64:h * 64 + 64].T
            B[s, :, 1:T + 1] = xbc[:, 640 + g * 128:640 + g * 128 + 128].T
            Cc[s, :, 1:T + 1] = xbc[:, 896 + g * 128:896 + g * 128 + 128].T
            DT[s, :, :, 0] = u[:, 1792 + h].reshape(T // SQ, SQ).T
            DT[s, :, :, 1] = u[:, 1802 + h].reshape(T // SQ, SQ).T
        d["sd_x" + sq] = X; d["sd_b" + sq] = B; d["sd_c" + sq] = Cc; d["sd_dt" + sq] = DT
    cw = np.concatenate([p["conv_w"], p["conv_b"][None]], 0).T
    px = np.zeros((2, 64, 4), np.float32); pb = np.zeros((2, 128, 4), np.float32); pc = np.zeros((2, 128, 4), np.float32)
    pd = np.zeros((2, 128, 6), np.float32)
    for s, h in enumerate(heads):
        g = h // 5
        px[s] = cw[h * 64:h * 64 + 64]; pb[s] = cw[640 + g * 128:640 + g * 128 + 128]; pc[s] = cw[896 + g * 128:896 + g * 128 + 128]
        pd[s, :, 0] = p["dt_bias"][0, h]; pd[s, :, 1] = p["dt_bias"][1, h]
        pd[s, :, 2] = p["a_log"][0, h]; pd[s, :, 3] = p["a_log"][1, h]
        pd[s, :, 4] = p["d_skip"][0, h]; pd[s, :, 5] = p["d_skip"][1, h]
    d.update({"sd_px": px, "sd_pb": pb, "sd_pc": pc, "sd_pd": pd})
    c = ssd_consts()
    d["c_tri"] = c["tri"]; d["c_identb"] = c["ident"]; d["c_ones"] = c["ones"]
    return d


def modvec_build(P, PS, L, KD, NCOL):
    D = KD * 128
    io = {}
    io["cT"] = P.dram("m_cT", [128, KD, 2], F32, "ExternalInput")
    io["w"] = P.dram("m_w", [L, KD, 128, NCOL], F32, "ExternalInput")
    io["b"] = P.dram("m_b", [L, 2, NCOL], F32, "ExternalInput")
    io["o"] = P.dram("m_o", [L, 2, NCOL], F32, "ExternalOutput")
    cT = P.sb("m_cTs", [128, KD, 2])
    P.dma(cT[:], io["cT"].ap())
    P.act(cT[:], cT[:], AF.Silu)
    P.ring("m_wt", 4, [128, NCOL]); P.ring("m_bt", 2, [2, NCOL]); P.ring("m_ot", 2, [2, NCOL])
    nt = (NCOL + 511) // 512
    for l in range(L):
        bt = P.nxt("m_bt")
        P.dma(bt[:], io["b"].ap()[l])
        for k in range(KD):
            wt = P.nxt("m_wt")
            P.dma(wt[:], io["w"].ap()[l, k])
            for j in range(nt):
                cs = slice(j * 512, min(NCOL, (j + 1) * 512))
                P.mm(PS[j][0:2, 0:cs.stop - cs.start], cT[:, k, :], wt[:, cs], start=(k == 0), stop=(k == KD - 1))
        ot = P.nxt("m_ot")
        for j in range(nt):
            cs = slice(j * 512, min(NCOL, (j + 1) * 512))
            P.tt(ot[:, cs], PS[j][0:2, 0:cs.stop - cs.start], bt[:, cs], ALU.add)
        P.dma(io["o"].ap()[l], ot[:], is_out=True)
    return io


EPS = 1e-6


def norm_mod(P, ps, ones, xT, hT, KD, segs, ab, tmpring, x_local=False, mkring=True):
    D = KD * 128
    if mkring:
        P.ring(tmpring + "_r", 2, [128, 512])
    for (t0, n, si) in segs:
        ts_ = slice(t0, t0 + n)
        xs_ = slice(0, n) if x_local else ts_
        for k in range(KD):
            sq = P.nxt(tmpring)[:, 0:n]
            P.act(sq, xT[:, k, xs_], AF.Square)
            P.mm(ps[:, 0:n], ones[:], sq, start=(k == 0), stop=(k == KD - 1))
        rstd = P.nxt(tmpring + "_r")[:, 0:n]
        P.act(rstd, ps[:, 0:n], AF.Sqrt, bias=EPS, scale=1.0 / D)
        P.recip(rstd, rstd)
        a, b = ab[si]
        for k in range(KD):
            t = P.nxt(tmpring)[:, 0:n]
            P.tt(t, xT[:, k, xs_], rstd, ALU.mult)
            P.ts(hT[:, k, ts_], t, a[:, k:k + 1], ALU.mult, b[:, k:k + 1], ALU.add)


def load_mods(P, io_mod, io_g, KD, which):
    mod = P.sb("mod_t", [128, 2, 6, KD])
    P.dma(mod[:], io_mod.ap())
    g = P.sb("mod_g", [128, KD])
    P.dma(g[:], io_g.ap())
    out = []
    for si in range(2):
        a = P.sb(f"mod_a{si}", [128, KD])
        P.ts(a[:], mod[:, si, 3 * which + 1, :], 1.0, ALU.add)
        P.tt(a[:], a[:], g[:], ALU.mult)
        out.append((a, mod[:, si, 3 * which + 0, :], mod[:, si, 3 * which + 2, :]))
    return out


IN_COLS = 6420
QK_EPS = 1e-6


def rope_consts():
    blk = np.zeros((128, 128), np.float32); blk[:64, :64] = 1; blk[64:, 64:] = 1
    rot = np.zeros((128, 128), np.float32)
    for base in (0, 64):
        for m in range(32):
            rot[base + m + 32, base + m] = -1.0
            rot[base + m, base + m + 32] = 1.0
    return blk, rot


def rope_tables(positions, is_ctx):
    n_freq = 16
    inv = (10000.0 ** (-np.arange(n_freq, dtype=np.float32) / n_freq)).astype(np.float32)
    row = (positions // 64).astype(np.float32); col = (positions % 64).astype(np.float32)
    ang = np.concatenate([row[:, None] * inv, col[:, None] * inv], axis=-1).astype(np.float32)
    cos = np.cos(ang).astype(np.float32); sin = np.sin(ang).astype(np.float32)
    cos = np.where(is_ctx[:, None], 1.0, cos).astype(np.float32); sin = np.where(is_ctx[:, None], 0.0, sin).astype(np.float32)
    cosT = np.tile(cos.T, (4, 1)); sinT = np.tile(sin.T, (4, 1))
    return np.ascontiguousarray(cosT), np.ascontiguousarray(sinT)


def projA_build(P, PS, KD, TCc, TLc):
    T = TCc + TLc
    io = {}
    io["xT"] = P.dram("a_xT", [128, KD, T], F32, "ExternalInput")
    io["mod"] = P.dram("a_mod", [128, 2, 6, KD], F32, "ExternalInput")
    io["g"] = P.dram("a_g", [128, KD], F32, "ExternalInput")
    io["w"] = P.dram("a_w", [KD, 128, IN_COLS], F32, "ExternalInput")
    io["cos"] = P.dram("a_cos", [128, T], F32, "ExternalInput")
    io["sin"] = P.dram("a_sin", [128, T], F32, "ExternalInput")
    io["gain"] = P.dram("a_gain", [128, 2], F32, "ExternalInput")
    io["blk"] = P.dram("a_blk", [128, 128], F32, "ExternalInput")
    io["rot"] = P.dram("a_rot", [128, 128], F32, "ExternalInput")
    io["ones"] = P.dram("a_ones", [128, 128], F32, "ExternalInput")
    io["qk"] = P.dram("a_qk", [12, 128, T], BF16, "ExternalOutput")
    io["v"] = P.dram("a_v", [6, 128, T], BF16, "ExternalOutput")
    io["u"] = P.dram("a_u", [IN_COLS - 2304, T], F32, "ExternalOutput")
    xT = P.sb("a_xTs", [128, KD, T]); hT = P.sb("a_hT", [128, KD, T], BF16)
    P.dma(xT[:], io["xT"].ap())
    cos = P.sb("a_cos_s", [128, T]); sin = P.sb("a_sin_s", [128, T])
    P.dma(cos[:], io["cos"].ap()); P.dma(sin[:], io["sin"].ap())
    gain = P.sb("a_gain_s", [128, 2]); P.dma(gain[:], io["gain"].ap())
    blk = P.sb("a_blk_s", [128, 128]); P.dma(blk[:], io["blk"].ap())
    rot = P.sb("a_rot_s", [128, 128]); P.dma(rot[:], io["rot"].ap())
    ones = P.sb("a_ones_s", [128, 128]); P.dma(ones[:], io["ones"].ap())
    mods = load_mods(P, io["mod"], io["g"], KD, 0)
    P.ring("a_tmp", 6, [128, 512])
    segs = [(0, TCc, 1)] + [(TCc + t0, min(512, TLc - t0), 0) for t0 in range(0, TLc, 512)]
    norm_mod(P, PS[7], ones, xT, hT, KD, segs, [(m[0], m[1]) for m in mods], "a_tmp")
    P.ring("a_w", 3, [128, KD, 128], BF16)
    P.ring("a_o", 4, [128, 512]); P.ring("a_ob", 4, [128, 512], BF16)
    nchunk = (IN_COLS + 127) // 128
    pi = 0
    for m in range(nchunk):
        c0 = m * 128; mc = min(128, IN_COLS - c0)
        wt = P.nxt("a_w")
        P.dma(wt[:, :, 0:mc], io["w"].ap()[:, :, c0:c0 + mc].rearrange("k p c -> p k c"), q="pool")
        for (t0, n, si) in segs:
            ts_ = slice(t0, t0 + n)
            ps = PS[pi % 4]; pi += 1
            for k in range(KD):
                P.mm(ps[0:mc, 0:n], wt[:, k, 0:mc], hT[:, k, ts_], start=(k == 0), stop=(k == KD - 1))
            if m < 12:
                x = P.nxt("a_o")[:, 0:n]
                P.copy(x, ps[:, 0:n], eng="act")
                sq = P.nxt("a_o")[:, 0:n]
                P.tt(sq, x, x, ALU.mult)
                P.mm(PS[4][:, 0:n], blk[:], sq)
                rs = P.nxt("a_o")[:, 0:n]
                P.act(rs, PS[4][:, 0:n], AF.Sqrt, bias=QK_EPS, scale=1.0 / 64)
                P.recip(rs, rs)
                P.stt(x, x, gain[:, (0 if m < 6 else 1):(1 if m < 6 else 2)], rs, ALU.mult, ALU.mult)
                P.mm(PS[5][:, 0:n], rot[:], x)
                r2 = P.nxt("a_o")[:, 0:n]
                P.tt(r2, PS[5][:, 0:n], sin[:, ts_], ALU.mult)
                P.tt(x, x, cos[:, ts_], ALU.mult)
                ob = P.nxt("a_ob")[:, 0:n]
                P.tt(ob, x, r2, ALU.add)
                P.dma(io["qk"].ap()[m][:, ts_], ob, is_out=True)
            elif m < 18:
                ob = P.nxt("a_ob")[:, 0:n]
                P.copy(ob, ps[:, 0:n], eng="act")
                P.dma(io["v"].ap()[m - 12][:, ts_], ob, is_out=True)
            else:
                o = P.nxt("a_o")
                P.copy(o[0:mc, 0:n], ps[0:mc, 0:n], eng="act")
                P.dma(io["u"].ap()[c0 - 2304:c0 - 2304 + mc, ts_], o[0:mc, 0:n], is_out=True)
    return io


def attn_consts():
    blk = np.zeros((128, 128), np.float32); blk[:64, :64] = 1; blk[64:, 64:] = 1
    sgn = np.zeros((128, 128), np.float32); sgn[:64, :] = 1.0 / 64; sgn[64:, :] = -1.0 / 64
    E = np.zeros((640, 640), np.float32); E[:320, :320] = 1; E[320:, 320:] = 1
    return blk, sgn, np.ascontiguousarray(E.reshape(5, 128, 640))


def mixB1_build(P, PS, KD, TCc, TLc, TCX, TLAT):
    T = TCc + TLc; TK = TCX + TLAT; KT = TK // 128; D = KD * 128
    io = {}
    dr = lambda nm, shp, dt=F32, kind="ExternalInput": P.dram("b_" + nm, shp, dt, kind)
    io["xT"] = dr("xT", [128, KD, T]); io["q"] = dr("q", [6, 128, T], BF16); io["kT"] = dr("kT", [6, 128, TK], BF16)
    io["v"] = dr("v", [128, KT, 768], BF16)
    io["rT"] = dr("rT", [128, 5, T]); io["yT"] = dr("yT", [128, 5, T]); io["zT"] = dr("zT", [128, 5, T])
    io["mod"] = dr("mod", [128, 2, 6, KD]); io["g"] = dr("g", [128, KD])
    io["w"] = dr("w", [16, 128, D])
    io["subln"] = dr("subln", [128, 1]); io["sng"] = dr("sng", [128, 5]); io["lamq"] = dr("lamq", [128, 1]); io["lamk"] = dr("lamk", [128, 1])
    io["lami"] = dr("lami", [128, 1]); io["gains"] = dr("gains", [128, 2, 64])
    io["blk"] = dr("blk", [128, 128]); io["sgn"] = dr("sgn", [128, 128]); io["E"] = dr("E", [5, 128, 640]); io["ones"] = dr("ones", [128, 128])
    io["o"] = dr("o", [128, KD, T], F32, "ExternalOutput")

    def cst(nm, shape, src=None, dt=F32, q="sp"):
        t = P.sb("bk_" + nm, shape, dt)
        P.dma(t[:], io[nm].ap() if src is None else src, q=q)
        return t
    subln = cst("subln", [128, 1]); sng = cst("sng", [128, 5]); lamq = cst("lamq", [128, 1]); lamk = cst("lamk", [128, 1])
    lami = cst("lami", [128, 1]); gains = cst("gains", [128, 2, 64]); blk = cst("blk", [128, 128]); sgn = cst("sgn", [128, 128])
    ones = cst("ones", [128, 128])
    onesb = P.sb("bk_onesb", [128, 128], BF16)
    P.copy(onesb[:], ones[:])
    Eg = [cst(f"E{k}", [128, 640], io["E"].ap()[k]) for k in range(5)]
    mods = load_mods(P, io["mod"], io["g"], KD, 0)
    qs = P.sb("bk_q", [128, 6, T], BF16)
    P.dma(qs[:], io["q"].ap().rearrange("h p t -> p h t"))
    sm = P.sb("bk_sm", [128, 8])
    P.tt(sm[:, 0:1], lamq[:], lamk[:], ALU.mult)
    P.mm(PS[0][:, 0:1], blk[:], sm[:, 0:1])
    P.act(sm[:, 1:2], PS[0][:, 0:1], AF.Exp)
    P.mm(PS[1][:, 0:1], sgn[:], sm[:, 1:2])
    P.tt(sm[:, 2:3], PS[1][:, 0:1], lami[:], ALU.add)
    P.ts(sm[:, 3:4], sm[:, 2:3], -1.0, ALU.mult)
    P.ts(sm[:, 4:5], lami[:], -1.0, ALU.mult, 1.0, ALU.add)
    P.tt(sm[:, 4:5], sm[:, 4:5], subln[:], ALU.mult)
    ga = P.sb("bk_ga", [128, 2, 64])
    P.act(ga[:], gains[:], AF.Abs)
    P.reduce(sm[:, 5:7], ga[:], ALU.max)
    P.tt(sm[:, 7:8], sm[:, 5:6], sm[:, 6:7], ALU.mult)
    P.ts(sm[:, 7:8], sm[:, 7:8], -8.0, ALU.mult)
    neg_lam, sgl, ebias = sm[:, 3:4], sm[:, 4:5], sm[:, 7:8]

    mixT = P.sb("bk_mixT", [128, 16, T], BF16)
    P.ring("b_kT", 2, [128, TK], BF16); P.ring("b_vh", 2, [128, KT, 128], BF16)
    P.ring("b_E", 8, [128, 512], BF16); P.ring("b_t", 8, [128, 512])
    qtiles = [(0, TCc, TCX)] + [(TCc + t0, min(512, TLc - t0), TK) for t0 in range(0, TLc, 512)]
    si_ = 0
    for h in range(6):
        kh = P.nxt("b_kT"); vh = P.nxt("b_vh")
        P.dma(kh[:], io["kT"].ap()[h])
        P.dma(vh[:], io["v"].ap()[:, :, h * 128:(h + 1) * 128])
        for (t0, n, nkeys) in qtiles:
            ts_ = slice(t0, t0 + n)
            nkt = nkeys // 128
            Eq = []
            LA = 2
            for it in range(nkt + LA):
                if it < nkt:
                    ks = slice(it * 128, (it + 1) * 128)
                    Es = []
                    for c in range(2):
                        hs = slice(64 * c, 64 * c + 64)
                        Sp = PS[si_ % 4]; si_ += 1
                        P.mm(Sp[:, 0:n], kh[hs, ks], qs[hs, h, ts_])
                        E = P.nxt("b_E")[:, 0:n]
                        P.act(E, Sp[:, 0:n], AF.Exp, bias=ebias, scale=0.125)
                        Es.append(E)
                    Eq.append(Es)
                if it >= LA:
                    kt = it - LA
                    for c in range(2):
                        E = Eq[kt][c]
                        P.mm(PS[4 + c][:, 0:n], vh[:, kt, :], E, start=(kt == 0), stop=(kt == nkt - 1))
                        P.mm(PS[6 + c][:, 0:n], onesb[:], E, start=(kt == 0), stop=(kt == nkt - 1))
            rl0 = P.nxt("b_t")[:, 0:n]; rl1 = P.nxt("b_t")[:, 0:n]
            P.recip(rl0, PS[6][:, 0:n]); P.recip(rl1, PS[7][:, 0:n])
            o0 = P.nxt("b_t")[:, 0:n]; o1 = P.nxt("b_t")[:, 0:n]
            P.tt(o0, PS[4][:, 0:n], rl0, ALU.mult)
            P.tt(o1, PS[5][:, 0:n], rl1, ALU.mult)
            P.stt(o0, o1, neg_lam, o0, ALU.mult, ALU.add)
            sq = P.nxt("b_t")[:, 0:n]
            P.tt(sq, o0, o0, ALU.mult)
            P.mm(PS[0][:, 0:n], ones[:], sq)
            rs = P.nxt("b_t")[:, 0:n]
            P.act(rs, PS[0][:, 0:n], AF.Sqrt, bias=EPS, scale=1.0 / 128)
            P.recip(rs, rs)
            P.stt(mixT[:, h, ts_], o0, sgl, rs, ALU.mult, ALU.mult)
    P.ring("b_in5", 3, [128, 5, 512])
    toks = [(0, TCc, 1)] + [(TCc + t0, min(512, TLc - t0), 0) for t0 in range(0, TLc, 512)]
    for (t0, n, si) in toks:
        ts_ = slice(t0, t0 + n)
        rt = P.nxt("b_in5"); yt = P.nxt("b_in5"); zt = P.nxt("b_in5")
        P.dma(rt[:, :, 0:n], io["rT"].ap()[:, :, ts_]); P.dma(yt[:, :, 0:n], io["yT"].ap()[:, :, ts_]); P.dma(zt[:, :, 0:n], io["zT"].ap()[:, :, ts_])
        P.copy(mixT[:, 6:11, ts_], rt[:, :, 0:n])
        P.act(zt[:, :, 0:n], zt[:, :, 0:n], AF.Silu)
        P.tt(yt[:, :, 0:n], yt[:, :, 0:n], zt[:, :, 0:n], ALU.mult)
        P.tt(zt[:, :, 0:n], yt[:, :, 0:n], yt[:, :, 0:n], ALU.mult)
        for m in range(5):
            for k in range(5):
                P.mm(PS[m % 4][:, 0:n], Eg[k][:, m * 128:(m + 1) * 128], zt[:, k, 0:n], start=(k == 0), stop=(k == 4))
            rs = P.nxt("b_t")[:, 0:n]
            P.act(rs, PS[m % 4][:, 0:n], AF.Sqrt, bias=EPS, scale=1.0 / 320)
            P.recip(rs, rs)
            P.stt(mixT[:, 11 + m, ts_], yt[:, m, 0:n], sng[:, m:m + 1], rs, ALU.mult, ALU.mult)
    P.ring("b_w", 3, [128, 16, 128], BF16); P.ring("b_x", 4, [128, 512])
    pi = 0
    for m in range(KD):
        wt = P.nxt("b_w")
        P.dma(wt[:], io["w"].ap()[:, :, m * 128:(m + 1) * 128].rearrange("k p c -> p k c"), q="pool")
        for (t0, n, si) in toks:
            ts_ = slice(t0, t0 + n)
            ps = PS[pi % 4]; pi += 1
            for k in range(16):
                P.mm(ps[:, 0:n], wt[:, k, :], mixT[:, k, ts_], start=(k == 0), stop=(k == 15))
            xt = P.nxt("b_x")[:, 0:n]
            P.dma(xt, io["xT"].ap()[:, m, ts_])
            P.stt(xt, ps[:, 0:n], mods[si][2][:, m:m + 1], xt, ALU.mult, ALU.add)
            P.dma(io["o"].ap()[:, m, ts_], xt, is_out=True)
    return io


def ffnB2_build(P, PS, KD, FC, TCc, TLc):
    T = TCc + TLc; D = KD * 128
    io = {}
    dr = lambda nm, shp, dt=F32, kind="ExternalInput": P.dram("f_" + nm, shp, dt, kind)
    io["xT"] = dr("xT", [128, KD, T]); io["mod"] = dr("mod", [128, 2, 6, KD]); io["g"] = dr("g", [128, KD])
    io["wi"] = dr("wi", [KD, 128, 2 * FC * 128]); io["wo"] = dr("wo", [FC, 128, D]); io["ones"] = dr("ones", [128, 128])
    io["o"] = dr("o", [128, KD, T], F32, "ExternalOutput")
    hT = P.sb("f_hT", [128, KD, T], BF16)
    ones = P.sb("f_ones_s", [128, 128]); P.dma(ones[:], io["ones"].ap())
    mods = load_mods(P, io["mod"], io["g"], KD, 1)
    P.ring("f_tmp", 6, [128, 512]); P.ring("f_xt", 1, [128, KD, 256])
    toks = [(0, TCc, 1)] + [(TCc + t0, min(512, TLc - t0), 0) for t0 in range(0, TLc, 512)]
    nsegs = [(0, TCc, 1)] + [(TCc + t0, min(256, TLc - t0), 0) for t0 in range(0, TLc, 256)]
    ab = [(m[0], m[1]) for m in mods]
    for i, (t0, n, si) in enumerate(nsegs):
        xt = P.nxt("f_xt")
        P.dma(xt[:, :, 0:n], io["xT"].ap()[:, :, t0:t0 + n])
        norm_mod(P, PS[7], ones, xt, hT, KD, [(t0, n, si)], ab, "f_tmp", x_local=True, mkring=(i == 0))
    actT = P.sb("f_actT", [128, FC, T], BF16)
    P.ring("f_wi", 2, [128, KD, 256], BF16); P.ring("f_wo", 2, [128, FC, 128], BF16); P.ring("f_o", 3, [128, 512])
    pi = 0
    for m in range(FC):
        wt = P.nxt("f_wi")
        P.dma(wt[:, :, 0:128], io["wi"].ap()[:, :, m * 128:(m + 1) * 128].rearrange("k p c -> p k c"), q="pool")
        P.dma(wt[:, :, 128:256], io["wi"].ap()[:, :, (FC + m) * 128:(FC + m + 1) * 128].rearrange("k p c -> p k c"), q="pool")
        for (t0, n, si) in toks:
            ts_ = slice(t0, t0 + n)
            pg = PS[pi % 6]; pu = PS[(pi + 1) % 6]; pi += 2
            for k in range(KD):
                P.mm(pg[:, 0:n], wt[:, k, 0:128], hT[:, k, ts_], start=(k == 0), stop=(k == KD - 1))
            for k in range(KD):
                P.mm(pu[:, 0:n], wt[:, k, 128:256], hT[:, k, ts_], start=(k == 0), stop=(k == KD - 1))
            gs = P.nxt("f_tmp")[:, 0:n]
            P.act(gs, pg[:, 0:n], AF.Silu)
            P.tt(actT[:, m, ts_], gs, pu[:, 0:n], ALU.mult)
    for mo in range(KD):
        wt = P.nxt("f_wo")
        P.dma(wt[:], io["wo"].ap()[:, :, mo * 128:(mo + 1) * 128].rearrange("m p c -> p m c"), q="pool")
        for (t0, n, si) in toks:
            ts_ = slice(t0, t0 + n)
            ps = PS[pi % 6]; pi += 1
            for m in range(FC):
                P.mm(ps[:, 0:n], wt[:, m, :], actT[:, m, ts_], start=(m == 0), stop=(m == FC - 1))
            ot = P.nxt("f_o")[:, 0:n]
            P.dma(ot, io["xT"].ap()[:, mo, ts_])
            P.stt(ot, ps[:, 0:n], mods[si][2][:, mo:mo + 1], ot, ALU.mult, ALU.add)
            P.dma(io["o"].ap()[:, mo, ts_], ot, is_out=True)
    return io


NCORES = 8
_PROGS = {}
_DBG = None


def _fm(x2d, KD):
    T = x2d.shape[0]
    return np.ascontiguousarray(x2d.T.reshape(KD, 128, T).transpose(1, 0, 2))


def _unfm(a):
    p, KD, T = a.shape
    return np.ascontiguousarray(a.transpose(1, 0, 2).reshape(KD * 128, T).T)


def _run(nc, in_maps):
    res = run_bass_kernel_spmd(nc, in_maps, core_ids=list(range(NCORES)))
    return res.results


def _prog(key, builder):
    if key not in _PROGS:
        P = Prog()
        PS = [P.ps(f"ps{i}", [128, 512]) for i in range(8)]
        builder(P, PS)
        _PROGS[key] = P.finish()
    return _PROGS[key]


def _slots(j):
    rh = (2 * j, 2 * j + 1) if j < 5 else (0, 1)
    sh = (2 * (j - 3), 2 * (j - 3) + 1) if j >= 3 else (0, 1)
    return rh, sh


def kernel(x, c, ctx, c_ctx, ada_w, ada_b, norm1_g, norm2_g, w_in, w_out, qk_gain, lam_q, lam_k, subln_g,
           shift_mu, w0, w_up, a0, a_up, g_up, k_k, k_a, r_k, lnx_g, lnx_b, conv_w, conv_b, dt_bias, a_log,
           d_skip, ssm_norm_g, w_ffn_in, w_ffn_out):
    f = lambda a: np.ascontiguousarray(np.asarray(a, dtype=np.float32))
    x, c, ctx, c_ctx = f(x), f(c), f(ctx), f(c_ctx)
    L = ada_w.shape[0]; SEQ = x.shape[1]; CTX = ctx.shape[1]; D = x.shape[2]; KD = D // 128
    DFF = w_ffn_out.shape[1]; FC = DFF // 128
    NC = NCORES; TCc = CTX // NC; TLc = SEQ // NC; T = TCc + TLc
    ones = np.ones((128, 128), np.float32)
    NCOL = 6 * D // NC
    ncM = _prog(("M", L, KD, NCOL), lambda P, PS: modvec_build(P, PS, L, KD, NCOL))
    cT = _fm(np.stack([c[0], c_ctx]), KD)
    ada_w = np.asarray(ada_w, np.float32); ada_b = np.asarray(ada_b, np.float32)
    ims = []
    for j in range(NC):
        cs = slice(j * NCOL, (j + 1) * NCOL)
        ims.append({"m_cT": cT, "m_w": np.ascontiguousarray(ada_w[:, :, cs].reshape(L, KD, 128, NCOL)),
                    "m_b": np.ascontiguousarray(np.repeat(ada_b[:, None, cs], 2, axis=1))})
    rs = _run(ncM, ims)
    mod = np.concatenate([r["m_o"] for r in rs], axis=-1)
    modT = [np.ascontiguousarray(mod[l].reshape(2, 6, KD, 128).transpose(3, 0, 1, 2)) for l in range(L)]

    xl = x[0].copy(); xc = ctx[0].copy()
    blk, rot = rope_consts()
    ablk, sgn, Eg = attn_consts()
    pos = np.arange(SEQ)
    ncA = _prog(("A", KD, TCc, TLc), lambda P, PS: projA_build(P, PS, KD, TCc, TLc))
    ncS = _prog(("S", CTX, SEQ), lambda P, PS: (rwkv_build2(P, PS, CTX, SEQ), ssd_build2(P, PS, CTX, SEQ)))
    ncB1 = _prog(("B1", KD, TCc, TLc, CTX, SEQ), lambda P, PS: mixB1_build(P, PS, KD, TCc, TLc, CTX, SEQ))
    ncB2 = _prog(("B2", KD, FC, TCc, TLc), lambda P, PS: ffnB2_build(P, PS, KD, FC, TCc, TLc))
    tabs = []
    for j in range(NC):
        pj = np.concatenate([np.zeros(TCc, np.int64), pos[j * TLc:(j + 1) * TLc]])
        isc = np.concatenate([np.ones(TCc, bool), np.zeros(TLc, bool)])
        tabs.append(rope_tables(pj, isc))
    for l in range(L):
        g = lambda a: np.asarray(a[l], np.float32)
        xTs = [_fm(np.concatenate([xc[j * TCc:(j + 1) * TCc], xl[j * TLc:(j + 1) * TLc]]), KD) for j in range(NC)]
        gain = np.ascontiguousarray(np.tile(g(qk_gain), (1, 2)).T)
        wA = np.ascontiguousarray(g(w_in).reshape(KD, 128, IN_COLS))
        n1 = _fm(g(norm1_g)[None], KD)[:, :, 0]
        ims = [{"a_xT": xTs[j], "a_mod": modT[l], "a_g": n1, "a_w": wA, "a_cos": tabs[j][0], "a_sin": tabs[j][1],
                "a_gain": gain, "a_blk": blk, "a_rot": rot, "a_ones": ones} for j in range(NC)]
        ra = _run(ncA, ims)
        kT_all = np.ascontiguousarray(np.concatenate([r["a_qk"][6:12, :, :TCc] for r in ra] + [r["a_qk"][6:12, :, TCc:] for r in ra], axis=2))
        vT_all = np.concatenate([r["a_v"][:, :, :TCc] for r in ra] + [r["a_v"][:, :, TCc:] for r in ra], axis=2)
        TK = CTX + SEQ
        v_all = np.ascontiguousarray(vT_all.reshape(768, TK).T.reshape(TK // 128, 128, 768).transpose(1, 0, 2))
        u_c = np.concatenate([r["a_u"][:, :TCc] for r in ra], axis=1).T
        u_l = np.concatenate([r["a_u"][:, TCc:] for r in ra], axis=1).T
        p = {"shift_mu": g(shift_mu), "w0": g(w0), "w_up": g(w_up), "a0": g(a0), "a_up": g(a_up), "g_up": g(g_up),
             "k_k": g(k_k), "k_a": g(k_a), "r_k": g(r_k), "lnx_g": g(lnx_g), "lnx_b": g(lnx_b), "conv_w": g(conv_w),
             "conv_b": g(conv_b), "dt_bias": g(dt_bias), "a_log": g(a_log), "d_skip": g(d_skip)}
        ims = []
        for j in range(NC):
            d_ = rwkv_host_inputs(u_c[:, :2304], u_l[:, :2304], p, _slots(j)[0])
            d_.update(ssd_host_inputs(u_c[:, 2304:], u_l[:, 2304:], p, _slots(j)[1]))
            ims.append(d_)
        rsS = _run(ncS, ims)
        rsR = rsS
        if _DBG is not None:
            _DBG[f'mod{l}'] = mod[l]; _DBG[f'u_c{l}'] = u_c; _DBG[f'u_l{l}'] = u_l; _DBG[f'kT{l}'] = kT_all; _DBG[f'v{l}'] = v_all; _DBG[f'q{l}'] = [r['a_qk'][0:6] for r in ra]
        r_c = np.zeros((CTX, 640), np.float32); r_l = np.zeros((SEQ, 640), np.float32)
        y_c = np.zeros((CTX, 640), np.float32); y_l = np.zeros((SEQ, 640), np.float32)
        for h in range(10):
            j, s = h // 2, h % 2
            r_c[:, h * 64:(h + 1) * 64] = rsR[j]["rw_oc"][64 * s:64 * s + 64].T
            r_l[:, h * 64:(h + 1) * 64] = rsR[j]["rw_ol"][64 * s:64 * s + 64].T
            j = 3 + h // 2
            y_c[:, h * 64:(h + 1) * 64] = rsS[j]["sd_yc"][s].transpose(1, 0, 2).reshape(CTX, 64)
            y_l[:, h * 64:(h + 1) * 64] = rsS[j]["sd_yl"][s].transpose(1, 0, 2).reshape(SEQ, 64)
        lam_init = 0.8 - 0.6 * math.exp(-0.3 * l)
        own = lambda ac, al, j: np.concatenate([ac[j * TCc:(j + 1) * TCc], al[j * TLc:(j + 1) * TLc]])
        wO = np.ascontiguousarray(g(w_out).reshape(16, 128, D))
        common = {"b_kT": kT_all, "b_v": v_all, "b_mod": modT[l], "b_g": n1, "b_w": wO,
                  "b_subln": np.ascontiguousarray(g(subln_g)[:, None]), "b_sng": _fm(g(ssm_norm_g)[None], 5)[:, :, 0],
                  "b_lamq": np.ascontiguousarray(g(lam_q).reshape(128, 1)), "b_lamk": np.ascontiguousarray(g(lam_k).reshape(128, 1)),
                  "b_lami": np.full((128, 1), lam_init, np.float32),
                  "b_gains": np.ascontiguousarray(np.broadcast_to(g(qk_gain)[None], (128, 2, 64))),
                  "b_blk": ablk, "b_sgn": sgn, "b_E": Eg, "b_ones": ones}
        ims = []
        for j in range(NC):
            d = dict(common)
            d["b_xT"] = xTs[j]; d["b_q"] = np.ascontiguousarray(ra[j]["a_qk"][0:6])
            d["b_rT"] = _fm(own(r_c, r_l, j), 5); d["b_yT"] = _fm(own(y_c, y_l, j), 5)
            d["b_zT"] = _fm(own(u_c[:, 2304:2304 + 640], u_l[:, 2304:2304 + 640], j), 5)
            ims.append(d)
        rb1 = _run(ncB1, ims)
        if _DBG is not None:
            _DBG[f'r_l{l}'] = r_l; _DBG[f'y_l{l}'] = y_l; _DBG[f'r_c{l}'] = r_c; _DBG[f'y_c{l}'] = y_c; _DBG[f'x1_{l}'] = [_unfm(r['b_o']) for r in rb1]
        n2 = _fm(g(norm2_g)[None], KD)[:, :, 0]
        wi = np.ascontiguousarray(g(w_ffn_in).reshape(KD, 128, 2 * DFF)); wo = np.ascontiguousarray(g(w_ffn_out).reshape(FC, 128, D))
        ims = [{"f_xT": rb1[j]["b_o"], "f_mod": modT[l], "f_g": n2, "f_wi": wi, "f_wo": wo, "f_ones": ones} for j in range(NC)]
        rb2 = _run(ncB2, ims)
        for j in range(NC):
            xo = _unfm(rb2[j]["f_o"])
            xc[j * TCc:(j + 1) * TCc] = xo[:TCc]
            xl[j * TLc:(j + 1) * TLc] = xo[TCc:]
        if _DBG is not None:
            _DBG[f'xl{l}'] = xl.copy(); _DBG[f'xc{l}'] = xc.copy()
            if _DBG.get('stop_after') == l:
                return xl[None].astype(np.float32)
    return xl[None].astype(np.float32)


_NO_ILV = False


def _interleave(gens):
    gens = [g for g in gens if g is not None]
    while gens:
        for g in list(gens):
            try:
                next(g)
            except StopIteration:
                gens.remove(g)


def rwkv_build2(P, PS, Tc, Tl):
    C = RW_C; TB = 256; NCB = TB // C
    io = {}
    io["uc"] = P.dram("rw_uc", [128, 6, Tc + 2], F32, "ExternalInput")
    io["ul"] = P.dram("rw_ul", [128, 6, Tl + 2], F32, "ExternalInput")
    io["pr"] = P.dram("rw_pr", [128, PR_N], F32, "ExternalInput")
    io["wup"] = P.dram("rw_wup", [128, 128], F32, "ExternalInput")
    io["aup"] = P.dram("rw_aup", [128, 128], F32, "ExternalInput")
    io["gup"] = P.dram("rw_gup", [128, 128], F32, "ExternalInput")
    io["blk64"] = P.dram("c_blk64", [128, 128], F32, "ExternalInput")
    io["ident"] = P.dram("c_ident", [128, 128], F32, "ExternalInput")
    io["i2"] = P.dram("c_i2", [128, 64], F32, "ExternalInput")
    io["mask5"] = P.dram("c_mask5", [2, 128, 320], F32, "ExternalInput")
    io["oc"] = P.dram("rw_oc", [128, Tc], F32, "ExternalOutput")
    io["ol"] = P.dram("rw_ol", [128, Tl], F32, "ExternalOutput")
    yf = {"c": P.dram("rw_yfc", [128, Tc], F32), "l": P.dram("rw_yfl", [128, Tl], F32)}
    usrc = {"c": io["uc"], "l": io["ul"]}
    odst = {"c": io["oc"], "l": io["ol"]}

    def cst(name, shape):
        t = P.sb("k_" + name, shape)
        P.dma(t[:], io[name].ap())
        return t
    pr = cst("pr", [128, PR_N]); wup = cst("wup", [128, 128]); aup = cst("aup", [128, 128]); gup = cst("gup", [128, 128])
    blk = cst("blk64", [128, 128]); ident = cst("ident", [128, 128]); i2 = cst("i2", [128, 64])
    i24 = P.sb("k_i24", [128, NCB, 64])
    for ci in range(NCB):
        P.copy(i24[:, ci, :], i2[:])
    mask5 = []
    for d in range(2):
        t = P.sb(f"k_mask5_{d}", [128, 320])
        P.dma(t[:], io["mask5"].ap()[d])
        mask5.append(t)
    rst = P.sb("k_rst", [128, TB])
    P.memset(rst[:], 1.0)
    P.memset(rst[:].rearrange("p (c t) -> p c t", t=C)[:, :, 0:1], 0.0)
    pc = lambda i: pr[:, i:i + 1]
    P.ts(pr[:, PR_M2:PR_M2 + 6], pr[:, PR_MU0:PR_MU0 + 6], -1.0, ALU.mult, 1.0, ALU.add)
    P.tt(pr[:, PR_M2:PR_M2 + 6], pr[:, PR_M2:PR_M2 + 6], pr[:, PR_MU1:PR_MU1 + 6], ALU.subtract)
    P.ts(pr[:, PR_1MKA:PR_1MKA + 1], pr[:, PR_KA:PR_KA + 1], -1.0, ALU.mult, 1.0, ALU.add)

    P.ring("rw_u6", 2, [128, 6, TB + 2]); P.ring("rw_xs", 2, [128, 6, TB])
    for nm in ("tw", "sg", "kx", "rn", "kk", "g", "lw0", "lw1", "ic0", "ic1", "k0", "k1", "b0", "b1",
               "cw", "cwb", "ep", "en", "ea", "At", "Bt", "Kt", "Rt", "yT", "t2", "t3", "t4", "yfl"):
        P.ring("rw_" + nm, 2, [128, TB])
    for nm in ("sq", "f1", "t1"):
        P.ring("rw_" + nm, 3, [128, TB])
    P.ring("rw_tok", 2, [128, NCB, 3, 64]); P.ring("rw_M5", 2, [128, NCB, 5, 64]); P.ring("rw_TT", 2, [128, NCB, 64])
    P.ring("rw_PP", 3, [128, NCB, 128]); P.ring("rw_Xs", 3, [128, 64]); P.ring("rw_Us", 3, [128, 64])
    S = [P.sb("rw_S0", [128, 64]), P.sb("rw_S1", [128, 64])]
    Stmp = P.sb("rw_Stmp", [128, 64])
    HS = [slice(0, 64), slice(64, 128)]

    def prep_gen(seq, t0, n, o):
        u6 = P.nxt("rw_u6")
        P.dma(u6[:, :, 0:n + 2], usrc[seq].ap()[:, :, t0:t0 + n + 2])
        xs = P.nxt("rw_xs")
        for j in range(6):
            P.ts(xs[:, j, 0:n], u6[:, j, 1:n + 1], pc(PR_M2 + j), ALU.mult)
            P.stt(xs[:, j, 0:n], u6[:, j, 0:n], pc(PR_MU0 + j), xs[:, j, 0:n], ALU.mult, ALU.add)
            P.stt(xs[:, j, 0:n], u6[:, j, 2:n + 2], pc(PR_MU1 + j), xs[:, j, 0:n], ALU.mult, ALU.add)
        yield
        r, k, v, wd, ad, gd = [xs[:, j, 0:n] for j in range(6)]
        o["r"] = r; o["v"] = v
        tw = P.nxt("rw_tw")[:, 0:n]
        P.act(tw, wd, AF.Tanh)
        sg = P.nxt("rw_sg")[:, 0:n]
        P.act(sg, gd, AF.Sigmoid)
        for d in range(2):
            hs = HS[d]
            P.mm(PS[2 + d][:, 0:n], wup[hs, :], tw[hs, :])
            P.mm(PS[4 + d][:, 0:n], aup[hs, :], ad[hs, :])
        P.mm(PS[2][:, 256:256 + n], gup[:], sg)
        for d in range(2):
            lw = P.nxt(f"rw_lw{d}")[:, 0:n]
            P.act(lw, PS[2 + d][:, 0:n], AF.Sigmoid, bias=pc(PR_W0 + d))
            ic = P.nxt(f"rw_ic{d}")[:, 0:n]
            P.act(ic, PS[4 + d][:, 0:n], AF.Sigmoid, bias=pc(PR_A0 + d))
            P.ts(lw, lw, -E05, ALU.mult)
            o[f"lw{d}"] = lw; o[f"ic{d}"] = ic
        g = P.nxt("rw_g")[:, 0:n]
        P.copy(g, PS[2][:, 256:256 + n], eng="act")
        o["g"] = g
        yield
        kx = P.nxt("rw_kx")[:, 0:n]
        P.ts(kx, k, pc(PR_KK), ALU.mult)
        sq = P.nxt("rw_sq")[:, 0:n]
        P.tt(sq, kx, kx, ALU.mult)
        P.mm(PS[3][:, 256:256 + n], blk[:], sq)
        rn = P.nxt("rw_rn")[:, 0:n]
        P.act(rn, PS[3][:, 256:256 + n], AF.Sqrt, bias=1e-12)
        P.recip(rn, rn)
        kk = P.nxt("rw_kk")[:, 0:n]
        P.tt(kk, kx, rn, ALU.mult)
        o["kk"] = kk
        for d in range(2):
            f1 = P.nxt("rw_f1")[:, 0:n]
            P.ts(f1, o[f"ic{d}"], pc(PR_KA), ALU.mult, pc(PR_1MKA), ALU.add)
            kd = P.nxt(f"rw_k{d}")[:, 0:n]
            P.tt(kd, k, f1, ALU.mult)
            bd = P.nxt(f"rw_b{d}")[:, 0:n]
            P.tt(bd, kk, o[f"ic{d}"], ALU.mult)
            o[f"k{d}"] = kd; o[f"b{d}"] = bd
        yield

    def pre_gen(d, o, n, X):
        nch = n // C
        lw = o[f"lw{d}"]
        cw = P.nxt("rw_cw")[:, 0:n]
        P.scan(cw, rst[:, 0:n], lw, 0.0, ALU.mult, ALU.add)
        if d == 1:
            cwb = P.nxt("rw_cwb")[:, 0:n]
            P.tt(cwb, lw, cw, ALU.subtract)
            for c in range(nch):
                P.ts(cwb[:, c * C:(c + 1) * C], cwb[:, c * C:(c + 1) * C], cw[:, c * C + C - 1:c * C + C], ALU.add)
            cw = cwb
        ep = P.nxt("rw_ep")[:, 0:n]; en = P.nxt("rw_en")[:, 0:n]; ea = P.nxt("rw_ea")[:, 0:n]
        P.act(ep, cw, AF.Exp)
        P.act(en, cw, AF.Exp, scale=-1.0)
        t1 = P.nxt("rw_t1")[:, 0:n]
        P.tt(t1, cw, lw, ALU.subtract)
        P.act(ea, t1, AF.Exp)
        At = P.nxt("rw_At")[:, 0:n]; Bt = P.nxt("rw_Bt")[:, 0:n]; Kt = P.nxt("rw_Kt")[:, 0:n]; Rt = P.nxt("rw_Rt")[:, 0:n]
        P.stt(At, o["kk"], -1.0, ea, ALU.mult, ALU.mult)
        P.tt(Bt, o[f"b{d}"], en, ALU.mult)
        P.tt(Kt, o[f"k{d}"], en, ALU.mult)
        P.tt(Rt, o["r"], ep, ALU.mult)
        v = o["v"]
        X.update(At=At, Rt=Rt, ep=ep)
        yield
        tok = P.nxt("rw_tok")
        for ci in range(nch):
            cs = slice(ci * C, (ci + 1) * C)
            bank = PS[ci // 2]; off = (ci % 2) * 192
            for j, Xm in enumerate((Bt, Kt, v)):
                for s in range(2):
                    P.mm(bank[HS[s], off + j * 64:off + (j + 1) * 64], Xm[HS[s], cs], ident[HS[s], HS[s]])
        for b in range((nch + 1) // 2):
            k2 = min(2, nch - 2 * b)
            P.copy(tok[:, 2 * b:2 * b + k2].rearrange("p a b c -> p (a b c)"), PS[b][:, 0:192 * k2], eng="act")
        yield
        M5 = P.nxt("rw_M5")
        for ci in range(nch):
            cs = slice(ci * C, (ci + 1) * C)
            for s in range(2):
                for j, (l, rr) in enumerate(((At, Bt), (Bt, At), (Kt, At), (Bt, Rt), (Kt, Rt))):
                    P.mm(PS[2 + ci][HS[s], j * 64:(j + 1) * 64], l[HS[s], cs], rr[HS[s], cs])
            P.tt(M5[:, ci].rearrange("p a b -> p (a b)"), PS[2 + ci][:, 0:320], mask5[d][:], ALU.mult)
        yield
        TT = P.nxt("rw_TT")
        for ci in range(nch):
            P.tt(TT[:, ci, :], M5[:, ci, 1, :], i2[:], ALU.add)
        Pm = [M5[:, ci, 0, :] for ci in range(nch)]; PTm = [M5[:, ci, 1, :] for ci in range(nch)]
        def sq_mm(Pm, PTm, both):
            for ci in range(nch):
                for s in range(2):
                    P.mm(PS[6][HS[s], ci * 128:ci * 128 + 64], PTm[ci][HS[s], :], Pm[ci][HS[s], :])
                    if both:
                        P.mm(PS[6][HS[s], ci * 128 + 64:ci * 128 + 128], Pm[ci][HS[s], :], PTm[ci][HS[s], :])

        def sq_evac():
            PP = P.nxt("rw_PP")
            P.copy(PP[:, 0:nch].rearrange("p a b -> p (a b)"), PS[6][:, 0:128 * nch], eng="act")
            return [PP[:, ci, 0:64] for ci in range(nch)], [PP[:, ci, 64:128] for ci in range(nch)]

        def tt_mm(Pm):
            for ci in range(nch):
                for s in range(2):
                    P.mm(PS[ci // 2][HS[s], 384 + (ci % 2) * 64:384 + (ci % 2) * 64 + 64], Pm[ci][HS[s], :], TT[HS[s], ci, :])

        def tt_evac():
            for b in range((nch + 1) // 2):
                k2 = min(2, nch - 2 * b)
                P.tt(TT[:, 2 * b:2 * b + k2].rearrange("p a b -> p (a b)"), TT[:, 2 * b:2 * b + k2].rearrange("p a b -> p (a b)"),
                     PS[b][:, 384:384 + 64 * k2], ALU.add)
        sq_mm(Pm, PTm, True)
        Pm, PTm = sq_evac()
        yield
        for lvl in range(2, 6):
            tt_mm(Pm)
            sq_mm(Pm, PTm, lvl < 5)
            Pn, PTn = sq_evac()
            tt_evac()
            Pm, PTm = Pn, PTn
            yield
        tt_mm(Pm)
        tt_evac()
        yield
        X.update(tok=tok, M5=M5, TT=TT)

    def chain_gen(d, n, X, fin):
        nch = n // C
        Sd = S[d]
        At, Rt, ep, tok, M5, TT = X["At"], X["Rt"], X["ep"], X["tok"], X["M5"], X["TT"]
        yT = P.nxt("rw_yT")[:, 0:n]
        wcol = (C - 1) if d == 0 else 0
        order = range(nch) if d == 0 else range(nch - 1, -1, -1)
        bank = PS[7]
        for c in order:
            cs = slice(c * C, (c + 1) * C)
            Btok, Ktok, Vtok = tok[:, c, 0, :], tok[:, c, 1, :], tok[:, c, 2, :]
            AkT, LrbT, LrkT = M5[:, c, 2, :], M5[:, c, 3, :], M5[:, c, 4, :]
            for s in range(2):
                hs = HS[s]
                P.mm(bank[hs, 0:64], At[hs, cs], Sd[hs, :], start=True, stop=False)
                P.mm(bank[hs, 0:64], AkT[hs, :], Vtok[hs, :], start=False, stop=True)
            Xs = P.nxt("rw_Xs")
            P.copy(Xs[:], bank[:, 0:64], eng="act")
            yield
            for s in range(2):
                hs = HS[s]
                P.mm(bank[hs, 64:128], TT[hs, c, :], Xs[hs, :])
            Us = P.nxt("rw_Us")
            P.copy(Us[:], bank[:, 64:128])
            yield
            for s in range(2):
                hs = HS[s]
                P.mm(bank[hs, 192:256], Sd[hs, :], Rt[hs, cs], start=True, stop=False)
                P.mm(bank[hs, 192:256], Us[hs, :], LrbT[hs, :], start=False, stop=False)
                P.mm(bank[hs, 192:256], Vtok[hs, :], LrkT[hs, :], start=False, stop=True)
                P.mm(bank[hs, 128:192], Btok[hs, :], Us[hs, :], start=True, stop=False)
                P.mm(bank[hs, 128:192], Ktok[hs, :], Vtok[hs, :], start=False, stop=True)
            P.copy(yT[:, cs], bank[:, 192:256], eng="act")
            P.tt(Stmp[:], Sd[:], bank[:, 128:192], ALU.add)
            P.ts(Sd[:], Stmp[:], ep[:, c * C + wcol:c * C + wcol + 1], ALU.mult)
            yield
        fin(yT)

    def blocks(seq, T):
        return [(seq, t0, min(TB, T - t0)) for t0 in range(0, T, TB)]
    fwd = blocks("c", Tc) + blocks("l", Tl)
    bwd = blocks("c", Tc)[::-1] + blocks("l", Tl)[::-1]
    P.memset(S[0][:], 0.0); P.memset(S[1][:], 0.0)

    def fin_fwd(seq, t0, n, o):
        def f(yT):
            P.dma(yf[seq].ap()[:, t0:t0 + n], yT)
        return f

    def fin_bwd(seq, t0, n, o):
        def f(yb):
            yfl = P.nxt("rw_yfl")[:, 0:n]
            P.dma(yfl, yf[seq].ap()[:, t0:t0 + n])
            y = P.nxt("rw_t1")[:, 0:n]
            P.tt(y, yb, yfl, ALU.add)
            P.mm(PS[2][:, 0:n], blk[:], y)
            yc = P.nxt("rw_t2")[:, 0:n]
            P.stt(yc, PS[2][:, 0:n], -1.0 / 64, y, ALU.mult, ALU.add)
            sq = P.nxt("rw_sq")[:, 0:n]
            P.tt(sq, yc, yc, ALU.mult)
            P.mm(PS[3][:, 0:n], blk[:], sq)
            sd = P.nxt("rw_rn")[:, 0:n]
            P.act(sd, PS[3][:, 0:n], AF.Sqrt, bias=LN_X_EPS, scale=1.0 / 64)
            P.recip(sd, sd)
            P.tt(yc, yc, sd, ALU.mult)
            ov = P.nxt("rw_t3")[:, 0:n]
            P.ts(ov, yc, pc(PR_LG), ALU.mult, pc(PR_LB), ALU.add)
            ks = P.nxt("rw_t4")[:, 0:n]
            P.tt(ks, o["k0"], o["k1"], ALU.add)
            P.stt(ks, ks, pc(PR_RK), o["r"], ALU.mult, ALU.mult)
            P.mm(PS[4][:, 0:n], blk[:], ks)
            P.tt(ks, PS[4][:, 0:n], o["v"], ALU.mult)
            P.tt(ov, ov, ks, ALU.add)
            P.tt(ov, ov, o["g"], ALU.mult)
            P.dma(odst[seq].ap()[:, t0:t0 + n], ov, is_out=True)
        return f

    for d, blist, finf in ((0, fwd, fin_fwd), (1, bwd, fin_bwd)):
        prev = None
        for (seq, t0, n) in blist:
            o = {}; X = {}

            def both(seq=seq, t0=t0, n=n, o=o, X=X, d=d):
                yield from prep_gen(seq, t0, n, o)
                yield from pre_gen(d, o, n, X)
            if _NO_ILV:
                _interleave([prev]); _interleave([both()])
            else:
                _interleave([prev, both()])
            prev = chain_gen(d, n, X, finf(seq, t0, n, o))
        _interleave([prev])
    return io


def _ilv_gen(gens):
    gens = [g for g in gens if g is not None]
    while gens:
        for g in list(gens):
            try:
                next(g)
                yield
            except StopIteration:
                gens.remove(g)


def ssd_build2(P, PS, Tc, Tl, TB=512):
    Q = SQ
    io = {}
    T_ = {"c": Tc, "l": Tl}
    for sq in ("c", "l"):
        T = T_[sq]
        io["x" + sq] = P.dram("sd_x" + sq, [2, 64, T + 2], F32, "ExternalInput")
        io["b" + sq] = P.dram("sd_b" + sq, [2, 128, T + 2], F32, "ExternalInput")
        io["c" + sq] = P.dram("sd_c" + sq, [2, 128, T + 2], F32, "ExternalInput")
        io["dt" + sq] = P.dram("sd_dt" + sq, [2, 128, T // Q, 2], F32, "ExternalInput")
        io["y" + sq] = P.dram("sd_y" + sq, [2, 128, T // Q, 64], F32, "ExternalOutput")
    io["px"] = P.dram("sd_px", [2, 64, 4], F32, "ExternalInput")
    io["pb"] = P.dram("sd_pb", [2, 128, 4], F32, "ExternalInput")
    io["pc"] = P.dram("sd_pc", [2, 128, 4], F32, "ExternalInput")
    io["pd"] = P.dram("sd_pd", [2, 128, 6], F32, "ExternalInput")
    io["tri"] = P.dram("c_tri", [2, 128, 128], F32, "ExternalInput")
    io["ident"] = P.dram("c_identb", [128, 128], F32, "ExternalInput")
    io["ones"] = P.dram("c_ones", [128, 128], F32, "ExternalInput")
    yfd = {sq: P.dram("sd_yf" + sq, [2, 128, T_[sq] // Q, 64], F32) for sq in ("c", "l")}

    def cst(nm, src, shape):
        t = P.sb("sk_" + nm, shape)
        P.dma(t[:], src)
        return t
    tri = [cst(f"tri{d}", io["tri"].ap()[d], [128, 128]) for d in range(2)]
    ident = cst("ident", io["ident"].ap(), [128, 128])
    ones = cst("ones", io["ones"].ap(), [128, 128])
    px = [cst(f"px{s}", io["px"].ap()[s], [64, 4]) for s in range(2)]
    pb = [cst(f"pb{s}", io["pb"].ap()[s], [128, 4]) for s in range(2)]
    pcc = [cst(f"pc{s}", io["pc"].ap()[s], [128, 4]) for s in range(2)]
    pd = [cst(f"pd{s}", io["pd"].ap()[s], [128, 6]) for s in range(2)]
    Aneg = [P.sb(f"sk_A{s}", [128, 2]) for s in range(2)]
    dsum = [P.sb(f"sk_ds{s}", [128, 1]) for s in range(2)]
    for s in range(2):
        P.act(Aneg[s][:], pd[s][:, 2:4], AF.Exp)
        P.ts(Aneg[s][:], Aneg[s][:], -1.0, ALU.mult)
        P.tt(dsum[s][:], pd[s][:, 4:5], pd[s][:, 5:6], ALU.add)

    NQ = TB // Q
    P.ring("sd_xin", 2, [64, TB + 2]); P.ring("sd_bin", 2, [128, TB + 2]); P.ring("sd_cin", 2, [128, TB + 2])
    P.ring("sd_xf", 4, [64, TB]); P.ring("sd_bf", 4, [128, TB]); P.ring("sd_cf", 4, [128, TB])
    P.ring("sd_dt", 8, [128, NQ, 2]); P.ring("sd_dta", 4, [128, NQ, 2]); P.ring("sd_tmp8", 8, [128, NQ, 2])
    P.ring("sd_xtok", 8, [128, 64]); P.ring("sd_btok", 8, [128, 128]); P.ring("sd_cbm", 16, [128, 128])
    P.ring("sd_bc", 4, [128, 128]); P.ring("sd_E", 8, [128, 128]); P.ring("sd_G", 8, [128, 128]); P.ring("sd_Cs", 8, [128, 128])
    P.ring("sd_col", 16, [128, 4]); P.ring("sd_xdt", 8, [128, 64]); P.ring("sd_xw", 8, [128, 64]); P.ring("sd_st", 8, [128, 64])
    P.ring("sd_yblk", 2, [128, NQ, 64]); P.ring("sd_yfl", 2, [128, NQ, 64])
    H = [[P.sb(f"sd_h{s}{d}", [128, 64]) for d in range(2)] for s in range(2)]
    psrr = [0]

    def psn():
        b = PS[psrr[0] % 8]; psrr[0] += 1
        return b

    def prep_gen(s, seq, t0, n, o):
        nq = n // Q
        for nm, ring_in, ring_f, src, par in (("x", "sd_xin", "sd_xf", io["x" + seq], px[s]),
                                              ("b", "sd_bin", "sd_bf", io["b" + seq], pb[s]),
                                              ("c", "sd_cin", "sd_cf", io["c" + seq], pcc[s])):
            tin = P.nxt(ring_in)
            P.dma(tin[:, 0:n + 2], src.ap()[s][:, t0:t0 + n + 2])
            tf = P.nxt(ring_f)[:, 0:n]
            P.ts(tf, tin[:, 1:n + 1], par[:, 1:2], ALU.mult, par[:, 3:4], ALU.add)
            P.stt(tf, tin[:, 0:n], par[:, 0:1], tf, ALU.mult, ALU.add)
            P.stt(tf, tin[:, 2:n + 2], par[:, 2:3], tf, ALU.mult, ALU.add)
            P.act(tf, tf, AF.Silu)
            o[nm] = tf
            yield
        dtr = P.nxt("sd_dt")[:, 0:nq, :]
        P.dma(dtr, io["dt" + seq].ap()[s][:, t0 // Q:t0 // Q + nq, :])
        dt = P.nxt("sd_dt")[:, 0:nq, :]
        dta = P.nxt("sd_dta")[:, 0:nq, :]
        xb = P.nxt("sd_tmp8")[:, 0:nq, :]
        for d in range(2):
            P.ts(xb[:, :, d:d + 1], dtr[:, :, d:d + 1], pd[s][:, d:d + 1], ALU.add)
        ab = P.nxt("sd_tmp8")[:, 0:nq, :]
        P.act(ab, xb, AF.Abs)
        P.act(ab, ab, AF.Exp, scale=-1.0)
        P.act(ab, ab, AF.Ln, bias=1.0)
        P.ts(xb, xb, 0.0, ALU.max)
        P.tt(dt, xb, ab, ALU.add)
        for d in range(2):
            P.ts(dta[:, :, d:d + 1], dt[:, :, d:d + 1], Aneg[s][:, d:d + 1], ALU.mult)
        o["dt"] = dt; o["dta"] = dta
        yield
        o["xtok"] = []; o["btok"] = []; o["cb"] = []
        for c in range(nq):
            cs = slice(c * Q, (c + 1) * Q)
            pa = psn()
            P.mm(pa[:, 0:64], o["x"][:, cs], ident[0:64, 0:64])
            P.mm(pa[:, 128:256], o["b"][:, cs], ident[:])
            P.mm(pa[:, 256:384], o["b"][:, cs], o["c"][:, cs])
            xt = P.nxt("sd_xtok"); bt = P.nxt("sd_btok")
            P.copy(xt[:], pa[:, 0:64], eng="act")
            P.copy(bt[:], pa[:, 128:256], eng="act")
            cbm = []
            for d in range(2):
                m = P.nxt("sd_cbm")
                P.tt(m[:], pa[:, 256:384], tri[d][:], ALU.mult)
                cbm.append(m)
            o["xtok"].append(xt); o["btok"].append(bt); o["cb"].append(cbm)
            yield

    def chunk_gen(s, d, o, c, R):
        cs = slice(c * Q, (c + 1) * Q)
        dta = o["dta"][:, c, d:d + 1]; dt = o["dt"][:, c, d:d + 1]
        last = Q - 1 if d == 0 else 0
        bc = P.nxt("sd_bc")
        P.ts(bc[:], ones[:], dta, ALU.mult)
        pd_ = psn()
        P.mm(pd_[:, 0:128], bc[:], tri[d][:])
        P.mm(pd_[:, 128:129], tri[d][:], dta)
        col = P.nxt("sd_col")
        P.copy(col[:, 0:1], pd_[:, 128:129])
        yield
        E = P.nxt("sd_E")
        P.ts(E[:], pd_[:, 0:128], col[:, 0:1], ALU.subtract, 0.0, ALU.min)
        P.ts(col[:, 1:2], col[:, 0:1], -1.0, ALU.mult, pd_[:, last:last + 1], ALU.add)
        Cs = P.nxt("sd_Cs")
        P.act(Cs[:], pd_[:, 0:128], AF.Exp)
        P.act(col[:, 2:3], pd_[:, last:last + 1], AF.Exp)
        yield
        P.act(E[:], E[:], AF.Exp)
        P.act(col[:, 1:2], col[:, 1:2], AF.Exp)
        P.tt(Cs[:], Cs[:], o["c"][:, cs], ALU.mult)
        xdt = P.nxt("sd_xdt")
        P.ts(xdt[:], o["xtok"][c][:], dt, ALU.mult)
        yield
        G = P.nxt("sd_G")
        P.tt(G[:], E[:], o["cb"][c][d][:], ALU.mult)
        P.tt(col[:, 1:2], col[:, 1:2], dt, ALU.mult)
        xw = P.nxt("sd_xw")
        P.ts(xw[:], o["xtok"][c][:], col[:, 1:2], ALU.mult)
        yield
        pg_ = psn()
        P.mm(pg_[:, 0:64], o["btok"][c][:], xw[:])
        st = P.nxt("sd_st")
        P.copy(st[:], pg_[:, 0:64], eng="act")
        R[c] = (G, xdt, Cs, col, st)

    def slot_gen(s, d, seq, t0, n):
        nq = n // Q
        o = {}; R = {}
        yield from prep_gen(s, seq, t0, n, o)
        yield from _ilv_gen([chunk_gen(s, d, o, c, R) for c in range(nq)])
        yb = P.nxt("sd_yblk")
        h = H[s][d]
        order = range(nq) if d == 0 else range(nq - 1, -1, -1)
        for c in order:
            G, xdt, Cs, col, st = R[c]
            pf = psn()
            P.mm(pf[:, 0:64], G[:], xdt[:], start=True, stop=False)
            P.mm(pf[:, 0:64], Cs[:], h[:], start=False, stop=True)
            P.copy(yb[:, c, :], pf[:, 0:64], eng="act")
            P.stt(h[:], h[:], col[:, 2:3], st[:], ALU.mult, ALU.add)
            yield
        if d == 0:
            P.dma(yfd[seq].ap()[s][:, t0 // Q:t0 // Q + nq, :], yb[:, 0:nq, :])
        else:
            yfl = P.nxt("sd_yfl")
            P.dma(yfl[:, 0:nq, :], yfd[seq].ap()[s][:, t0 // Q:t0 // Q + nq, :])
            P.tt(yb[:, 0:nq, :], yb[:, 0:nq, :], yfl[:, 0:nq, :], ALU.add)
            for c in range(nq):
                P.stt(yb[:, c, :], o["xtok"][c][:], dsum[s][:, 0:1], yb[:, c, :], ALU.mult, ALU.add)
            P.dma(io["y" + seq].ap()[s][:, t0 // Q:t0 // Q + nq, :], yb[:, 0:nq, :], is_out=True)

    def blocks(seq, T):
        return [(seq, t0, min(TB, T - t0)) for t0 in range(0, T, TB)]
    fwd = blocks("c", Tc) + blocks("l", Tl)
    bwd = blocks("c", Tc)[::-1] + blocks("l", Tl)[::-1]
    for s in range(2):
        for d in range(2):
            P.memset(H[s][d][:], 0.0)
    for d, blist in ((0, fwd), (1, bwd)):
        for (seq, t0, n) in blist:
            _interleave([slot_gen(0, d, seq, t0, n), slot_gen(1, d, seq, t0, n)])
    return io
```
